# Optimizing a Trainium2 kernel written in Bass

```python
import jax, jax.numpy as jnp
from jax import lax
import numpy as np

D_MODEL = 1024
BATCH = 8
SEQ = 4096
DEPTH = 2

N_MIXERS = 2
EPS = 1e-6
NEG_INF = -1e30
HG_HEADS = 8
HG_DK = 128
HG_DV = D_MODEL // HG_HEADS
HG_CHUNK = 64
NSA_HEADS = 16
NSA_KV = 4
NSA_GROUP = NSA_HEADS // NSA_KV
NSA_HD = D_MODEL // NSA_HEADS
CMP_LEN = 32
CMP_STRIDE = 16
SEL_LEN = 64
SEL_TOPK = 16
WINDOW = 512
Q_BLOCK = 32
FORCED_SCORE = 1e4
D_FF = 2816
CONV_W = 3

kernel_name = 'hybrid_hgrn2_nsa_convffn_adaln'


def rmsnorm(x, g):
    xf = x.astype(jnp.float32)
    y = xf * lax.rsqrt(jnp.mean(xf * xf, axis=-1, keepdims=True) + EPS)
    return (y * g).astype(x.dtype)


def alibi_slopes(n):
    return jnp.asarray([2.0 ** (-8.0 * (h + 1) / n) for h in range(n)], jnp.float32)


def hgrn2_mixer(h, w_in, w_out, g_norm, lb):
    B, S, _ = h.shape
    fd = HG_HEADS * HG_DK
    proj = h @ w_in
    q, f_pre, i_in, g = jnp.split(proj, [fd, 2 * fd, 2 * fd + D_MODEL], axis=-1)
    f_pre = f_pre.astype(jnp.float32)
    log_f = jnp.log(lb + (1.0 - lb) * jax.nn.sigmoid(f_pre))
    k = (1.0 - lb) * jax.nn.sigmoid(-f_pre)
    n = S // HG_CHUNK

    def chunks(t, d):
        return t.astype(jnp.float32).reshape(B, n, HG_CHUNK, HG_HEADS, d).transpose(1, 0, 3, 2, 4)

    xs = (chunks(q, HG_DK), chunks(k, HG_DK), chunks(i_in, HG_DV), chunks(log_f, HG_DK))
    causal = jnp.tril(jnp.ones((HG_CHUNK, HG_CHUNK), bool))[:, :, None]

    def step(state, inp):
        qc, kc, vc, lfc = inp
        b = jnp.cumsum(lfc, axis=2)
        o_inter = jnp.einsum('bhtk,bhkv->bhtv', qc * jnp.exp(b), state)
        diff = b[:, :, :, None, :] - b[:, :, None, :, :]
        decay = jnp.exp(jnp.where(causal, diff, NEG_INF))
        scores = jnp.einsum('bhtk,bhsk,bhtsk->bhts', qc, kc, decay)
        o_intra = jnp.einsum('bhts,bhsv->bhtv', scores, vc)
        b_last = b[:, :, -1:, :]
        state = state * jnp.exp(b_last[:, :, 0, :, None]) + jnp.einsum(
            'bhsk,bhsv->bhkv', kc * jnp.exp(b_last - b), vc)
        return state, o_inter + o_intra

    s0 = jnp.zeros((B, HG_HEADS, HG_DK, HG_DV), jnp.float32)
    _, o = lax.scan(step, s0, xs)
    o = o.transpose(1, 0, 3, 2, 4).reshape(B, S, HG_HEADS, HG_DV)
    o = rmsnorm(o, g_norm).reshape(B, S, D_MODEL)
    o = o * jax.nn.silu(g.astype(jnp.float32))
    return o.astype(h.dtype) @ w_out


def nsa_mixer(h, w_in, w_out, cmp_pe, cmp_w1, cmp_w2):
    B, S, _ = h.shape
    G, HPG, HD = NSA_KV, NSA_GROUP, NSA_HD
    nc = (S - CMP_LEN) // CMP_STRIDE + 1
    ns = S // SEL_LEN
    k_sel_n = min(SEL_TOPK, ns)
    kvd = G * HD
    split_at = [D_MODEL + j * kvd for j in range(7)]
    q, k_c, v_c, k_s, v_s, k_w, v_w, gate = jnp.split(h @ w_in, split_at, axis=-1)
    q = q.reshape(B, S, G, HPG, HD)
    gate = jax.nn.sigmoid(gate.astype(jnp.float32)).reshape(B, S, G, HPG, 3)

    def kv(t):
        return t.reshape(B, S, G, HD)

    cidx = (np.arange(nc) * CMP_STRIDE)[:, None] + np.arange(CMP_LEN)[None, :]

    def compress(t, pe, w1, w2):
        blocks = kv(t)[:, cidx] + pe[None, None, :, None, :]
        blocks = blocks.transpose(0, 1, 3, 2, 4).reshape(B, nc, G, CMP_LEN * HD)
        return jax.nn.silu(blocks @ w1) @ w2

    kc = compress(k_c, cmp_pe[0], cmp_w1[0], cmp_w2[0])
    vc = compress(v_c, cmp_pe[1], cmp_w1[1], cmp_w2[1])
    cmp_end = jnp.asarray(np.arange(nc) * CMP_STRIDE + CMP_LEN - 1, jnp.int32)
    cmp_mid = jnp.asarray(np.arange(nc) * CMP_STRIDE + (CMP_LEN - 1) / 2.0, jnp.float32)
    ci = np.arange(nc)[:, None] * CMP_STRIDE
    sj = np.arange(ns)[None, :] * SEL_LEN
    overlap = jnp.asarray((ci <= sj + SEL_LEN - 1) & (ci + CMP_LEN - 1 >= sj), jnp.float32)

    ks_blk = kv(k_s).reshape(B, ns, SEL_LEN, G, HD).transpose(0, 3, 1, 2, 4)
    vs_blk = kv(v_s).reshape(B, ns, SEL_LEN, G, HD).transpose(0, 3, 1, 2, 4)
    pad = ((0, 0), (WINDOW, 0), (0, 0), (0, 0))
    kw_pad = jnp.pad(kv(k_w), pad)
    vw_pad = jnp.pad(kv(v_w), pad)

    slopes = alibi_slopes(NSA_HEADS).reshape(G, HPG)
    scale = HD ** -0.5
    blk_ids = jnp.arange(ns)
    bi = jnp.arange(B)[:, None, None, None]
    gi = jnp.arange(G)[None, :, None, None]

    def block(i):
        q0 = i * Q_BLOCK
        t = q0 + jnp.arange(Q_BLOCK)
        qb = lax.dynamic_slice_in_dim(q, q0, Q_BLOCK, axis=1)
        gb = lax.dynamic_slice_in_dim(gate, q0, Q_BLOCK, axis=1)
        s_c = jnp.einsum('bqghd,bngd->bghqn', qb, kc, preferred_element_type=jnp.float32) * scale
        dist_c = t[:, None].astype(jnp.float32) - cmp_mid[None, :]
        s_c = s_c - slopes[None, :, :, None, None] * dist_c
        valid_c = cmp_end[None, :] <= t[:, None]
        p_c = jax.nn.softmax(jnp.where(valid_c, s_c, NEG_INF), axis=-1)
        p_c = p_c * (t >= CMP_LEN - 1).astype(jnp.float32)[None, None, None, :, None]
        o_c = jnp.einsum('bghqn,bngd->bqghd', p_c, vc)
        imp = jnp.einsum('bghqn,nj->bgqj', p_c, overlap)
        cur = t // SEL_LEN
        valid_s = blk_ids[None, :] * SEL_LEN <= t[:, None]
        forced = ((blk_ids[None, :] == 0) | (blk_ids[None, :] == cur[:, None])
                  | (blk_ids[None, :] == cur[:, None] - 1)) & valid_s
        score = jnp.where(forced, FORCED_SCORE, jnp.where(valid_s, imp, -1.0))
        _, idx = lax.top_k(score, k_sel_n)
        k_sel = ks_blk[bi, gi, idx]
        v_sel = vs_blk[bi, gi, idx].reshape(B, G, Q_BLOCK, k_sel_n * SEL_LEN, HD)
        pos = idx[..., None] * SEL_LEN + jnp.arange(SEL_LEN)
        s_s = jnp.einsum('bqghd,bgqkld->bghqkl', qb, k_sel, preferred_element_type=jnp.float32) * scale
        dist_s = (t[None, None, :, None, None] - pos).astype(jnp.float32)
        s_s = s_s - slopes[None, :, :, None, None, None] * dist_s[:, :, None]
        mask_s = (pos <= t[None, None, :, None, None])[:, :, None]
        s_s = jnp.where(mask_s, s_s, NEG_INF).reshape(B, G, HPG, Q_BLOCK, k_sel_n * SEL_LEN)
        o_s = jnp.einsum('bghqm,bgqmd->bqghd', jax.nn.softmax(s_s, axis=-1), v_sel)
        kw = lax.dynamic_slice_in_dim(kw_pad, q0, Q_BLOCK + WINDOW, axis=1)
        vw = lax.dynamic_slice_in_dim(vw_pad, q0, Q_BLOCK + WINDOW, axis=1)
        s_pos = q0 - WINDOW + jnp.arange(Q_BLOCK + WINDOW)
        s_w = jnp.einsum('bqghd,bsgd->bghqs', qb, kw, preferred_element_type=jnp.float32) * scale
        s_w = s_w - slopes[None, :, :, None, None] * (t[:, None] - s_pos[None, :]).astype(jnp.float32)
        mask_w = (s_pos[None, :] <= t[:, None]) & (s_pos[None, :] > t[:, None] - WINDOW) & (s_pos[None, :] >= 0)
        o_w = jnp.einsum('bghqs,bsgd->bqghd', jax.nn.softmax(jnp.where(mask_w, s_w, NEG_INF), axis=-1), vw)
        return gb[..., 0, None] * o_c + gb[..., 1, None] * o_s + gb[..., 2, None] * o_w

    o = lax.map(block, jnp.arange(S // Q_BLOCK))
    o = o.transpose(1, 0, 2, 3, 4, 5).reshape(B, S, D_MODEL)
    return o.astype(h.dtype) @ w_out


def conv_ffn(h, w_up, conv_w, conv_b, w_down):
    a, v = jnp.split(h @ w_up, 2, axis=-1)
    a = lax.conv_general_dilated(a, conv_w[:, None, :].astype(a.dtype), window_strides=(1,),
                                 padding=[(CONV_W - 1, 0)], dimension_numbers=('NWC', 'WIO', 'NWC'),
                                 feature_group_count=D_FF) + conv_b
    return (jax.nn.silu(a) * v) @ w_down


def setup_inputs(seed: int = 0) -> dict:
    key = jax.random.key(seed)
    ks = jax.random.split(key, 24)
    n_a = (DEPTH + 1) // 2
    n_b = DEPTH // 2
    D = D_MODEL
    nsa_in = D + 6 * NSA_KV * NSA_HD + 3 * NSA_HEADS

    def nrm(k, shape, s):
        return jax.random.normal(k, shape, jnp.float32) * s

    return {
        'x': nrm(ks[0], (BATCH, SEQ, D), 1.0),
        'c': nrm(ks[1], (BATCH, D), 1.0),
        'ada_w': nrm(ks[2], (DEPTH, D, 6 * D), 0.5 * D ** -0.5),
        'ada_b': nrm(ks[3], (DEPTH, 6 * D), 0.02),
        'norm_mix': 1.0 + nrm(ks[4], (DEPTH, D), 0.02),
        'norm_ffn': 1.0 + nrm(ks[5], (DEPTH, D), 0.02),
        'final_norm': 1.0 + nrm(ks[6], (D,), 0.02),
        'hg_w_in': nrm(ks[7], (n_a, D, 2 * HG_HEADS * HG_DK + 2 * D), D ** -0.5),
        'hg_w_out': nrm(ks[8], (n_a, D, D), D ** -0.5),
        'hg_gnorm': 1.0 + nrm(ks[9], (n_a, HG_DV), 0.02),
        'hg_lb': nrm(ks[10], (n_a + 1, HG_HEADS * HG_DK), 0.5),
        'nsa_w_in': nrm(ks[11], (n_b, D, nsa_in), D ** -0.5),
        'nsa_w_out': nrm(ks[12], (n_b, D, D), D ** -0.5),
        'nsa_cmp_pe': nrm(ks[13], (n_b, 2, CMP_LEN, NSA_HD), 0.1),
        'nsa_cmp_w1': nrm(ks[14], (n_b, 2, CMP_LEN * NSA_HD, NSA_HD), (CMP_LEN * NSA_HD) ** -0.5),
        'nsa_cmp_w2': nrm(ks[15], (n_b, 2, NSA_HD, NSA_HD), NSA_HD ** -0.5),
        'ffn_w_up': nrm(ks[16], (DEPTH, D, 2 * D_FF), D ** -0.5),
        'ffn_conv_w': nrm(ks[17], (DEPTH, CONV_W, D_FF), CONV_W ** -0.5),
        'ffn_conv_b': nrm(ks[18], (DEPTH, D_FF), 0.02),
        'ffn_w_down': nrm(ks[19], (DEPTH, D_FF, D), D_FF ** -0.5),
    }


def reference(x, c, ada_w, ada_b, norm_mix, norm_ffn, final_norm, hg_w_in, hg_w_out, hg_gnorm, hg_lb,
              nsa_w_in, nsa_w_out, nsa_cmp_pe, nsa_cmp_w1, nsa_cmp_w2, ffn_w_up, ffn_conv_w, ffn_conv_b,
              ffn_w_down):
    lb_all = jnp.cumsum(jax.nn.softmax(hg_lb.astype(jnp.float32), axis=0), axis=0)
    c_act = jax.nn.silu(c)
    for layer in range(DEPTH):
        mod = c_act @ ada_w[layer] + ada_b[layer]
        sh1, sc1, g1, sh2, sc2, g2 = [m[:, None, :] for m in jnp.split(mod, 6, axis=-1)]
        hmix = rmsnorm(x, norm_mix[layer]) * (1.0 + sc1) + sh1
        j = layer // N_MIXERS
        if layer % N_MIXERS == 0:
            y = hgrn2_mixer(hmix, hg_w_in[j], hg_w_out[j], hg_gnorm[j], lb_all[j])
        else:
            y = nsa_mixer(hmix, nsa_w_in[j], nsa_w_out[j], nsa_cmp_pe[j], nsa_cmp_w1[j], nsa_cmp_w2[j])
        x = x + g1 * y
        hffn = rmsnorm(x, norm_ffn[layer]) * (1.0 + sc2) + sh2
        x = x + g2 * conv_ffn(hffn, ffn_w_up[layer], ffn_conv_w[layer], ffn_conv_b[layer], ffn_w_down[layer])
    return rmsnorm(x, final_norm)
```

```python
import contextlib
import numpy as np
import concourse.bass as bass
import concourse.mybir as mybir

F32 = mybir.dt.float32
BF16 = mybir.dt.bfloat16
AF = mybir.ActivationFunctionType
ALU = mybir.AluOpType
AX = mybir.AxisListType

ENGS = ["pe", "act", "dve", "pool", "sp"]


class T:
    __slots__ = ("name", "ap", "last_w", "readers", "dsem", "dcnt", "uid")
    _n = [0]

    def __init__(self, name, ap=None):
        T._n[0] += 1
        self.uid = T._n[0]
        self.name = name
        self.ap = ap
        self.last_w = None
        self.readers = []
        self.dsem = None
        self.dcnt = 0

    def __getitem__(self, idx):
        return self.ap[idx]


class Sched:
    def __init__(self, nc, stack):
        self.nc = nc
        self.stack = stack
        self.ops = {e: [] for e in ENGS}
        self.cnt = {e: 0 for e in ENGS}
        self.clock = {e: {} for e in ENGS}
        self.sem = {}
        for e in ["pe", "act", "dve", "pool"]:
            self.sem[e] = stack.enter_context(nc.semaphore("s_" + e))
        self.sem["bar"] = stack.enter_context(nc.semaphore("s_bar"))
        self.bar_n = 0
        self.dma_live = {}
        self.gstack = stack
        self.nsem = 5
        self.final_waits = []
        self.n_wait = 0
        self.uid = 0
        self.dsem_pool = []
        self.dsem_owner = []

    def sbuf(self, name, shape, dtype):
        self.uid += 1
        name = "%s_u%d" % (name, self.uid)
        t = self.stack.enter_context(self.nc.sbuf_tensor(name, list(shape), dtype))
        return T(name, t)

    def psum(self, name, shape, dtype=F32):
        self.uid += 1
        name = "%s_u%d" % (name, self.uid)
        t = self.stack.enter_context(self.nc.psum_tensor(name, list(shape), dtype))
        return T(name, t)

    def view(self, name, ap):
        return T(name, ap)

    def _dsem(self, t):
        if t.dsem is None:
            if self.dsem_pool:
                t.dsem, t.dcnt = self.dsem_pool.pop()
            else:
                self.nsem += 1
                t.dsem = self.gstack.enter_context(self.nc.semaphore("dsem%d" % self.nsem))
                t.dcnt = 0
            self.dsem_owner.append(t)
        return t.dsem

    def phase_end(self):
        for t in self.dsem_owner:
            self.dsem_pool.append((t.dsem, t.dcnt))
            t.dsem = None
        self.dsem_owner = []
        self.dma_live = {}

    def _need(self, eng, ev, waits):
        key, val, snap = ev
        if eng == "pe" and key == "pe":
            return
        ck = self.clock[eng]
        if ck.get(key, 0) >= val:
            return
        waits[key] = max(waits.get(key, 0), val)
        ck[key] = val
        if snap:
            for k, v in snap.items():
                if ck.get(k, 0) < v:
                    ck[k] = v

    def op(self, eng, fn, reads=(), writes=(), dma_sem_tile=None):
        waits = {}
        for t in reads:
            if t.last_w is not None:
                self._need(eng, t.last_w, waits)
        for t in writes:
            if t.last_w is not None:
                self._need(eng, t.last_w, waits)
            for ev in t.readers:
                self._need(eng, ev, waits)
        if dma_sem_tile is not None:
            st = dma_sem_tile
            sem = self._dsem(st)
            st.dcnt += 16
            key = ("d", st.uid)
            self.sem[key] = sem
            ev = (key, st.dcnt, dict(self.clock[eng]))
            self.dma_live[key] = (st, st.dcnt)
            inc = (sem, 16)
        else:
            self.cnt[eng] += 1
            ev = (eng, self.cnt[eng], None)
            inc = (self.sem[eng], 1)
        self.ops[eng].append((list(waits.items()), fn, inc))
        self.n_wait += len(waits)
        if dma_sem_tile is None:
            snap = dict(self.clock[eng])
            ev = (eng, self.cnt[eng], snap)
        for t in writes:
            t.last_w = ev
            t.readers = []
        for t in reads:
            if t not in writes:
                t.readers.append(ev)
        return ev

    def finish_wait(self, eng, tiles):
        waits = {}
        for t in tiles:
            if t.last_w is not None:
                self._need(eng, t.last_w, waits)
            for ev in t.readers:
                self._need(eng, ev, waits)
        self.ops[eng].append((list(waits.items()), None, None))

    def barrier(self):
        evs = []
        for e in ["pe", "act", "dve", "pool"]:
            if self.cnt[e] > 0:
                evs.append((e, self.cnt[e], None))
        for key, (t, val) in self.dma_live.items():
            evs.append((key, val, None))
        for eng in ENGS:
            waits = {}
            for ev in evs:
                if ev[0] == eng and eng == "pe":
                    continue
                ck = self.clock[eng]
                if ck.get(ev[0], 0) < ev[1]:
                    waits[ev[0]] = ev[1]
                    ck[ev[0]] = ev[1]
            self.ops[eng].append((list(waits.items()), None, None))
        self.bar_n += 1
        for eng in ENGS:
            self.ops[eng].append(([], "barinc", None))
        for eng in ENGS:
            self.ops[eng].append(([("bar", 5 * self.bar_n)], None, None))

    def emit(self):
        nc = self.nc
        with nc.Block() as block:
            def run(eng_name):
                def body(e):
                    for waits, fn, inc in self.ops[eng_name]:
                        for key, val in waits:
                            e.wait_ge(self.sem[key], val)
                        if fn == "barinc":
                            e.sem_inc(self.sem["bar"], 1)
                        elif fn is not None:
                            ins = fn(e)
                            ins.then_inc(inc[0], inc[1])
                return body
            block.tensor(run("pe"))
            block.scalar(run("act"))
            block.vector(run("dve"))
            block.gpsimd(run("pool"))
            block.sync(run("sp"))
        self.ops = {e: [] for e in ENGS}

    def dma(self, out_t, out_ap, in_ap, in_t=None, eng="sp", **kw):
        reads = [in_t] if in_t is not None else []
        writes = [out_t] if out_t is not None else []
        st = out_t if out_t is not None else in_t
        return self.op(eng, lambda e: e.dma_start(out=out_ap, in_=in_ap, **kw), reads, writes,
                       dma_sem_tile=st)

    def mm(self, out_t, out_ap, lhsT, rhs, reads, start=True, stop=True, **kw):
        return self.op("pe", lambda e: e.matmul(out_ap, lhsT, rhs, start=start, stop=stop, **kw),
                       reads, [out_t])

    def transpose(self, out_t, out_ap, in_ap, ident_ap, reads):
        return self.op("pe", lambda e: e.transpose(out_ap, in_ap, ident_ap), reads, [out_t])

    def act(self, out_t, out_ap, in_ap, func, reads, bias=None, scale=None, accum_out=None,
            extra_writes=()):
        kw = {}
        if bias is not None:
            kw["bias"] = bias
        if scale is not None:
            kw["scale"] = scale
        if accum_out is not None:
            kw["accum_out"] = accum_out
        return self.op("act", lambda e: e.activation(out_ap, in_ap, func, **kw), reads,
                       [out_t] + list(extra_writes))

    def tt(self, eng, out_t, out_ap, in0, in1, op, reads):
        return self.op(eng, lambda e: e.tensor_tensor(out_ap, in0, in1, op), reads, [out_t])

    def ts(self, eng, out_t, out_ap, in0, s1, s2, op0, op1=None, reads=(), accum_out=None,
           extra_writes=()):
        def f(e):
            kw = {}
            if accum_out is not None:
                kw["accum_out"] = accum_out
            if op1 is None:
                return e.tensor_scalar(out_ap, in0, s1, None, op0, **kw)
            return e.tensor_scalar(out_ap, in0, s1, s2, op0, op1, **kw)
        return self.op(eng, f, reads, [out_t] + list(extra_writes))

    def stt(self, eng, out_t, out_ap, in0, scalar, in1, op0, op1, reads):
        eng = "dve"
        return self.op(eng, lambda e: e.scalar_tensor_tensor(out_ap, in0, scalar, in1, op0, op1),
                       reads, [out_t])

    def copy(self, eng, out_t, out_ap, in_ap, reads):
        if eng == "act":
            return self.op("act", lambda e: e.copy(out_ap, in_ap), reads, [out_t])
        return self.op(eng, lambda e: e.tensor_copy(out_ap, in_ap), reads, [out_t])

    def memset(self, eng, out_t, out_ap, val):
        return self.op(eng, lambda e: e.memset(out_ap, val), [], [out_t])

from concourse.bass_utils import run_bass_kernel_spmd

D = 1024
NH_HG = 8
DFF = 2816
NFC = DFF // 128
EPS = 1e-6
NEG = -30000.0


def bcast_rows(ap_row, n):
    return ap_row.partition_broadcast(n)


class K:
    pass


def load_weight_bf16(S, Wb, w_dram, KC, N, stg, grow=None, col_off=0, rowscale=None, kc_off=0):
    i = 0
    for kc in range(KC):
        for n0 in range(0, N, 1024):
            n1 = min(N, n0 + 1024)
            st = stg[i % len(stg)]
            S.dma(st, st[:, 0:n1 - n0], w_dram[kc * 128:(kc + 1) * 128, n0:n1])
            eng = "pool" if i % 2 == 0 else "dve"
            o = Wb[:, kc_off + kc, col_off + n0:col_off + n1]
            if grow is None:
                S.copy(eng, Wb, o, st[:, 0:n1 - n0], [st])
            elif rowscale is None:
                S.tt(eng, Wb, o, st[:, 0:n1 - n0], grow[:, n0:n1], ALU.mult, [st, grow])
            else:
                S.stt(eng, Wb, o, st[:, 0:n1 - n0], rowscale[:, kc:kc + 1], grow[:, n0:n1],
                      ALU.mult, ALU.mult, [st, grow, rowscale])
            i += 1


def rstd_from_ss(S, rstd, ss, n, width):
    S.ts("dve", rstd, rstd[:, 0:width], ss[:, 0:width], 1.0 / n, EPS, ALU.mult, ALU.add, reads=[ss])
    S.act(rstd, rstd[:, 0:width], rstd[:, 0:width], AF.Ln, [rstd])
    S.act(rstd, rstd[:, 0:width], rstd[:, 0:width], AF.Exp, [rstd], scale=-0.5)


def norm_to_hT(S, k, xt_t, xt_ap, gcol, shcol, hT, col0, xn, sq, ss, rstd, pT):
    S.act(sq, sq[:], xt_ap, AF.Square, [xt_t], accum_out=ss[:, 0:1], extra_writes=[ss])
    rstd_from_ss(S, rstd, ss, D, 1)
    S.act(xn, xn[:], xt_ap, AF.Copy, [xt_t, rstd], scale=rstd[:, 0:1])
    for kc in range(8):
        S.transpose(pT, pT[:, kc * 128:(kc + 1) * 128], xn[:, kc * 128:(kc + 1) * 128],
                    k.ident[:], [xn, k.ident])
    for kc in range(8):
        eng = "dve" if kc % 2 == 0 else "pool"
        eng = "dve"
        S.ts(eng, hT, hT[:, kc, col0:col0 + 128], pT[:, kc * 128:(kc + 1) * 128],
             gcol[:, kc:kc + 1], shcol[:, kc:kc + 1], ALU.mult, ALU.add, reads=[pT, gcol, shcol])


def rows_to_cols(S, k, rows, name):
    n = sum(r.shape[0] // 128 for r in rows)
    assert n <= 128
    rt = S.sbuf(name + "_r", [n, 128], F32)
    ct = S.sbuf(name, [128, n], F32)
    j = 0
    for r in rows:
        m = r.shape[0] // 128
        S.dma(rt, rt[j:j + m, :], r.rearrange("(j p) -> j p", p=128))
        j += m
    ps = k.ps[1]
    S.mm(ps, ps[:, 0:n], rt[0:n, :], k.identf[0:n, 0:n], [rt, k.identf])
    S.copy("dve", ct, ct[:], ps[:, 0:n], [ps])
    return ct


def phase_prologue(S, k):
    cc = S.sbuf("cc", [128, 8], F32)
    ca = S.sbuf("ca", [128, 8], F32)
    S.dma(cc, cc[:], k.c_col[:, :])
    S.act(ca, ca[:], cc[:], AF.Silu, [cc])
    wst = [S.sbuf("adw%d" % i, [128, 8, 512], F32) for i in range(2)]
    brow = S.sbuf("brow", [1, 6144], F32)
    mrow = S.sbuf("mrow", [1, 6144], F32)
    i = 0
    for l in range(2):
        S.dma(brow, brow[:], k.ada_b[l:l + 1, :])
        for nt in range(12):
            wt = wst[i % 2]
            S.dma(wt, wt[:], k.ada_w[l].rearrange("(kc p) n -> p kc n", p=128)[:, :, nt * 512:(nt + 1) * 512])
            ps = k.ps[2 + (i % 2)]
            for kc in range(8):
                S.mm(ps, ps[0:1, :], ca[:, kc:kc + 1], wt[:, kc, :], [ca, wt],
                     start=(kc == 0), stop=(kc == 7))
            S.tt("dve", mrow, mrow[0:1, nt * 512:(nt + 1) * 512], ps[0:1, :],
                 brow[0:1, nt * 512:(nt + 1) * 512], ALU.add, [ps, brow])
            i += 1
        S.dma(None, k.modd[l:l + 1, :], mrow[:], in_t=mrow)
    S.finish_wait("sp", [mrow])


def load_consts(S, k):
    k.identf = S.sbuf("identf", [128, 128], F32)
    k.ident = S.sbuf("ident", [128, 128], BF16)
    S.dma(k.identf, k.identf[:], k.c_ident[:, :])
    S.copy("dve", k.ident, k.ident[:], k.identf[:], [k.identf])


def alloc_psum(S, k):
    k.ps = [S.psum("psb0", [128, 1024], BF16)] + [S.psum("ps%d" % i, [128, 512], F32) for i in range(1, 8)]


def phase_hgrn(S, k, l, xin, xout, NT):
    TT = 256
    ntile = NT // TT
    load_consts(S, k)
    alloc_psum(S, k)
    j = 0
    cols = rows_to_cols(S, k, [k.modd[l, :], k.norm_mix[l, :], k.hg_lb[0, :], k.hg_lb[1, :],
                               k.hg_gnorm[0, :]], "hcols")
    gnc = S.sbuf("gnc", [128, 8], F32)
    S.copy("dve", gnc, gnc[:], cols[:, 72:73].to_broadcast([128, 8]), [cols])
    gcol = S.sbuf("gcol", [128, 8], F32)
    S.stt("dve", gcol, gcol[:], cols[:, 8:16], 1.0, cols[:, 48:56], ALU.add, ALU.mult, [cols])
    lbc = S.sbuf("lbc", [128, 8], F32)
    l1m = S.sbuf("l1m", [128, 8], F32)
    S.tt("dve", lbc, lbc[:], cols[:, 64:72], cols[:, 56:64], ALU.subtract, [cols])
    S.act(lbc, lbc[:], lbc[:], AF.Exp, [lbc])
    S.ts("dve", lbc, lbc[:], lbc[:], 1.0, None, ALU.add, reads=[lbc])
    S.op("dve", lambda e: e.reciprocal(lbc[:], lbc[:]), [lbc], [lbc])
    S.ts("dve", l1m, l1m[:], lbc[:], -1.0, 1.0, ALU.mult, ALU.add, reads=[lbc])
    S.act(l1m, l1m[:], l1m[:], AF.Ln, [l1m])
    sq = S.sbuf("sq", [128, 1024], F32)
    g1row = sq
    S.dma(g1row, g1row[:], k.modd[l, 2048:3072].partition_broadcast(128))
    Win = S.sbuf("Win", [128, 8, 4096], BF16)
    Wout = S.sbuf("Wout", [128, 8, 1024], BF16)
    stg = [S.sbuf("stg%d" % i, [128, 1024], F32) for i in range(2)]
    load_weight_bf16(S, Win, k.hg_w_in[0], 8, 4096, stg)
    load_weight_bf16(S, Wout, k.hg_w_out[0], 8, 1024, stg, grow=g1row, rowscale=gnc)
    rmask = S.sbuf("rmask", [128, TT], F32)
    S.memset("pool", rmask, rmask[:], 1.0)
    S.memset("pool", rmask, rmask[:].rearrange("p (c j) -> p c j", j=64)[:, :, 0:1], 0.0)
    cmask = S.sbuf("cmask", [128, 128], F32)
    S.dma(cmask, cmask[:], k.c_bdmask[:, :])
    st32 = S.sbuf("st32", [128, 8, 128], F32)
    stb = S.sbuf("stb", [128, 8, 128], BF16)
    S.memset("pool", st32, st32[:], 0.0)
    S.memset("pool", stb, stb[:], 0.0)
    sts = [S.sbuf("sts%d" % i, [128, 128], F32) for i in range(2)]
    xts = [S.sbuf("xt%d" % i, [128, 1024], F32) for i in range(3)]
    xn = S.sbuf("xn", [128, 1024], BF16)
    ss = S.sbuf("ss", [128, 8], F32)
    rstd = S.sbuf("rstd", [128, 8], F32)
    hT = S.sbuf("hT", [128, 8, TT], BF16)
    tu = S.sbuf("tu", [128, TT], F32)
    tA = S.sbuf("tA", [128, TT], F32)
    tB = S.sbuf("tB", [128, TT], F32)
    tb = S.sbuf("tb", [128, TT], F32)
    teb = S.sbuf("teb", [128, TT], F32)
    t1 = S.sbuf("t1", [128, TT], F32)
    qdT = S.sbuf("qdT", [128, 8, 2, 2, 128], BF16)
    kdT = S.sbuf("kdT", [128, 8, TT], BF16)
    kdtok = S.sbuf("kdtok", [128, 2, 8, 128], BF16)
    ebl = S.sbuf("ebl", [128, 8, 4], F32)
    vt = S.sbuf("vt", [128, 2, 1024], BF16)
    gs = S.sbuf("gs", [128, 2, 1024], BF16)
    otok = S.sbuf("otok", [128, 1024], F32)
    oss = S.sbuf("oss", [128, 8], F32)
    orstd = S.sbuf("orstd", [128, 8], F32)
    on = S.sbuf("on", [128, 1024], BF16)
    oT = S.sbuf("oT", [128, 8, TT], BF16)
    k.sc4 = [S.sbuf("sc4%d" % i, [128, 512], BF16) for i in range(2)]
    k.st1 = [S.sbuf("st1_%d" % i, [128, 128], F32) for i in range(8)]
    k.st1b = [S.sbuf("st1b_%d" % i, [128, 128], BF16) for i in range(8)]
    xo = [S.sbuf("xo%d" % i, [128, 1024], F32) for i in range(2)]
    S.memset("pool", qdT, qdT[:], 0.0)
    pT = k.ps[0]
    for T0 in range(ntile):
        r0 = T0 * TT
        for sub in range(2):
            xt = xts[(T0 * 2 + sub) % 3]
            S.dma(xt, xt[:], xin[r0 + sub * 128:r0 + (sub + 1) * 128, :])
            norm_to_hT(S, k, xt, xt[:], gcol, cols, hT, sub * 128, xn, sq, ss, rstd, pT)
        for h in range(8):
            pq = k.ps[1 + (h % 2)]
            for kc in range(8):
                S.mm(pq, pq[:, 0:TT], Win[:, kc, h * 128:(h + 1) * 128], hT[:, kc, :], [Win, hT],
                     start=(kc == 0), stop=(kc == 7))
            for kc in range(8):
                S.mm(pq, pq[:, TT:2 * TT], Win[:, kc, 1024 + h * 128:1024 + (h + 1) * 128], hT[:, kc, :],
                     [Win, hT], start=(kc == 0), stop=(kc == 7))
            z = pq[:, TT:2 * TT]
            S.act(tu, tu[:], z, AF.Exp, [pq], scale=-1.0)
            S.act(tA, tA[:], tu[:], AF.Ln, [tu], bias=1.0)
            S.act(tB, tB[:], tu[:], AF.Ln, [tu, lbc], bias=1.0, scale=lbc[:, h:h + 1])
            S.tt("pool", tB, tB[:], tB[:], tA[:], ALU.subtract, [tB, tA])
            S.op("dve", lambda e, tb=tb, tB=tB: e.tensor_tensor_scan(tb[:], rmask[:], tB[:], 0.0, ALU.mult, ALU.add),
                 [rmask, tB], [tb])
            S.act(teb, teb[:], tb[:], AF.Exp, [tb])
            for sub in range(2):
                for c2 in range(2):
                    cs = sub * 128 + c2 * 64
                    S.tt("dve", qdT, qdT[:, h, sub, c2, c2 * 64:(c2 + 1) * 64], pq[:, cs:cs + 64],
                         teb[:, cs:cs + 64], ALU.mult, [pq, teb])
            S.tt("dve", t1, t1[:], z, tA[:], ALU.add, [pq, tA])
            S.tt("pool", t1, t1[:], t1[:], tb[:], ALU.add, [t1, tb])
            S.act(kdT, kdT[:, h, :], t1[:], AF.Exp, [t1, l1m], scale=-1.0, bias=l1m[:, h:h + 1])
            S.copy("pool", ebl, ebl[:, h, :], teb[:].rearrange("p (c j) -> p c j", j=64)[:, :, 63], [teb])
        for sub in range(2):
            for half in range(4):
                pv = k.ps[3 + (half % 2)]
                c0 = 2048 + half * 512
                for kc in range(8):
                    S.mm(pv, pv[:], hT[:, kc, sub * 128:(sub + 1) * 128], Win[:, kc, c0:c0 + 512],
                         [hT, Win], start=(kc == 0), stop=(kc == 7))
                if half < 2:
                    S.copy("dve", vt, vt[:, sub, half * 512:(half + 1) * 512], pv[:], [pv])
                else:
                    S.act(gs, gs[:, sub, (half - 2) * 512:(half - 1) * 512], pv[:], AF.Silu, [pv])
        for sub in range(2):
            for h in range(8):
                S.transpose(pT, pT[:, h * 128:(h + 1) * 128], kdT[:, h, sub * 128:(sub + 1) * 128],
                            k.ident[:], [kdT, k.ident])
            S.copy("act", kdtok, kdtok[:, sub, :, :].rearrange("p h k -> p (h k)"), pT[:], [pT])
        for sub in range(2):
            for hg in range(2):
                pA, pB, pC, pD = k.ps[1], k.ps[2], k.ps[5], k.ps[6]
                if hg == 1:
                    pA, pB, pC, pD = k.ps[3], k.ps[4], k.ps[7], k.ps[6]
                hs = list(range(hg * 4, hg * 4 + 4))
                for i, h in enumerate(hs):
                    S.mm(pA, pA[:, i * 128:i * 128 + 64], kdT[:, h, sub * 128:(sub + 1) * 128],
                         qdT[:, h, sub, 0, 0:64], [kdT, qdT])
                    S.mm(pA, pA[:, i * 128 + 64:(i + 1) * 128], kdT[:, h, sub * 128:(sub + 1) * 128],
                         qdT[:, h, sub, 1, 64:128], [kdT, qdT])
                    S.mm(pB, pB[:, i * 128:(i + 1) * 128], kdtok[0:64, sub, h, :],
                         vt[0:64, sub, h * 128:(h + 1) * 128], [kdtok, vt])
                sc4 = k.sc4[hg]
                S.tt("dve", sc4, sc4[:].rearrange("p (i t) -> p i t", t=128),
                     pA[:].rearrange("p (i t) -> p i t", t=128),
                     cmask[:].unsqueeze(1).to_broadcast([128, 4, 128]), ALU.mult, [pA, cmask])
                for i, h in enumerate(hs):
                    c = sub * 2
                    stt_ = sts[i % 2]
                    S.ts("pool", stt_, stt_[:], st32[:, h, :], ebl[:, h, c:c + 1], None, ALU.mult,
                         reads=[st32, ebl])
                    S.stt("dve", k.st1[h], k.st1[h][:], pB[:, i * 128:(i + 1) * 128], ebl[:, h, c:c + 1],
                          stt_[:], ALU.mult, ALU.add, [pB, ebl, stt_])
                    S.copy("act", k.st1b[h], k.st1b[h][:], k.st1[h][:], [k.st1[h]])
                for i, h in enumerate(hs):
                    o_ap = pC[:, i * 128:(i + 1) * 128]
                    S.mm(pC, o_ap, sc4[:, i * 128:(i + 1) * 128], vt[:, sub, h * 128:(h + 1) * 128],
                         [sc4, vt], start=True, stop=False)
                    S.mm(pC, o_ap, qdT[:, h, sub, 0, :], stb[:, h, :], [qdT, stb], start=False, stop=False)
                    S.mm(pC, o_ap, qdT[:, h, sub, 1, :], k.st1b[h][:], [qdT, k.st1b[h]], start=False, stop=True)
                    S.mm(pD, pD[:, i * 128:(i + 1) * 128], kdtok[64:128, sub, h, :],
                         vt[64:128, sub, h * 128:(h + 1) * 128], [kdtok, vt])
                for i, h in enumerate(hs):
                    c = sub * 2 + 1
                    stt_ = sts[i % 2]
                    S.ts("pool", stt_, stt_[:], k.st1[h][:], ebl[:, h, c:c + 1], None, ALU.mult,
                         reads=[k.st1[h], ebl])
                    S.stt("dve", st32, st32[:, h, :], pD[:, i * 128:(i + 1) * 128], ebl[:, h, c:c + 1],
                          stt_[:], ALU.mult, ALU.add, [pD, ebl, stt_])
                    S.copy("act", stb, stb[:, h, :], st32[:, h, :], [st32])
                S.copy("dve", otok, otok[:, hg * 512:(hg + 1) * 512], pC[:], [pC])
                for i, h in enumerate(hs):
                    S.act(sq, sq[:, 0:128], otok[:, h * 128:(h + 1) * 128], AF.Square, [otok],
                          accum_out=oss[:, h:h + 1], extra_writes=[oss])
            rstd_from_ss(S, orstd, oss, 128, 8)
            S.tt("dve", otok, otok[:].rearrange("p (h v) -> p h v", v=128),
                 otok[:].rearrange("p (h v) -> p h v", v=128),
                 orstd[:].unsqueeze(2).to_broadcast([128, 8, 128]), ALU.mult, [otok, orstd])
            S.tt("pool", on, on[:], otok[:], gs[:, sub, :], ALU.mult, [otok, gs])
            for kc in range(8):
                S.transpose(pT, pT[:, kc * 128:(kc + 1) * 128], on[:, kc * 128:(kc + 1) * 128],
                            k.ident[:], [on, k.ident])
            S.copy("act", oT, oT[:, :, sub * 128:(sub + 1) * 128],
                   pT[:].rearrange("p (c t) -> p c t", t=128), [pT])
            xt = xts[(T0 * 2 + sub) % 3]
            xo_ = xo[sub]
            for half in range(2):
                py = k.ps[3 + half]
                for kc in range(8):
                    S.mm(py, py[:], oT[:, kc, sub * 128:(sub + 1) * 128], Wout[:, kc, half * 512:(half + 1) * 512],
                         [oT, Wout], start=(kc == 0), stop=(kc == 7))
                S.tt("dve", xo_, xo_[:, half * 512:(half + 1) * 512], py[:], xt[:, half * 512:(half + 1) * 512],
                     ALU.add, [py, xt])
            S.dma(None, xout[r0 + sub * 128:r0 + (sub + 1) * 128, :], xo_[:], in_t=xo_)
    S.finish_wait("sp", xo)


def phase_ffn(S, k, l, xnorm, xres, xout, NT, fc0, fc1, final=False):
    TT = 512
    ntile = NT // TT
    nfc = fc1 - fc0
    W = nfc * 128
    same = xres is xnorm
    load_consts(S, k)
    alloc_psum(S, k)
    cols = rows_to_cols(S, k, [k.modd[l, :], k.norm_ffn[l, :]], "fcols")
    gcol = S.sbuf("gcol", [128, 8], F32)
    S.stt("dve", gcol, gcol[:], cols[:, 32:40], 1.0, cols[:, 48:56], ALU.add, ALU.mult, [cols])
    shc = S.sbuf("shc", [128, 8], F32)
    S.copy("dve", shc, shc[:], cols[:, 24:32], [cols])
    ccols = rows_to_cols(S, k, [k.ffn_conv_w[l, 0, :], k.ffn_conv_w[l, 1, :], k.ffn_conv_w[l, 2, :],
                                k.ffn_conv_b[l, :]], "ccols")
    sq = S.sbuf("sq", [128, 1024], F32)
    g2row = sq
    S.dma(g2row, g2row[:], k.modd[l, 5120:6144].partition_broadcast(128))
    if final:
        fnrow = S.sbuf("fnrow", [128, 1024], F32)
        S.dma(fnrow, fnrow[:], k.final_norm[:].partition_broadcast(128))
    Wup = S.sbuf("Wup", [128, 8, 2 * W], BF16)
    Wdn = S.sbuf("Wdn", [128, nfc, 1024], BF16)
    stg = [S.sbuf("stg%d" % i, [128, 1024], F32) for i in range(3)]
    load_weight_bf16(S, Wup, k.ffn_w_up[l][:, fc0 * 128:fc1 * 128], 8, W, stg)
    load_weight_bf16(S, Wup, k.ffn_w_up[l][:, DFF + fc0 * 128:DFF + fc1 * 128], 8, W, stg, col_off=W)
    load_weight_bf16(S, Wdn, k.ffn_w_down[l][fc0 * 128:fc1 * 128, :], nfc, 1024, stg, grow=g2row)
    xts = [S.sbuf("xt%d" % i, [128, 1024], F32) for i in range(5 if same else 2)]
    xrs = xts if same else [S.sbuf("xr%d" % i, [128, 1024], F32) for i in range(2)]
    xn = S.sbuf("xn", [128, 1024], BF16)
    ss = S.sbuf("ss", [128, 8], F32)
    rstd = S.sbuf("rstd", [128, 8], F32)
    hT = S.sbuf("hT", [128, 8, TT], BF16)
    halo = S.sbuf("halo", [128, nfc, 2], F32)
    S.memset("pool", halo, halo[:], 0.0)
    abuf = [S.sbuf("abuf%d" % i, [128, TT + 2], F32) for i in range(2)]
    c1 = [S.sbuf("c1_%d" % i, [128, TT], F32) for i in range(2)]
    c2 = [S.sbuf("c2_%d" % i, [128, TT], F32) for i in range(2)]
    mT = S.sbuf("mT", [128, nfc, TT], BF16)
    xo = [S.sbuf("xo%d" % i, [128, 1024], F32) for i in range(2)]
    pT = k.ps[0]
    for T0 in range(ntile):
        r0 = T0 * TT
        for sub in range(4):
            xt = xts[(T0 * 4 + sub) % len(xts)]
            S.dma(xt, xt[:], xnorm[r0 + sub * 128:r0 + (sub + 1) * 128, :])
            norm_to_hT(S, k, xt, xt[:], gcol, shc, hT, sub * 128, xn, sq, ss, rstd, pT)
        for j in range(nfc):
            fc = fc0 + j
            pa = k.ps[1 + (j % 3)]
            pv = k.ps[4 + (j % 3)]
            for kc in range(8):
                S.mm(pa, pa[:], Wup[:, kc, j * 128:(j + 1) * 128], hT[:, kc, :], [Wup, hT],
                     start=(kc == 0), stop=(kc == 7))
            for kc in range(8):
                S.mm(pv, pv[:], Wup[:, kc, W + j * 128:W + (j + 1) * 128], hT[:, kc, :], [Wup, hT],
                     start=(kc == 0), stop=(kc == 7))
            ab = abuf[j % 2]
            c1_ = c1[j % 2]
            c2_ = c2[j % 2]
            S.copy("pool", ab, ab[:, 0:2], halo[:, j, :], [halo])
            S.copy("act", ab, ab[:, 2:TT + 2], pa[:], [pa])
            S.copy("pool", halo, halo[:, j, :], ab[:, TT:TT + 2], [ab])
            S.ts("pool", c1_, c1_[:], ab[:, 2:TT + 2], ccols[:, 44 + fc:45 + fc], ccols[:, 66 + fc:67 + fc],
                 ALU.mult, ALU.add, reads=[ab, ccols])
            S.stt("pool", c2_, c2_[:], ab[:, 1:TT + 1], ccols[:, 22 + fc:23 + fc], c1_[:], ALU.mult, ALU.add,
                  [ab, ccols, c1_])
            S.stt("dve", c1_, c1_[:], ab[:, 0:TT], ccols[:, fc:fc + 1], c2_[:], ALU.mult, ALU.add,
                  [ab, ccols, c2_])
            S.act(c2_, c2_[:], c1_[:], AF.Silu, [c1_])
            S.tt("dve", mT, mT[:, j, :], pv[:], c2_[:], ALU.mult, [pv, c2_])
        for sub in range(4):
            if same:
                xt = xts[(T0 * 4 + sub) % len(xts)]
            else:
                xt = xrs[sub % 2]
                S.dma(xt, xt[:], xres[r0 + sub * 128:r0 + (sub + 1) * 128, :])
            xo_ = xo[sub % 2]
            for half in range(2):
                py = k.ps[1 + half] if sub % 2 == 0 else k.ps[3 + half]
                for j in range(nfc):
                    S.mm(py, py[:], mT[:, j, sub * 128:(sub + 1) * 128], Wdn[:, j, half * 512:(half + 1) * 512],
                         [mT, Wdn], start=(j == 0), stop=(j == nfc - 1))
                S.tt("dve", xo_, xo_[:, half * 512:(half + 1) * 512], py[:], xt[:, half * 512:(half + 1) * 512],
                     ALU.add, [py, xt])
            if final:
                S.act(sq, sq[:], xo_[:], AF.Square, [xo_], accum_out=ss[:, 1:2], extra_writes=[ss])
                S.ts("dve", rstd, rstd[:, 1:2], ss[:, 1:2], 1.0 / D, EPS, ALU.mult, ALU.add, reads=[ss])
                S.act(rstd, rstd[:, 1:2], rstd[:, 1:2], AF.Ln, [rstd])
                S.act(rstd, rstd[:, 1:2], rstd[:, 1:2], AF.Exp, [rstd], scale=-0.5)
                S.stt("pool", xo_, xo_[:], xo_[:], rstd[:, 1:2], fnrow[:], ALU.mult, ALU.mult, [xo_, rstd, fnrow])
            S.dma(None, xout[r0 + sub * 128:r0 + (sub + 1) * 128, :], xo_[:], in_t=xo_)
    S.finish_wait("sp", xo)


def build_program(NT, phases=("pro", "hg", "ffn0", "nsa", "ffn1")):
    nc = bass.Bass("TRN2", target_bir_lowering=False)
    k = K()

    def inp(name, shape):
        return nc.dram_tensor(name, list(shape), F32, kind="ExternalInput").ap()

    k.x = inp("x", [NT, D])
    k.c_col = inp("c_col", [128, 8])
    k.ada_w = inp("ada_w", [2, D, 6 * D])
    k.ada_b = inp("ada_b", [2, 6 * D])
    k.norm_mix = inp("norm_mix", [2, D])
    k.norm_ffn = inp("norm_ffn", [2, D])
    k.final_norm = inp("final_norm", [D])
    k.hg_w_in = inp("hg_w_in", [1, D, 4096])
    k.hg_w_out = inp("hg_w_out", [1, D, D])
    k.hg_gnorm = inp("hg_gnorm", [1, 128])
    k.hg_lb = inp("hg_lb", [2, D])
    k.ffn_w_up = inp("ffn_w_up", [2, D, 2 * DFF])
    k.ffn_conv_w = inp("ffn_conv_w", [2, 3, DFF])
    k.ffn_conv_b = inp("ffn_conv_b", [2, DFF])
    k.ffn_w_down = inp("ffn_w_down", [2, DFF, D])
    k.c_ident = inp("c_ident", [128, 128])
    k.c_bdmask = inp("c_bdmask", [128, 128])
    nsa_declare(nc, k, NT)
    k.out = nc.dram_tensor("out", [NT, D], F32, kind="ExternalOutput").ap()
    k.modd = nc.dram_tensor("modd", [2, 6 * D], F32).ap()
    bufs = [nc.dram_tensor("xs%d" % i, [NT, D], F32).ap() for i in range(4)]
    xc = nc.dram_tensor("xc", [NT, D], F32).ap()
    main = [p for p in phases if p != "pro"]
    with contextlib.ExitStack() as gst:
        S = Sched(nc, gst)
        k.S = S
        plist = []
        if "pro" in phases:
            plist.append(lambda: (alloc_psum(S, k), phase_prologue(S, k)))
        src = k.x
        for i, p in enumerate(main):
            last = (i == len(main) - 1)
            dst = k.out if last else bufs[i]
            if p == "hg":
                plist.append(lambda src=src, dst=dst: phase_hgrn(S, k, 0, src, dst, NT))
            elif p in ("ffn0", "ffn1"):
                l = int(p[3])
                fin = (p == "ffn1")
                plist.append(lambda src=src, l=l: phase_ffn(S, k, l, src, src, xc, NT, 0, 11))
                plist.append(lambda src=src, dst=dst, l=l, fin=fin: phase_ffn(S, k, l, src, xc, dst, NT, 11, 22, final=fin))
            elif p == "nsa":
                plist.append(lambda src=src: phase_nsa_proj(S, k, 1, src, NT))
                plist.append(lambda: phase_nsa_cmp(S, k, NT))
                plist.append(lambda: phase_nsa_attn(S, k, NT))
                plist.append(lambda src=src, dst=dst: phase_nsa_out(S, k, 1, src, dst, NT))
            src = dst
        for i, p in enumerate(plist):
            with contextlib.ExitStack() as st:
                S.stack = st
                p()
                S.barrier()
                S.emit()
                S.phase_end()
    return nc


def host_consts():
    ident = np.eye(128, dtype=np.float32)
    s = np.arange(128)[:, None]
    t = np.arange(128)[None, :]
    bd = ((s // 64 == t // 64) & (s <= t)).astype(np.float32)
    return {"c_ident": ident, "c_bdmask": bd}


NSA_W = 2608
SLOPES = [2.0 ** (-8.0 * (h + 1) / 16) for h in range(16)]


def nsa_dims(NT):
    ncb = (NT - 32) // 16 + 1
    nsb = NT // 64
    return ncb, nsb


def nsa_host_consts(NT):
    import ml_dtypes
    bf = ml_dtypes.bfloat16
    ncb, nsb = nsa_dims(NT)
    t = np.arange(NT)
    c = {}
    c["c_vrow"] = np.stack([-SLOPES[h] * t for h in range(16)]).astype(bf)
    KT = NT // 128
    i = np.arange(128)
    sb = np.zeros((128, KT * 16), np.float32)
    for kt in range(KT):
        for h in range(16):
            sb[:, kt * 16 + h] = SLOPES[h] * (128 * kt + i)
    c["c_sbias"] = sb
    cb = np.zeros((128, 2 * 16), np.float32)
    for bt in range(2):
        for h in range(16):
            cb[:, bt * 16 + h] = SLOPES[h] * (16 * (128 * bt + i) + 15.5)
    c["c_cbias"] = cb
    c["c_maskd"] = np.where(i[:, None] > i[None, :], NEG, 0.0).astype(bf)
    c["c_maskw"] = np.where(i[None, :] >= i[:, None], NEG, 0.0).astype(bf)
    QT = NT // 512
    cm = np.zeros((QT, 2, 128, 512), np.float32)
    for T in range(QT):
        for bt in range(2):
            n = 128 * bt + i
            tt = 512 * T + np.arange(512)
            cm[T, bt] = np.where(16 * n[:, None] + 31 <= tt[None, :], 0.0, NEG)
    c["c_cmask"] = cm.astype(bf)
    ov = np.zeros((256, 64), np.float32)
    ci = np.arange(256)[:, None] * 16
    sj = np.arange(64)[None, :] * 64
    ov[:] = ((ci <= sj + 63) & (ci + 31 >= sj))
    ov[ncb:] = 0
    ov[:, nsb:] = 0
    c["c_overlap"] = ov.astype(bf)
    blk = np.arange(64)[None, :]
    cur = (t // 64)[:, None]
    valid = blk * 64 <= t[:, None]
    forced = ((blk == 0) | (blk == cur) | (blk == cur - 1)) & valid
    vm = valid.astype(np.float32)
    am = np.where(forced, 1e4, np.where(valid, 0.0, -1.0)).astype(np.float32)
    vm[:, nsb:] = 0.0
    am[:, nsb:] = -1.0
    c["c_vmask"] = vm
    c["c_amask"] = am
    si = np.zeros((64, NT), np.float32)
    si[0] = 1.0
    for j in range(1, 64):
        si[j] = (t // 64 == j)
    c["c_selind"] = si.astype(bf)
    kw = np.zeros((64, NT), np.float32)
    kw[0] = 1.0
    c["c_kwrows"] = kw.astype(bf)
    return c


def nsa_declare(nc, k, NT):
    def inp(name, shape, dt=F32):
        return nc.dram_tensor(name, list(shape), dt, kind="ExternalInput").ap()
    QT = NT // 512
    KT = NT // 128
    k.nsa_w_in = inp("nsa_w_in", [1, D, NSA_W])
    k.nsa_w_out = inp("nsa_w_out", [1, D, D])
    k.nsa_cmp_pe = inp("nsa_cmp_pe", [1, 2, 32, 64])
    k.nsa_cmp_w1 = inp("nsa_cmp_w1", [1, 2, 2048, 64])
    k.nsa_cmp_w2 = inp("nsa_cmp_w2", [1, 2, 64, 64])
    k.c_vrow = inp("c_vrow", [16, NT], BF16)
    k.c_sbias = inp("c_sbias", [128, KT * 16])
    k.c_cbias = inp("c_cbias", [128, 32])
    k.c_maskd = inp("c_maskd", [128, 128], BF16)
    k.c_maskw = inp("c_maskw", [128, 128], BF16)
    k.c_cmask = inp("c_cmask", [QT, 2, 128, 512], BF16)
    k.c_overlap = inp("c_overlap", [256, 64], BF16)
    k.c_vmask = inp("c_vmask", [NT, 64])
    k.c_amask = inp("c_amask", [NT, 64])
    k.c_selind = inp("c_selind", [64, NT], BF16)
    k.c_kwrows = inp("c_kwrows", [64, NT], BF16)
    k.qT_d = nc.dram_tensor("qT_d", [1024, NT], BF16).ap()
    k.kT_d = nc.dram_tensor("kT_d", [4, 256, NT], BF16).ap()
    k.vtok_d = nc.dram_tensor("vtok_d", [2, NT, 256], BF16).ap()
    k.gT_d = nc.dram_tensor("gT_d", [48, NT], F32).ap()
    k.kcT_d = nc.dram_tensor("kcT_d", [4, 64, 256], BF16).ap()
    k.vc_d = nc.dram_tensor("vc_d", [4, 256, 64], BF16).ap()
    k.oT_d = nc.dram_tensor("oT_d", [1024, NT], BF16).ap()


def phase_nsa_proj(S, k, l, xin, NT):
    TT = 512
    ntile = NT // TT
    load_consts(S, k)
    alloc_psum(S, k)
    cols = rows_to_cols(S, k, [k.modd[l, :], k.norm_mix[l, :]], "ncols")
    gcol = S.sbuf("gcol", [128, 8], F32)
    S.stt("dve", gcol, gcol[:], cols[:, 8:16], 1.0, cols[:, 48:56], ALU.add, ALU.mult, [cols])
    W = S.sbuf("Wn", [128, 8, NSA_W], BF16)
    stg = [S.sbuf("stg%d" % i, [128, 1024], F32) for i in range(3)]
    load_weight_bf16(S, W, k.nsa_w_in[0], 8, NSA_W, stg)
    xts = [S.sbuf("xt%d" % i, [128, 1024], F32) for i in range(2)]
    xn = S.sbuf("xn", [128, 1024], BF16)
    sq = S.sbuf("sq", [128, 1024], F32)
    ss = S.sbuf("ss", [128, 8], F32)
    rstd = S.sbuf("rstd", [128, 8], F32)
    hT = S.sbuf("hT", [128, 8, TT], BF16)
    fsb = [S.sbuf("fsb%d" % i, [128, TT], BF16) for i in range(3)]
    vsb = [S.sbuf("vsb%d" % i, [128, 512], BF16) for i in range(2)]
    gsb = [S.sbuf("gsb%d" % i, [48, TT], F32) for i in range(2)]
    pT = k.ps[0]
    outs = fsb + vsb + gsb
    n = 0
    for T0 in range(ntile):
        r0 = T0 * TT
        for sub in range(4):
            xt = xts[sub % 2]
            S.dma(xt, xt[:], xin[r0 + sub * 128:r0 + (sub + 1) * 128, :])
            norm_to_hT(S, k, xt, xt[:], gcol, cols, hT, sub * 128, xn, sq, ss, rstd, pT)
        fm = [(hp * 128, ("q", hp)) for hp in range(8)]
        for kind, c0 in ((0, 1024), (1, 1280), (2, 1536), (3, 2048)):
            for cpart in range(2):
                fm.append((c0 + cpart * 128, ("k", kind, cpart)))
        for c0, tag in fm:
            ps = k.ps[1 + (n % 3)]
            f = fsb[n % 3]
            n += 1
            for kc in range(8):
                S.mm(ps, ps[:], W[:, kc, c0:c0 + 128], hT[:, kc, :], [W, hT], start=(kc == 0), stop=(kc == 7))
            if tag[0] == "q":
                S.act(f, f[:], ps[:], AF.Copy, [ps], scale=0.125)
                S.dma(None, k.qT_d[tag[1] * 128:(tag[1] + 1) * 128, r0:r0 + TT], f[:], in_t=f)
            else:
                S.copy("dve", f, f[:], ps[:], [ps])
                S.dma(None, k.kT_d[tag[1], tag[2] * 128:(tag[2] + 1) * 128, r0:r0 + TT], f[:], in_t=f)
        for sub in range(4):
            ps = k.ps[4 + (sub % 2)]
            v = vsb[sub % 2]
            for j, c0 in enumerate((1792, 2304)):
                for kc in range(8):
                    S.mm(ps, ps[:, j * 256:(j + 1) * 256], hT[:, kc, sub * 128:(sub + 1) * 128],
                         W[:, kc, c0:c0 + 256], [hT, W], start=(kc == 0), stop=(kc == 7))
            S.copy("dve", v, v[:], ps[:], [ps])
            for j in range(2):
                S.dma(None, k.vtok_d[j, r0 + sub * 128:r0 + (sub + 1) * 128, :], v[:, j * 256:(j + 1) * 256], in_t=v)
        ps = k.ps[6]
        g_ = gsb[T0 % 2]
        for kc in range(8):
            S.mm(ps, ps[0:48, :], W[:, kc, 2560:2608], hT[:, kc, :], [W, hT], start=(kc == 0), stop=(kc == 7))
        S.act(g_, g_[:], ps[0:48, :], AF.Sigmoid, [ps])
        S.dma(None, k.gT_d[:, r0:r0 + TT], g_[:], in_t=g_)
    S.finish_wait("sp", outs)


def phase_nsa_cmp(S, k, NT):
    ncb, nsb = nsa_dims(NT)
    load_consts(S, k)
    alloc_psum(S, k)
    xc = S.sbuf("xc", [64, 2, 4, NT], BF16)
    for kv in range(2):
        for g in range(4):
            S.dma(xc, xc[:, kv, g, :], k.kT_d[kv, g * 64:(g + 1) * 64, :])
    w1f = S.sbuf("w1f", [64, 2, 32, 64], F32)
    w1 = S.sbuf("w1", [64, 2, 32, 64], BF16)
    for kv in range(2):
        S.dma(w1f, w1f[:, kv, :, :], k.nsa_cmp_w1[0, kv].rearrange("(l d) e -> d l e", d=64))
    S.copy("pool", w1, w1[:], w1f[:], [w1f])
    w2f = S.sbuf("w2f", [64, 2, 64], F32)
    w2p = S.sbuf("w2p", [64, 2, 128], BF16)
    for kv in range(2):
        S.dma(w2f, w2f[:, kv, :], k.nsa_cmp_w2[0, kv])
    S.memset("pool", w2p, w2p[:], 0.0)
    S.copy("pool", w2p, w2p[:, :, 64:128], w2f[:], [w2f])
    pef = S.sbuf("pef", [32, 2, 64], F32)
    for kv in range(2):
        S.dma(pef, pef[:, kv, :], k.nsa_cmp_pe[0, kv])
    peT = S.sbuf("peT", [64, 2, 32], BF16)
    ps = k.ps[1]
    for kv in range(2):
        S.mm(ps, ps[0:64, kv * 32:(kv + 1) * 32], pef[:, kv, :], k.identf[0:32, 0:32], [pef, k.identf])
    S.copy("dve", peT, peT[:].rearrange("p a l -> p (a l)"), ps[0:64, 0:64], [ps])
    bias = S.sbuf("cbias", [64, 2], F32)
    ps = k.ps[2]
    for kv in range(2):
        for l in range(32):
            S.mm(ps, ps[0:64, kv:kv + 1], w1[:, kv, l, :], peT[:, kv, l:l + 1], [w1, peT],
                 start=(l == 0), stop=(l == 31))
    S.copy("dve", bias, bias[:], ps[0:64, 0:2], [ps])
    hid = [S.sbuf("hid%d" % i, [64, 256], BF16) for i in range(2)]
    osb = [S.sbuf("osb%d" % i, [128, 256], BF16) for i in range(2)]
    n = 0
    for kv in range(2):
        for g in range(4):
            ph = k.ps[3 + (n % 2)]
            hd = hid[n % 2]
            ob = osb[n % 2]
            for l in range(32):
                rhs = xc[:, kv, g, l:l + 16 * (ncb - 1) + 1:16]
                S.mm(ph, ph[0:64, 0:ncb], w1[:, kv, l, :], rhs, [w1, xc], start=(l == 0), stop=(l == 31))
            S.act(hd, hd[:, 0:ncb], ph[0:64, 0:ncb], AF.Silu, [ph, bias], bias=bias[:, kv:kv + 1])
            po = k.ps[5 + (n % 2)]
            if kv == 0:
                S.mm(po, po[:, 0:ncb], w2p[:, 0, :], hd[:, 0:ncb], [w2p, hd])
                S.copy("dve", ob, ob[64:128, 0:ncb], po[64:128, 0:ncb], [po])
                S.dma(None, k.kcT_d[g, :, 0:ncb], ob[64:128, 0:ncb], in_t=ob)
            else:
                for bt in range((ncb + 127) // 128):
                    nb = min(128, ncb - bt * 128)
                    S.mm(po, po[0:nb, bt * 64:(bt + 1) * 64], hd[:, bt * 128:bt * 128 + nb], w2p[:, 1, 64:128],
                         [hd, w2p])
                    S.copy("dve", ob, ob[0:nb, bt * 64:(bt + 1) * 64], po[0:nb, bt * 64:(bt + 1) * 64], [po])
                    S.dma(None, k.vc_d[g, bt * 128:bt * 128 + nb, :], ob[0:nb, bt * 64:(bt + 1) * 64], in_t=ob)
            n += 1
    S.finish_wait("sp", osb)


def phase_nsa_attn(S, k, NT):
    ncb, nsb = nsa_dims(NT)
    QT = NT // 512
    KT = NT // 128
    NBT = (ncb + 127) // 128
    load_consts(S, k)
    alloc_psum(S, k)
    sbias = S.sbuf("sbias", [128, KT * 16], F32)
    S.dma(sbias, sbias[:], k.c_sbias[:, :])
    cbias = S.sbuf("cbias", [128, 32], F32)
    S.dma(cbias, cbias[:], k.c_cbias[:, :])
    maskd = S.sbuf("maskd", [128, 128], BF16)
    S.dma(maskd, maskd[:], k.c_maskd[:, :])
    maskw = S.sbuf("maskw", [128, 128], BF16)
    S.dma(maskw, maskw[:], k.c_maskw[:, :])
    ovl = S.sbuf("ovl", [128, 2, 64], BF16)
    S.dma(ovl, ovl[:], k.c_overlap.rearrange("(bt p) j -> p bt j", p=128))
    Ks = S.sbuf("Ks", [128, NT], BF16)
    Kw = S.sbuf("Kw", [128, NT], BF16)
    Kc = S.sbuf("Kc", [128, 256], BF16)
    Vs = S.sbuf("Vs", [128, KT, 128], BF16)
    Vw = S.sbuf("Vw", [128, KT, 128], BF16)
    Vc = S.sbuf("Vc", [128, 2, 128], BF16)
    S.memset("pool", Vs, Vs[:], 1.0)
    S.memset("pool", Vw, Vw[:], 1.0)
    S.memset("pool", Vc, Vc[:], 1.0)
    S.memset("pool", Kc, Kc[:], 0.0)
    S.dma(Ks, Ks[0:64, :], k.c_selind[:, :])
    S.dma(Kw, Kw[0:64, :], k.c_kwrows[:, :])
    S.dma(Kc, Kc[0:64, :], k.c_kwrows[:, 0:256])
    Qa = [S.sbuf("Qa%d" % i, [128, 512], BF16) for i in range(4)]
    for q in Qa:
        S.memset("pool", q, q[:], 0.0)
    vrow = S.sbuf("vrow", [1, 4, 512], BF16)
    gb = [[S.sbuf("gb%d_%d" % (i, j), [128, 512], F32) for j in range(3)] for i in range(4)]
    Pt = [S.sbuf("Pt%d" % i, [128, 512], BF16) for i in range(4)]
    cmk = [S.sbuf("cmk%d" % i, [128, 512], BF16) for i in range(2)]
    rd = [S.sbuf("rd%d" % i, [128, 512], F32) for i in range(2)]
    coef = [S.sbuf("coef%d" % i, [128, 512], F32) for i in range(2)]
    tmp = [S.sbuf("tmp%d" % i, [128, 512], F32) for i in range(2)]
    oacc = [S.sbuf("oacc%d" % i, [128, 512], F32) for i in range(4)]
    osb = [S.sbuf("osb%d" % i, [128, 512], BF16) for i in range(2)]
    impacc = S.sbuf("impacc", [128, 512], F32)
    vm = S.sbuf("vm", [128, 4, 64], F32)
    am = S.sbuf("am", [128, 4, 64], F32)
    sc = S.sbuf("sc", [128, 4, 64], F32)
    sc2 = S.sbuf("sc2", [128, 64], F32)
    mx = S.sbuf("mx", [128, 8], F32)
    thr = S.sbuf("thr", [128, 4], F32)
    selb = S.sbuf("selb", [128, 4, 64], BF16)
    selrows = S.sbuf("selrows", [128, 512], BF16)
    pT = k.ps[0]
    pSs = [k.ps[1], k.ps[2], k.ps[6], k.ps[7]]
    pO, pI, pTk = k.ps[3], k.ps[4], k.ps[5]
    cnt = {"s": 0, "p": 0, "r": 0, "cm": 0, "o": 0}

    def finish_branch(hh, br, first):
        i = cnt["r"] % 2
        cnt["r"] += 1
        r_, c_, t_ = rd[i], coef[i], tmp[i]
        if br == 0:
            S.ts("dve", r_, r_[64:128, :], pO[64:128, :], 1e-30, None, ALU.add, reads=[pO])
            S.op("dve", lambda e: e.reciprocal(r_[64:128, :], r_[64:128, :]), [r_], [r_])
        else:
            S.op("dve", lambda e: e.reciprocal(r_[64:128, :], pO[64:128, :]), [pO], [r_])
        S.tt("pool", c_, c_[64:128, :], r_[64:128, :], gb[hh][br][64:128, :], ALU.mult, [r_, gb[hh][br]])
        if first:
            S.tt("dve", oacc[hh], oacc[hh][0:64, :], pO[0:64, :], c_[64:128, :], ALU.mult, [pO, c_])
        else:
            S.tt("dve", t_, t_[0:64, :], pO[0:64, :], c_[64:128, :], ALU.mult, [pO, c_])
            S.tt("pool", oacc[hh], oacc[hh][0:64, :], oacc[hh][0:64, :], t_[0:64, :], ALU.add, [oacc[hh], t_])
        return r_

    def attend(hh, h, Kt, Vt, tiles, br):
        for idx, (kt, clo, chi, masks) in enumerate(tiles):
            pS = pSs[cnt["s"] % 4]
            cnt["s"] += 1
            S.mm(pS, pS[:, clo:chi], Kt[:, kt * 128:(kt + 1) * 128], Qa[hh][:, clo:chi], [Kt, Qa[hh]],
                 start=True, stop=(len(masks) == 0))
            for mi, (mk, c0) in enumerate(masks):
                S.mm(pS, pS[:, c0:c0 + 128], k.ident[:], mk[:], [k.ident, mk], start=False,
                     stop=(mi == len(masks) - 1))
            P = Pt[cnt["p"] % 4]
            cnt["p"] += 1
            S.act(P, P[:, clo:chi], pS[:, clo:chi], AF.Exp, [pS, sbias], bias=sbias[:, kt * 16 + h:kt * 16 + h + 1])
            S.mm(pO, pO[:, clo:chi], Vt[:, kt, :], P[:, clo:chi], [Vt, P], start=(idx == 0),
                 stop=(idx == len(tiles) - 1))

    for g in range(4):
        S.dma(Ks, Ks[64:128, :], k.kT_d[2, g * 64:(g + 1) * 64, :])
        S.dma(Kw, Kw[64:128, :], k.kT_d[3, g * 64:(g + 1) * 64, :])
        S.dma(Kc, Kc[64:128, 0:ncb], k.kcT_d[g, :, 0:ncb])
        for k0 in range(0, KT, 8):
            k1 = min(KT, k0 + 8)
            S.dma(Vs, Vs[:, k0:k1, 0:64],
                  k.vtok_d[0][k0 * 128:k1 * 128, g * 64:(g + 1) * 64].rearrange("(kt p) d -> p kt d", p=128))
            S.dma(Vw, Vw[:, k0:k1, 0:64],
                  k.vtok_d[1][k0 * 128:k1 * 128, g * 64:(g + 1) * 64].rearrange("(kt p) d -> p kt d", p=128))
        for bt in range(NBT):
            nb = min(128, ncb - bt * 128)
            S.dma(Vc, Vc[0:nb, bt, 0:64], k.vc_d[g, bt * 128:bt * 128 + nb, :])
        for T in range(QT):
            T0 = 512 * T
            for hh in range(4):
                h = 4 * g + hh
                for br in range(3):
                    S.dma(gb[hh][br], gb[hh][br][64:128, :],
                          k.gT_d[3 * h + br, T0:T0 + 512].partition_broadcast(64))
                S.dma(Qa[hh], Qa[hh][64:128, :], k.qT_d[h * 64:(h + 1) * 64, T0:T0 + 512])
                S.dma(vrow, vrow[0:1, hh, :], k.c_vrow[h:h + 1, T0:T0 + 512])
            for hh in range(4):
                S.copy("pool", Qa[hh], Qa[hh][0:1, :], vrow[0:1, hh, :], [vrow])
            bts = []
            for bt in range(NBT):
                nb = min(128, ncb - bt * 128)
                n_lo, n_hi = 128 * bt, 128 * bt + nb - 1
                if 16 * n_lo + 31 > T0 + 511:
                    continue
                partial = 16 * n_hi + 31 > T0
                bts.append((bt, nb, partial))
            for hh in range(4):
                h = 4 * g + hh
                for bi, (bt, nb, partial) in enumerate(bts):
                    pS = pSs[cnt["s"] % 4]
                    cnt["s"] += 1
                    S.mm(pS, pS[0:nb, :], Kc[:, bt * 128:bt * 128 + nb], Qa[hh][:, :], [Kc, Qa[hh]],
                         start=True, stop=(not partial))
                    if partial:
                        cm_ = cmk[cnt["cm"] % 2]
                        cnt["cm"] += 1
                        S.dma(cm_, cm_[:], k.c_cmask[T, bt])
                        S.mm(pS, pS[0:nb, :], k.ident[0:nb, 0:nb], cm_[0:nb, :], [k.ident, cm_], start=False, stop=True)
                    P = Pt[cnt["p"] % 4]
                    cnt["p"] += 1
                    S.act(P, P[0:nb, :], pS[0:nb, :], AF.Exp, [pS, cbias],
                          bias=cbias[0:nb, bt * 16 + h:bt * 16 + h + 1])
                    S.mm(pO, pO[:], Vc[0:nb, bt, :], P[0:nb, :], [Vc, P], start=(bi == 0), stop=(bi == len(bts) - 1))
                    S.mm(pI, pI[0:64, :], ovl[0:nb, bt, :], P[0:nb, :], [ovl, P], start=(bi == 0),
                         stop=(bi == len(bts) - 1))
                r_ = finish_branch(hh, 0, True)
                if hh == 0:
                    S.tt("dve", impacc, impacc[0:64, :], pI[0:64, :], r_[64:128, :], ALU.mult, [pI, r_])
                else:
                    t_ = tmp[cnt["r"] % 2]
                    S.tt("dve", t_, t_[0:64, :], pI[0:64, :], r_[64:128, :], ALU.mult, [pI, r_])
                    S.tt("pool", impacc, impacc[0:64, :], impacc[0:64, :], t_[0:64, :], ALU.add, [impacc, t_])
            for n in range(4):
                S.mm(pTk, pTk[:, n * 64:(n + 1) * 64], impacc[0:64, n * 128:(n + 1) * 128], k.identf[0:64, 0:64],
                     [impacc, k.identf])
            S.dma(vm, vm[:], k.c_vmask[T0:T0 + 512, :].rearrange("(n p) j -> p n j", p=128))
            S.dma(am, am[:], k.c_amask[T0:T0 + 512, :].rearrange("(n p) j -> p n j", p=128))
            S.tt("dve", sc, sc[:].rearrange("p n j -> p (n j)"), pTk[:, 0:256], vm[:].rearrange("p n j -> p (n j)"),
                 ALU.mult, [pTk, vm])
            S.tt("pool", sc, sc[:], sc[:], am[:], ALU.add, [sc, am])
            for n in range(4):
                S.op("dve", lambda e, n=n: e.max(mx[:], sc[:, n, :]), [sc], [mx])
                S.op("dve", lambda e, n=n: e.match_replace(sc2[:], mx[:], sc[:, n, :], -1e9), [sc, mx], [sc2])
                S.op("dve", lambda e: e.max(mx[:], sc2[:]), [sc2], [mx])
                S.ts("dve", thr, thr[:, n:n + 1], mx[:, 7:8], -0.5, None, ALU.max, reads=[mx])
                S.ts("dve", sc2, sc2[:], sc[:, n, :], thr[:, n:n + 1], None, ALU.is_ge, reads=[sc, thr])
                S.ts("dve", selb, selb[:, n, :], sc2[:], -1.0, -NEG, ALU.add, ALU.mult, reads=[sc2])
            for n in range(4):
                S.transpose(pT, pT[0:64, n * 128:(n + 1) * 128], selb[:, n, :], k.ident[:], [selb, k.ident])
            S.copy("act", selrows, selrows[0:64, :], pT[0:64, 0:512], [pT])
            for hh in range(4):
                S.copy("pool", Qa[hh], Qa[hh][0:64, :], selrows[0:64, :], [selrows])
                S.copy("pool", Qa[hh], Qa[hh][0:1, :], vrow[0:1, hh, :], [vrow])
            for hh in range(4):
                h = 4 * g + hh
                tiles = []
                for kt in range(4 * T + 4):
                    j = kt - 4 * T
                    if j < 0:
                        tiles.append((kt, 0, 512, []))
                    else:
                        tiles.append((kt, 128 * j, 512, [(maskd, 128 * j)]))
                attend(hh, h, Ks, Vs, tiles, 1)
                finish_branch(hh, 1, False)
                tiles = []
                for kt in range(max(0, 4 * T - 4), 4 * T + 4):
                    m = kt - 4 * T
                    n_lo, n_hi = max(m, 0), min(m + 4, 3)
                    masks = []
                    if m >= 0:
                        masks.append((maskd, 128 * m))
                    if m <= -1:
                        masks.append((maskw, 128 * (m + 4)))
                    tiles.append((kt, 128 * n_lo, 128 * (n_hi + 1), masks))
                tiles.sort(key=lambda tl: -(tl[2] - tl[1]))
                attend(hh, h, Kw, Vw, tiles, 2)
                finish_branch(hh, 2, False)
                ob = osb[cnt["o"] % 2]
                cnt["o"] += 1
                S.copy("act", ob, ob[0:64, :], oacc[hh][0:64, :], [oacc[hh]])
                S.dma(None, k.oT_d[h * 64:(h + 1) * 64, T0:T0 + 512], ob[0:64, :], in_t=ob)
    S.finish_wait("sp", osb)


def phase_nsa_out(S, k, l, xin, xout, NT):
    TT = 512
    ntile = NT // TT
    load_consts(S, k)
    alloc_psum(S, k)
    g1row = S.sbuf("g1row", [128, 1024], F32)
    S.dma(g1row, g1row[:], k.modd[l, 2048:3072].partition_broadcast(128))
    Wo = S.sbuf("Wo", [128, 8, 1024], BF16)
    stg = [S.sbuf("stg%d" % i, [128, 1024], F32) for i in range(2)]
    load_weight_bf16(S, Wo, k.nsa_w_out[0], 8, 1024, stg, grow=g1row)
    oT = [S.sbuf("oT%d" % i, [128, 8, TT], BF16) for i in range(2)]
    xts = [S.sbuf("xt%d" % i, [128, 1024], F32) for i in range(2)]
    xo = [S.sbuf("xo%d" % i, [128, 1024], F32) for i in range(2)]
    n = 0
    for T0 in range(ntile):
        r0 = T0 * TT
        o_ = oT[T0 % 2]
        S.dma(o_, o_[:], k.oT_d[:, r0:r0 + TT].rearrange("(kc p) t -> p kc t", p=128))
        for sub in range(4):
            xt = xts[sub % 2]
            xo_ = xo[sub % 2]
            S.dma(xt, xt[:], xin[r0 + sub * 128:r0 + (sub + 1) * 128, :])
            for half in range(2):
                py = k.ps[1 + (n % 4)]
                n += 1
                for kc in range(8):
                    S.mm(py, py[:], o_[:, kc, sub * 128:(sub + 1) * 128], Wo[:, kc, half * 512:(half + 1) * 512],
                         [o_, Wo], start=(kc == 0), stop=(kc == 7))
                S.tt("dve", xo_, xo_[:, half * 512:(half + 1) * 512], py[:], xt[:, half * 512:(half + 1) * 512],
                     ALU.add, [py, xt])
            S.dma(None, xout[r0 + sub * 128:r0 + (sub + 1) * 128, :], xo_[:], in_t=xo_)
    S.finish_wait("sp", xo)


W_KEYS = ["ada_w", "ada_b", "norm_mix", "norm_ffn", "final_norm", "hg_w_in", "hg_w_out", "hg_gnorm", "hg_lb",
          "ffn_w_up", "ffn_conv_w", "ffn_conv_b", "ffn_w_down", "nsa_w_in", "nsa_w_out", "nsa_cmp_pe",
          "nsa_cmp_w1", "nsa_cmp_w2"]


def make_in_map(inp, x, c, NT, consts=None):
    im = {"x": np.ascontiguousarray(x, dtype=np.float32),
          "c_col": np.ascontiguousarray(np.asarray(c, dtype=np.float32).reshape(8, 128).T)}
    for k_ in W_KEYS:
        im[k_] = np.asarray(inp[k_], dtype=np.float32)
    if consts is None:
        consts = dict(host_consts())
        consts.update(nsa_host_consts(NT))
    im.update(consts)
    return im


_CACHE = {}


def kernel(**inputs):
    x = np.asarray(inputs["x"], dtype=np.float32)
    c = np.asarray(inputs["c"], dtype=np.float32)
    B, NT, _ = x.shape
    if "nc" not in _CACHE:
        _CACHE["nc"] = build_program(NT)
        consts = dict(host_consts())
        consts.update(nsa_host_consts(NT))
        _CACHE["consts"] = consts
    nc = _CACHE["nc"]
    in_maps = [make_in_map(inputs, x[b], c[b], NT, _CACHE["consts"]) for b in range(B)]
    res = run_bass_kernel_spmd(nc, in_maps, core_ids=list(range(B)))
    out = np.stack([np.asarray(r["out"], dtype=np.float32) for r in res.results], axis=0)
    return out
```

```python
import contextlib
import numpy as np
import concourse.bass as bass
import concourse.mybir as mybir

F32 = mybir.dt.float32
BF16 = mybir.dt.bfloat16
AF = mybir.ActivationFunctionType
ALU = mybir.AluOpType
AX = mybir.AxisListType

ENGS = ["pe", "act", "dve", "pool", "sp"]


class T:
    __slots__ = ("name", "ap", "last_w", "readers", "dsem", "dcnt", "uid")
    _n = [0]

    def __init__(self, name, ap=None):
        T._n[0] += 1
        self.uid = T._n[0]
        self.name = name
        self.ap = ap
        self.last_w = None
        self.readers = []
        self.dsem = None
        self.dcnt = 0

    def __getitem__(self, idx):
        return self.ap[idx]


class Sched:
    def __init__(self, nc, stack):
        self.nc = nc
        self.stack = stack
        self.ops = {e: [] for e in ENGS}
        self.cnt = {e: 0 for e in ENGS}
        self.clock = {e: {} for e in ENGS}
        self.sem = {}
        for e in ["pe", "act", "dve", "pool"]:
            self.sem[e] = stack.enter_context(nc.semaphore("s_" + e))
        self.sem["bar"] = stack.enter_context(nc.semaphore("s_bar"))
        self.bar_n = 0
        self.dma_live = {}
        self.gstack = stack
        self.nsem = 5
        self.final_waits = []
        self.n_wait = 0
        self.uid = 0
        self.dsem_pool = []
        self.dsem_owner = []

    def sbuf(self, name, shape, dtype):
        self.uid += 1
        name = "%s_u%d" % (name, self.uid)
        t = self.stack.enter_context(self.nc.sbuf_tensor(name, list(shape), dtype))
        return T(name, t)

    def psum(self, name, shape, dtype=F32):
        self.uid += 1
        name = "%s_u%d" % (name, self.uid)
        t = self.stack.enter_context(self.nc.psum_tensor(name, list(shape), dtype))
        return T(name, t)

    def view(self, name, ap):
        return T(name, ap)

    def _dsem(self, t):
        if t.dsem is None:
            if self.dsem_pool:
                t.dsem, t.dcnt = self.dsem_pool.pop()
            else:
                self.nsem += 1
                t.dsem = self.gstack.enter_context(self.nc.semaphore("dsem%d" % self.nsem))
                t.dcnt = 0
            self.dsem_owner.append(t)
        return t.dsem

    def phase_end(self):
        for t in self.dsem_owner:
            self.dsem_pool.append((t.dsem, t.dcnt))
            t.dsem = None
        self.dsem_owner = []
        self.dma_live = {}

    def _need(self, eng, ev, waits):
        key, val, snap = ev
        if eng == "pe" and key == "pe":
            return
        ck = self.clock[eng]
        if ck.get(key, 0) >= val:
            return
        waits[key] = max(waits.get(key, 0), val)
        ck[key] = val
        if snap:
            for k, v in snap.items():
                if ck.get(k, 0) < v:
                    ck[k] = v

    def op(self, eng, fn, reads=(), writes=(), dma_sem_tile=None):
        waits = {}
        for t in reads:
            if t.last_w is not None:
                self._need(eng, t.last_w, waits)
        for t in writes:
            if t.last_w is not None:
                self._need(eng, t.last_w, waits)
            for ev in t.readers:
                self._need(eng, ev, waits)
        if dma_sem_tile is not None:
            st = dma_sem_tile
            sem = self._dsem(st)
            st.dcnt += 16
            key = ("d", st.uid)
            self.sem[key] = sem
            ev = (key, st.dcnt, dict(self.clock[eng]))
            self.dma_live[key] = (st, st.dcnt)
            inc = (sem, 16)
        else:
            self.cnt[eng] += 1
            ev = (eng, self.cnt[eng], None)
            inc = (self.sem[eng], 1)
        self.ops[eng].append((list(waits.items()), fn, inc))
        self.n_wait += len(waits)
        if dma_sem_tile is None:
            snap = dict(self.clock[eng])
            ev = (eng, self.cnt[eng], snap)
        for t in writes:
            t.last_w = ev
            t.readers = []
        for t in reads:
            if t not in writes:
                t.readers.append(ev)
        return ev

    def finish_wait(self, eng, tiles):
        waits = {}
        for t in tiles:
            if t.last_w is not None:
                self._need(eng, t.last_w, waits)
            for ev in t.readers:
                self._need(eng, ev, waits)
        self.ops[eng].append((list(waits.items()), None, None))

    def barrier(self):
        evs = []
        for e in ["pe", "act", "dve", "pool"]:
            if self.cnt[e] > 0:
                evs.append((e, self.cnt[e], None))
        for key, (t, val) in self.dma_live.items():
            evs.append((key, val, None))
        for eng in ENGS:
            waits = {}
            for ev in evs:
                if ev[0] == eng and eng == "pe":
                    continue
                ck = self.clock[eng]
                if ck.get(ev[0], 0) < ev[1]:
                    waits[ev[0]] = ev[1]
                    ck[ev[0]] = ev[1]
            self.ops[eng].append((list(waits.items()), None, None))
        self.bar_n += 1
        for eng in ENGS:
            self.ops[eng].append(([], "barinc", None))
        for eng in ENGS:
            self.ops[eng].append(([("bar", 5 * self.bar_n)], None, None))

    def emit(self):
        nc = self.nc
        with nc.Block() as block:
            def run(eng_name):
                def body(e):
                    for waits, fn, inc in self.ops[eng_name]:
                        for key, val in waits:
                            e.wait_ge(self.sem[key], val)
                        if fn == "barinc":
                            e.sem_inc(self.sem["bar"], 1)
                        elif fn is not None:
                            ins = fn(e)
                            ins.then_inc(inc[0], inc[1])
                return body
            block.tensor(run("pe"))
            block.scalar(run("act"))
            block.vector(run("dve"))
            block.gpsimd(run("pool"))
            block.sync(run("sp"))
        self.ops = {e: [] for e in ENGS}

    def dma(self, out_t, out_ap, in_ap, in_t=None, eng="sp", **kw):
        reads = [in_t] if in_t is not None else []
        writes = [out_t] if out_t is not None else []
        st = out_t if out_t is not None else in_t
        return self.op(eng, lambda e: e.dma_start(out=out_ap, in_=in_ap, **kw), reads, writes,
                       dma_sem_tile=st)

    def mm(self, out_t, out_ap, lhsT, rhs, reads, start=True, stop=True, **kw):
        return self.op("pe", lambda e: e.matmul(out_ap, lhsT, rhs, start=start, stop=stop, **kw),
                       reads, [out_t])

    def transpose(self, out_t, out_ap, in_ap, ident_ap, reads):
        return self.op("pe", lambda e: e.transpose(out_ap, in_ap, ident_ap), reads, [out_t])

    def act(self, out_t, out_ap, in_ap, func, reads, bias=None, scale=None, accum_out=None,
            extra_writes=()):
        kw = {}
        if bias is not None:
            kw["bias"] = bias
        if scale is not None:
            kw["scale"] = scale
        if accum_out is not None:
            kw["accum_out"] = accum_out
        return self.op("act", lambda e: e.activation(out_ap, in_ap, func, **kw), reads,
                       [out_t] + list(extra_writes))

    def tt(self, eng, out_t, out_ap, in0, in1, op, reads):
        return self.op(eng, lambda e: e.tensor_tensor(out_ap, in0, in1, op), reads, [out_t])

    def ts(self, eng, out_t, out_ap, in0, s1, s2, op0, op1=None, reads=(), accum_out=None,
           extra_writes=()):
        def f(e):
            kw = {}
            if accum_out is not None:
                kw["accum_out"] = accum_out
            if op1 is None:
                return e.tensor_scalar(out_ap, in0, s1, None, op0, **kw)
            return e.tensor_scalar(out_ap, in0, s1, s2, op0, op1, **kw)
        return self.op(eng, f, reads, [out_t] + list(extra_writes))

    def stt(self, eng, out_t, out_ap, in0, scalar, in1, op0, op1, reads):
        eng = "dve"
        return self.op(eng, lambda e: e.scalar_tensor_tensor(out_ap, in0, scalar, in1, op0, op1),
                       reads, [out_t])

    def copy(self, eng, out_t, out_ap, in_ap, reads):
        if eng == "act":
            return self.op("act", lambda e: e.copy(out_ap, in_ap), reads, [out_t])
        return self.op(eng, lambda e: e.tensor_copy(out_ap, in_ap), reads, [out_t])

    def memset(self, eng, out_t, out_ap, val):
        return self.op(eng, lambda e: e.memset(out_ap, val), [], [out_t])

from concourse.bass_utils import run_bass_kernel_spmd

D = 1024
NH_HG = 8
DFF = 2816
NFC = DFF // 128
EPS = 1e-6
NEG = -30000.0


def bcast_rows(ap_row, n):
    return ap_row.partition_broadcast(n)


class K:
    pass


def load_weight_bf16(S, Wb, w_dram, KC, N, stg, grow=None, col_off=0, rowscale=None, kc_off=0):
    i = 0
    for kc in range(KC):
        for n0 in range(0, N, 1024):
            n1 = min(N, n0 + 1024)
            st = stg[i % len(stg)]
            S.dma(st, st[:, 0:n1 - n0], w_dram[kc * 128:(kc + 1) * 128, n0:n1])
            eng = "pool" if i % 2 == 0 else "dve"
            o = Wb[:, kc_off + kc, col_off + n0:col_off + n1]
            if grow is None:
                S.copy(eng, Wb, o, st[:, 0:n1 - n0], [st])
            elif rowscale is None:
                S.tt(eng, Wb, o, st[:, 0:n1 - n0], grow[:, n0:n1], ALU.mult, [st, grow])
            else:
                S.stt(eng, Wb, o, st[:, 0:n1 - n0], rowscale[:, kc:kc + 1], grow[:, n0:n1],
                      ALU.mult, ALU.mult, [st, grow, rowscale])
            i += 1


def rstd_from_ss(S, rstd, ss, n, width):
    S.ts("dve", rstd, rstd[:, 0:width], ss[:, 0:width], 1.0 / n, EPS, ALU.mult, ALU.add, reads=[ss])
    S.act(rstd, rstd[:, 0:width], rstd[:, 0:width], AF.Ln, [rstd])
    S.act(rstd, rstd[:, 0:width], rstd[:, 0:width], AF.Exp, [rstd], scale=-0.5)


def norm_to_hT(S, k, xt_t, xt_ap, gcol, shcol, hT, col0, xn, sq, ss, rstd, pT):
    S.act(sq, sq[:], xt_ap, AF.Square, [xt_t], accum_out=ss[:, 0:1], extra_writes=[ss])
    rstd_from_ss(S, rstd, ss, D, 1)
    S.act(xn, xn[:], xt_ap, AF.Copy, [xt_t, rstd], scale=rstd[:, 0:1])
    for kc in range(8):
        S.transpose(pT, pT[:, kc * 128:(kc + 1) * 128], xn[:, kc * 128:(kc + 1) * 128],
                    k.ident[:], [xn, k.ident])
    for kc in range(8):
        eng = "dve" if kc % 2 == 0 else "pool"
        eng = "dve"
        S.ts(eng, hT, hT[:, kc, col0:col0 + 128], pT[:, kc * 128:(kc + 1) * 128],
             gcol[:, kc:kc + 1], shcol[:, kc:kc + 1], ALU.mult, ALU.add, reads=[pT, gcol, shcol])


def rows_to_cols(S, k, rows, name):
    n = sum(r.shape[0] // 128 for r in rows)
    assert n <= 128
    rt = S.sbuf(name + "_r", [n, 128], F32)
    ct = S.sbuf(name, [128, n], F32)
    j = 0
    for r in rows:
        m = r.shape[0] // 128
        S.dma(rt, rt[j:j + m, :], r.rearrange("(j p) -> j p", p=128))
        j += m
    ps = k.ps[1]
    S.mm(ps, ps[:, 0:n], rt[0:n, :], k.identf[0:n, 0:n], [rt, k.identf])
    S.copy("dve", ct, ct[:], ps[:, 0:n], [ps])
    return ct


def phase_prologue(S, k):
    cc = S.sbuf("cc", [128, 8], F32)
    ca = S.sbuf("ca", [128, 8], F32)
    S.dma(cc, cc[:], k.c_col[:, :])
    S.act(ca, ca[:], cc[:], AF.Silu, [cc])
    wst = [S.sbuf("adw%d" % i, [128, 8, 512], F32) for i in range(2)]
    brow = S.sbuf("brow", [1, 6144], F32)
    mrow = S.sbuf("mrow", [1, 6144], F32)
    i = 0
    for l in range(2):
        S.dma(brow, brow[:], k.ada_b[l:l + 1, :])
        for nt in range(12):
            wt = wst[i % 2]
            S.dma(wt, wt[:], k.ada_w[l].rearrange("(kc p) n -> p kc n", p=128)[:, :, nt * 512:(nt + 1) * 512])
            ps = k.ps[2 + (i % 2)]
            for kc in range(8):
                S.mm(ps, ps[0:1, :], ca[:, kc:kc + 1], wt[:, kc, :], [ca, wt],
                     start=(kc == 0), stop=(kc == 7))
            S.tt("dve", mrow, mrow[0:1, nt * 512:(nt + 1) * 512], ps[0:1, :],
                 brow[0:1, nt * 512:(nt + 1) * 512], ALU.add, [ps, brow])
            i += 1
        S.dma(None, k.modd[l:l + 1, :], mrow[:], in_t=mrow)
    S.finish_wait("sp", [mrow])


def load_consts(S, k):
    k.identf = S.sbuf("identf", [128, 128], F32)
    k.ident = S.sbuf("ident", [128, 128], BF16)
    S.dma(k.identf, k.identf[:], k.c_ident[:, :])
    S.copy("dve", k.ident, k.ident[:], k.identf[:], [k.identf])


def alloc_psum(S, k):
    k.ps = [S.psum("psb0", [128, 1024], BF16)] + [S.psum("ps%d" % i, [128, 512], F32) for i in range(1, 8)]


def phase_hgrn(S, k, l, xin, xout, NT):
    TT = 256
    ntile = NT // TT
    load_consts(S, k)
    alloc_psum(S, k)
    j = 0
    cols = rows_to_cols(S, k, [k.modd[l, :], k.norm_mix[l, :], k.hg_lb[0, :], k.hg_lb[1, :],
                               k.hg_gnorm[0, :]], "hcols")
    gnc = S.sbuf("gnc", [128, 8], F32)
    S.copy("dve", gnc, gnc[:], cols[:, 72:73].to_broadcast([128, 8]), [cols])
    gcol = S.sbuf("gcol", [128, 8], F32)
    S.stt("dve", gcol, gcol[:], cols[:, 8:16], 1.0, cols[:, 48:56], ALU.add, ALU.mult, [cols])
    lbc = S.sbuf("lbc", [128, 8], F32)
    l1m = S.sbuf("l1m", [128, 8], F32)
    S.tt("dve", lbc, lbc[:], cols[:, 64:72], cols[:, 56:64], ALU.subtract, [cols])
    S.act(lbc, lbc[:], lbc[:], AF.Exp, [lbc])
    S.ts("dve", lbc, lbc[:], lbc[:], 1.0, None, ALU.add, reads=[lbc])
    S.op("dve", lambda e: e.reciprocal(lbc[:], lbc[:]), [lbc], [lbc])
    S.ts("dve", l1m, l1m[:], lbc[:], -1.0, 1.0, ALU.mult, ALU.add, reads=[lbc])
    S.act(l1m, l1m[:], l1m[:], AF.Ln, [l1m])
    sq = S.sbuf("sq", [128, 1024], F32)
    g1row = sq
    S.dma(g1row, g1row[:], k.modd[l, 2048:3072].partition_broadcast(128))
    Win = S.sbuf("Win", [128, 8, 4096], BF16)
    Wout = S.sbuf("Wout", [128, 8, 1024], BF16)
    stg = [S.sbuf("stg%d" % i, [128, 1024], F32) for i in range(2)]
    load_weight_bf16(S, Win, k.hg_w_in[0], 8, 4096, stg)
    load_weight_bf16(S, Wout, k.hg_w_out[0], 8, 1024, stg, grow=g1row, rowscale=gnc)
    rmask = S.sbuf("rmask", [128, TT], F32)
    S.memset("pool", rmask, rmask[:], 1.0)
    S.memset("pool", rmask, rmask[:].rearrange("p (c j) -> p c j", j=64)[:, :, 0:1], 0.0)
    cmask = S.sbuf("cmask", [128, 128], F32)
    S.dma(cmask, cmask[:], k.c_bdmask[:, :])
    st32 = S.sbuf("st32", [128, 8, 128], F32)
    stb = S.sbuf("stb", [128, 8, 128], BF16)
    S.memset("pool", st32, st32[:], 0.0)
    S.memset("pool", stb, stb[:], 0.0)
    sts = [S.sbuf("sts%d" % i, [128, 128], F32) for i in range(2)]
    xts = [S.sbuf("xt%d" % i, [128, 1024], F32) for i in range(3)]
    xn = S.sbuf("xn", [128, 1024], BF16)
    ss = S.sbuf("ss", [128, 8], F32)
    rstd = S.sbuf("rstd", [128, 8], F32)
    hT = S.sbuf("hT", [128, 8, TT], BF16)
    tu = S.sbuf("tu", [128, TT], F32)
    tA = S.sbuf("tA", [128, TT], F32)
    tB = S.sbuf("tB", [128, TT], F32)
    tb = S.sbuf("tb", [128, TT], F32)
    teb = S.sbuf("teb", [128, TT], F32)
    t1 = S.sbuf("t1", [128, TT], F32)
    qdT = S.sbuf("qdT", [128, 8, 2, 2, 128], BF16)
    kdT = S.sbuf("kdT", [128, 8, TT], BF16)
    kdtok = S.sbuf("kdtok", [128, 2, 8, 128], BF16)
    ebl = S.sbuf("ebl", [128, 8, 4], F32)
    vt = S.sbuf("vt", [128, 2, 1024], BF16)
    gs = S.sbuf("gs", [128, 2, 1024], BF16)
    otok = S.sbuf("otok", [128, 1024], F32)
    oss = S.sbuf("oss", [128, 8], F32)
    orstd = S.sbuf("orstd", [128, 8], F32)
    on = S.sbuf("on", [128, 1024], BF16)
    oT = S.sbuf("oT", [128, 8, TT], BF16)
    k.sc4 = [S.sbuf("sc4%d" % i, [128, 512], BF16) for i in range(2)]
    k.st1 = [S.sbuf("st1_%d" % i, [128, 128], F32) for i in range(8)]
    k.st1b = [S.sbuf("st1b_%d" % i, [128, 128], BF16) for i in range(8)]
    xo = [S.sbuf("xo%d" % i, [128, 1024], F32) for i in range(2)]
    S.memset("pool", qdT, qdT[:], 0.0)
    pT = k.ps[0]
    for T0 in range(ntile):
        r0 = T0 * TT
        for sub in range(2):
            xt = xts[(T0 * 2 + sub) % 3]
            S.dma(xt, xt[:], xin[r0 + sub * 128:r0 + (sub + 1) * 128, :])
            norm_to_hT(S, k, xt, xt[:], gcol, cols, hT, sub * 128, xn, sq, ss, rstd, pT)
        for h in range(8):
            pq = k.ps[1 + (h % 2)]
            for kc in range(8):
                S.mm(pq, pq[:, 0:TT], Win[:, kc, h * 128:(h + 1) * 128], hT[:, kc, :], [Win, hT],
                     start=(kc == 0), stop=(kc == 7))
            for kc in range(8):
                S.mm(pq, pq[:, TT:2 * TT], Win[:, kc, 1024 + h * 128:1024 + (h + 1) * 128], hT[:, kc, :],
                     [Win, hT], start=(kc == 0), stop=(kc == 7))
            z = pq[:, TT:2 * TT]
            S.act(tu, tu[:], z, AF.Exp, [pq], scale=-1.0)
            S.act(tA, tA[:], tu[:], AF.Ln, [tu], bias=1.0)
            S.act(tB, tB[:], tu[:], AF.Ln, [tu, lbc], bias=1.0, scale=lbc[:, h:h + 1])
            S.tt("pool", tB, tB[:], tB[:], tA[:], ALU.subtract, [tB, tA])
            S.op("dve", lambda e, tb=tb, tB=tB: e.tensor_tensor_scan(tb[:], rmask[:], tB[:], 0.0, ALU.mult, ALU.add),
                 [rmask, tB], [tb])
            S.act(teb, teb[:], tb[:], AF.Exp, [tb])
            for sub in range(2):
                for c2 in range(2):
                    cs = sub * 128 + c2 * 64
                    S.tt("dve", qdT, qdT[:, h, sub, c2, c2 * 64:(c2 + 1) * 64], pq[:, cs:cs + 64],
                         teb[:, cs:cs + 64], ALU.mult, [pq, teb])
            S.tt("dve", t1, t1[:], z, tA[:], ALU.add, [pq, tA])
            S.tt("pool", t1, t1[:], t1[:], tb[:], ALU.add, [t1, tb])
            S.act(kdT, kdT[:, h, :], t1[:], AF.Exp, [t1, l1m], scale=-1.0, bias=l1m[:, h:h + 1])
            S.copy("pool", ebl, ebl[:, h, :], teb[:].rearrange("p (c j) -> p c j", j=64)[:, :, 63], [teb])
        for sub in range(2):
            for half in range(4):
                pv = k.ps[3 + (half % 2)]
                c0 = 2048 + half * 512
                for kc in range(8):
                    S.mm(pv, pv[:], hT[:, kc, sub * 128:(sub + 1) * 128], Win[:, kc, c0:c0 + 512],
                         [hT, Win], start=(kc == 0), stop=(kc == 7))
                if half < 2:
                    S.copy("dve", vt, vt[:, sub, half * 512:(half + 1) * 512], pv[:], [pv])
                else:
                    S.act(gs, gs[:, sub, (half - 2) * 512:(half - 1) * 512], pv[:], AF.Silu, [pv])
        for sub in range(2):
            for h in range(8):
                S.transpose(pT, pT[:, h * 128:(h + 1) * 128], kdT[:, h, sub * 128:(sub + 1) * 128],
                            k.ident[:], [kdT, k.ident])
            S.copy("act", kdtok, kdtok[:, sub, :, :].rearrange("p h k -> p (h k)"), pT[:], [pT])
        for sub in range(2):
            for hg in range(2):
                pA, pB, pC, pD = k.ps[1], k.ps[2], k.ps[5], k.ps[6]
                if hg == 1:
                    pA, pB, pC, pD = k.ps[3], k.ps[4], k.ps[7], k.ps[6]
                hs = list(range(hg * 4, hg * 4 + 4))
                for i, h in enumerate(hs):
                    S.mm(pA, pA[:, i * 128:i * 128 + 64], kdT[:, h, sub * 128:(sub + 1) * 128],
                         qdT[:, h, sub, 0, 0:64], [kdT, qdT])
                    S.mm(pA, pA[:, i * 128 + 64:(i + 1) * 128], kdT[:, h, sub * 128:(sub + 1) * 128],
                         qdT[:, h, sub, 1, 64:128], [kdT, qdT])
                    S.mm(pB, pB[:, i * 128:(i + 1) * 128], kdtok[0:64, sub, h, :],
                         vt[0:64, sub, h * 128:(h + 1) * 128], [kdtok, vt])
                sc4 = k.sc4[hg]
                S.tt("dve", sc4, sc4[:].rearrange("p (i t) -> p i t", t=128),
                     pA[:].rearrange("p (i t) -> p i t", t=128),
                     cmask[:].unsqueeze(1).to_broadcast([128, 4, 128]), ALU.mult, [pA, cmask])
                for i, h in enumerate(hs):
                    c = sub * 2
                    stt_ = sts[i % 2]
                    S.ts("pool", stt_, stt_[:], st32[:, h, :], ebl[:, h, c:c + 1], None, ALU.mult,
                         reads=[st32, ebl])
                    S.stt("dve", k.st1[h], k.st1[h][:], pB[:, i * 128:(i + 1) * 128], ebl[:, h, c:c + 1],
                          stt_[:], ALU.mult, ALU.add, [pB, ebl, stt_])
                    S.copy("act", k.st1b[h], k.st1b[h][:], k.st1[h][:], [k.st1[h]])
                for i, h in enumerate(hs):
                    o_ap = pC[:, i * 128:(i + 1) * 128]
                    S.mm(pC, o_ap, sc4[:, i * 128:(i + 1) * 128], vt[:, sub, h * 128:(h + 1) * 128],
                         [sc4, vt], start=True, stop=False)
                    S.mm(pC, o_ap, qdT[:, h, sub, 0, :], stb[:, h, :], [qdT, stb], start=False, stop=False)
                    S.mm(pC, o_ap, qdT[:, h, sub, 1, :], k.st1b[h][:], [qdT, k.st1b[h]], start=False, stop=True)
                    S.mm(pD, pD[:, i * 128:(i + 1) * 128], kdtok[64:128, sub, h, :],
                         vt[64:128, sub, h * 128:(h + 1) * 128], [kdtok, vt])
                for i, h in enumerate(hs):
                    c = sub * 2 + 1
                    stt_ = sts[i % 2]
                    S.ts("pool", stt_, stt_[:], k.st1[h][:], ebl[:, h, c:c + 1], None, ALU.mult,
                         reads=[k.st1[h], ebl])
                    S.stt("dve", st32, st32[:, h, :], pD[:, i * 128:(i + 1) * 128], ebl[:, h, c:c + 1],
                          stt_[:], ALU.mult, ALU.add, [pD, ebl, stt_])
                    S.copy("act", stb, stb[:, h, :], st32[:, h, :], [st32])
                S.copy("dve", otok, otok[:, hg * 512:(hg + 1) * 512], pC[:], [pC])
                for i, h in enumerate(hs):
                    S.act(sq, sq[:, 0:128], otok[:, h * 128:(h + 1) * 128], AF.Square, [otok],
                          accum_out=oss[:, h:h + 1], extra_writes=[oss])
            rstd_from_ss(S, orstd, oss, 128, 8)
            S.tt("dve", otok, otok[:].rearrange("p (h v) -> p h v", v=128),
                 otok[:].rearrange("p (h v) -> p h v", v=128),
                 orstd[:].unsqueeze(2).to_broadcast([128, 8, 128]), ALU.mult, [otok, orstd])
            S.tt("pool", on, on[:], otok[:], gs[:, sub, :], ALU.mult, [otok, gs])
            for kc in range(8):
                S.transpose(pT, pT[:, kc * 128:(kc + 1) * 128], on[:, kc * 128:(kc + 1) * 128],
                            k.ident[:], [on, k.ident])
            S.copy("act", oT, oT[:, :, sub * 128:(sub + 1) * 128],
                   pT[:].rearrange("p (c t) -> p c t", t=128), [pT])
            xt = xts[(T0 * 2 + sub) % 3]
            xo_ = xo[sub]
            for half in range(2):
                py = k.ps[3 + half]
                for kc in range(8):
                    S.mm(py, py[:], oT[:, kc, sub * 128:(sub + 1) * 128], Wout[:, kc, half * 512:(half + 1) * 512],
                         [oT, Wout], start=(kc == 0), stop=(kc == 7))
                S.tt("dve", xo_, xo_[:, half * 512:(half + 1) * 512], py[:], xt[:, half * 512:(half + 1) * 512],
                     ALU.add, [py, xt])
            S.dma(None, xout[r0 + sub * 128:r0 + (sub + 1) * 128, :], xo_[:], in_t=xo_)
    S.finish_wait("sp", xo)


def phase_ffn(S, k, l, xnorm, xres, xout, NT, fc0, fc1, final=False):
    TT = 512
    ntile = NT // TT
    nfc = fc1 - fc0
    W = nfc * 128
    same = xres is xnorm
    load_consts(S, k)
    alloc_psum(S, k)
    cols = rows_to_cols(S, k, [k.modd[l, :], k.norm_ffn[l, :]], "fcols")
    gcol = S.sbuf("gcol", [128, 8], F32)
    S.stt("dve", gcol, gcol[:], cols[:, 32:40], 1.0, cols[:, 48:56], ALU.add, ALU.mult, [cols])
    shc = S.sbuf("shc", [128, 8], F32)
    S.copy("dve", shc, shc[:], cols[:, 24:32], [cols])
    ccols = rows_to_cols(S, k, [k.ffn_conv_w[l, 0, :], k.ffn_conv_w[l, 1, :], k.ffn_conv_w[l, 2, :],
                                k.ffn_conv_b[l, :]], "ccols")
    sq = S.sbuf("sq", [128, 1024], F32)
    g2row = sq
    S.dma(g2row, g2row[:], k.modd[l, 5120:6144].partition_broadcast(128))
    if final:
        fnrow = S.sbuf("fnrow", [128, 1024], F32)
        S.dma(fnrow, fnrow[:], k.final_norm[:].partition_broadcast(128))
    Wup = S.sbuf("Wup", [128, 8, 2 * W], BF16)
    Wdn = S.sbuf("Wdn", [128, nfc, 1024], BF16)
    stg = [S.sbuf("stg%d" % i, [128, 1024], F32) for i in range(3)]
    load_weight_bf16(S, Wup, k.ffn_w_up[l][:, fc0 * 128:fc1 * 128], 8, W, stg)
    load_weight_bf16(S, Wup, k.ffn_w_up[l][:, DFF + fc0 * 128:DFF + fc1 * 128], 8, W, stg, col_off=W)
    load_weight_bf16(S, Wdn, k.ffn_w_down[l][fc0 * 128:fc1 * 128, :], nfc, 1024, stg, grow=g2row)
    xts = [S.sbuf("xt%d" % i, [128, 1024], F32) for i in range(5 if same else 2)]
    xrs = xts if same else [S.sbuf("xr%d" % i, [128, 1024], F32) for i in range(2)]
    xn = S.sbuf("xn", [128, 1024], BF16)
    ss = S.sbuf("ss", [128, 8], F32)
    rstd = S.sbuf("rstd", [128, 8], F32)
    hT = S.sbuf("hT", [128, 8, TT], BF16)
    halo = S.sbuf("halo", [128, nfc, 2], F32)
    S.memset("pool", halo, halo[:], 0.0)
    abuf = [S.sbuf("abuf%d" % i, [128, TT + 2], F32) for i in range(2)]
    c1 = [S.sbuf("c1_%d" % i, [128, TT], F32) for i in range(2)]
    c2 = [S.sbuf("c2_%d" % i, [128, TT], F32) for i in range(2)]
    mT = S.sbuf("mT", [128, nfc, TT], BF16)
    xo = [S.sbuf("xo%d" % i, [128, 1024], F32) for i in range(2)]
    pT = k.ps[0]
    for T0 in range(ntile):
        r0 = T0 * TT
        for sub in range(4):
            xt = xts[(T0 * 4 + sub) % len(xts)]
            S.dma(xt, xt[:], xnorm[r0 + sub * 128:r0 + (sub + 1) * 128, :])
            norm_to_hT(S, k, xt, xt[:], gcol, shc, hT, sub * 128, xn, sq, ss, rstd, pT)
        for j in range(nfc):
            fc = fc0 + j
            pa = k.ps[1 + (j % 3)]
            pv = k.ps[4 + (j % 3)]
            for kc in range(8):
                S.mm(pa, pa[:], Wup[:, kc, j * 128:(j + 1) * 128], hT[:, kc, :], [Wup, hT],
                     start=(kc == 0), stop=(kc == 7))
            for kc in range(8):
                S.mm(pv, pv[:], Wup[:, kc, W + j * 128:W + (j + 1) * 128], hT[:, kc, :], [Wup, hT],
                     start=(kc == 0), stop=(kc == 7))
            ab = abuf[j % 2]
            c1_ = c1[j % 2]
            c2_ = c2[j % 2]
            S.copy("pool", ab, ab[:, 0:2], halo[:, j, :], [halo])
            S.copy("act", ab, ab[:, 2:TT + 2], pa[:], [pa])
            S.copy("pool", halo, halo[:, j, :], ab[:, TT:TT + 2], [ab])
            S.ts("pool", c1_, c1_[:], ab[:, 2:TT + 2], ccols[:, 44 + fc:45 + fc], ccols[:, 66 + fc:67 + fc],
                 ALU.mult, ALU.add, reads=[ab, ccols])
            S.stt("pool", c2_, c2_[:], ab[:, 1:TT + 1], ccols[:, 22 + fc:23 + fc], c1_[:], ALU.mult, ALU.add,
                  [ab, ccols, c1_])
            S.stt("dve", c1_, c1_[:], ab[:, 0:TT], ccols[:, fc:fc + 1], c2_[:], ALU.mult, ALU.add,
                  [ab, ccols, c2_])
            S.act(c2_, c2_[:], c1_[:], AF.Silu, [c1_])
            S.tt("dve", mT, mT[:, j, :], pv[:], c2_[:], ALU.mult, [pv, c2_])
        for sub in range(4):
            if same:
                xt = xts[(T0 * 4 + sub) % len(xts)]
            else:
                xt = xrs[sub % 2]
                S.dma(xt, xt[:], xres[r0 + sub * 128:r0 + (sub + 1) * 128, :])
            xo_ = xo[sub % 2]
            for half in range(2):
                py = k.ps[1 + half] if sub % 2 == 0 else k.ps[3 + half]
                for j in range(nfc):
                    S.mm(py, py[:], mT[:, j, sub * 128:(sub + 1) * 128], Wdn[:, j, half * 512:(half + 1) * 512],
                         [mT, Wdn], start=(j == 0), stop=(j == nfc - 1))
                S.tt("dve", xo_, xo_[:, half * 512:(half + 1) * 512], py[:], xt[:, half * 512:(half + 1) * 512],
                     ALU.add, [py, xt])
            if final:
                S.act(sq, sq[:], xo_[:], AF.Square, [xo_], accum_out=ss[:, 1:2], extra_writes=[ss])
                S.ts("dve", rstd, rstd[:, 1:2], ss[:, 1:2], 1.0 / D, EPS, ALU.mult, ALU.add, reads=[ss])
                S.act(rstd, rstd[:, 1:2], rstd[:, 1:2], AF.Ln, [rstd])
                S.act(rstd, rstd[:, 1:2], rstd[:, 1:2], AF.Exp, [rstd], scale=-0.5)
                S.stt("pool", xo_, xo_[:], xo_[:], rstd[:, 1:2], fnrow[:], ALU.mult, ALU.mult, [xo_, rstd, fnrow])
            S.dma(None, xout[r0 + sub * 128:r0 + (sub + 1) * 128, :], xo_[:], in_t=xo_)
    S.finish_wait("sp", xo)


def build_program(NT, phases=("pro", "hg", "ffn0", "nsa", "ffn1")):
    nc = bass.Bass("TRN2", target_bir_lowering=False)
    k = K()

    def inp(name, shape):
        return nc.dram_tensor(name, list(shape), F32, kind="ExternalInput").ap()

    k.x = inp("x", [NT, D])
    k.c_col = inp("c_col", [128, 8])
    k.ada_w = inp("ada_w", [2, D, 6 * D])
    k.ada_b = inp("ada_b", [2, 6 * D])
    k.norm_mix = inp("norm_mix", [2, D])
    k.norm_ffn = inp("norm_ffn", [2, D])
    k.final_norm = inp("final_norm", [D])
    k.hg_w_in = inp("hg_w_in", [1, D, 4096])
    k.hg_w_out = inp("hg_w_out", [1, D, D])
    k.hg_gnorm = inp("hg_gnorm", [1, 128])
    k.hg_lb = inp("hg_lb", [2, D])
    k.ffn_w_up = inp("ffn_w_up", [2, D, 2 * DFF])
    k.ffn_conv_w = inp("ffn_conv_w", [2, 3, DFF])
    k.ffn_conv_b = inp("ffn_conv_b", [2, DFF])
    k.ffn_w_down = inp("ffn_w_down", [2, DFF, D])
    k.c_ident = inp("c_ident", [128, 128])
    k.c_bdmask = inp("c_bdmask", [128, 128])
    nsa_declare(nc, k, NT)
    k.out = nc.dram_tensor("out", [NT, D], F32, kind="ExternalOutput").ap()
    k.modd = nc.dram_tensor("modd", [2, 6 * D], F32).ap()
    bufs = [nc.dram_tensor("xs%d" % i, [NT, D], F32).ap() for i in range(4)]
    xc = nc.dram_tensor("xc", [NT, D], F32).ap()
    main = [p for p in phases if p != "pro"]
    with contextlib.ExitStack() as gst:
        S = Sched(nc, gst)
        k.S = S
        plist = []
        if "pro" in phases:
            plist.append(lambda: (alloc_psum(S, k), phase_prologue(S, k)))
        src = k.x
        for i, p in enumerate(main):
            last = (i == len(main) - 1)
            dst = k.out if last else bufs[i]
            if p == "hg":
                plist.append(lambda src=src, dst=dst: phase_hgrn(S, k, 0, src, dst, NT))
            elif p in ("ffn0", "ffn1"):
                l = int(p[3])
                fin = (p == "ffn1")
                plist.append(lambda src=src, l=l: phase_ffn(S, k, l, src, src, xc, NT, 0, 11))
                plist.append(lambda src=src, dst=dst, l=l, fin=fin: phase_ffn(S, k, l, src, xc, dst, NT, 11, 22, final=fin))
            elif p == "nsa":
                import os
                stop = int(os.environ.get("NSA_STOP", "4"))
                plist.append(lambda src=src: phase_nsa_proj(S, k, 1, src, NT))
                if stop >= 2:
                    plist.append(lambda: phase_nsa_cmp(S, k, NT))
                if stop >= 3:
                    plist.append(lambda: phase_nsa_attn(S, k, NT))
                if stop >= 4:
                    plist.append(lambda src=src, dst=dst: phase_nsa_out(S, k, 1, src, dst, NT))
            src = dst
        for i, p in enumerate(plist):
            with contextlib.ExitStack() as st:
                S.stack = st
                p()
                S.barrier()
                S.emit()
                S.phase_end()
    return nc


def host_consts():
    ident = np.eye(128, dtype=np.float32)
    s = np.arange(128)[:, None]
    t = np.arange(128)[None, :]
    bd = ((s // 64 == t // 64) & (s <= t)).astype(np.float32)
    return {"c_ident": ident, "c_bdmask": bd}


NSA_W = 2608
SLOPES = [2.0 ** (-8.0 * (h + 1) / 16) for h in range(16)]


def nsa_dims(NT):
    ncb = (NT - 32) // 16 + 1
    nsb = NT // 64
    return ncb, nsb


def nsa_host_consts(NT):
    import ml_dtypes
    bf = ml_dtypes.bfloat16
    ncb, nsb = nsa_dims(NT)
    t = np.arange(NT)
    c = {}
    c["c_vrow"] = np.stack([-SLOPES[h] * t for h in range(16)]).astype(bf)
    KT = NT // 128
    i = np.arange(128)
    sb = np.zeros((128, KT * 16), np.float32)
    for kt in range(KT):
        for h in range(16):
            sb[:, kt * 16 + h] = SLOPES[h] * (128 * kt + i)
    c["c_sbias"] = sb
    cb = np.zeros((128, 2 * 16), np.float32)
    for bt in range(2):
        for h in range(16):
            cb[:, bt * 16 + h] = SLOPES[h] * (16 * (128 * bt + i) + 15.5)
    c["c_cbias"] = cb
    c["c_maskd"] = np.where(i[:, None] > i[None, :], NEG, 0.0).astype(bf)
    c["c_maskw"] = np.where(i[None, :] >= i[:, None], NEG, 0.0).astype(bf)
    QT = NT // 512
    cm = np.zeros((QT, 2, 128, 512), np.float32)
    for T in range(QT):
        for bt in range(2):
            n = 128 * bt + i
            tt = 512 * T + np.arange(512)
            cm[T, bt] = np.where(16 * n[:, None] + 31 <= tt[None, :], 0.0, NEG)
    c["c_cmask"] = cm.astype(bf)
    ov = np.zeros((256, 64), np.float32)
    ci = np.arange(256)[:, None] * 16
    sj = np.arange(64)[None, :] * 64
    ov[:] = ((ci <= sj + 63) & (ci + 31 >= sj))
    ov[ncb:] = 0
    ov[:, nsb:] = 0
    c["c_overlap"] = ov.astype(bf)
    blk = np.arange(64)[None, :]
    cur = (t // 64)[:, None]
    valid = blk * 64 <= t[:, None]
    forced = ((blk == 0) | (blk == cur) | (blk == cur - 1)) & valid
    vm = valid.astype(np.float32)
    am = np.where(forced, 1e4, np.where(valid, 0.0, -1.0)).astype(np.float32)
    vm[:, nsb:] = 0.0
    am[:, nsb:] = -1.0
    c["c_vmask"] = vm
    c["c_amask"] = am
    si = np.zeros((64, NT), np.float32)
    si[0] = 1.0
    for j in range(1, 64):
        si[j] = (t // 64 == j)
    c["c_selind"] = si.astype(bf)
    kw = np.zeros((64, NT), np.float32)
    kw[0] = 1.0
    c["c_kwrows"] = kw.astype(bf)
    return c


def nsa_declare(nc, k, NT):
    def inp(name, shape, dt=F32):
        return nc.dram_tensor(name, list(shape), dt, kind="ExternalInput").ap()
    QT = NT // 512
    KT = NT // 128
    k.nsa_w_in = inp("nsa_w_in", [1, D, NSA_W])
    k.nsa_w_out = inp("nsa_w_out", [1, D, D])
    k.nsa_cmp_pe = inp("nsa_cmp_pe", [1, 2, 32, 64])
    k.nsa_cmp_w1 = inp("nsa_cmp_w1", [1, 2, 2048, 64])
    k.nsa_cmp_w2 = inp("nsa_cmp_w2", [1, 2, 64, 64])
    k.c_vrow = inp("c_vrow", [16, NT], BF16)
    k.c_sbias = inp("c_sbias", [128, KT * 16])
    k.c_cbias = inp("c_cbias", [128, 32])
    k.c_maskd = inp("c_maskd", [128, 128], BF16)
    k.c_maskw = inp("c_maskw", [128, 128], BF16)
    k.c_cmask = inp("c_cmask", [QT, 2, 128, 512], BF16)
    k.c_overlap = inp("c_overlap", [256, 64], BF16)
    k.c_vmask = inp("c_vmask", [NT, 64])
    k.c_amask = inp("c_amask", [NT, 64])
    k.c_selind = inp("c_selind", [64, NT], BF16)
    k.c_kwrows = inp("c_kwrows", [64, NT], BF16)
    k.qT_d = nc.dram_tensor("qT_d", [1024, NT], BF16).ap()
    k.kT_d = nc.dram_tensor("kT_d", [4, 256, NT], BF16).ap()
    k.vtok_d = nc.dram_tensor("vtok_d", [2, NT, 256], BF16).ap()
    k.gT_d = nc.dram_tensor("gT_d", [48, NT], F32).ap()
    k.kcT_d = nc.dram_tensor("kcT_d", [4, 64, 256], BF16).ap()
    k.vc_d = nc.dram_tensor("vc_d", [4, 256, 64], BF16).ap()
    k.oT_d = nc.dram_tensor("oT_d", [1024, NT], BF16).ap()


def phase_nsa_proj(S, k, l, xin, NT):
    TT = 512
    ntile = NT // TT
    load_consts(S, k)
    alloc_psum(S, k)
    cols = rows_to_cols(S, k, [k.modd[l, :], k.norm_mix[l, :]], "ncols")
    gcol = S.sbuf("gcol", [128, 8], F32)
    S.stt("dve", gcol, gcol[:], cols[:, 8:16], 1.0, cols[:, 48:56], ALU.add, ALU.mult, [cols])
    W = S.sbuf("Wn", [128, 8, NSA_W], BF16)
    stg = [S.sbuf("stg%d" % i, [128, 1024], F32) for i in range(3)]
    load_weight_bf16(S, W, k.nsa_w_in[0], 8, NSA_W, stg)
    xts = [S.sbuf("xt%d" % i, [128, 1024], F32) for i in range(2)]
    xn = S.sbuf("xn", [128, 1024], BF16)
    sq = S.sbuf("sq", [128, 1024], F32)
    ss = S.sbuf("ss", [128, 8], F32)
    rstd = S.sbuf("rstd", [128, 8], F32)
    hT = S.sbuf("hT", [128, 8, TT], BF16)
    fsb = [S.sbuf("fsb%d" % i, [128, TT], BF16) for i in range(3)]
    vsb = [S.sbuf("vsb%d" % i, [128, 512], BF16) for i in range(2)]
    gsb = [S.sbuf("gsb%d" % i, [48, TT], F32) for i in range(2)]
    pT = k.ps[0]
    outs = fsb + vsb + gsb
    n = 0
    for T0 in range(ntile):
        r0 = T0 * TT
        for sub in range(4):
            xt = xts[sub % 2]
            S.dma(xt, xt[:], xin[r0 + sub * 128:r0 + (sub + 1) * 128, :])
            norm_to_hT(S, k, xt, xt[:], gcol, cols, hT, sub * 128, xn, sq, ss, rstd, pT)
        fm = [(hp * 128, ("q", hp)) for hp in range(8)]
        for kind, c0 in ((0, 1024), (1, 1280), (2, 1536), (3, 2048)):
            for cpart in range(2):
                fm.append((c0 + cpart * 128, ("k", kind, cpart)))
        for c0, tag in fm:
            ps = k.ps[1 + (n % 3)]
            f = fsb[n % 3]
            n += 1
            for kc in range(8):
                S.mm(ps, ps[:], W[:, kc, c0:c0 + 128], hT[:, kc, :], [W, hT], start=(kc == 0), stop=(kc == 7))
            if tag[0] == "q":
                S.act(f, f[:], ps[:], AF.Copy, [ps], scale=0.125)
                S.dma(None, k.qT_d[tag[1] * 128:(tag[1] + 1) * 128, r0:r0 + TT], f[:], in_t=f)
            else:
                S.copy("dve", f, f[:], ps[:], [ps])
                S.dma(None, k.kT_d[tag[1], tag[2] * 128:(tag[2] + 1) * 128, r0:r0 + TT], f[:], in_t=f)
        for sub in range(4):
            ps = k.ps[4 + (sub % 2)]
            v = vsb[sub % 2]
            for j, c0 in enumerate((1792, 2304)):
                for kc in range(8):
                    S.mm(ps, ps[:, j * 256:(j + 1) * 256], hT[:, kc, sub * 128:(sub + 1) * 128],
                         W[:, kc, c0:c0 + 256], [hT, W], start=(kc == 0), stop=(kc == 7))
            S.copy("dve", v, v[:], ps[:], [ps])
            for j in range(2):
                S.dma(None, k.vtok_d[j, r0 + sub * 128:r0 + (sub + 1) * 128, :], v[:, j * 256:(j + 1) * 256], in_t=v)
        ps = k.ps[6]
        g_ = gsb[T0 % 2]
        for kc in range(8):
            S.mm(ps, ps[0:48, :], W[:, kc, 2560:2608], hT[:, kc, :], [W, hT], start=(kc == 0), stop=(kc == 7))
        S.act(g_, g_[:], ps[0:48, :], AF.Sigmoid, [ps])
        S.dma(None, k.gT_d[:, r0:r0 + TT], g_[:], in_t=g_)
    S.finish_wait("sp", outs)


def phase_nsa_cmp(S, k, NT):
    ncb, nsb = nsa_dims(NT)
    load_consts(S, k)
    alloc_psum(S, k)
    xc = S.sbuf("xc", [64, 2, 4, NT], BF16)
    for kv in range(2):
        for g in range(4):
            S.dma(xc, xc[:, kv, g, :], k.kT_d[kv, g * 64:(g + 1) * 64, :])
    w1f = S.sbuf("w1f", [64, 2, 32, 64], F32)
    w1 = S.sbuf("w1", [64, 2, 32, 64], BF16)
    for kv in range(2):
        S.dma(w1f, w1f[:, kv, :, :], k.nsa_cmp_w1[0, kv].rearrange("(l d) e -> d l e", d=64))
    S.copy("pool", w1, w1[:], w1f[:], [w1f])
    w2f = S.sbuf("w2f", [64, 2, 64], F32)
    w2p = S.sbuf("w2p", [64, 2, 128], BF16)
    for kv in range(2):
        S.dma(w2f, w2f[:, kv, :], k.nsa_cmp_w2[0, kv])
    S.memset("pool", w2p, w2p[:], 0.0)
    S.copy("pool", w2p, w2p[:, :, 64:128], w2f[:], [w2f])
    pef = S.sbuf("pef", [32, 2, 64], F32)
    for kv in range(2):
        S.dma(pef, pef[:, kv, :], k.nsa_cmp_pe[0, kv])
    peT = S.sbuf("peT", [64, 2, 32], BF16)
    ps = k.ps[1]
    for kv in range(2):
        S.mm(ps, ps[0:64, kv * 32:(kv + 1) * 32], pef[:, kv, :], k.identf[0:32, 0:32], [pef, k.identf])
    S.copy("dve", peT, peT[:].rearrange("p a l -> p (a l)"), ps[0:64, 0:64], [ps])
    bias = S.sbuf("cbias", [64, 2], F32)
    ps = k.ps[2]
    for kv in range(2):
        for l in range(32):
            S.mm(ps, ps[0:64, kv:kv + 1], w1[:, kv, l, :], peT[:, kv, l:l + 1], [w1, peT],
                 start=(l == 0), stop=(l == 31))
    S.copy("dve", bias, bias[:], ps[0:64, 0:2], [ps])
    hid = [S.sbuf("hid%d" % i, [64, 256], BF16) for i in range(2)]
    osb = [S.sbuf("osb%d" % i, [128, 256], BF16) for i in range(2)]
    n = 0
    for kv in range(2):
        for g in range(4):
            ph = k.ps[3 + (n % 2)]
            hd = hid[n % 2]
            ob = osb[n % 2]
            for l in range(32):
                rhs = xc[:, kv, g, l:l + 16 * (ncb - 1) + 1:16]
                S.mm(ph, ph[0:64, 0:ncb], w1[:, kv, l, :], rhs, [w1, xc], start=(l == 0), stop=(l == 31))
            S.act(hd, hd[:, 0:ncb], ph[0:64, 0:ncb], AF.Silu, [ph, bias], bias=bias[:, kv:kv + 1])
            po = k.ps[5 + (n % 2)]
            if kv == 0:
                S.mm(po, po[:, 0:ncb], w2p[:, 0, :], hd[:, 0:ncb], [w2p, hd])
                S.copy("dve", ob, ob[64:128, 0:ncb], po[64:128, 0:ncb], [po])
                S.dma(None, k.kcT_d[g, :, 0:ncb], ob[64:128, 0:ncb], in_t=ob)
            else:
                for bt in range((ncb + 127) // 128):
                    nb = min(128, ncb - bt * 128)
                    S.mm(po, po[0:nb, bt * 64:(bt + 1) * 64], hd[:, bt * 128:bt * 128 + nb], w2p[:, 1, 64:128],
                         [hd, w2p])
                    S.copy("dve", ob, ob[0:nb, bt * 64:(bt + 1) * 64], po[0:nb, bt * 64:(bt + 1) * 64], [po])
                    S.dma(None, k.vc_d[g, bt * 128:bt * 128 + nb, :], ob[0:nb, bt * 64:(bt + 1) * 64], in_t=ob)
            n += 1
    S.finish_wait("sp", osb)


def phase_nsa_attn(S, k, NT):
    ncb, nsb = nsa_dims(NT)
    QT = NT // 512
    KT = NT // 128
    NBT = (ncb + 127) // 128
    load_consts(S, k)
    alloc_psum(S, k)
    sbias = S.sbuf("sbias", [128, KT * 16], F32)
    S.dma(sbias, sbias[:], k.c_sbias[:, :])
    cbias = S.sbuf("cbias", [128, 32], F32)
    S.dma(cbias, cbias[:], k.c_cbias[:, :])
    maskd = S.sbuf("maskd", [128, 128], BF16)
    S.dma(maskd, maskd[:], k.c_maskd[:, :])
    maskw = S.sbuf("maskw", [128, 128], BF16)
    S.dma(maskw, maskw[:], k.c_maskw[:, :])
    ovl = S.sbuf("ovl", [128, 2, 64], BF16)
    S.dma(ovl, ovl[:], k.c_overlap.rearrange("(bt p) j -> p bt j", p=128))
    Ks = S.sbuf("Ks", [128, NT], BF16)
    Kw = S.sbuf("Kw", [128, NT], BF16)
    Kc = S.sbuf("Kc", [128, 256], BF16)
    Vs = S.sbuf("Vs", [128, KT, 128], BF16)
    Vw = S.sbuf("Vw", [128, KT, 128], BF16)
    Vc = S.sbuf("Vc", [128, 2, 128], BF16)
    S.memset("pool", Vs, Vs[:], 1.0)
    S.memset("pool", Vw, Vw[:], 1.0)
    S.memset("pool", Vc, Vc[:], 1.0)
    S.memset("pool", Kc, Kc[:], 0.0)
    S.dma(Ks, Ks[0:64, :], k.c_selind[:, :])
    S.dma(Kw, Kw[0:64, :], k.c_kwrows[:, :])
    S.dma(Kc, Kc[0:64, :], k.c_kwrows[:, 0:256])
    QaT = [S.sbuf("Qa%d" % i, [128, 4, 512], BF16) for i in range(2)]
    Qav = [[S.view("Qa%d_%d" % (i, hh), QaT[i].ap[:, hh, :]) for hh in range(4)] for i in range(2)]
    for q in QaT:
        S.memset("pool", q, q[:], 0.0)
    for i in range(2):
        for hh in range(4):
            Qav[i][hh].last_w = QaT[i].last_w
    gbT = [S.sbuf("gb%d" % i, [128, 12, 512], F32) for i in range(2)]
    Pt = [S.sbuf("Pt%d" % i, [128, 512], BF16) for i in range(4)]
    cmk = [S.sbuf("cmk%d" % i, [128, 512], BF16) for i in range(2)]
    rd = [S.sbuf("rd%d" % i, [128, 512], F32) for i in range(2)]
    coef = [S.sbuf("coef%d" % i, [128, 512], F32) for i in range(2)]
    tmp = [S.sbuf("tmp%d" % i, [128, 512], F32) for i in range(2)]
    oacc = [S.sbuf("oacc%d" % i, [128, 512], F32) for i in range(4)]
    osbT = [S.sbuf("osb%d" % i, [128, 4, 512], BF16) for i in range(2)]
    impacc = S.sbuf("impacc", [128, 512], F32)
    vm = S.sbuf("vm", [128, 4, 64], F32)
    am = S.sbuf("am", [128, 4, 64], F32)
    sc = S.sbuf("sc", [128, 4, 64], F32)
    sc2 = S.sbuf("sc2", [128, 64], F32)
    mx = S.sbuf("mx", [128, 8], F32)
    thr = S.sbuf("thr", [128, 4], F32)
    selb = S.sbuf("selb", [128, 4, 64], BF16)
    pT = k.ps[0]
    pSs = [k.ps[1], k.ps[2], k.ps[6]]
    pOs = [k.ps[3], k.ps[7]]
    pI, pTk = k.ps[4], k.ps[5]
    cnt = {"s": 0, "p": 0, "r": 0, "cm": 0, "o": 0, "po": 0}
    NP = len(Pt)
    LOOK = 2
    pend = []
    cur = {}

    def finish_branch(pO, hh, br, first, want_imp):
        i = cnt["r"] % 2
        cnt["r"] += 1
        r_, c_, t_ = rd[i], coef[i], tmp[i]
        gbt = cur["gb"]
        S.act(r_, r_[64:128, :], pO[64:128, :], AF.Ln, [pO], bias=(1e-30 if br == 0 else 0.0))
        S.act(r_, r_[64:128, :], r_[64:128, :], AF.Exp, [r_], scale=-1.0)
        S.tt("dve", c_, c_[64:128, :], r_[64:128, :], gbt[64:128, 3 * hh + br, :], ALU.mult, [r_, gbt])
        if first:
            S.tt("dve", oacc[hh], oacc[hh][0:64, :], pO[0:64, :], c_[64:128, :], ALU.mult, [pO, c_])
        else:
            S.tt("dve", t_, t_[0:64, :], pO[0:64, :], c_[64:128, :], ALU.mult, [pO, c_])
            S.tt("pool", oacc[hh], oacc[hh][0:64, :], oacc[hh][0:64, :], t_[0:64, :], ALU.add, [oacc[hh], t_])
        if want_imp:
            if hh == 0:
                S.tt("dve", impacc, impacc[0:64, :], pI[0:64, :], r_[64:128, :], ALU.mult, [pI, r_])
            else:
                S.tt("dve", t_, t_[0:64, :], pI[0:64, :], r_[64:128, :], ALU.mult, [pI, r_])
                S.tt("pool", impacc, impacc[0:64, :], impacc[0:64, :], t_[0:64, :], ALU.add, [impacc, t_])

    def emit_pv(item):
        (pO, lhsV, P, np_, clo, chi, first, last, ovl_ap, cb, vt_) = item
        S.mm(pO, pO[:, clo:chi], lhsV, P[0:np_, clo:chi], [vt_, P], start=first, stop=last)
        if ovl_ap is not None:
            S.mm(pI, pI[0:64, clo:chi], ovl_ap, P[0:np_, clo:chi], [ovl, P], start=first, stop=last)
        if cb is not None:
            cb()

    def push(item):
        pend.append(item)
        while len(pend) > LOOK:
            emit_pv(pend.pop(0))

    def flush():
        while pend:
            emit_pv(pend.pop(0))

    def attend(hh, h, Kt, Vt, tiles, br, first_branch):
        pO = pOs[cnt["po"] % 2]
        cnt["po"] += 1
        Qh = cur["Qa"][hh]
        for idx, (kt, clo, chi, masks) in enumerate(tiles):
            pS = pSs[cnt["s"] % len(pSs)]
            cnt["s"] += 1
            S.mm(pS, pS[:, clo:chi], Kt[:, kt * 128:(kt + 1) * 128], Qh[:, clo:chi], [Kt, Qh],
                 start=True, stop=(len(masks) == 0))
            for mi, (mk, c0) in enumerate(masks):
                S.mm(pS, pS[:, c0:c0 + 128], k.ident[:], mk[:], [k.ident, mk], start=False,
                     stop=(mi == len(masks) - 1))
            P = Pt[cnt["p"] % NP]
            cnt["p"] += 1
            S.act(P, P[:, clo:chi], pS[:, clo:chi], AF.Exp, [pS, sbias], bias=sbias[:, kt * 16 + h:kt * 16 + h + 1])
            last = (idx == len(tiles) - 1)
            cb = (lambda pO=pO, hh=hh, br=br, fb=first_branch: finish_branch(pO, hh, br, fb, False)) if last else None
            push((pO, Vt[:, kt, :], P, 128, clo, chi, idx == 0, last, None, cb, Vt))

    units = [(g, T) for g in range(4) for T in range(QT)]

    def load_unit(ui):
        g, T = units[ui]
        T0 = 512 * T
        par = ui % 2
        S.op("sp", lambda e: e.dma_start(out=gbT[par][64:128, :, :],
                                         in_=k.gT_d[12 * g:12 * g + 12, T0:T0 + 512].partition_broadcast(64)),
             [], [gbT[par]], dma_sem_tile=gbT[par])
        S.op("sp", lambda e: e.dma_start(out=QaT[par][64:128, :, :],
                                         in_=k.qT_d[256 * g:256 * g + 256, T0:T0 + 512].rearrange("(hh d) t -> d hh t", d=64)),
             [], Qav[par], dma_sem_tile=Qav[par][0])
        S.op("sp", lambda e: e.dma_start(out=QaT[par][0:1, :, :], in_=k.c_vrow[4 * g:4 * g + 4, T0:T0 + 512].rearrange("(o h) t -> o h t", o=1)),
             [], Qav[par], dma_sem_tile=Qav[par][0])

    def load_group(g):
        S.dma(Ks, Ks[64:128, :], k.kT_d[2, g * 64:(g + 1) * 64, :])
        S.dma(Kw, Kw[64:128, :], k.kT_d[3, g * 64:(g + 1) * 64, :])
        S.dma(Kc, Kc[64:128, 0:ncb], k.kcT_d[g, :, 0:ncb])
        for k0 in range(0, KT, 8):
            k1 = min(KT, k0 + 8)
            S.dma(Vs, Vs[:, k0:k1, 0:64],
                  k.vtok_d[0][k0 * 128:k1 * 128, g * 64:(g + 1) * 64].rearrange("(kt p) d -> p kt d", p=128))
            S.dma(Vw, Vw[:, k0:k1, 0:64],
                  k.vtok_d[1][k0 * 128:k1 * 128, g * 64:(g + 1) * 64].rearrange("(kt p) d -> p kt d", p=128))
        for bt in range(NBT):
            nb = min(128, ncb - bt * 128)
            S.dma(Vc, Vc[0:nb, bt, 0:64], k.vc_d[g, bt * 128:bt * 128 + nb, :])

    load_unit(0)
    for ui, (g, T) in enumerate(units):
        if T == 0:
            load_group(g)
        T0 = 512 * T
        par = ui % 2
        cur["Qa"] = Qav[par]
        cur["gb"] = gbT[par]
        Qa = Qav[par]
        bts = []
        for bt in range(NBT):
            nb = min(128, ncb - bt * 128)
            n_lo, n_hi = 128 * bt, 128 * bt + nb - 1
            if 16 * n_lo + 31 > T0 + 511:
                continue
            partial = 16 * n_hi + 31 > T0
            bts.append((bt, nb, partial))
        cms = {}
        for (bt, nb, partial) in bts:
            if partial:
                cm_ = cmk[cnt["cm"] % 2]
                cnt["cm"] += 1
                S.dma(cm_, cm_[:], k.c_cmask[T, bt])
                cms[bt] = cm_
        S.dma(vm, vm[:], k.c_vmask[T0:T0 + 512, :].rearrange("(n p) j -> p n j", p=128))
        S.dma(am, am[:], k.c_amask[T0:T0 + 512, :].rearrange("(n p) j -> p n j", p=128))
        for hh in range(4):
            h = 4 * g + hh
            pO = pOs[cnt["po"] % 2]
            cnt["po"] += 1
            for bi, (bt, nb, partial) in enumerate(bts):
                pS = pSs[cnt["s"] % len(pSs)]
                cnt["s"] += 1
                S.mm(pS, pS[0:nb, :], Kc[:, bt * 128:bt * 128 + nb], Qa[hh][:, :], [Kc, Qa[hh]],
                     start=True, stop=(not partial))
                if partial:
                    cm_ = cms[bt]
                    S.mm(pS, pS[0:nb, :], k.ident[0:nb, 0:nb], cm_[0:nb, :], [k.ident, cm_], start=False, stop=True)
                P = Pt[cnt["p"] % NP]
                cnt["p"] += 1
                S.act(P, P[0:nb, :], pS[0:nb, :], AF.Exp, [pS, cbias],
                      bias=cbias[0:nb, bt * 16 + h:bt * 16 + h + 1])
                last = (bi == len(bts) - 1)
                cb = (lambda pO=pO, hh=hh: finish_branch(pO, hh, 0, True, True)) if last else None
                push((pO, Vc[0:nb, bt, :], P, nb, 0, 512, bi == 0, last, ovl[0:nb, bt, :], cb, Vc))
        for hh in range(4):
            h = 4 * g + hh
            tiles = []
            for kt in range(max(0, 4 * T - 4), 4 * T + 4):
                m = kt - 4 * T
                n_lo, n_hi = max(m, 0), min(m + 4, 3)
                masks = []
                if m >= 0:
                    masks.append((maskd, 128 * m))
                if m <= -1:
                    masks.append((maskw, 128 * (m + 4)))
                tiles.append((kt, 128 * n_lo, 128 * (n_hi + 1), masks))
            tiles.sort(key=lambda tl: -(tl[2] - tl[1]))
            attend(hh, h, Kw, Vw, tiles, 2, False)
            if hh == 1:
                for n in range(4):
                    S.mm(pTk, pTk[:, n * 64:(n + 1) * 64], impacc[0:64, n * 128:(n + 1) * 128],
                         k.identf[0:64, 0:64], [impacc, k.identf])
                S.tt("dve", sc, sc[:].rearrange("p n j -> p (n j)"), pTk[:, 0:256],
                     vm[:].rearrange("p n j -> p (n j)"), ALU.mult, [pTk, vm])
                S.tt("pool", sc, sc[:], sc[:], am[:], ALU.add, [sc, am])
                for n in range(4):
                    S.op("dve", lambda e, n=n: e.max(mx[:], sc[:, n, :]), [sc], [mx])
                    S.op("dve", lambda e, n=n: e.match_replace(sc2[:], mx[:], sc[:, n, :], -1e9), [sc, mx], [sc2])
                    S.op("dve", lambda e: e.max(mx[:], sc2[:]), [sc2], [mx])
                    S.ts("dve", thr, thr[:, n:n + 1], mx[:, 7:8], -0.5, None, ALU.max, reads=[mx])
                    S.ts("dve", sc2, sc2[:], sc[:, n, :], thr[:, n:n + 1], None, ALU.is_ge, reads=[sc, thr])
                    S.ts("dve", selb, selb[:, n, :], sc2[:], -1.0, -NEG, ALU.add, ALU.mult, reads=[sc2])
        if ui + 1 < len(units):
            load_unit(ui + 1)
        for n in range(4):
            S.transpose(pT, pT[0:64, n * 128:(n + 1) * 128], selb[:, n, :], k.ident[:], [selb, k.ident])
        flush()
        for hh in range(4):
            S.copy("act", Qa[hh], Qa[hh][0:64, :], pT[0:64, 0:512], [pT])
        S.op("sp", lambda e, par=par, g=g, T0=T0: e.dma_start(
            out=QaT[par][0:1, :, :], in_=k.c_vrow[4 * g:4 * g + 4, T0:T0 + 512].rearrange("(o h) t -> o h t", o=1)),
            [], Qa, dma_sem_tile=Qa[0])
        for hh in range(4):
            h = 4 * g + hh
            tiles = []
            for kt in range(4 * T + 4):
                j = kt - 4 * T
                if j < 0:
                    tiles.append((kt, 0, 512, []))
                else:
                    tiles.append((kt, 128 * j, 512, [(maskd, 128 * j)]))
            attend(hh, h, Ks, Vs, tiles, 1, False)
        flush()
        ob = osbT[cnt["o"] % 2]
        cnt["o"] += 1
        for hh in range(4):
            S.copy("act", ob, ob[0:64, hh, :], oacc[hh][0:64, :], [oacc[hh]])
        S.dma(None, k.oT_d[256 * g:256 * g + 256, T0:T0 + 512].rearrange("(hh d) t -> d hh t", d=64),
              ob[0:64, :, :], in_t=ob)
    S.finish_wait("sp", osbT)


def phase_nsa_out(S, k, l, xin, xout, NT):
    TT = 512
    ntile = NT // TT
    load_consts(S, k)
    alloc_psum(S, k)
    g1row = S.sbuf("g1row", [128, 1024], F32)
    S.dma(g1row, g1row[:], k.modd[l, 2048:3072].partition_broadcast(128))
    Wo = S.sbuf("Wo", [128, 8, 1024], BF16)
    stg = [S.sbuf("stg%d" % i, [128, 1024], F32) for i in range(2)]
    load_weight_bf16(S, Wo, k.nsa_w_out[0], 8, 1024, stg, grow=g1row)
    oT = [S.sbuf("oT%d" % i, [128, 8, TT], BF16) for i in range(2)]
    xts = [S.sbuf("xt%d" % i, [128, 1024], F32) for i in range(2)]
    xo = [S.sbuf("xo%d" % i, [128, 1024], F32) for i in range(2)]
    n = 0
    for T0 in range(ntile):
        r0 = T0 * TT
        o_ = oT[T0 % 2]
        S.dma(o_, o_[:], k.oT_d[:, r0:r0 + TT].rearrange("(kc p) t -> p kc t", p=128))
        for sub in range(4):
            xt = xts[sub % 2]
            xo_ = xo[sub % 2]
            S.dma(xt, xt[:], xin[r0 + sub * 128:r0 + (sub + 1) * 128, :])
            for half in range(2):
                py = k.ps[1 + (n % 4)]
                n += 1
                for kc in range(8):
                    S.mm(py, py[:], o_[:, kc, sub * 128:(sub + 1) * 128], Wo[:, kc, half * 512:(half + 1) * 512],
                         [o_, Wo], start=(kc == 0), stop=(kc == 7))
                S.tt("dve", xo_, xo_[:, half * 512:(half + 1) * 512], py[:], xt[:, half * 512:(half + 1) * 512],
                     ALU.add, [py, xt])
            S.dma(None, xout[r0 + sub * 128:r0 + (sub + 1) * 128, :], xo_[:], in_t=xo_)
    S.finish_wait("sp", xo)


W_KEYS = ["ada_w", "ada_b", "norm_mix", "norm_ffn", "final_norm", "hg_w_in", "hg_w_out", "hg_gnorm", "hg_lb",
          "ffn_w_up", "ffn_conv_w", "ffn_conv_b", "ffn_w_down", "nsa_w_in", "nsa_w_out", "nsa_cmp_pe",
          "nsa_cmp_w1", "nsa_cmp_w2"]


def make_in_map(inp, x, c, NT, consts=None):
    im = {"x": np.ascontiguousarray(x, dtype=np.float32),
          "c_col": np.ascontiguousarray(np.asarray(c, dtype=np.float32).reshape(8, 128).T)}
    for k_ in W_KEYS:
        im[k_] = np.asarray(inp[k_], dtype=np.float32)
    if consts is None:
        consts = dict(host_consts())
        consts.update(nsa_host_consts(NT))
    im.update(consts)
    return im


_CACHE = {}


def kernel(**inputs):
    x = np.asarray(inputs["x"], dtype=np.float32)
    c = np.asarray(inputs["c"], dtype=np.float32)
    B, NT, _ = x.shape
    if "nc" not in _CACHE:
        _CACHE["nc"] = build_program(NT)
        consts = dict(host_consts())
        consts.update(nsa_host_consts(NT))
        _CACHE["consts"] = consts
    nc = _CACHE["nc"]
    in_maps = [make_in_map(inputs, x[b], c[b], NT, _CACHE["consts"]) for b in range(B)]
    res = run_bass_kernel_spmd(nc, in_maps, core_ids=list(range(B)))
    out = np.stack([np.asarray(r["out"], dtype=np.float32) for r in res.results], axis=0)
    return out
```

```python
import contextlib
import numpy as np
import concourse.bass as bass
import concourse.mybir as mybir

F32 = mybir.dt.float32
BF16 = mybir.dt.bfloat16
AF = mybir.ActivationFunctionType
ALU = mybir.AluOpType
AX = mybir.AxisListType

ENGS = ["pe", "act", "dve", "pool", "sp"]


class T:
    __slots__ = ("name", "ap", "last_w", "readers", "dsem", "dcnt", "uid")
    _n = [0]

    def __init__(self, name, ap=None):
        T._n[0] += 1
        self.uid = T._n[0]
        self.name = name
        self.ap = ap
        self.last_w = None
        self.readers = []
        self.dsem = None
        self.dcnt = 0

    def __getitem__(self, idx):
        return self.ap[idx]


class Sched:
    def __init__(self, nc, stack):
        self.nc = nc
        self.stack = stack
        self.ops = {e: [] for e in ENGS}
        self.cnt = {e: 0 for e in ENGS}
        self.clock = {e: {} for e in ENGS}
        self.sem = {}
        for e in ["pe", "act", "dve", "pool"]:
            self.sem[e] = stack.enter_context(nc.semaphore("s_" + e))
        self.sem["bar"] = stack.enter_context(nc.semaphore("s_bar"))
        self.bar_n = 0
        self.dma_live = {}
        self.gstack = stack
        self.nsem = 5
        self.final_waits = []
        self.n_wait = 0
        self.uid = 0
        self.dsem_pool = []
        self.dsem_owner = []

    def sbuf(self, name, shape, dtype):
        self.uid += 1
        name = "%s_u%d" % (name, self.uid)
        t = self.stack.enter_context(self.nc.sbuf_tensor(name, list(shape), dtype))
        return T(name, t)

    def psum(self, name, shape, dtype=F32):
        self.uid += 1
        name = "%s_u%d" % (name, self.uid)
        t = self.stack.enter_context(self.nc.psum_tensor(name, list(shape), dtype))
        return T(name, t)

    def view(self, name, ap):
        return T(name, ap)

    def _dsem(self, t):
        if t.dsem is None:
            if self.dsem_pool:
                t.dsem, t.dcnt = self.dsem_pool.pop()
            else:
                self.nsem += 1
                t.dsem = self.gstack.enter_context(self.nc.semaphore("dsem%d" % self.nsem))
                t.dcnt = 0
            self.dsem_owner.append(t)
        return t.dsem

    def phase_end(self):
        for t in self.dsem_owner:
            self.dsem_pool.append((t.dsem, t.dcnt))
            t.dsem = None
        self.dsem_owner = []
        self.dma_live = {}

    def _need(self, eng, ev, waits):
        key, val, snap = ev
        if eng == "pe" and key == "pe":
            return
        ck = self.clock[eng]
        if ck.get(key, 0) >= val:
            return
        waits[key] = max(waits.get(key, 0), val)
        ck[key] = val
        if snap:
            for k, v in snap.items():
                if ck.get(k, 0) < v:
                    ck[k] = v

    def op(self, eng, fn, reads=(), writes=(), dma_sem_tile=None):
        waits = {}
        for t in reads:
            if t.last_w is not None:
                self._need(eng, t.last_w, waits)
        for t in writes:
            if t.last_w is not None:
                self._need(eng, t.last_w, waits)
            for ev in t.readers:
                self._need(eng, ev, waits)
        if dma_sem_tile is not None:
            st = dma_sem_tile
            sem = self._dsem(st)
            st.dcnt += 16
            key = ("d", st.uid)
            self.sem[key] = sem
            ev = (key, st.dcnt, dict(self.clock[eng]))
            self.dma_live[key] = (st, st.dcnt)
            inc = (sem, 16)
        else:
            self.cnt[eng] += 1
            ev = (eng, self.cnt[eng], None)
            inc = (self.sem[eng], 1)
        self.ops[eng].append((list(waits.items()), fn, inc))
        self.n_wait += len(waits)
        if dma_sem_tile is None:
            snap = dict(self.clock[eng])
            ev = (eng, self.cnt[eng], snap)
        for t in writes:
            t.last_w = ev
            t.readers = []
        for t in reads:
            if t not in writes:
                t.readers.append(ev)
        return ev

    def finish_wait(self, eng, tiles):
        waits = {}
        for t in tiles:
            if t.last_w is not None:
                self._need(eng, t.last_w, waits)
            for ev in t.readers:
                self._need(eng, ev, waits)
        self.ops[eng].append((list(waits.items()), None, None))

    def barrier(self):
        evs = []
        for e in ["pe", "act", "dve", "pool"]:
            if self.cnt[e] > 0:
                evs.append((e, self.cnt[e], None))
        for key, (t, val) in self.dma_live.items():
            evs.append((key, val, None))
        for eng in ENGS:
            waits = {}
            for ev in evs:
                if ev[0] == eng and eng == "pe":
                    continue
                ck = self.clock[eng]
                if ck.get(ev[0], 0) < ev[1]:
                    waits[ev[0]] = ev[1]
                    ck[ev[0]] = ev[1]
            self.ops[eng].append((list(waits.items()), None, None))
        self.bar_n += 1
        for eng in ENGS:
            self.ops[eng].append(([], "barinc", None))
        for eng in ENGS:
            self.ops[eng].append(([("bar", 5 * self.bar_n)], None, None))

    def emit(self):
        nc = self.nc
        with nc.Block() as block:
            def run(eng_name):
                def body(e):
                    for waits, fn, inc in self.ops[eng_name]:
                        for key, val in waits:
                            e.wait_ge(self.sem[key], val)
                        if fn == "barinc":
                            e.sem_inc(self.sem["bar"], 1)
                        elif fn is not None:
                            ins = fn(e)
                            ins.then_inc(inc[0], inc[1])
                return body
            block.tensor(run("pe"))
            block.scalar(run("act"))
            block.vector(run("dve"))
            block.gpsimd(run("pool"))
            block.sync(run("sp"))
        self.ops = {e: [] for e in ENGS}

    def dma(self, out_t, out_ap, in_ap, in_t=None, eng="sp", **kw):
        reads = [in_t] if in_t is not None else []
        writes = [out_t] if out_t is not None else []
        st = out_t if out_t is not None else in_t
        return self.op(eng, lambda e: e.dma_start(out=out_ap, in_=in_ap, **kw), reads, writes,
                       dma_sem_tile=st)

    def mm(self, out_t, out_ap, lhsT, rhs, reads, start=True, stop=True, **kw):
        return self.op("pe", lambda e: e.matmul(out_ap, lhsT, rhs, start=start, stop=stop, **kw),
                       reads, [out_t])

    def transpose(self, out_t, out_ap, in_ap, ident_ap, reads):
        return self.op("pe", lambda e: e.transpose(out_ap, in_ap, ident_ap), reads, [out_t])

    def act(self, out_t, out_ap, in_ap, func, reads, bias=None, scale=None, accum_out=None,
            extra_writes=()):
        kw = {}
        if bias is not None:
            kw["bias"] = bias
        if scale is not None:
            kw["scale"] = scale
        if accum_out is not None:
            kw["accum_out"] = accum_out
        return self.op("act", lambda e: e.activation(out_ap, in_ap, func, **kw), reads,
                       [out_t] + list(extra_writes))

    def tt(self, eng, out_t, out_ap, in0, in1, op, reads):
        return self.op(eng, lambda e: e.tensor_tensor(out_ap, in0, in1, op), reads, [out_t])

    def ts(self, eng, out_t, out_ap, in0, s1, s2, op0, op1=None, reads=(), accum_out=None,
           extra_writes=()):
        def f(e):
            kw = {}
            if accum_out is not None:
                kw["accum_out"] = accum_out
            if op1 is None:
                return e.tensor_scalar(out_ap, in0, s1, None, op0, **kw)
            return e.tensor_scalar(out_ap, in0, s1, s2, op0, op1, **kw)
        return self.op(eng, f, reads, [out_t] + list(extra_writes))

    def stt(self, eng, out_t, out_ap, in0, scalar, in1, op0, op1, reads):
        eng = "dve"
        return self.op(eng, lambda e: e.scalar_tensor_tensor(out_ap, in0, scalar, in1, op0, op1),
                       reads, [out_t])

    def copy(self, eng, out_t, out_ap, in_ap, reads):
        if eng == "act":
            return self.op("act", lambda e: e.copy(out_ap, in_ap), reads, [out_t])
        return self.op(eng, lambda e: e.tensor_copy(out_ap, in_ap), reads, [out_t])

    def memset(self, eng, out_t, out_ap, val):
        return self.op(eng, lambda e: e.memset(out_ap, val), [], [out_t])

from concourse.bass_utils import run_bass_kernel_spmd

D = 1024
NH_HG = 8
DFF = 2816
NFC = DFF // 128
EPS = 1e-6
NEG = -30000.0


def bcast_rows(ap_row, n):
    return ap_row.partition_broadcast(n)


class K:
    pass


def load_weight_bf16(S, Wb, w_dram, KC, N, stg, grow=None, col_off=0, rowscale=None, kc_off=0):
    i = 0
    for kc in range(KC):
        for n0 in range(0, N, 1024):
            n1 = min(N, n0 + 1024)
            st = stg[i % len(stg)]
            S.dma(st, st[:, 0:n1 - n0], w_dram[kc * 128:(kc + 1) * 128, n0:n1])
            eng = "dve"
            o = Wb[:, kc_off + kc, col_off + n0:col_off + n1]
            if grow is None:
                S.copy("act" if i % 2 == 0 else "dve", Wb, o, st[:, 0:n1 - n0], [st])
            elif rowscale is None:
                S.tt(eng, Wb, o, st[:, 0:n1 - n0], grow[:, n0:n1], ALU.mult, [st, grow])
            else:
                S.stt(eng, Wb, o, st[:, 0:n1 - n0], rowscale[:, kc:kc + 1], grow[:, n0:n1],
                      ALU.mult, ALU.mult, [st, grow, rowscale])
            i += 1


def rstd_from_ss(S, rstd, ss, n, width):
    S.act(rstd, rstd[:, 0:width], ss[:, 0:width], AF.Ln, [ss], scale=1.0 / n, bias=EPS)
    S.act(rstd, rstd[:, 0:width], rstd[:, 0:width], AF.Exp, [rstd], scale=-0.5)


def norm_to_hT(S, k, xt_t, xt_ap, gcol, shcol, hT, col0, xn, sq, ss, rstd, pT):
    S.act(sq, sq[:], xt_ap, AF.Square, [xt_t], accum_out=ss[:, 0:1], extra_writes=[ss])
    rstd_from_ss(S, rstd, ss, D, 1)
    S.act(xn, xn[:], xt_ap, AF.Copy, [xt_t, rstd], scale=rstd[:, 0:1])
    for kc in range(8):
        S.transpose(pT, pT[:, kc * 128:(kc + 1) * 128], xn[:, kc * 128:(kc + 1) * 128],
                    k.ident[:], [xn, k.ident])
    for kc in range(8):
        eng = "dve" if kc % 2 == 0 else "pool"
        eng = "dve"
        S.ts(eng, hT, hT[:, kc, col0:col0 + 128], pT[:, kc * 128:(kc + 1) * 128],
             gcol[:, kc:kc + 1], shcol[:, kc:kc + 1], ALU.mult, ALU.add, reads=[pT, gcol, shcol])


def rows_to_cols(S, k, rows, name):
    n = sum(r.shape[0] // 128 for r in rows)
    assert n <= 128
    rt = S.sbuf(name + "_r", [n, 128], F32)
    ct = S.sbuf(name, [128, n], F32)
    j = 0
    for r in rows:
        m = r.shape[0] // 128
        S.dma(rt, rt[j:j + m, :], r.rearrange("(j p) -> j p", p=128))
        j += m
    ps = k.ps[1]
    S.mm(ps, ps[:, 0:n], rt[0:n, :], k.identf[0:n, 0:n], [rt, k.identf])
    S.copy("dve", ct, ct[:], ps[:, 0:n], [ps])
    return ct


def phase_prologue(S, k):
    cc = S.sbuf("cc", [128, 8], F32)
    ca = S.sbuf("ca", [128, 8], F32)
    S.dma(cc, cc[:], k.c_col[:, :])
    S.act(ca, ca[:], cc[:], AF.Silu, [cc])
    wst = [S.sbuf("adw%d" % i, [128, 8, 512], F32) for i in range(2)]
    brow = S.sbuf("brow", [1, 6144], F32)
    mrow = S.sbuf("mrow", [1, 6144], F32)
    i = 0
    for l in range(2):
        S.dma(brow, brow[:], k.ada_b[l:l + 1, :])
        for nt in range(12):
            wt = wst[i % 2]
            S.dma(wt, wt[:], k.ada_w[l].rearrange("(kc p) n -> p kc n", p=128)[:, :, nt * 512:(nt + 1) * 512])
            ps = k.ps[2 + (i % 2)]
            for kc in range(8):
                S.mm(ps, ps[0:1, :], ca[:, kc:kc + 1], wt[:, kc, :], [ca, wt],
                     start=(kc == 0), stop=(kc == 7))
            S.tt("dve", mrow, mrow[0:1, nt * 512:(nt + 1) * 512], ps[0:1, :],
                 brow[0:1, nt * 512:(nt + 1) * 512], ALU.add, [ps, brow])
            i += 1
        S.dma(None, k.modd[l:l + 1, :], mrow[:], in_t=mrow)
    S.finish_wait("sp", [mrow])


def load_consts(S, k):
    k.identf = S.sbuf("identf", [128, 128], F32)
    k.ident = S.sbuf("ident", [128, 128], BF16)
    S.dma(k.identf, k.identf[:], k.c_ident[:, :])
    S.copy("dve", k.ident, k.ident[:], k.identf[:], [k.identf])


def alloc_psum(S, k, bf7=False):
    k.ps = [S.psum("psb0", [128, 1024], BF16)] + [S.psum("ps%d" % i, [128, 512], F32) for i in range(1, 7)]
    if bf7:
        k.ps.append(S.psum("psb7", [128, 1024], BF16))
    else:
        k.ps.append(S.psum("ps7", [128, 512], F32))


def phase_hgrn(S, k, l, xin, xout, NT):
    TT = 256
    ntile = NT // TT
    load_consts(S, k)
    alloc_psum(S, k)
    j = 0
    cols = rows_to_cols(S, k, [k.modd[l, :], k.norm_mix[l, :], k.hg_lb[0, :], k.hg_lb[1, :],
                               k.hg_gnorm[0, :]], "hcols")
    gnc = S.sbuf("gnc", [128, 8], F32)
    S.copy("dve", gnc, gnc[:], cols[:, 72:73].to_broadcast([128, 8]), [cols])
    gcol = S.sbuf("gcol", [128, 8], F32)
    S.stt("dve", gcol, gcol[:], cols[:, 8:16], 1.0, cols[:, 48:56], ALU.add, ALU.mult, [cols])
    lbc = S.sbuf("lbc", [128, 8], F32)
    l1m = S.sbuf("l1m", [128, 8], F32)
    S.tt("dve", lbc, lbc[:], cols[:, 64:72], cols[:, 56:64], ALU.subtract, [cols])
    S.act(lbc, lbc[:], lbc[:], AF.Exp, [lbc])
    S.ts("dve", lbc, lbc[:], lbc[:], 1.0, None, ALU.add, reads=[lbc])
    S.op("dve", lambda e: e.reciprocal(lbc[:], lbc[:]), [lbc], [lbc])
    S.ts("dve", l1m, l1m[:], lbc[:], -1.0, 1.0, ALU.mult, ALU.add, reads=[lbc])
    S.act(l1m, l1m[:], l1m[:], AF.Ln, [l1m])
    sq = S.sbuf("sq", [128, 1024], F32)
    g1row = sq
    S.dma(g1row, g1row[:], k.modd[l, 2048:3072].partition_broadcast(128))
    Win = S.sbuf("Win", [128, 8, 4096], BF16)
    Wout = S.sbuf("Wout", [128, 8, 1024], BF16)
    stg = [S.sbuf("stg%d" % i, [128, 1024], F32) for i in range(2)]
    load_weight_bf16(S, Win, k.hg_w_in[0], 8, 4096, stg)
    load_weight_bf16(S, Wout, k.hg_w_out[0], 8, 1024, stg, grow=g1row, rowscale=gnc)
    rmask = S.sbuf("rmask", [128, TT], F32)
    S.memset("pool", rmask, rmask[:], 1.0)
    S.memset("pool", rmask, rmask[:].rearrange("p (c j) -> p c j", j=64)[:, :, 0:1], 0.0)
    cmask = S.sbuf("cmask", [128, 128], F32)
    S.dma(cmask, cmask[:], k.c_bdmask[:, :])
    st32 = S.sbuf("st32", [128, 8, 128], F32)
    stb = S.sbuf("stb", [128, 8, 128], BF16)
    S.memset("pool", st32, st32[:], 0.0)
    S.memset("pool", stb, stb[:], 0.0)
    sts = [S.sbuf("sts%d" % i, [128, 128], F32) for i in range(4)]
    xts = [S.sbuf("xt%d" % i, [128, 1024], F32) for i in range(3)]
    xn = S.sbuf("xn", [128, 1024], BF16)
    ss = S.sbuf("ss", [128, 8], F32)
    rstd = S.sbuf("rstd", [128, 8], F32)
    hT = S.sbuf("hT", [128, 8, TT], BF16)
    tu = S.sbuf("tu", [128, TT], F32)
    tA = S.sbuf("tA", [128, TT], F32)
    tB = S.sbuf("tB", [128, TT], F32)
    tb = S.sbuf("tb", [128, TT], F32)
    teb = S.sbuf("teb", [128, TT], F32)
    t1 = S.sbuf("t1", [128, TT], F32)
    qdT = S.sbuf("qdT", [128, 8, 2, 2, 128], BF16)
    kdT = S.sbuf("kdT", [128, 8, TT], BF16)
    kdtok = S.sbuf("kdtok", [128, 2, 8, 128], BF16)
    ebl = S.sbuf("ebl", [128, 8, 4], F32)
    vt = S.sbuf("vt", [128, 2, 1024], BF16)
    gs = S.sbuf("gs", [128, 2, 1024], BF16)
    otok = S.sbuf("otok", [128, 1024], F32)
    oss = S.sbuf("oss", [128, 8], F32)
    orstd = S.sbuf("orstd", [128, 8], F32)
    on = S.sbuf("on", [128, 1024], BF16)
    oT = S.sbuf("oT", [128, 8, TT], BF16)
    k.sc4 = [S.sbuf("sc4%d" % i, [128, 512], BF16) for i in range(2)]
    k.st1 = [S.sbuf("st1_%d" % i, [128, 128], F32) for i in range(8)]
    k.st1b = [S.sbuf("st1b_%d" % i, [128, 128], BF16) for i in range(8)]
    xo = [S.sbuf("xo%d" % i, [128, 1024], F32) for i in range(2)]
    S.memset("pool", qdT, qdT[:], 0.0)
    pT = k.ps[0]
    for T0 in range(ntile):
        r0 = T0 * TT
        for sub in range(2):
            xt = xts[(T0 * 2 + sub) % 3]
            S.dma(xt, xt[:], xin[r0 + sub * 128:r0 + (sub + 1) * 128, :])
            norm_to_hT(S, k, xt, xt[:], gcol, cols, hT, sub * 128, xn, sq, ss, rstd, pT)
        for h in range(8):
            pq = k.ps[1 + (h % 2)]
            for kc in range(8):
                S.mm(pq, pq[:, 0:TT], Win[:, kc, h * 128:(h + 1) * 128], hT[:, kc, :], [Win, hT],
                     start=(kc == 0), stop=(kc == 7))
            for kc in range(8):
                S.mm(pq, pq[:, TT:2 * TT], Win[:, kc, 1024 + h * 128:1024 + (h + 1) * 128], hT[:, kc, :],
                     [Win, hT], start=(kc == 0), stop=(kc == 7))
            z = pq[:, TT:2 * TT]
            S.act(tu, tu[:], z, AF.Exp, [pq], scale=-1.0)
            S.act(tA, tA[:], tu[:], AF.Ln, [tu], bias=1.0)
            S.act(tB, tB[:], tu[:], AF.Ln, [tu, lbc], bias=1.0, scale=lbc[:, h:h + 1])
            S.tt("pool", tB, tB[:], tB[:], tA[:], ALU.subtract, [tB, tA])
            S.op("dve", lambda e, tb=tb, tB=tB: e.tensor_tensor_scan(tb[:], rmask[:], tB[:], 0.0, ALU.mult, ALU.add),
                 [rmask, tB], [tb])
            S.act(teb, teb[:], tb[:], AF.Exp, [tb])
            for sub in range(2):
                for c2 in range(2):
                    cs = sub * 128 + c2 * 64
                    S.tt("dve", qdT, qdT[:, h, sub, c2, c2 * 64:(c2 + 1) * 64], pq[:, cs:cs + 64],
                         teb[:, cs:cs + 64], ALU.mult, [pq, teb])
            S.tt("dve", t1, t1[:], z, tA[:], ALU.add, [pq, tA])
            S.tt("pool", t1, t1[:], t1[:], tb[:], ALU.add, [t1, tb])
            S.act(kdT, kdT[:, h, :], t1[:], AF.Exp, [t1, l1m], scale=-1.0, bias=l1m[:, h:h + 1])
            S.copy("pool", ebl, ebl[:, h, :], teb[:].rearrange("p (c j) -> p c j", j=64)[:, :, 63], [teb])
        for sub in range(2):
            for half in range(4):
                pv = k.ps[3 + (half % 2)]
                c0 = 2048 + half * 512
                for kc in range(8):
                    S.mm(pv, pv[:], hT[:, kc, sub * 128:(sub + 1) * 128], Win[:, kc, c0:c0 + 512],
                         [hT, Win], start=(kc == 0), stop=(kc == 7))
                if half < 2:
                    S.copy("dve", vt, vt[:, sub, half * 512:(half + 1) * 512], pv[:], [pv])
                else:
                    S.act(gs, gs[:, sub, (half - 2) * 512:(half - 1) * 512], pv[:], AF.Silu, [pv])
        for sub in range(2):
            for h in range(8):
                S.transpose(pT, pT[:, h * 128:(h + 1) * 128], kdT[:, h, sub * 128:(sub + 1) * 128],
                            k.ident[:], [kdT, k.ident])
            S.copy("act", kdtok, kdtok[:, sub, :, :].rearrange("p h k -> p (h k)"), pT[:], [pT])
        for sub in range(2):
            for hg in range(2):
                pA, pB, pC, pD = k.ps[1], k.ps[2], k.ps[5], k.ps[6]
                if hg == 1:
                    pA, pB, pC, pD = k.ps[3], k.ps[4], k.ps[7], k.ps[6]
                hs = list(range(hg * 4, hg * 4 + 4))
                for i, h in enumerate(hs):
                    S.mm(pA, pA[:, i * 128:i * 128 + 64], kdT[:, h, sub * 128:(sub + 1) * 128],
                         qdT[:, h, sub, 0, 0:64], [kdT, qdT])
                    S.mm(pA, pA[:, i * 128 + 64:(i + 1) * 128], kdT[:, h, sub * 128:(sub + 1) * 128],
                         qdT[:, h, sub, 1, 64:128], [kdT, qdT])
                    S.mm(pB, pB[:, i * 128:(i + 1) * 128], kdtok[0:64, sub, h, :],
                         vt[0:64, sub, h * 128:(h + 1) * 128], [kdtok, vt])
                sc4 = k.sc4[hg]
                S.tt("dve", sc4, sc4[:].rearrange("p (i t) -> p i t", t=128),
                     pA[:].rearrange("p (i t) -> p i t", t=128),
                     cmask[:].unsqueeze(1).to_broadcast([128, 4, 128]), ALU.mult, [pA, cmask])
                for i, h in enumerate(hs):
                    c = sub * 2
                    stt_ = sts[i % 4]
                    S.act(stt_, stt_[:], st32[:, h, :], AF.Copy, [st32, ebl], scale=ebl[:, h, c:c + 1])
                    S.stt("dve", k.st1[h], k.st1[h][:], pB[:, i * 128:(i + 1) * 128], ebl[:, h, c:c + 1],
                          stt_[:], ALU.mult, ALU.add, [pB, ebl, stt_])
                    S.copy("act", k.st1b[h], k.st1b[h][:], k.st1[h][:], [k.st1[h]])
                for i, h in enumerate(hs):
                    o_ap = pC[:, i * 128:(i + 1) * 128]
                    S.mm(pC, o_ap, sc4[:, i * 128:(i + 1) * 128], vt[:, sub, h * 128:(h + 1) * 128],
                         [sc4, vt], start=True, stop=False)
                    S.mm(pC, o_ap, qdT[:, h, sub, 0, :], stb[:, h, :], [qdT, stb], start=False, stop=False)
                    S.mm(pC, o_ap, qdT[:, h, sub, 1, :], k.st1b[h][:], [qdT, k.st1b[h]], start=False, stop=True)
                    S.mm(pD, pD[:, i * 128:(i + 1) * 128], kdtok[64:128, sub, h, :],
                         vt[64:128, sub, h * 128:(h + 1) * 128], [kdtok, vt])
                for i, h in enumerate(hs):
                    c = sub * 2 + 1
                    stt_ = sts[i % 4]
                    S.act(stt_, stt_[:], k.st1[h][:], AF.Copy, [k.st1[h], ebl], scale=ebl[:, h, c:c + 1])
                    S.stt("dve", st32, st32[:, h, :], pD[:, i * 128:(i + 1) * 128], ebl[:, h, c:c + 1],
                          stt_[:], ALU.mult, ALU.add, [pD, ebl, stt_])
                    S.copy("act", stb, stb[:, h, :], st32[:, h, :], [st32])
                S.copy("dve", otok, otok[:, hg * 512:(hg + 1) * 512], pC[:], [pC])
                for i, h in enumerate(hs):
                    S.act(sq, sq[:, 0:128], otok[:, h * 128:(h + 1) * 128], AF.Square, [otok],
                          accum_out=oss[:, h:h + 1], extra_writes=[oss])
            rstd_from_ss(S, orstd, oss, 128, 8)
            S.tt("dve", otok, otok[:].rearrange("p (h v) -> p h v", v=128),
                 otok[:].rearrange("p (h v) -> p h v", v=128),
                 orstd[:].unsqueeze(2).to_broadcast([128, 8, 128]), ALU.mult, [otok, orstd])
            S.tt("pool", on, on[:], otok[:], gs[:, sub, :], ALU.mult, [otok, gs])
            for kc in range(8):
                S.transpose(pT, pT[:, kc * 128:(kc + 1) * 128], on[:, kc * 128:(kc + 1) * 128],
                            k.ident[:], [on, k.ident])
            S.copy("act", oT, oT[:, :, sub * 128:(sub + 1) * 128],
                   pT[:].rearrange("p (c t) -> p c t", t=128), [pT])
            xt = xts[(T0 * 2 + sub) % 3]
            xo_ = xo[sub]
            for half in range(2):
                py = k.ps[3 + half]
                for kc in range(8):
                    S.mm(py, py[:], oT[:, kc, sub * 128:(sub + 1) * 128], Wout[:, kc, half * 512:(half + 1) * 512],
                         [oT, Wout], start=(kc == 0), stop=(kc == 7))
                S.tt("dve", xo_, xo_[:, half * 512:(half + 1) * 512], py[:], xt[:, half * 512:(half + 1) * 512],
                     ALU.add, [py, xt])
            S.dma(None, xout[r0 + sub * 128:r0 + (sub + 1) * 128, :], xo_[:], in_t=xo_)
    S.finish_wait("sp", xo)


def phase_ffn(S, k, l, xnorm, xres, xout, NT, fc0, fc1, final=False):
    TT = 512
    ntile = NT // TT
    nfc = fc1 - fc0
    W = nfc * 128
    same = xres is xnorm
    load_consts(S, k)
    alloc_psum(S, k, bf7=True)
    cols = rows_to_cols(S, k, [k.modd[l, :], k.norm_ffn[l, :]], "fcols")
    gcol = S.sbuf("gcol", [128, 8], F32)
    S.stt("dve", gcol, gcol[:], cols[:, 32:40], 1.0, cols[:, 48:56], ALU.add, ALU.mult, [cols])
    shc = S.sbuf("shc", [128, 8], F32)
    S.copy("dve", shc, shc[:], cols[:, 24:32], [cols])
    ccols = rows_to_cols(S, k, [k.ffn_conv_w[l, 0, :], k.ffn_conv_w[l, 1, :], k.ffn_conv_w[l, 2, :],
                                k.ffn_conv_b[l, :]], "ccols")
    sq = S.sbuf("sq", [128, 1024], F32)
    g2row = sq
    S.dma(g2row, g2row[:], k.modd[l, 5120:6144].partition_broadcast(128))
    if final:
        fnrow = S.sbuf("fnrow", [128, 1024], F32)
        S.dma(fnrow, fnrow[:], k.final_norm[:].partition_broadcast(128))
    Wup = S.sbuf("Wup", [128, 8, 2 * W], BF16)
    Wdn = S.sbuf("Wdn", [128, nfc, 1024], BF16)
    stg = [S.sbuf("stg%d" % i, [128, 1024], F32) for i in range(2)]
    load_weight_bf16(S, Wup, k.ffn_w_up[l][:, fc0 * 128:fc1 * 128], 8, W, stg)
    load_weight_bf16(S, Wup, k.ffn_w_up[l][:, DFF + fc0 * 128:DFF + fc1 * 128], 8, W, stg, col_off=W)
    load_weight_bf16(S, Wdn, k.ffn_w_down[l][fc0 * 128:fc1 * 128, :], nfc, 1024, stg, grow=g2row)
    xts = [S.sbuf("xt%d" % i, [128, 1024], F32) for i in range(2)]
    xrs = [S.sbuf("xr%d" % i, [128, 1024], F32) for i in range(2)]
    xn = S.sbuf("xn", [128, 1024], BF16)
    ss = S.sbuf("ss", [128, 8], F32)
    rstd = S.sbuf("rstd", [128, 8], F32)
    hTs = [S.sbuf("hT%d" % i, [128, 8, TT], BF16) for i in range(2)]
    halo = S.sbuf("halo", [128, nfc, 2], F32)
    S.memset("pool", halo, halo[:], 0.0)
    abuf = [S.sbuf("abuf%d" % i, [128, TT + 2], F32) for i in range(3)]
    c1 = [S.sbuf("c1_%d" % i, [128, TT], F32) for i in range(3)]
    c2 = [S.sbuf("c2_%d" % i, [128, TT], F32) for i in range(3)]
    mTs = [S.sbuf("mT%d" % i, [128, nfc, TT], BF16) for i in range(2)]
    xo = [S.sbuf("xo%d" % i, [128, 1024], F32) for i in range(2)]
    pT = k.ps[0]
    ncnt = [0]

    xns = [xn] + [S.sbuf("xn%d" % i, [128, 1024], BF16) for i in range(1, 4)]
    sss = [S.sbuf("ss%d" % i, [128, 2], F32) for i in range(4)]
    rss = [S.sbuf("rs%d" % i, [128, 2], F32) for i in range(4)]
    pTs = [k.ps[0], k.ps[7]]

    def norm_a(T0, sub):
        i = sub
        xt = xts[sub % 2]
        r = T0 * TT + sub * 128
        S.dma(xt, xt[:], xnorm[r:r + 128, :])
        S.act(sq, sq[:], xt[:], AF.Square, [xt], accum_out=sss[i][:, 0:1], extra_writes=[sss[i]])
        rstd_from_ss(S, rss[i], sss[i], D, 1)
        S.act(xns[i], xns[i][:], xt[:], AF.Copy, [xt, rss[i]], scale=rss[i][:, 0:1])

    def norm_b(T0, sub):
        i = sub
        hT_ = hTs[T0 % 2]
        pT_ = pTs[sub % 2]
        for kc in range(8):
            S.transpose(pT_, pT_[:, kc * 128:(kc + 1) * 128], xns[i][:, kc * 128:(kc + 1) * 128],
                        k.ident[:], [xns[i], k.ident])
        for kc in range(8):
            S.act(hT_, hT_[:, kc, sub * 128:(sub + 1) * 128], pT_[:, kc * 128:(kc + 1) * 128], AF.Identity,
                  [pT_, gcol, shc], scale=gcol[:, kc:kc + 1], bias=shc[:, kc:kc + 1])

    for sub in range(4):
        norm_a(0, sub)
        norm_b(0, sub)
    def down_proj(T0):
        r0 = T0 * TT
        mT = mTs[T0 % 2]
        for sub in range(4):
            xt = xrs[sub % 2]
            S.dma(xt, xt[:], xres[r0 + sub * 128:r0 + (sub + 1) * 128, :])
            xo_ = xo[sub % 2]
            for half in range(2):
                py = k.ps[1 + half]
                for j in range(nfc):
                    S.mm(py, py[:], mT[:, j, sub * 128:(sub + 1) * 128], Wdn[:, j, half * 512:(half + 1) * 512],
                         [mT, Wdn], start=(j == 0), stop=(j == nfc - 1))
                S.tt("dve", xo_, xo_[:, half * 512:(half + 1) * 512], py[:], xt[:, half * 512:(half + 1) * 512],
                     ALU.add, [py, xt])
            if final:
                S.act(sq, sq[:], xo_[:], AF.Square, [xo_], accum_out=ss[:, 1:2], extra_writes=[ss])
                rstd_from_ss(S, rstd, ss[:, 1:2] if False else ss, D, 2)
                S.stt("dve", xo_, xo_[:], xo_[:], rstd[:, 1:2], fnrow[:], ALU.mult, ALU.mult, [xo_, rstd, fnrow])
            S.dma(None, xout[r0 + sub * 128:r0 + (sub + 1) * 128, :], xo_[:], in_t=xo_)

    for T0 in range(ntile):
        r0 = T0 * TT
        hT = hTs[T0 % 2]
        mT = mTs[T0 % 2]
        def st_A(j):
            ab = abuf[j % 3]
            pa = k.ps[1 + (j % 2)]
            fc = fc0 + j
            S.copy("pool", ab, ab[:, 0:2], halo[:, j, :], [halo])
            S.copy("act", ab, ab[:, 2:TT + 2], pa[:], [pa])
            S.copy("pool", halo, halo[:, j, :], ab[:, TT:TT + 2], [ab])
            S.ts("pool", c1[j % 3], c1[j % 3][:], ab[:, 2:TT + 2], ccols[:, 44 + fc:45 + fc],
                 ccols[:, 66 + fc:67 + fc], ALU.mult, ALU.add, reads=[ab, ccols])

        def st_B(j):
            ab = abuf[j % 3]
            fc = fc0 + j
            c1_, c2_ = c1[j % 3], c2[j % 3]
            S.stt("dve", c2_, c2_[:], ab[:, 1:TT + 1], ccols[:, 22 + fc:23 + fc], c1_[:], ALU.mult, ALU.add,
                  [ab, ccols, c1_])
            S.stt("dve", c1_, c1_[:], ab[:, 0:TT], ccols[:, fc:fc + 1], c2_[:], ALU.mult, ALU.add,
                  [ab, ccols, c2_])

        def st_C(j):
            S.act(c2[j % 3], c2[j % 3][:], c1[j % 3][:], AF.Silu, [c1[j % 3]])

        def st_D(j):
            pv = k.ps[3 + (j % 4)]
            S.tt("dve", mT, mT[:, j, :], pv[:], c2[j % 3][:], ALU.mult, [pv, c2[j % 3]])

        for j in range(nfc + 2):
            if j < nfc:
                pa = k.ps[1 + (j % 2)]
                pv = k.ps[3 + (j % 4)]
                for kc in range(8):
                    S.mm(pa, pa[:], Wup[:, kc, j * 128:(j + 1) * 128], hT[:, kc, :], [Wup, hT],
                         start=(kc == 0), stop=(kc == 7))
                for kc in range(8):
                    S.mm(pv, pv[:], Wup[:, kc, W + j * 128:W + (j + 1) * 128], hT[:, kc, :], [Wup, hT],
                         start=(kc == 0), stop=(kc == 7))
                st_A(j)
            if 0 <= j - 1 < nfc:
                st_B(j - 1)
                st_C(j - 1)
            if 0 <= j - 2 < nfc:
                st_D(j - 2)
            if T0 + 1 < ntile:
                if j == 0:
                    for sub_ in range(4):
                        norm_a(T0 + 1, sub_)
                if j in (3, 5, 7, 9):
                    norm_b(T0 + 1, (j - 3) // 2)
            if j == 1 and T0 > 0:
                down_proj(T0 - 1)
    down_proj(ntile - 1)
    S.finish_wait("sp", xo)


def build_program(NT, phases=("pro", "hg", "ffn0", "nsa", "ffn1")):
    nc = bass.Bass("TRN2", target_bir_lowering=False)
    k = K()

    def inp(name, shape):
        return nc.dram_tensor(name, list(shape), F32, kind="ExternalInput").ap()

    k.x = inp("x", [NT, D])
    k.c_col = inp("c_col", [128, 8])
    k.ada_w = inp("ada_w", [2, D, 6 * D])
    k.ada_b = inp("ada_b", [2, 6 * D])
    k.norm_mix = inp("norm_mix", [2, D])
    k.norm_ffn = inp("norm_ffn", [2, D])
    k.final_norm = inp("final_norm", [D])
    k.hg_w_in = inp("hg_w_in", [1, D, 4096])
    k.hg_w_out = inp("hg_w_out", [1, D, D])
    k.hg_gnorm = inp("hg_gnorm", [1, 128])
    k.hg_lb = inp("hg_lb", [2, D])
    k.ffn_w_up = inp("ffn_w_up", [2, D, 2 * DFF])
    k.ffn_conv_w = inp("ffn_conv_w", [2, 3, DFF])
    k.ffn_conv_b = inp("ffn_conv_b", [2, DFF])
    k.ffn_w_down = inp("ffn_w_down", [2, DFF, D])
    k.c_ident = inp("c_ident", [128, 128])
    k.c_bdmask = inp("c_bdmask", [128, 128])
    nsa_declare(nc, k, NT)
    k.out = nc.dram_tensor("out", [NT, D], F32, kind="ExternalOutput").ap()
    k.modd = nc.dram_tensor("modd", [2, 6 * D], F32).ap()
    bufs = [nc.dram_tensor("xs%d" % i, [NT, D], F32).ap() for i in range(4)]
    xc = nc.dram_tensor("xc", [NT, D], F32).ap()
    main = [p for p in phases if p != "pro"]
    with contextlib.ExitStack() as gst:
        S = Sched(nc, gst)
        k.S = S
        plist = []
        if "pro" in phases:
            plist.append(lambda: (alloc_psum(S, k), phase_prologue(S, k)))
        src = k.x
        for i, p in enumerate(main):
            last = (i == len(main) - 1)
            dst = k.out if last else bufs[i]
            if p == "hg":
                plist.append(lambda src=src, dst=dst: phase_hgrn(S, k, 0, src, dst, NT))
            elif p in ("ffn0", "ffn1"):
                l = int(p[3])
                fin = (p == "ffn1")
                plist.append(lambda src=src, l=l: phase_ffn(S, k, l, src, src, xc, NT, 0, 11))
                plist.append(lambda src=src, dst=dst, l=l, fin=fin: phase_ffn(S, k, l, src, xc, dst, NT, 11, 22, final=fin))
            elif p == "nsa":
                import os
                stop = int(os.environ.get("NSA_STOP", "4"))
                plist.append(lambda src=src: phase_nsa_proj(S, k, 1, src, NT))
                if stop >= 2:
                    plist.append(lambda: phase_nsa_cmp(S, k, NT))
                if stop >= 3:
                    plist.append(lambda: phase_nsa_attn(S, k, NT))
                if stop >= 4:
                    plist.append(lambda src=src, dst=dst: phase_nsa_out(S, k, 1, src, dst, NT))
            src = dst
        for i, p in enumerate(plist):
            with contextlib.ExitStack() as st:
                S.stack = st
                p()
                S.barrier()
                S.emit()
                S.phase_end()
    return nc


def host_consts():
    ident = np.eye(128, dtype=np.float32)
    s = np.arange(128)[:, None]
    t = np.arange(128)[None, :]
    bd = ((s // 64 == t // 64) & (s <= t)).astype(np.float32)
    return {"c_ident": ident, "c_bdmask": bd}


NSA_W = 2608
SLOPES = [2.0 ** (-8.0 * (h + 1) / 16) for h in range(16)]


def nsa_dims(NT):
    ncb = (NT - 32) // 16 + 1
    nsb = NT // 64
    return ncb, nsb


def nsa_host_consts(NT):
    import ml_dtypes
    bf = ml_dtypes.bfloat16
    ncb, nsb = nsa_dims(NT)
    t = np.arange(NT)
    c = {}
    c["c_vrow"] = np.stack([-SLOPES[h] * t for h in range(16)]).astype(bf)
    KT = NT // 128
    i = np.arange(128)
    sb = np.zeros((128, KT * 16), np.float32)
    for kt in range(KT):
        for h in range(16):
            sb[:, kt * 16 + h] = SLOPES[h] * (128 * kt + i)
    c["c_sbias"] = sb
    cb = np.zeros((128, 2 * 16), np.float32)
    for bt in range(2):
        for h in range(16):
            cb[:, bt * 16 + h] = SLOPES[h] * (16 * (128 * bt + i) + 15.5)
    c["c_cbias"] = cb
    c["c_maskd"] = np.where(i[:, None] > i[None, :], NEG, 0.0).astype(bf)
    c["c_maskw"] = np.where(i[None, :] >= i[:, None], NEG, 0.0).astype(bf)
    QT = NT // 512
    cm = np.zeros((QT, 2, 128, 512), np.float32)
    for T in range(QT):
        for bt in range(2):
            n = 128 * bt + i
            tt = 512 * T + np.arange(512)
            cm[T, bt] = np.where(16 * n[:, None] + 31 <= tt[None, :], 0.0, NEG)
    c["c_cmask"] = cm.astype(bf)
    ov = np.zeros((256, 64), np.float32)
    ci = np.arange(256)[:, None] * 16
    sj = np.arange(64)[None, :] * 64
    ov[:] = ((ci <= sj + 63) & (ci + 31 >= sj))
    ov[ncb:] = 0
    ov[:, nsb:] = 0
    c["c_overlap"] = ov.astype(bf)
    blk = np.arange(64)[None, :]
    cur = (t // 64)[:, None]
    valid = blk * 64 <= t[:, None]
    forced = ((blk == 0) | (blk == cur) | (blk == cur - 1)) & valid
    vm = valid.astype(np.float32)
    am = np.where(forced, 1e4, np.where(valid, 0.0, -1.0)).astype(np.float32)
    vm[:, nsb:] = 0.0
    am[:, nsb:] = -1.0
    c["c_vmask"] = vm
    c["c_amask"] = am
    si = np.zeros((64, NT), np.float32)
    si[0] = 1.0
    for j in range(1, 64):
        si[j] = (t // 64 == j)
    c["c_selind"] = si.astype(bf)
    kw = np.zeros((64, NT), np.float32)
    kw[0] = 1.0
    c["c_kwrows"] = kw.astype(bf)
    return c


def nsa_declare(nc, k, NT):
    def inp(name, shape, dt=F32):
        return nc.dram_tensor(name, list(shape), dt, kind="ExternalInput").ap()
    QT = NT // 512
    KT = NT // 128
    k.nsa_w_in = inp("nsa_w_in", [1, D, NSA_W])
    k.nsa_w_out = inp("nsa_w_out", [1, D, D])
    k.nsa_cmp_pe = inp("nsa_cmp_pe", [1, 2, 32, 64])
    k.nsa_cmp_w1 = inp("nsa_cmp_w1", [1, 2, 2048, 64])
    k.nsa_cmp_w2 = inp("nsa_cmp_w2", [1, 2, 64, 64])
    k.c_vrow = inp("c_vrow", [16, NT], BF16)
    k.c_sbias = inp("c_sbias", [128, KT * 16])
    k.c_cbias = inp("c_cbias", [128, 32])
    k.c_maskd = inp("c_maskd", [128, 128], BF16)
    k.c_maskw = inp("c_maskw", [128, 128], BF16)
    k.c_cmask = inp("c_cmask", [QT, 2, 128, 512], BF16)
    k.c_overlap = inp("c_overlap", [256, 64], BF16)
    k.c_vmask = inp("c_vmask", [NT, 64])
    k.c_amask = inp("c_amask", [NT, 64])
    k.c_selind = inp("c_selind", [64, NT], BF16)
    k.c_kwrows = inp("c_kwrows", [64, NT], BF16)
    k.qT_d = nc.dram_tensor("qT_d", [1024, NT], BF16).ap()
    k.kT_d = nc.dram_tensor("kT_d", [4, 256, NT], BF16).ap()
    k.vtok_d = nc.dram_tensor("vtok_d", [2, NT, 256], BF16).ap()
    k.gT_d = nc.dram_tensor("gT_d", [48, NT], F32).ap()
    k.kcT_d = nc.dram_tensor("kcT_d", [4, 64, 256], BF16).ap()
    k.vc_d = nc.dram_tensor("vc_d", [4, 256, 64], BF16).ap()
    k.oT_d = nc.dram_tensor("oT_d", [1024, NT], BF16).ap()


def phase_nsa_proj(S, k, l, xin, NT):
    TT = 512
    ntile = NT // TT
    load_consts(S, k)
    alloc_psum(S, k)
    cols = rows_to_cols(S, k, [k.modd[l, :], k.norm_mix[l, :]], "ncols")
    gcol = S.sbuf("gcol", [128, 8], F32)
    S.stt("dve", gcol, gcol[:], cols[:, 8:16], 1.0, cols[:, 48:56], ALU.add, ALU.mult, [cols])
    W = S.sbuf("Wn", [128, 8, NSA_W], BF16)
    stg = [S.sbuf("stg%d" % i, [128, 1024], F32) for i in range(3)]
    load_weight_bf16(S, W, k.nsa_w_in[0], 8, NSA_W, stg)
    xts = [S.sbuf("xt%d" % i, [128, 1024], F32) for i in range(2)]
    xn = S.sbuf("xn", [128, 1024], BF16)
    sq = S.sbuf("sq", [128, 1024], F32)
    ss = S.sbuf("ss", [128, 8], F32)
    rstd = S.sbuf("rstd", [128, 8], F32)
    hT = S.sbuf("hT", [128, 8, TT], BF16)
    fsb = [S.sbuf("fsb%d" % i, [128, TT], BF16) for i in range(3)]
    vsb = [S.sbuf("vsb%d" % i, [128, 512], BF16) for i in range(2)]
    gsb = [S.sbuf("gsb%d" % i, [48, TT], F32) for i in range(2)]
    pT = k.ps[0]
    outs = fsb + vsb + gsb
    n = 0
    for T0 in range(ntile):
        r0 = T0 * TT
        for sub in range(4):
            xt = xts[sub % 2]
            S.dma(xt, xt[:], xin[r0 + sub * 128:r0 + (sub + 1) * 128, :])
            norm_to_hT(S, k, xt, xt[:], gcol, cols, hT, sub * 128, xn, sq, ss, rstd, pT)
        fm = [(hp * 128, ("q", hp)) for hp in range(8)]
        for kind, c0 in ((0, 1024), (1, 1280), (2, 1536), (3, 2048)):
            for cpart in range(2):
                fm.append((c0 + cpart * 128, ("k", kind, cpart)))
        for c0, tag in fm:
            ps = k.ps[1 + (n % 3)]
            f = fsb[n % 3]
            n += 1
            for kc in range(8):
                S.mm(ps, ps[:], W[:, kc, c0:c0 + 128], hT[:, kc, :], [W, hT], start=(kc == 0), stop=(kc == 7))
            if tag[0] == "q":
                S.act(f, f[:], ps[:], AF.Copy, [ps], scale=0.125)
                S.dma(None, k.qT_d[tag[1] * 128:(tag[1] + 1) * 128, r0:r0 + TT], f[:], in_t=f)
            else:
                S.copy("dve", f, f[:], ps[:], [ps])
                S.dma(None, k.kT_d[tag[1], tag[2] * 128:(tag[2] + 1) * 128, r0:r0 + TT], f[:], in_t=f)
        for sub in range(4):
            ps = k.ps[4 + (sub % 2)]
            v = vsb[sub % 2]
            for j, c0 in enumerate((1792, 2304)):
                for kc in range(8):
                    S.mm(ps, ps[:, j * 256:(j + 1) * 256], hT[:, kc, sub * 128:(sub + 1) * 128],
                         W[:, kc, c0:c0 + 256], [hT, W], start=(kc == 0), stop=(kc == 7))
            S.copy("dve", v, v[:], ps[:], [ps])
            for j in range(2):
                S.dma(None, k.vtok_d[j, r0 + sub * 128:r0 + (sub + 1) * 128, :], v[:, j * 256:(j + 1) * 256], in_t=v)
        ps = k.ps[6]
        g_ = gsb[T0 % 2]
        for kc in range(8):
            S.mm(ps, ps[0:48, :], W[:, kc, 2560:2608], hT[:, kc, :], [W, hT], start=(kc == 0), stop=(kc == 7))
        S.act(g_, g_[:], ps[0:48, :], AF.Sigmoid, [ps])
        S.dma(None, k.gT_d[:, r0:r0 + TT], g_[:], in_t=g_)
    S.finish_wait("sp", outs)


def phase_nsa_cmp(S, k, NT):
    ncb, nsb = nsa_dims(NT)
    load_consts(S, k)
    alloc_psum(S, k)
    xc = S.sbuf("xc", [64, 2, 4, NT], BF16)
    for kv in range(2):
        for g in range(4):
            S.dma(xc, xc[:, kv, g, :], k.kT_d[kv, g * 64:(g + 1) * 64, :])
    w1f = S.sbuf("w1f", [64, 2, 32, 64], F32)
    w1 = S.sbuf("w1", [64, 2, 32, 64], BF16)
    for kv in range(2):
        S.dma(w1f, w1f[:, kv, :, :], k.nsa_cmp_w1[0, kv].rearrange("(l d) e -> d l e", d=64))
    S.copy("pool", w1, w1[:], w1f[:], [w1f])
    w2f = S.sbuf("w2f", [64, 2, 64], F32)
    w2p = S.sbuf("w2p", [64, 2, 128], BF16)
    for kv in range(2):
        S.dma(w2f, w2f[:, kv, :], k.nsa_cmp_w2[0, kv])
    S.memset("pool", w2p, w2p[:], 0.0)
    S.copy("pool", w2p, w2p[:, :, 64:128], w2f[:], [w2f])
    pef = S.sbuf("pef", [32, 2, 64], F32)
    for kv in range(2):
        S.dma(pef, pef[:, kv, :], k.nsa_cmp_pe[0, kv])
    peT = S.sbuf("peT", [64, 2, 32], BF16)
    ps = k.ps[1]
    for kv in range(2):
        S.mm(ps, ps[0:64, kv * 32:(kv + 1) * 32], pef[:, kv, :], k.identf[0:32, 0:32], [pef, k.identf])
    S.copy("dve", peT, peT[:].rearrange("p a l -> p (a l)"), ps[0:64, 0:64], [ps])
    bias = S.sbuf("cbias", [64, 2], F32)
    ps = k.ps[2]
    for kv in range(2):
        for l in range(32):
            S.mm(ps, ps[0:64, kv:kv + 1], w1[:, kv, l, :], peT[:, kv, l:l + 1], [w1, peT],
                 start=(l == 0), stop=(l == 31))
    S.copy("dve", bias, bias[:], ps[0:64, 0:2], [ps])
    hid = [S.sbuf("hid%d" % i, [64, 256], BF16) for i in range(2)]
    osb = [S.sbuf("osb%d" % i, [128, 256], BF16) for i in range(2)]
    n = 0
    for kv in range(2):
        for g in range(4):
            ph = k.ps[3 + (n % 2)]
            hd = hid[n % 2]
            ob = osb[n % 2]
            for l in range(32):
                rhs = xc[:, kv, g, l:l + 16 * (ncb - 1) + 1:16]
                S.mm(ph, ph[0:64, 0:ncb], w1[:, kv, l, :], rhs, [w1, xc], start=(l == 0), stop=(l == 31))
            S.act(hd, hd[:, 0:ncb], ph[0:64, 0:ncb], AF.Silu, [ph, bias], bias=bias[:, kv:kv + 1])
            po = k.ps[5 + (n % 2)]
            if kv == 0:
                S.mm(po, po[:, 0:ncb], w2p[:, 0, :], hd[:, 0:ncb], [w2p, hd])
                S.copy("dve", ob, ob[64:128, 0:ncb], po[64:128, 0:ncb], [po])
                S.dma(None, k.kcT_d[g, :, 0:ncb], ob[64:128, 0:ncb], in_t=ob)
            else:
                for bt in range((ncb + 127) // 128):
                    nb = min(128, ncb - bt * 128)
                    S.mm(po, po[0:nb, bt * 64:(bt + 1) * 64], hd[:, bt * 128:bt * 128 + nb], w2p[:, 1, 64:128],
                         [hd, w2p])
                    S.copy("dve", ob, ob[0:nb, bt * 64:(bt + 1) * 64], po[0:nb, bt * 64:(bt + 1) * 64], [po])
                    S.dma(None, k.vc_d[g, bt * 128:bt * 128 + nb, :], ob[0:nb, bt * 64:(bt + 1) * 64], in_t=ob)
            n += 1
    S.finish_wait("sp", osb)


def phase_nsa_attn(S, k, NT):
    ncb, nsb = nsa_dims(NT)
    QT = NT // 512
    KT = NT // 128
    NBT = (ncb + 127) // 128
    load_consts(S, k)
    alloc_psum(S, k)
    sbias = S.sbuf("sbias", [128, KT * 16], F32)
    S.dma(sbias, sbias[:], k.c_sbias[:, :])
    cbias = S.sbuf("cbias", [128, 32], F32)
    S.dma(cbias, cbias[:], k.c_cbias[:, :])
    maskd = S.sbuf("maskd", [128, 128], BF16)
    S.dma(maskd, maskd[:], k.c_maskd[:, :])
    maskw = S.sbuf("maskw", [128, 128], BF16)
    S.dma(maskw, maskw[:], k.c_maskw[:, :])
    ovl = S.sbuf("ovl", [128, 2, 64], BF16)
    S.dma(ovl, ovl[:], k.c_overlap.rearrange("(bt p) j -> p bt j", p=128))
    Ks = S.sbuf("Ks", [128, NT], BF16)
    Kw = S.sbuf("Kw", [128, NT], BF16)
    Kc = S.sbuf("Kc", [128, 256], BF16)
    Vs = S.sbuf("Vs", [128, KT, 128], BF16)
    Vw = S.sbuf("Vw", [128, KT, 128], BF16)
    Vc = S.sbuf("Vc", [128, 2, 128], BF16)
    S.memset("pool", Vs, Vs[:], 1.0)
    S.memset("pool", Vw, Vw[:], 1.0)
    S.memset("pool", Vc, Vc[:], 1.0)
    S.memset("pool", Kc, Kc[:], 0.0)
    S.dma(Ks, Ks[0:64, :], k.c_selind[:, :])
    S.dma(Kw, Kw[0:64, :], k.c_kwrows[:, :])
    S.dma(Kc, Kc[0:64, :], k.c_kwrows[:, 0:256])
    QaT = [S.sbuf("Qa%d" % i, [128, 4, 512], BF16) for i in range(2)]
    Qav = [[S.view("Qa%d_%d" % (i, hh), QaT[i].ap[:, hh, :]) for hh in range(4)] for i in range(2)]
    for q in QaT:
        S.memset("pool", q, q[:], 0.0)
    for i in range(2):
        for hh in range(4):
            Qav[i][hh].last_w = QaT[i].last_w
    gbT = [S.sbuf("gb%d" % i, [128, 12, 512], F32) for i in range(2)]
    Pt = [S.sbuf("Pt%d" % i, [128, 512], BF16) for i in range(4)]
    cmk = [S.sbuf("cmk%d" % i, [128, 512], BF16) for i in range(2)]
    rd = [S.sbuf("rd%d" % i, [128, 512], F32) for i in range(2)]
    coef = [S.sbuf("coef%d" % i, [128, 512], F32) for i in range(2)]
    tmp = [S.sbuf("tmp%d" % i, [128, 512], F32) for i in range(2)]
    oacc = [S.sbuf("oacc%d" % i, [128, 512], F32) for i in range(4)]
    osbT = [S.sbuf("osb%d" % i, [128, 4, 512], BF16) for i in range(2)]
    impacc = S.sbuf("impacc", [128, 512], F32)
    vm = S.sbuf("vm", [128, 4, 64], F32)
    am = S.sbuf("am", [128, 4, 64], F32)
    sc = S.sbuf("sc", [128, 4, 64], F32)
    sc2 = S.sbuf("sc2", [128, 64], F32)
    mx = S.sbuf("mx", [128, 8], F32)
    thr = S.sbuf("thr", [128, 4], F32)
    selb = S.sbuf("selb", [128, 4, 64], BF16)
    pT = k.ps[0]
    pSs = [k.ps[1], k.ps[2], k.ps[6]]
    pOs = [k.ps[3], k.ps[7]]
    pI, pTk = k.ps[4], k.ps[5]
    cnt = {"s": 0, "p": 0, "r": 0, "cm": 0, "o": 0, "po": 0}
    NP = len(Pt)
    LOOK = 2
    pend = []
    cur = {}

    def finish_branch(pO, hh, br, first, want_imp):
        i = cnt["r"] % 2
        cnt["r"] += 1
        r_, c_, t_ = rd[i], coef[i], tmp[i]
        gbt = cur["gb"]
        S.act(r_, r_[64:128, :], pO[64:128, :], AF.Ln, [pO], bias=(1e-30 if br == 0 else 0.0))
        S.act(r_, r_[64:128, :], r_[64:128, :], AF.Exp, [r_], scale=-1.0)
        S.tt("dve", c_, c_[64:128, :], r_[64:128, :], gbt[64:128, 3 * hh + br, :], ALU.mult, [r_, gbt])
        if first:
            S.tt("dve", oacc[hh], oacc[hh][0:64, :], pO[0:64, :], c_[64:128, :], ALU.mult, [pO, c_])
        else:
            S.tt("dve", t_, t_[0:64, :], pO[0:64, :], c_[64:128, :], ALU.mult, [pO, c_])
            S.tt("pool", oacc[hh], oacc[hh][0:64, :], oacc[hh][0:64, :], t_[0:64, :], ALU.add, [oacc[hh], t_])
        if want_imp:
            if hh == 0:
                S.tt("dve", impacc, impacc[0:64, :], pI[0:64, :], r_[64:128, :], ALU.mult, [pI, r_])
            else:
                S.tt("dve", t_, t_[0:64, :], pI[0:64, :], r_[64:128, :], ALU.mult, [pI, r_])
                S.tt("pool", impacc, impacc[0:64, :], impacc[0:64, :], t_[0:64, :], ALU.add, [impacc, t_])

    def emit_pv(item):
        (pO, lhsV, P, np_, clo, chi, first, last, ovl_ap, cb, vt_) = item
        S.mm(pO, pO[:, clo:chi], lhsV, P[0:np_, clo:chi], [vt_, P], start=first, stop=last)
        if ovl_ap is not None:
            S.mm(pI, pI[0:64, clo:chi], ovl_ap, P[0:np_, clo:chi], [ovl, P], start=first, stop=last)
        if cb is not None:
            cb()

    def push(item):
        pend.append(item)
        while len(pend) > LOOK:
            emit_pv(pend.pop(0))

    def flush():
        while pend:
            emit_pv(pend.pop(0))

    def attend(hh, h, Kt, Vt, tiles, br, first_branch):
        pO = pOs[cnt["po"] % 2]
        cnt["po"] += 1
        Qh = cur["Qa"][hh]
        for idx, (kt, clo, chi, masks) in enumerate(tiles):
            pS = pSs[cnt["s"] % len(pSs)]
            cnt["s"] += 1
            S.mm(pS, pS[:, clo:chi], Kt[:, kt * 128:(kt + 1) * 128], Qh[:, clo:chi], [Kt, Qh],
                 start=True, stop=(len(masks) == 0))
            for mi, (mk, c0) in enumerate(masks):
                S.mm(pS, pS[:, c0:c0 + 128], k.ident[:], mk[:], [k.ident, mk], start=False,
                     stop=(mi == len(masks) - 1))
            P = Pt[cnt["p"] % NP]
            cnt["p"] += 1
            S.act(P, P[:, clo:chi], pS[:, clo:chi], AF.Exp, [pS, sbias], bias=sbias[:, kt * 16 + h:kt * 16 + h + 1])
            last = (idx == len(tiles) - 1)
            cb = (lambda pO=pO, hh=hh, br=br, fb=first_branch: finish_branch(pO, hh, br, fb, False)) if last else None
            push((pO, Vt[:, kt, :], P, 128, clo, chi, idx == 0, last, None, cb, Vt))

    units = [(g, T) for g in range(4) for T in range(QT)]

    def load_unit(ui):
        g, T = units[ui]
        T0 = 512 * T
        par = ui % 2
        S.op("sp", lambda e: e.dma_start(out=gbT[par][64:128, :, :],
                                         in_=k.gT_d[12 * g:12 * g + 12, T0:T0 + 512].partition_broadcast(64)),
             [], [gbT[par]], dma_sem_tile=gbT[par])
        S.op("sp", lambda e: e.dma_start(out=QaT[par][64:128, :, :],
                                         in_=k.qT_d[256 * g:256 * g + 256, T0:T0 + 512].rearrange("(hh d) t -> d hh t", d=64)),
             [], Qav[par], dma_sem_tile=Qav[par][0])
        S.op("sp", lambda e: e.dma_start(out=QaT[par][0:1, :, :], in_=k.c_vrow[4 * g:4 * g + 4, T0:T0 + 512].rearrange("(o h) t -> o h t", o=1)),
             [], Qav[par], dma_sem_tile=Qav[par][0])

    def load_group(g):
        S.dma(Ks, Ks[64:128, :], k.kT_d[2, g * 64:(g + 1) * 64, :])
        S.dma(Kw, Kw[64:128, :], k.kT_d[3, g * 64:(g + 1) * 64, :])
        S.dma(Kc, Kc[64:128, 0:ncb], k.kcT_d[g, :, 0:ncb])
        for k0 in range(0, KT, 8):
            k1 = min(KT, k0 + 8)
            S.dma(Vs, Vs[:, k0:k1, 0:64],
                  k.vtok_d[0][k0 * 128:k1 * 128, g * 64:(g + 1) * 64].rearrange("(kt p) d -> p kt d", p=128))
            S.dma(Vw, Vw[:, k0:k1, 0:64],
                  k.vtok_d[1][k0 * 128:k1 * 128, g * 64:(g + 1) * 64].rearrange("(kt p) d -> p kt d", p=128))
        for bt in range(NBT):
            nb = min(128, ncb - bt * 128)
            S.dma(Vc, Vc[0:nb, bt, 0:64], k.vc_d[g, bt * 128:bt * 128 + nb, :])

    load_unit(0)
    for ui, (g, T) in enumerate(units):
        if T == 0:
            load_group(g)
        T0 = 512 * T
        par = ui % 2
        cur["Qa"] = Qav[par]
        cur["gb"] = gbT[par]
        Qa = Qav[par]
        bts = []
        for bt in range(NBT):
            nb = min(128, ncb - bt * 128)
            n_lo, n_hi = 128 * bt, 128 * bt + nb - 1
            if 16 * n_lo + 31 > T0 + 511:
                continue
            partial = 16 * n_hi + 31 > T0
            bts.append((bt, nb, partial))
        cms = {}
        for (bt, nb, partial) in bts:
            if partial:
                cm_ = cmk[cnt["cm"] % 2]
                cnt["cm"] += 1
                S.dma(cm_, cm_[:], k.c_cmask[T, bt])
                cms[bt] = cm_
        S.dma(vm, vm[:], k.c_vmask[T0:T0 + 512, :].rearrange("(n p) j -> p n j", p=128))
        S.dma(am, am[:], k.c_amask[T0:T0 + 512, :].rearrange("(n p) j -> p n j", p=128))
        for hh in range(4):
            h = 4 * g + hh
            pO = pOs[cnt["po"] % 2]
            cnt["po"] += 1
            for bi, (bt, nb, partial) in enumerate(bts):
                pS = pSs[cnt["s"] % len(pSs)]
                cnt["s"] += 1
                S.mm(pS, pS[0:nb, :], Kc[:, bt * 128:bt * 128 + nb], Qa[hh][:, :], [Kc, Qa[hh]],
                     start=True, stop=(not partial))
                if partial:
                    cm_ = cms[bt]
                    S.mm(pS, pS[0:nb, :], k.ident[0:nb, 0:nb], cm_[0:nb, :], [k.ident, cm_], start=False, stop=True)
                P = Pt[cnt["p"] % NP]
                cnt["p"] += 1
                S.act(P, P[0:nb, :], pS[0:nb, :], AF.Exp, [pS, cbias],
                      bias=cbias[0:nb, bt * 16 + h:bt * 16 + h + 1])
                last = (bi == len(bts) - 1)
                cb = (lambda pO=pO, hh=hh: finish_branch(pO, hh, 0, True, True)) if last else None
                push((pO, Vc[0:nb, bt, :], P, nb, 0, 512, bi == 0, last, ovl[0:nb, bt, :], cb, Vc))
        for hh in range(4):
            h = 4 * g + hh
            tiles = []
            for kt in range(max(0, 4 * T - 4), 4 * T + 4):
                m = kt - 4 * T
                n_lo, n_hi = max(m, 0), min(m + 4, 3)
                masks = []
                if m >= 0:
                    masks.append((maskd, 128 * m))
                if m <= -1:
                    masks.append((maskw, 128 * (m + 4)))
                tiles.append((kt, 128 * n_lo, 128 * (n_hi + 1), masks))
            tiles.sort(key=lambda tl: -(tl[2] - tl[1]))
            attend(hh, h, Kw, Vw, tiles, 2, False)
            if hh == 1:
                for n in range(4):
                    S.mm(pTk, pTk[:, n * 64:(n + 1) * 64], impacc[0:64, n * 128:(n + 1) * 128],
                         k.identf[0:64, 0:64], [impacc, k.identf])
                S.tt("dve", sc, sc[:].rearrange("p n j -> p (n j)"), pTk[:, 0:256],
                     vm[:].rearrange("p n j -> p (n j)"), ALU.mult, [pTk, vm])
                S.tt("pool", sc, sc[:], sc[:], am[:], ALU.add, [sc, am])
                for n in range(4):
                    S.op("dve", lambda e, n=n: e.max(mx[:], sc[:, n, :]), [sc], [mx])
                    S.op("dve", lambda e, n=n: e.match_replace(sc2[:], mx[:], sc[:, n, :], -1e9), [sc, mx], [sc2])
                    S.op("dve", lambda e: e.max(mx[:], sc2[:]), [sc2], [mx])
                    S.ts("dve", thr, thr[:, n:n + 1], mx[:, 7:8], -0.5, None, ALU.max, reads=[mx])
                    S.ts("dve", sc2, sc2[:], sc[:, n, :], thr[:, n:n + 1], None, ALU.is_ge, reads=[sc, thr])
                    S.ts("dve", selb, selb[:, n, :], sc2[:], -1.0, -NEG, ALU.add, ALU.mult, reads=[sc2])
        if ui + 1 < len(units):
            load_unit(ui + 1)
        for n in range(4):
            S.transpose(pT, pT[0:64, n * 128:(n + 1) * 128], selb[:, n, :], k.ident[:], [selb, k.ident])
        flush()
        for hh in range(4):
            S.copy("act", Qa[hh], Qa[hh][0:64, :], pT[0:64, 0:512], [pT])
        S.op("sp", lambda e, par=par, g=g, T0=T0: e.dma_start(
            out=QaT[par][0:1, :, :], in_=k.c_vrow[4 * g:4 * g + 4, T0:T0 + 512].rearrange("(o h) t -> o h t", o=1)),
            [], Qa, dma_sem_tile=Qa[0])
        for hh in range(4):
            h = 4 * g + hh
            tiles = []
            for kt in range(4 * T + 4):
                j = kt - 4 * T
                if j < 0:
                    tiles.append((kt, 0, 512, []))
                else:
                    tiles.append((kt, 128 * j, 512, [(maskd, 128 * j)]))
            attend(hh, h, Ks, Vs, tiles, 1, False)
        flush()
        ob = osbT[cnt["o"] % 2]
        cnt["o"] += 1
        for hh in range(4):
            S.copy("act", ob, ob[0:64, hh, :], oacc[hh][0:64, :], [oacc[hh]])
        S.dma(None, k.oT_d[256 * g:256 * g + 256, T0:T0 + 512].rearrange("(hh d) t -> d hh t", d=64),
              ob[0:64, :, :], in_t=ob)
    S.finish_wait("sp", osbT)


def phase_nsa_out(S, k, l, xin, xout, NT):
    TT = 512
    ntile = NT // TT
    load_consts(S, k)
    alloc_psum(S, k)
    g1row = S.sbuf("g1row", [128, 1024], F32)
    S.dma(g1row, g1row[:], k.modd[l, 2048:3072].partition_broadcast(128))
    Wo = S.sbuf("Wo", [128, 8, 1024], BF16)
    stg = [S.sbuf("stg%d" % i, [128, 1024], F32) for i in range(2)]
    load_weight_bf16(S, Wo, k.nsa_w_out[0], 8, 1024, stg, grow=g1row)
    oT = [S.sbuf("oT%d" % i, [128, 8, TT], BF16) for i in range(2)]
    xts = [S.sbuf("xt%d" % i, [128, 1024], F32) for i in range(2)]
    xo = [S.sbuf("xo%d" % i, [128, 1024], F32) for i in range(2)]
    n = 0
    for T0 in range(ntile):
        r0 = T0 * TT
        o_ = oT[T0 % 2]
        S.dma(o_, o_[:], k.oT_d[:, r0:r0 + TT].rearrange("(kc p) t -> p kc t", p=128))
        for sub in range(4):
            xt = xts[sub % 2]
            xo_ = xo[sub % 2]
            S.dma(xt, xt[:], xin[r0 + sub * 128:r0 + (sub + 1) * 128, :])
            for half in range(2):
                py = k.ps[1 + (n % 4)]
                n += 1
                for kc in range(8):
                    S.mm(py, py[:], o_[:, kc, sub * 128:(sub + 1) * 128], Wo[:, kc, half * 512:(half + 1) * 512],
                         [o_, Wo], start=(kc == 0), stop=(kc == 7))
                S.tt("dve", xo_, xo_[:, half * 512:(half + 1) * 512], py[:], xt[:, half * 512:(half + 1) * 512],
                     ALU.add, [py, xt])
            S.dma(None, xout[r0 + sub * 128:r0 + (sub + 1) * 128, :], xo_[:], in_t=xo_)
    S.finish_wait("sp", xo)


W_KEYS = ["ada_w", "ada_b", "norm_mix", "norm_ffn", "final_norm", "hg_w_in", "hg_w_out", "hg_gnorm", "hg_lb",
          "ffn_w_up", "ffn_conv_w", "ffn_conv_b", "ffn_w_down", "nsa_w_in", "nsa_w_out", "nsa_cmp_pe",
          "nsa_cmp_w1", "nsa_cmp_w2"]


def make_in_map(inp, x, c, NT, consts=None):
    im = {"x": np.ascontiguousarray(x, dtype=np.float32),
          "c_col": np.ascontiguousarray(np.asarray(c, dtype=np.float32).reshape(8, 128).T)}
    for k_ in W_KEYS:
        im[k_] = np.asarray(inp[k_], dtype=np.float32)
    if consts is None:
        consts = dict(host_consts())
        consts.update(nsa_host_consts(NT))
    im.update(consts)
    return im


_CACHE = {}


def kernel(**inputs):
    x = np.asarray(inputs["x"], dtype=np.float32)
    c = np.asarray(inputs["c"], dtype=np.float32)
    B, NT, _ = x.shape
    if "nc" not in _CACHE:
        _CACHE["nc"] = build_program(NT)
        consts = dict(host_consts())
        consts.update(nsa_host_consts(NT))
        _CACHE["consts"] = consts
    nc = _CACHE["nc"]
    in_maps = [make_in_map(inputs, x[b], c[b], NT, _CACHE["consts"]) for b in range(B)]
    res = run_bass_kernel_spmd(nc, in_maps, core_ids=list(range(B)))
    out = np.stack([np.asarray(r["out"], dtype=np.float32) for r in res.results], axis=0)
    return out
```

```python
import contextlib
import numpy as np
import concourse.bass as bass
import concourse.mybir as mybir

F32 = mybir.dt.float32
BF16 = mybir.dt.bfloat16
AF = mybir.ActivationFunctionType
ALU = mybir.AluOpType
AX = mybir.AxisListType

ENGS = ["pe", "act", "dve", "pool", "sp"]


class T:
    __slots__ = ("name", "ap", "last_w", "readers", "dsem", "dcnt", "uid")
    _n = [0]

    def __init__(self, name, ap=None):
        T._n[0] += 1
        self.uid = T._n[0]
        self.name = name
        self.ap = ap
        self.last_w = None
        self.readers = []
        self.dsem = None
        self.dcnt = 0

    def __getitem__(self, idx):
        return self.ap[idx]


class Sched:
    def __init__(self, nc, stack):
        self.nc = nc
        self.stack = stack
        self.ops = {e: [] for e in ENGS}
        self.cnt = {e: 0 for e in ENGS}
        self.clock = {e: {} for e in ENGS}
        self.sem = {}
        for e in ["pe", "act", "dve", "pool"]:
            self.sem[e] = stack.enter_context(nc.semaphore("s_" + e))
        self.sem["bar"] = stack.enter_context(nc.semaphore("s_bar"))
        self.bar_n = 0
        self.dma_live = {}
        self.gstack = stack
        self.nsem = 5
        self.final_waits = []
        self.n_wait = 0
        self.uid = 0
        self.dsem_pool = []
        self.dsem_owner = []

    def sbuf(self, name, shape, dtype):
        self.uid += 1
        name = "%s_u%d" % (name, self.uid)
        t = self.stack.enter_context(self.nc.sbuf_tensor(name, list(shape), dtype))
        return T(name, t)

    def psum(self, name, shape, dtype=F32):
        self.uid += 1
        name = "%s_u%d" % (name, self.uid)
        t = self.stack.enter_context(self.nc.psum_tensor(name, list(shape), dtype))
        return T(name, t)

    def view(self, name, ap):
        return T(name, ap)

    def _dsem(self, t):
        if t.dsem is None:
            if self.dsem_pool:
                t.dsem, t.dcnt = self.dsem_pool.pop()
            else:
                self.nsem += 1
                t.dsem = self.gstack.enter_context(self.nc.semaphore("dsem%d" % self.nsem))
                t.dcnt = 0
            self.dsem_owner.append(t)
        return t.dsem

    def phase_end(self):
        for t in self.dsem_owner:
            self.dsem_pool.append((t.dsem, t.dcnt))
            t.dsem = None
        self.dsem_owner = []
        self.dma_live = {}

    def _need(self, eng, ev, waits):
        key, val, snap = ev
        if eng == "pe" and key == "pe":
            return
        ck = self.clock[eng]
        if ck.get(key, 0) >= val:
            return
        waits[key] = max(waits.get(key, 0), val)
        ck[key] = val
        if snap:
            for k, v in snap.items():
                if ck.get(k, 0) < v:
                    ck[k] = v

    def op(self, eng, fn, reads=(), writes=(), dma_sem_tile=None):
        waits = {}
        for t in reads:
            if t.last_w is not None:
                self._need(eng, t.last_w, waits)
        for t in writes:
            if t.last_w is not None:
                self._need(eng, t.last_w, waits)
            for ev in t.readers:
                self._need(eng, ev, waits)
        if dma_sem_tile is not None:
            st = dma_sem_tile
            sem = self._dsem(st)
            st.dcnt += 16
            key = ("d", st.uid)
            self.sem[key] = sem
            ev = (key, st.dcnt, dict(self.clock[eng]))
            self.dma_live[key] = (st, st.dcnt)
            inc = (sem, 16)
        else:
            self.cnt[eng] += 1
            ev = (eng, self.cnt[eng], None)
            inc = (self.sem[eng], 1)
        self.ops[eng].append((list(waits.items()), fn, inc))
        self.n_wait += len(waits)
        if dma_sem_tile is None:
            snap = dict(self.clock[eng])
            ev = (eng, self.cnt[eng], snap)
        for t in writes:
            t.last_w = ev
            t.readers = []
        for t in reads:
            if t not in writes:
                t.readers.append(ev)
        return ev

    def finish_wait(self, eng, tiles):
        waits = {}
        for t in tiles:
            if t.last_w is not None:
                self._need(eng, t.last_w, waits)
            for ev in t.readers:
                self._need(eng, ev, waits)
        self.ops[eng].append((list(waits.items()), None, None))

    def barrier(self):
        evs = []
        for e in ["pe", "act", "dve", "pool"]:
            if self.cnt[e] > 0:
                evs.append((e, self.cnt[e], None))
        for key, (t, val) in self.dma_live.items():
            evs.append((key, val, None))
        for eng in ENGS:
            waits = {}
            for ev in evs:
                if ev[0] == eng and eng == "pe":
                    continue
                ck = self.clock[eng]
                if ck.get(ev[0], 0) < ev[1]:
                    waits[ev[0]] = ev[1]
                    ck[ev[0]] = ev[1]
            self.ops[eng].append((list(waits.items()), None, None))
        self.bar_n += 1
        for eng in ENGS:
            self.ops[eng].append(([], "barinc", None))
        for eng in ENGS:
            self.ops[eng].append(([("bar", 5 * self.bar_n)], None, None))

    def emit(self):
        nc = self.nc
        with nc.Block() as block:
            def run(eng_name):
                def body(e):
                    for waits, fn, inc in self.ops[eng_name]:
                        for key, val in waits:
                            e.wait_ge(self.sem[key], val)
                        if fn == "barinc":
                            e.sem_inc(self.sem["bar"], 1)
                        elif fn is not None:
                            ins = fn(e)
                            ins.then_inc(inc[0], inc[1])
                return body
            block.tensor(run("pe"))
            block.scalar(run("act"))
            block.vector(run("dve"))
            block.gpsimd(run("pool"))
            block.sync(run("sp"))
        self.ops = {e: [] for e in ENGS}

    def dma(self, out_t, out_ap, in_ap, in_t=None, eng="sp", **kw):
        reads = [in_t] if in_t is not None else []
        writes = [out_t] if out_t is not None else []
        st = out_t if out_t is not None else in_t
        return self.op(eng, lambda e: e.dma_start(out=out_ap, in_=in_ap, **kw), reads, writes,
                       dma_sem_tile=st)

    def mm(self, out_t, out_ap, lhsT, rhs, reads, start=True, stop=True, **kw):
        return self.op("pe", lambda e: e.matmul(out_ap, lhsT, rhs, start=start, stop=stop, **kw),
                       reads, [out_t])

    def transpose(self, out_t, out_ap, in_ap, ident_ap, reads):
        return self.op("pe", lambda e: e.transpose(out_ap, in_ap, ident_ap), reads, [out_t])

    def act(self, out_t, out_ap, in_ap, func, reads, bias=None, scale=None, accum_out=None,
            extra_writes=()):
        kw = {}
        if bias is not None:
            kw["bias"] = bias
        if scale is not None:
            kw["scale"] = scale
        if accum_out is not None:
            kw["accum_out"] = accum_out
        return self.op("act", lambda e: e.activation(out_ap, in_ap, func, **kw), reads,
                       [out_t] + list(extra_writes))

    def tt(self, eng, out_t, out_ap, in0, in1, op, reads):
        return self.op(eng, lambda e: e.tensor_tensor(out_ap, in0, in1, op), reads, [out_t])

    def ts(self, eng, out_t, out_ap, in0, s1, s2, op0, op1=None, reads=(), accum_out=None,
           extra_writes=()):
        def f(e):
            kw = {}
            if accum_out is not None:
                kw["accum_out"] = accum_out
            if op1 is None:
                return e.tensor_scalar(out_ap, in0, s1, None, op0, **kw)
            return e.tensor_scalar(out_ap, in0, s1, s2, op0, op1, **kw)
        return self.op(eng, f, reads, [out_t] + list(extra_writes))

    def stt(self, eng, out_t, out_ap, in0, scalar, in1, op0, op1, reads):
        eng = "dve"
        return self.op(eng, lambda e: e.scalar_tensor_tensor(out_ap, in0, scalar, in1, op0, op1),
                       reads, [out_t])

    def copy(self, eng, out_t, out_ap, in_ap, reads):
        if eng == "act":
            return self.op("act", lambda e: e.copy(out_ap, in_ap), reads, [out_t])
        return self.op(eng, lambda e: e.tensor_copy(out_ap, in_ap), reads, [out_t])

    def memset(self, eng, out_t, out_ap, val):
        return self.op(eng, lambda e: e.memset(out_ap, val), [], [out_t])

from concourse.bass_utils import run_bass_kernel_spmd

D = 1024
NH_HG = 8
DFF = 2816
NFC = DFF // 128
EPS = 1e-6
NEG = -30000.0


def bcast_rows(ap_row, n):
    return ap_row.partition_broadcast(n)


class K:
    pass


def load_weight_bf16(S, Wb, w_dram, KC, N, stg, grow=None, col_off=0, rowscale=None, kc_off=0):
    i = 0
    for kc in range(KC):
        for n0 in range(0, N, 1024):
            n1 = min(N, n0 + 1024)
            st = stg[i % len(stg)]
            S.dma(st, st[:, 0:n1 - n0], w_dram[kc * 128:(kc + 1) * 128, n0:n1])
            eng = "dve"
            o = Wb[:, kc_off + kc, col_off + n0:col_off + n1]
            if grow is None:
                S.copy("act" if i % 2 == 0 else "dve", Wb, o, st[:, 0:n1 - n0], [st])
            elif rowscale is None:
                S.tt(eng, Wb, o, st[:, 0:n1 - n0], grow[:, n0:n1], ALU.mult, [st, grow])
            else:
                S.stt(eng, Wb, o, st[:, 0:n1 - n0], rowscale[:, kc:kc + 1], grow[:, n0:n1],
                      ALU.mult, ALU.mult, [st, grow, rowscale])
            i += 1


def rstd_from_ss(S, rstd, ss, n, width):
    S.act(rstd, rstd[:, 0:width], ss[:, 0:width], AF.Ln, [ss], scale=1.0 / n, bias=EPS)
    S.act(rstd, rstd[:, 0:width], rstd[:, 0:width], AF.Exp, [rstd], scale=-0.5)


def norm_to_hT(S, k, xt_t, xt_ap, gcol, shcol, hT, col0, xn, sq, ss, rstd, pT):
    S.act(sq, sq[:], xt_ap, AF.Square, [xt_t], accum_out=ss[:, 0:1], extra_writes=[ss])
    rstd_from_ss(S, rstd, ss, D, 1)
    S.act(xn, xn[:], xt_ap, AF.Copy, [xt_t, rstd], scale=rstd[:, 0:1])
    for kc in range(8):
        S.transpose(pT, pT[:, kc * 128:(kc + 1) * 128], xn[:, kc * 128:(kc + 1) * 128],
                    k.ident[:], [xn, k.ident])
    for kc in range(8):
        eng = "dve" if kc % 2 == 0 else "pool"
        eng = "dve"
        S.ts(eng, hT, hT[:, kc, col0:col0 + 128], pT[:, kc * 128:(kc + 1) * 128],
             gcol[:, kc:kc + 1], shcol[:, kc:kc + 1], ALU.mult, ALU.add, reads=[pT, gcol, shcol])


def rows_to_cols(S, k, rows, name):
    n = sum(r.shape[0] // 128 for r in rows)
    assert n <= 128
    rt = S.sbuf(name + "_r", [n, 128], F32)
    ct = S.sbuf(name, [128, n], F32)
    j = 0
    for r in rows:
        m = r.shape[0] // 128
        S.dma(rt, rt[j:j + m, :], r.rearrange("(j p) -> j p", p=128))
        j += m
    ps = k.ps[1]
    S.mm(ps, ps[:, 0:n], rt[0:n, :], k.identf[0:n, 0:n], [rt, k.identf])
    S.copy("dve", ct, ct[:], ps[:, 0:n], [ps])
    return ct


def phase_prologue(S, k):
    cc = S.sbuf("cc", [128, 8], F32)
    ca = S.sbuf("ca", [128, 8], F32)
    S.dma(cc, cc[:], k.c_col[:, :])
    S.act(ca, ca[:], cc[:], AF.Silu, [cc])
    wst = [S.sbuf("adw%d" % i, [128, 8, 512], F32) for i in range(2)]
    brow = S.sbuf("brow", [1, 6144], F32)
    mrow = S.sbuf("mrow", [1, 6144], F32)
    i = 0
    for l in range(2):
        S.dma(brow, brow[:], k.ada_b[l:l + 1, :])
        for nt in range(12):
            wt = wst[i % 2]
            S.dma(wt, wt[:], k.ada_w[l].rearrange("(kc p) n -> p kc n", p=128)[:, :, nt * 512:(nt + 1) * 512])
            ps = k.ps[2 + (i % 2)]
            for kc in range(8):
                S.mm(ps, ps[0:1, :], ca[:, kc:kc + 1], wt[:, kc, :], [ca, wt],
                     start=(kc == 0), stop=(kc == 7))
            S.tt("dve", mrow, mrow[0:1, nt * 512:(nt + 1) * 512], ps[0:1, :],
                 brow[0:1, nt * 512:(nt + 1) * 512], ALU.add, [ps, brow])
            i += 1
        S.dma(None, k.modd[l:l + 1, :], mrow[:], in_t=mrow)
    S.finish_wait("sp", [mrow])


def load_consts(S, k):
    k.identf = S.sbuf("identf", [128, 128], F32)
    k.ident = S.sbuf("ident", [128, 128], BF16)
    S.dma(k.identf, k.identf[:], k.c_ident[:, :])
    S.copy("dve", k.ident, k.ident[:], k.identf[:], [k.identf])


def alloc_psum(S, k, bf7=False):
    k.ps = [S.psum("psb0", [128, 1024], BF16)] + [S.psum("ps%d" % i, [128, 512], F32) for i in range(1, 7)]
    if bf7:
        k.ps.append(S.psum("psb7", [128, 1024], BF16))
    else:
        k.ps.append(S.psum("ps7", [128, 512], F32))


def interleave(ga, gb, ra=1, rb=1):
    da = db = False
    while not (da and db):
        for _ in range(ra):
            if not da:
                try:
                    next(ga)
                except StopIteration:
                    da = True
        for _ in range(rb):
            if not db:
                try:
                    next(gb)
                except StopIteration:
                    db = True


def phase_hgrn(S, k, l, xin, xout, NT):
    TT = 256
    ntile = NT // TT
    load_consts(S, k)
    alloc_psum(S, k, bf7=True)
    cols = rows_to_cols(S, k, [k.modd[l, :], k.norm_mix[l, :], k.hg_lb[0, :], k.hg_lb[1, :],
                               k.hg_gnorm[0, :]], "hcols")
    gnc = S.sbuf("gnc", [128, 8], F32)
    S.copy("dve", gnc, gnc[:], cols[:, 72:73].to_broadcast([128, 8]), [cols])
    gcol = S.sbuf("gcol", [128, 8], F32)
    S.stt("dve", gcol, gcol[:], cols[:, 8:16], 1.0, cols[:, 48:56], ALU.add, ALU.mult, [cols])
    lbc = S.sbuf("lbc", [128, 8], F32)
    l1m = S.sbuf("l1m", [128, 8], F32)
    S.tt("dve", lbc, lbc[:], cols[:, 64:72], cols[:, 56:64], ALU.subtract, [cols])
    S.act(lbc, lbc[:], lbc[:], AF.Exp, [lbc])
    S.ts("dve", lbc, lbc[:], lbc[:], 1.0, None, ALU.add, reads=[lbc])
    S.op("dve", lambda e: e.reciprocal(lbc[:], lbc[:]), [lbc], [lbc])
    S.ts("dve", l1m, l1m[:], lbc[:], -1.0, 1.0, ALU.mult, ALU.add, reads=[lbc])
    S.act(l1m, l1m[:], l1m[:], AF.Ln, [l1m])
    sq = S.sbuf("sq", [128, 1024], F32)
    g1row = sq
    S.dma(g1row, g1row[:], k.modd[l, 2048:3072].partition_broadcast(128))
    Win = S.sbuf("Win", [128, 8, 4096], BF16)
    Wout = S.sbuf("Wout", [128, 8, 1024], BF16)
    otok = S.sbuf("otok", [128, 1024], F32)
    xo = [S.sbuf("xo%d" % i, [128, 1024], F32) for i in range(2)]
    stg = [otok, xo[0]]
    load_weight_bf16(S, Win, k.hg_w_in[0], 8, 4096, stg)
    load_weight_bf16(S, Wout, k.hg_w_out[0], 8, 1024, stg, grow=g1row, rowscale=gnc)
    rmask = S.sbuf("rmask", [128, TT], F32)
    S.memset("pool", rmask, rmask[:], 1.0)
    S.memset("pool", rmask, rmask[:].rearrange("p (c j) -> p c j", j=64)[:, :, 0:1], 0.0)
    cmask = S.sbuf("cmask", [128, 128], F32)
    S.dma(cmask, cmask[:], k.c_bdmask[:, :])
    st32s = [S.sbuf("st32_%d" % i, [128, 128], F32) for i in range(8)]
    stbs = [S.sbuf("stb_%d" % i, [128, 128], BF16) for i in range(8)]
    for i in range(8):
        S.memset("pool", st32s[i], st32s[i][:], 0.0)
        S.memset("pool", stbs[i], stbs[i][:], 0.0)
    sts = [S.sbuf("sts%d" % i, [128, 128], F32) for i in range(8)]
    xts = [S.sbuf("xt%d" % i, [128, 1024], F32) for i in range(4)]
    xns = [S.sbuf("xn%d" % i, [128, 1024], BF16) for i in range(2)]
    sss = [S.sbuf("ss%d" % i, [128, 2], F32) for i in range(2)]
    rss = [S.sbuf("rs%d" % i, [128, 2], F32) for i in range(2)]
    hT = S.sbuf("hT", [128, 8, TT], BF16)
    NTMP = 2
    tu = [S.sbuf("tu%d" % i, [128, TT], F32) for i in range(NTMP)]
    tA = [S.sbuf("tA%d" % i, [128, TT], F32) for i in range(NTMP)]
    tB = [S.sbuf("tB%d" % i, [128, TT], F32) for i in range(NTMP)]
    tb = [S.sbuf("tb%d" % i, [128, TT], F32) for i in range(NTMP)]
    teb = [S.sbuf("teb%d" % i, [128, TT], F32) for i in range(NTMP)]
    t1 = [S.sbuf("t1%d" % i, [128, TT], F32) for i in range(NTMP)]
    qdT2 = [S.sbuf("qdT%d" % i, [128, 8, 2, 2, 128], BF16) for i in range(2)]
    kdT2 = [S.sbuf("kdT%d" % i, [128, 8, TT], BF16) for i in range(2)]
    kdtok2 = [S.sbuf("kdtok%d" % i, [128, 2, 8, 128], BF16) for i in range(2)]
    ebl2 = [S.sbuf("ebl%d" % i, [128, 8, 4], F32) for i in range(2)]
    vt2 = [S.sbuf("vt%d" % i, [128, 2, 1024], BF16) for i in range(2)]
    gs2 = [S.sbuf("gs%d" % i, [128, 2, 1024], BF16) for i in range(2)]
    oss = S.sbuf("oss", [128, 8], F32)
    orstd = S.sbuf("orstd", [128, 8], F32)
    on = S.sbuf("on", [128, 1024], BF16)
    oT = S.sbuf("oT", [128, 8, 128], BF16)
    sc4 = S.sbuf("sc4", [128, 512], BF16)
    st1 = [S.sbuf("st1_%d" % i, [128, 128], F32) for i in range(8)]
    st1b = [S.sbuf("st1b_%d" % i, [128, 128], BF16) for i in range(8)]
    for q in qdT2:
        S.memset("pool", q, q[:], 0.0)
    pT, pT2 = k.ps[0], k.ps[7]
    pq = [k.ps[1], k.ps[2]]
    pv = k.ps[3]
    pA, pB, pC = k.ps[4], k.ps[5], k.ps[6]

    def gen_A(T0):
        par = T0 % 2
        qdT, kdT, kdtok, ebl, vt, gs = qdT2[par], kdT2[par], kdtok2[par], ebl2[par], vt2[par], gs2[par]
        r0 = T0 * TT
        for sub in range(2):
            xt = xts[(T0 * 2 + sub) % 4]
            S.dma(xt, xt[:], xin[r0 + sub * 128:r0 + (sub + 1) * 128, :])
            S.act(sq, sq[:], xt[:], AF.Square, [xt], accum_out=sss[sub][:, 0:1], extra_writes=[sss[sub]])
            rstd_from_ss(S, rss[sub], sss[sub], D, 1)
            S.act(xns[sub], xns[sub][:], xt[:], AF.Copy, [xt, rss[sub]], scale=rss[sub][:, 0:1])
            yield
        for sub in range(2):
            for kc in range(8):
                S.transpose(pT, pT[:, kc * 128:(kc + 1) * 128], xns[sub][:, kc * 128:(kc + 1) * 128],
                            k.ident[:], [xns[sub], k.ident])
            for kc in range(8):
                S.act(hT, hT[:, kc, sub * 128:(sub + 1) * 128], pT[:, kc * 128:(kc + 1) * 128], AF.Identity,
                      [pT, gcol, cols], scale=gcol[:, kc:kc + 1], bias=cols[:, kc:kc + 1])
            yield
        for h in range(8):
            i = h % NTMP
            pq_ = pq[h % 2]
            for kc in range(8):
                S.mm(pq_, pq_[:, 0:TT], Win[:, kc, h * 128:(h + 1) * 128], hT[:, kc, :], [Win, hT],
                     start=(kc == 0), stop=(kc == 7))
            for kc in range(8):
                S.mm(pq_, pq_[:, TT:2 * TT], Win[:, kc, 1024 + h * 128:1024 + (h + 1) * 128], hT[:, kc, :],
                     [Win, hT], start=(kc == 0), stop=(kc == 7))
            z = pq_[:, TT:2 * TT]
            S.act(tu[i], tu[i][:], z, AF.Exp, [pq_], scale=-1.0)
            S.act(tA[i], tA[i][:], tu[i][:], AF.Ln, [tu[i]], bias=1.0)
            S.act(tB[i], tB[i][:], tu[i][:], AF.Ln, [tu[i], lbc], bias=1.0, scale=lbc[:, h:h + 1])
            yield
            S.tt("pool", tB[i], tB[i][:], tB[i][:], tA[i][:], ALU.subtract, [tB[i], tA[i]])
            S.op("dve", lambda e, o=tb[i], m=tB[i]: e.tensor_tensor_scan(o[:], rmask[:], m[:], 0.0, ALU.mult, ALU.add),
                 [rmask, tB[i]], [tb[i]])
            S.tt("dve", t1[i], t1[i][:], z, tA[i][:], ALU.add, [pq_, tA[i]])
            S.act(teb[i], teb[i][:], tb[i][:], AF.Exp, [tb[i]])
            yield
            S.tt("pool", t1[i], t1[i][:], t1[i][:], tb[i][:], ALU.add, [t1[i], tb[i]])
            for sub in range(2):
                for c2 in range(2):
                    cs = sub * 128 + c2 * 64
                    S.tt("dve", qdT, qdT[:, h, sub, c2, c2 * 64:(c2 + 1) * 64], pq_[:, cs:cs + 64],
                         teb[i][:, cs:cs + 64], ALU.mult, [pq_, teb[i]])
            S.act(kdT, kdT[:, h, :], t1[i][:], AF.Exp, [t1[i], l1m], scale=-1.0, bias=l1m[:, h:h + 1])
            S.copy("pool", ebl, ebl[:, h, :], teb[i][:].rearrange("p (c j) -> p c j", j=64)[:, :, 63], [teb[i]])
            yield
        for sub in range(2):
            for half in range(4):
                c0 = 2048 + half * 512
                for kc in range(8):
                    S.mm(pv, pv[:], hT[:, kc, sub * 128:(sub + 1) * 128], Win[:, kc, c0:c0 + 512],
                         [hT, Win], start=(kc == 0), stop=(kc == 7))
                if half < 2:
                    S.copy("dve", vt, vt[:, sub, half * 512:(half + 1) * 512], pv[:], [pv])
                else:
                    S.act(gs, gs[:, sub, (half - 2) * 512:(half - 1) * 512], pv[:], AF.Silu, [pv])
                yield
        for sub in range(2):
            for h in range(8):
                S.transpose(pT, pT[:, h * 128:(h + 1) * 128], kdT[:, h, sub * 128:(sub + 1) * 128],
                            k.ident[:], [kdT, k.ident])
            S.copy("act", kdtok, kdtok[:, sub, :, :].rearrange("p h k -> p (h k)"), pT[:], [pT])
            yield

    def gen_B(T0):
        par = T0 % 2
        qdT, kdT, kdtok, ebl, vt, gs = qdT2[par], kdT2[par], kdtok2[par], ebl2[par], vt2[par], gs2[par]
        r0 = T0 * TT
        for sub in range(2):
            for hg in range(2):
                hs = list(range(hg * 4, hg * 4 + 4))
                for i, h in enumerate(hs):
                    S.mm(pA, pA[:, i * 128:i * 128 + 64], kdT[:, h, sub * 128:(sub + 1) * 128],
                         qdT[:, h, sub, 0, 0:64], [kdT, qdT])
                    S.mm(pA, pA[:, i * 128 + 64:(i + 1) * 128], kdT[:, h, sub * 128:(sub + 1) * 128],
                         qdT[:, h, sub, 1, 64:128], [kdT, qdT])
                    S.mm(pB, pB[:, i * 128:(i + 1) * 128], kdtok[0:64, sub, h, :],
                         vt[0:64, sub, h * 128:(h + 1) * 128], [kdtok, vt])
                S.tt("dve", sc4, sc4[:].rearrange("p (i t) -> p i t", t=128),
                     pA[:].rearrange("p (i t) -> p i t", t=128),
                     cmask[:].unsqueeze(1).to_broadcast([128, 4, 128]), ALU.mult, [pA, cmask])
                yield
                c = sub * 2
                for i, h in enumerate(hs):
                    S.act(sts[i], sts[i][:], st32s[h][:], AF.Copy, [st32s[h], ebl], scale=ebl[:, h, c:c + 1])
                for i, h in enumerate(hs):
                    S.stt("dve", st1b[h], st1b[h][:], pB[:, i * 128:(i + 1) * 128], ebl[:, h, c:c + 1],
                          sts[i][:], ALU.mult, ALU.add, [pB, ebl, sts[i]])
                for i, h in enumerate(hs):
                    S.stt("dve", st1[h], st1[h][:], pB[:, i * 128:(i + 1) * 128], ebl[:, h, c:c + 1],
                          sts[i][:], ALU.mult, ALU.add, [pB, ebl, sts[i]])
                yield
                for i, h in enumerate(hs):
                    o_ap = pC[:, i * 128:(i + 1) * 128]
                    S.mm(pC, o_ap, sc4[:, i * 128:(i + 1) * 128], vt[:, sub, h * 128:(h + 1) * 128],
                         [sc4, vt], start=True, stop=False)
                    S.mm(pC, o_ap, qdT[:, h, sub, 0, :], stbs[h][:], [qdT, stbs[h]], start=False, stop=False)
                    S.mm(pC, o_ap, qdT[:, h, sub, 1, :], st1b[h][:], [qdT, st1b[h]], start=False, stop=True)
                for i, h in enumerate(hs):
                    S.mm(pB, pB[:, i * 128:(i + 1) * 128], kdtok[64:128, sub, h, :],
                         vt[64:128, sub, h * 128:(h + 1) * 128], [kdtok, vt])
                yield
                c = sub * 2 + 1
                for i, h in enumerate(hs):
                    S.act(sts[4 + i], sts[4 + i][:], st1[h][:], AF.Copy, [st1[h], ebl], scale=ebl[:, h, c:c + 1])
                for i, h in enumerate(hs):
                    S.stt("dve", stbs[h], stbs[h][:], pB[:, i * 128:(i + 1) * 128], ebl[:, h, c:c + 1],
                          sts[4 + i][:], ALU.mult, ALU.add, [pB, ebl, sts[4 + i]])
                for i, h in enumerate(hs):
                    S.stt("dve", st32s[h], st32s[h][:], pB[:, i * 128:(i + 1) * 128], ebl[:, h, c:c + 1],
                          sts[4 + i][:], ALU.mult, ALU.add, [pB, ebl, sts[4 + i]])
                S.copy("dve", otok, otok[:, hg * 512:(hg + 1) * 512], pC[:], [pC])
                for i, h in enumerate(hs):
                    S.act(sq, sq[:, 0:128], otok[:, h * 128:(h + 1) * 128], AF.Square, [otok],
                          accum_out=oss[:, h:h + 1], extra_writes=[oss])
                yield
            rstd_from_ss(S, orstd, oss, 128, 8)
            S.tt("dve", otok, otok[:].rearrange("p (h v) -> p h v", v=128),
                 otok[:].rearrange("p (h v) -> p h v", v=128),
                 orstd[:].unsqueeze(2).to_broadcast([128, 8, 128]), ALU.mult, [otok, orstd])
            S.tt("pool", on, on[:], otok[:], gs[:, sub, :], ALU.mult, [otok, gs])
            yield
            for kc in range(8):
                S.transpose(pT2, pT2[:, kc * 128:(kc + 1) * 128], on[:, kc * 128:(kc + 1) * 128],
                            k.ident[:], [on, k.ident])
            S.copy("act", oT, oT[:].rearrange("p c t -> p (c t)"), pT2[:], [pT2])
            yield
            xt = xts[(T0 * 2 + sub) % 4]
            xo_ = xo[sub]
            for half in range(2):
                py = pv
                for kc in range(8):
                    S.mm(py, py[:], oT[:, kc, :], Wout[:, kc, half * 512:(half + 1) * 512],
                         [oT, Wout], start=(kc == 0), stop=(kc == 7))
                S.tt("dve", xo_, xo_[:, half * 512:(half + 1) * 512], py[:], xt[:, half * 512:(half + 1) * 512],
                     ALU.add, [py, xt])
            S.dma(None, xout[r0 + sub * 128:r0 + (sub + 1) * 128, :], xo_[:], in_t=xo_)
            yield

    for _ in gen_A(0):
        pass
    for T0 in range(ntile):
        if T0 + 1 < ntile:
            interleave(gen_B(T0), gen_A(T0 + 1), 2, 3)
        else:
            for _ in gen_B(T0):
                pass
    S.finish_wait("sp", xo)


def phase_ffn(S, k, l, xnorm, xres, xout, NT, fc0, fc1, final=False):
    TT = 512
    ntile = NT // TT
    nfc = fc1 - fc0
    W = nfc * 128
    same = xres is xnorm
    load_consts(S, k)
    alloc_psum(S, k, bf7=True)
    cols = rows_to_cols(S, k, [k.modd[l, :], k.norm_ffn[l, :]], "fcols")
    gcol = S.sbuf("gcol", [128, 8], F32)
    S.stt("dve", gcol, gcol[:], cols[:, 32:40], 1.0, cols[:, 48:56], ALU.add, ALU.mult, [cols])
    shc = S.sbuf("shc", [128, 8], F32)
    S.copy("dve", shc, shc[:], cols[:, 24:32], [cols])
    ccols = rows_to_cols(S, k, [k.ffn_conv_w[l, 0, :], k.ffn_conv_w[l, 1, :], k.ffn_conv_w[l, 2, :],
                                k.ffn_conv_b[l, :]], "ccols")
    sq = S.sbuf("sq", [128, 1024], F32)
    g2row = sq
    S.dma(g2row, g2row[:], k.modd[l, 5120:6144].partition_broadcast(128))
    if final:
        fnrow = S.sbuf("fnrow", [128, 1024], F32)
        S.dma(fnrow, fnrow[:], k.final_norm[:].partition_broadcast(128))
    Wup = S.sbuf("Wup", [128, 8, 2 * W], BF16)
    Wdn = S.sbuf("Wdn", [128, nfc, 1024], BF16)
    stg = [S.sbuf("stg%d" % i, [128, 1024], F32) for i in range(2)]
    load_weight_bf16(S, Wup, k.ffn_w_up[l][:, fc0 * 128:fc1 * 128], 8, W, stg)
    load_weight_bf16(S, Wup, k.ffn_w_up[l][:, DFF + fc0 * 128:DFF + fc1 * 128], 8, W, stg, col_off=W)
    load_weight_bf16(S, Wdn, k.ffn_w_down[l][fc0 * 128:fc1 * 128, :], nfc, 1024, stg, grow=g2row)
    xts = [S.sbuf("xt%d" % i, [128, 1024], F32) for i in range(2)]
    xrs = [S.sbuf("xr%d" % i, [128, 1024], F32) for i in range(2)]
    xn = S.sbuf("xn", [128, 1024], BF16)
    ss = S.sbuf("ss", [128, 8], F32)
    rstd = S.sbuf("rstd", [128, 8], F32)
    hTs = [S.sbuf("hT%d" % i, [128, 8, TT], BF16) for i in range(2)]
    halo = S.sbuf("halo", [128, nfc, 2], F32)
    S.memset("pool", halo, halo[:], 0.0)
    abuf = [S.sbuf("abuf%d" % i, [128, TT + 2], F32) for i in range(3)]
    c1 = [S.sbuf("c1_%d" % i, [128, TT], F32) for i in range(3)]
    c2 = [S.sbuf("c2_%d" % i, [128, TT], F32) for i in range(3)]
    mTs = [S.sbuf("mT%d" % i, [128, nfc, TT], BF16) for i in range(2)]
    xo = [S.sbuf("xo%d" % i, [128, 1024], F32) for i in range(2)]
    pT = k.ps[0]
    ncnt = [0]

    xns = [xn] + [S.sbuf("xn%d" % i, [128, 1024], BF16) for i in range(1, 4)]
    sss = [S.sbuf("ss%d" % i, [128, 2], F32) for i in range(4)]
    rss = [S.sbuf("rs%d" % i, [128, 2], F32) for i in range(4)]
    pTs = [k.ps[0], k.ps[7]]

    def norm_a(T0, sub):
        i = sub
        xt = xts[sub % 2]
        r = T0 * TT + sub * 128
        S.dma(xt, xt[:], xnorm[r:r + 128, :])
        S.act(sq, sq[:], xt[:], AF.Square, [xt], accum_out=sss[i][:, 0:1], extra_writes=[sss[i]])
        rstd_from_ss(S, rss[i], sss[i], D, 1)
        S.act(xns[i], xns[i][:], xt[:], AF.Copy, [xt, rss[i]], scale=rss[i][:, 0:1])

    def norm_b(T0, sub):
        i = sub
        hT_ = hTs[T0 % 2]
        pT_ = pTs[sub % 2]
        for kc in range(8):
            S.transpose(pT_, pT_[:, kc * 128:(kc + 1) * 128], xns[i][:, kc * 128:(kc + 1) * 128],
                        k.ident[:], [xns[i], k.ident])
        for kc in range(8):
            S.act(hT_, hT_[:, kc, sub * 128:(sub + 1) * 128], pT_[:, kc * 128:(kc + 1) * 128], AF.Identity,
                  [pT_, gcol, shc], scale=gcol[:, kc:kc + 1], bias=shc[:, kc:kc + 1])

    for sub in range(4):
        norm_a(0, sub)
        norm_b(0, sub)
    def down_proj(T0):
        r0 = T0 * TT
        mT = mTs[T0 % 2]
        for sub in range(4):
            xt = xrs[sub % 2]
            S.dma(xt, xt[:], xres[r0 + sub * 128:r0 + (sub + 1) * 128, :])
            xo_ = xo[sub % 2]
            for half in range(2):
                py = k.ps[1 + half]
                for j in range(nfc):
                    S.mm(py, py[:], mT[:, j, sub * 128:(sub + 1) * 128], Wdn[:, j, half * 512:(half + 1) * 512],
                         [mT, Wdn], start=(j == 0), stop=(j == nfc - 1))
                S.tt("dve", xo_, xo_[:, half * 512:(half + 1) * 512], py[:], xt[:, half * 512:(half + 1) * 512],
                     ALU.add, [py, xt])
            if final:
                S.act(sq, sq[:], xo_[:], AF.Square, [xo_], accum_out=ss[:, 1:2], extra_writes=[ss])
                rstd_from_ss(S, rstd, ss[:, 1:2] if False else ss, D, 2)
                S.stt("dve", xo_, xo_[:], xo_[:], rstd[:, 1:2], fnrow[:], ALU.mult, ALU.mult, [xo_, rstd, fnrow])
            S.dma(None, xout[r0 + sub * 128:r0 + (sub + 1) * 128, :], xo_[:], in_t=xo_)

    for T0 in range(ntile):
        r0 = T0 * TT
        hT = hTs[T0 % 2]
        mT = mTs[T0 % 2]
        def st_A(j):
            ab = abuf[j % 3]
            pa = k.ps[1 + (j % 2)]
            fc = fc0 + j
            S.copy("pool", ab, ab[:, 0:2], halo[:, j, :], [halo])
            S.copy("act", ab, ab[:, 2:TT + 2], pa[:], [pa])
            S.copy("pool", halo, halo[:, j, :], ab[:, TT:TT + 2], [ab])
            S.ts("pool", c1[j % 3], c1[j % 3][:], ab[:, 2:TT + 2], ccols[:, 44 + fc:45 + fc],
                 ccols[:, 66 + fc:67 + fc], ALU.mult, ALU.add, reads=[ab, ccols])

        def st_B(j):
            ab = abuf[j % 3]
            fc = fc0 + j
            c1_, c2_ = c1[j % 3], c2[j % 3]
            S.stt("dve", c2_, c2_[:], ab[:, 1:TT + 1], ccols[:, 22 + fc:23 + fc], c1_[:], ALU.mult, ALU.add,
                  [ab, ccols, c1_])
            S.stt("dve", c1_, c1_[:], ab[:, 0:TT], ccols[:, fc:fc + 1], c2_[:], ALU.mult, ALU.add,
                  [ab, ccols, c2_])

        def st_C(j):
            S.act(c2[j % 3], c2[j % 3][:], c1[j % 3][:], AF.Silu, [c1[j % 3]])

        def st_D(j):
            pv = k.ps[3 + (j % 4)]
            S.tt("dve", mT, mT[:, j, :], pv[:], c2[j % 3][:], ALU.mult, [pv, c2[j % 3]])

        for j in range(nfc + 2):
            if j < nfc:
                pa = k.ps[1 + (j % 2)]
                pv = k.ps[3 + (j % 4)]
                for kc in range(8):
                    S.mm(pa, pa[:], Wup[:, kc, j * 128:(j + 1) * 128], hT[:, kc, :], [Wup, hT],
                         start=(kc == 0), stop=(kc == 7))
                for kc in range(8):
                    S.mm(pv, pv[:], Wup[:, kc, W + j * 128:W + (j + 1) * 128], hT[:, kc, :], [Wup, hT],
                         start=(kc == 0), stop=(kc == 7))
                st_A(j)
            if 0 <= j - 1 < nfc:
                st_B(j - 1)
                st_C(j - 1)
            if 0 <= j - 2 < nfc:
                st_D(j - 2)
            if T0 + 1 < ntile:
                if j == 0:
                    for sub_ in range(4):
                        norm_a(T0 + 1, sub_)
                if j in (3, 5, 7, 9):
                    norm_b(T0 + 1, (j - 3) // 2)
            if j == 1 and T0 > 0:
                down_proj(T0 - 1)
    down_proj(ntile - 1)
    S.finish_wait("sp", xo)


def build_program(NT, phases=("pro", "hg", "ffn0", "nsa", "ffn1")):
    nc = bass.Bass("TRN2", target_bir_lowering=False)
    k = K()

    def inp(name, shape):
        return nc.dram_tensor(name, list(shape), F32, kind="ExternalInput").ap()

    k.x = inp("x", [NT, D])
    k.c_col = inp("c_col", [128, 8])
    k.ada_w = inp("ada_w", [2, D, 6 * D])
    k.ada_b = inp("ada_b", [2, 6 * D])
    k.norm_mix = inp("norm_mix", [2, D])
    k.norm_ffn = inp("norm_ffn", [2, D])
    k.final_norm = inp("final_norm", [D])
    k.hg_w_in = inp("hg_w_in", [1, D, 4096])
    k.hg_w_out = inp("hg_w_out", [1, D, D])
    k.hg_gnorm = inp("hg_gnorm", [1, 128])
    k.hg_lb = inp("hg_lb", [2, D])
    k.ffn_w_up = inp("ffn_w_up", [2, D, 2 * DFF])
    k.ffn_conv_w = inp("ffn_conv_w", [2, 3, DFF])
    k.ffn_conv_b = inp("ffn_conv_b", [2, DFF])
    k.ffn_w_down = inp("ffn_w_down", [2, DFF, D])
    k.c_ident = inp("c_ident", [128, 128])
    k.c_bdmask = inp("c_bdmask", [128, 128])
    nsa_declare(nc, k, NT)
    k.out = nc.dram_tensor("out", [NT, D], F32, kind="ExternalOutput").ap()
    k.modd = nc.dram_tensor("modd", [2, 6 * D], F32).ap()
    bufs = [nc.dram_tensor("xs%d" % i, [NT, D], F32).ap() for i in range(4)]
    xc = nc.dram_tensor("xc", [NT, D], F32).ap()
    main = [p for p in phases if p != "pro"]
    with contextlib.ExitStack() as gst:
        S = Sched(nc, gst)
        k.S = S
        plist = []
        if "pro" in phases:
            plist.append(lambda: (alloc_psum(S, k), phase_prologue(S, k)))
        src = k.x
        for i, p in enumerate(main):
            last = (i == len(main) - 1)
            dst = k.out if last else bufs[i]
            if p == "hg":
                plist.append(lambda src=src, dst=dst: phase_hgrn(S, k, 0, src, dst, NT))
            elif p in ("ffn0", "ffn1"):
                l = int(p[3])
                fin = (p == "ffn1")
                plist.append(lambda src=src, l=l: phase_ffn(S, k, l, src, src, xc, NT, 0, 11))
                plist.append(lambda src=src, dst=dst, l=l, fin=fin: phase_ffn(S, k, l, src, xc, dst, NT, 11, 22, final=fin))
            elif p == "nsa":
                import os
                stop = int(os.environ.get("NSA_STOP", "4"))
                plist.append(lambda src=src: phase_nsa_proj(S, k, 1, src, NT))
                if stop >= 2:
                    plist.append(lambda: phase_nsa_cmp(S, k, NT))
                if stop >= 3:
                    plist.append(lambda: phase_nsa_attn(S, k, NT))
                if stop >= 4:
                    plist.append(lambda src=src, dst=dst: phase_nsa_out(S, k, 1, src, dst, NT))
            src = dst
        for i, p in enumerate(plist):
            with contextlib.ExitStack() as st:
                S.stack = st
                p()
                S.barrier()
                S.emit()
                S.phase_end()
    return nc


def host_consts():
    ident = np.eye(128, dtype=np.float32)
    s = np.arange(128)[:, None]
    t = np.arange(128)[None, :]
    bd = ((s // 64 == t // 64) & (s <= t)).astype(np.float32)
    return {"c_ident": ident, "c_bdmask": bd}


NSA_W = 2608
SLOPES = [2.0 ** (-8.0 * (h + 1) / 16) for h in range(16)]


def nsa_dims(NT):
    ncb = (NT - 32) // 16 + 1
    nsb = NT // 64
    return ncb, nsb


def nsa_host_consts(NT):
    import ml_dtypes
    bf = ml_dtypes.bfloat16
    ncb, nsb = nsa_dims(NT)
    t = np.arange(NT)
    c = {}
    c["c_vrow"] = np.stack([-SLOPES[h] * t for h in range(16)]).astype(bf)
    KT = NT // 128
    i = np.arange(128)
    sb = np.zeros((128, KT * 16), np.float32)
    for kt in range(KT):
        for h in range(16):
            sb[:, kt * 16 + h] = SLOPES[h] * (128 * kt + i)
    c["c_sbias"] = sb
    cb = np.zeros((128, 2 * 16), np.float32)
    for bt in range(2):
        for h in range(16):
            cb[:, bt * 16 + h] = SLOPES[h] * (16 * (128 * bt + i) + 15.5)
    c["c_cbias"] = cb
    c["c_maskd"] = np.where(i[:, None] > i[None, :], NEG, 0.0).astype(bf)
    c["c_maskw"] = np.where(i[None, :] >= i[:, None], NEG, 0.0).astype(bf)
    QT = NT // 512
    cm = np.zeros((QT, 2, 128, 512), np.float32)
    for T in range(QT):
        for bt in range(2):
            n = 128 * bt + i
            tt = 512 * T + np.arange(512)
            cm[T, bt] = np.where(16 * n[:, None] + 31 <= tt[None, :], 0.0, NEG)
    c["c_cmask"] = cm.astype(bf)
    ov = np.zeros((256, 64), np.float32)
    ci = np.arange(256)[:, None] * 16
    sj = np.arange(64)[None, :] * 64
    ov[:] = ((ci <= sj + 63) & (ci + 31 >= sj))
    ov[ncb:] = 0
    ov[:, nsb:] = 0
    c["c_overlap"] = ov.astype(bf)
    blk = np.arange(64)[None, :]
    cur = (t // 64)[:, None]
    valid = blk * 64 <= t[:, None]
    forced = ((blk == 0) | (blk == cur) | (blk == cur - 1)) & valid
    vm = valid.astype(np.float32)
    am = np.where(forced, 1e4, np.where(valid, 0.0, -1.0)).astype(np.float32)
    vm[:, nsb:] = 0.0
    am[:, nsb:] = -1.0
    c["c_vmask"] = vm
    c["c_amask"] = am
    si = np.zeros((64, NT), np.float32)
    si[0] = 1.0
    for j in range(1, 64):
        si[j] = (t // 64 == j)
    c["c_selind"] = si.astype(bf)
    kw = np.zeros((64, NT), np.float32)
    kw[0] = 1.0
    c["c_kwrows"] = kw.astype(bf)
    return c


def nsa_declare(nc, k, NT):
    def inp(name, shape, dt=F32):
        return nc.dram_tensor(name, list(shape), dt, kind="ExternalInput").ap()
    QT = NT // 512
    KT = NT // 128
    k.nsa_w_in = inp("nsa_w_in", [1, D, NSA_W])
    k.nsa_w_out = inp("nsa_w_out", [1, D, D])
    k.nsa_cmp_pe = inp("nsa_cmp_pe", [1, 2, 32, 64])
    k.nsa_cmp_w1 = inp("nsa_cmp_w1", [1, 2, 2048, 64])
    k.nsa_cmp_w2 = inp("nsa_cmp_w2", [1, 2, 64, 64])
    k.c_vrow = inp("c_vrow", [16, NT], BF16)
    k.c_sbias = inp("c_sbias", [128, KT * 16])
    k.c_cbias = inp("c_cbias", [128, 32])
    k.c_maskd = inp("c_maskd", [128, 128], BF16)
    k.c_maskw = inp("c_maskw", [128, 128], BF16)
    k.c_cmask = inp("c_cmask", [QT, 2, 128, 512], BF16)
    k.c_overlap = inp("c_overlap", [256, 64], BF16)
    k.c_vmask = inp("c_vmask", [NT, 64])
    k.c_amask = inp("c_amask", [NT, 64])
    k.c_selind = inp("c_selind", [64, NT], BF16)
    k.c_kwrows = inp("c_kwrows", [64, NT], BF16)
    k.qT_d = nc.dram_tensor("qT_d", [1024, NT], BF16).ap()
    k.kT_d = nc.dram_tensor("kT_d", [4, 256, NT], BF16).ap()
    k.vtok_d = nc.dram_tensor("vtok_d", [2, NT, 256], BF16).ap()
    k.gT_d = nc.dram_tensor("gT_d", [48, NT], F32).ap()
    k.kcT_d = nc.dram_tensor("kcT_d", [4, 64, 256], BF16).ap()
    k.vc_d = nc.dram_tensor("vc_d", [4, 256, 64], BF16).ap()
    k.oT_d = nc.dram_tensor("oT_d", [1024, NT], BF16).ap()


def phase_nsa_proj(S, k, l, xin, NT):
    TT = 512
    ntile = NT // TT
    load_consts(S, k)
    alloc_psum(S, k)
    cols = rows_to_cols(S, k, [k.modd[l, :], k.norm_mix[l, :]], "ncols")
    gcol = S.sbuf("gcol", [128, 8], F32)
    S.stt("dve", gcol, gcol[:], cols[:, 8:16], 1.0, cols[:, 48:56], ALU.add, ALU.mult, [cols])
    W = S.sbuf("Wn", [128, 8, NSA_W], BF16)
    stg = [S.sbuf("stg%d" % i, [128, 1024], F32) for i in range(3)]
    load_weight_bf16(S, W, k.nsa_w_in[0], 8, NSA_W, stg)
    xts = [S.sbuf("xt%d" % i, [128, 1024], F32) for i in range(2)]
    xn = S.sbuf("xn", [128, 1024], BF16)
    sq = S.sbuf("sq", [128, 1024], F32)
    ss = S.sbuf("ss", [128, 8], F32)
    rstd = S.sbuf("rstd", [128, 8], F32)
    hT = S.sbuf("hT", [128, 8, TT], BF16)
    fsb = [S.sbuf("fsb%d" % i, [128, TT], BF16) for i in range(3)]
    vsb = [S.sbuf("vsb%d" % i, [128, 512], BF16) for i in range(2)]
    gsb = [S.sbuf("gsb%d" % i, [48, TT], F32) for i in range(2)]
    pT = k.ps[0]
    outs = fsb + vsb + gsb
    n = 0
    for T0 in range(ntile):
        r0 = T0 * TT
        for sub in range(4):
            xt = xts[sub % 2]
            S.dma(xt, xt[:], xin[r0 + sub * 128:r0 + (sub + 1) * 128, :])
            norm_to_hT(S, k, xt, xt[:], gcol, cols, hT, sub * 128, xn, sq, ss, rstd, pT)
        fm = [(hp * 128, ("q", hp)) for hp in range(8)]
        for kind, c0 in ((0, 1024), (1, 1280), (2, 1536), (3, 2048)):
            for cpart in range(2):
                fm.append((c0 + cpart * 128, ("k", kind, cpart)))
        for c0, tag in fm:
            ps = k.ps[1 + (n % 3)]
            f = fsb[n % 3]
            n += 1
            for kc in range(8):
                S.mm(ps, ps[:], W[:, kc, c0:c0 + 128], hT[:, kc, :], [W, hT], start=(kc == 0), stop=(kc == 7))
            if tag[0] == "q":
                S.act(f, f[:], ps[:], AF.Copy, [ps], scale=0.125)
                S.dma(None, k.qT_d[tag[1] * 128:(tag[1] + 1) * 128, r0:r0 + TT], f[:], in_t=f)
            else:
                S.copy("dve", f, f[:], ps[:], [ps])
                S.dma(None, k.kT_d[tag[1], tag[2] * 128:(tag[2] + 1) * 128, r0:r0 + TT], f[:], in_t=f)
        for sub in range(4):
            ps = k.ps[4 + (sub % 2)]
            v = vsb[sub % 2]
            for j, c0 in enumerate((1792, 2304)):
                for kc in range(8):
                    S.mm(ps, ps[:, j * 256:(j + 1) * 256], hT[:, kc, sub * 128:(sub + 1) * 128],
                         W[:, kc, c0:c0 + 256], [hT, W], start=(kc == 0), stop=(kc == 7))
            S.copy("dve", v, v[:], ps[:], [ps])
            for j in range(2):
                S.dma(None, k.vtok_d[j, r0 + sub * 128:r0 + (sub + 1) * 128, :], v[:, j * 256:(j + 1) * 256], in_t=v)
        ps = k.ps[6]
        g_ = gsb[T0 % 2]
        for kc in range(8):
            S.mm(ps, ps[0:48, :], W[:, kc, 2560:2608], hT[:, kc, :], [W, hT], start=(kc == 0), stop=(kc == 7))
        S.act(g_, g_[:], ps[0:48, :], AF.Sigmoid, [ps])
        S.dma(None, k.gT_d[:, r0:r0 + TT], g_[:], in_t=g_)
    S.finish_wait("sp", outs)


def phase_nsa_cmp(S, k, NT):
    ncb, nsb = nsa_dims(NT)
    load_consts(S, k)
    alloc_psum(S, k)
    xc = S.sbuf("xc", [64, 2, 4, NT], BF16)
    for kv in range(2):
        for g in range(4):
            S.dma(xc, xc[:, kv, g, :], k.kT_d[kv, g * 64:(g + 1) * 64, :])
    w1f = S.sbuf("w1f", [64, 2, 32, 64], F32)
    w1 = S.sbuf("w1", [64, 2, 32, 64], BF16)
    for kv in range(2):
        S.dma(w1f, w1f[:, kv, :, :], k.nsa_cmp_w1[0, kv].rearrange("(l d) e -> d l e", d=64))
    S.copy("pool", w1, w1[:], w1f[:], [w1f])
    w2f = S.sbuf("w2f", [64, 2, 64], F32)
    w2p = S.sbuf("w2p", [64, 2, 128], BF16)
    for kv in range(2):
        S.dma(w2f, w2f[:, kv, :], k.nsa_cmp_w2[0, kv])
    S.memset("pool", w2p, w2p[:], 0.0)
    S.copy("pool", w2p, w2p[:, :, 64:128], w2f[:], [w2f])
    pef = S.sbuf("pef", [32, 2, 64], F32)
    for kv in range(2):
        S.dma(pef, pef[:, kv, :], k.nsa_cmp_pe[0, kv])
    peT = S.sbuf("peT", [64, 2, 32], BF16)
    ps = k.ps[1]
    for kv in range(2):
        S.mm(ps, ps[0:64, kv * 32:(kv + 1) * 32], pef[:, kv, :], k.identf[0:32, 0:32], [pef, k.identf])
    S.copy("dve", peT, peT[:].rearrange("p a l -> p (a l)"), ps[0:64, 0:64], [ps])
    bias = S.sbuf("cbias", [64, 2], F32)
    ps = k.ps[2]
    for kv in range(2):
        for l in range(32):
            S.mm(ps, ps[0:64, kv:kv + 1], w1[:, kv, l, :], peT[:, kv, l:l + 1], [w1, peT],
                 start=(l == 0), stop=(l == 31))
    S.copy("dve", bias, bias[:], ps[0:64, 0:2], [ps])
    hid = [S.sbuf("hid%d" % i, [64, 256], BF16) for i in range(2)]
    osb = [S.sbuf("osb%d" % i, [128, 256], BF16) for i in range(2)]
    n = 0
    for kv in range(2):
        for g in range(4):
            ph = k.ps[3 + (n % 2)]
            hd = hid[n % 2]
            ob = osb[n % 2]
            for l in range(32):
                rhs = xc[:, kv, g, l:l + 16 * (ncb - 1) + 1:16]
                S.mm(ph, ph[0:64, 0:ncb], w1[:, kv, l, :], rhs, [w1, xc], start=(l == 0), stop=(l == 31))
            S.act(hd, hd[:, 0:ncb], ph[0:64, 0:ncb], AF.Silu, [ph, bias], bias=bias[:, kv:kv + 1])
            po = k.ps[5 + (n % 2)]
            if kv == 0:
                S.mm(po, po[:, 0:ncb], w2p[:, 0, :], hd[:, 0:ncb], [w2p, hd])
                S.copy("dve", ob, ob[64:128, 0:ncb], po[64:128, 0:ncb], [po])
                S.dma(None, k.kcT_d[g, :, 0:ncb], ob[64:128, 0:ncb], in_t=ob)
            else:
                for bt in range((ncb + 127) // 128):
                    nb = min(128, ncb - bt * 128)
                    S.mm(po, po[0:nb, bt * 64:(bt + 1) * 64], hd[:, bt * 128:bt * 128 + nb], w2p[:, 1, 64:128],
                         [hd, w2p])
                    S.copy("dve", ob, ob[0:nb, bt * 64:(bt + 1) * 64], po[0:nb, bt * 64:(bt + 1) * 64], [po])
                    S.dma(None, k.vc_d[g, bt * 128:bt * 128 + nb, :], ob[0:nb, bt * 64:(bt + 1) * 64], in_t=ob)
            n += 1
    S.finish_wait("sp", osb)


def phase_nsa_attn(S, k, NT):
    ncb, nsb = nsa_dims(NT)
    QT = NT // 512
    KT = NT // 128
    NBT = (ncb + 127) // 128
    load_consts(S, k)
    alloc_psum(S, k)
    sbias = S.sbuf("sbias", [128, KT * 16], F32)
    S.dma(sbias, sbias[:], k.c_sbias[:, :])
    cbias = S.sbuf("cbias", [128, 32], F32)
    S.dma(cbias, cbias[:], k.c_cbias[:, :])
    maskd = S.sbuf("maskd", [128, 128], BF16)
    S.dma(maskd, maskd[:], k.c_maskd[:, :])
    maskw = S.sbuf("maskw", [128, 128], BF16)
    S.dma(maskw, maskw[:], k.c_maskw[:, :])
    ovl = S.sbuf("ovl", [128, 2, 64], BF16)
    S.dma(ovl, ovl[:], k.c_overlap.rearrange("(bt p) j -> p bt j", p=128))
    Ks = S.sbuf("Ks", [128, NT], BF16)
    Kw = S.sbuf("Kw", [128, NT], BF16)
    Kc = S.sbuf("Kc", [128, 256], BF16)
    Vs = S.sbuf("Vs", [128, KT, 128], BF16)
    Vw = S.sbuf("Vw", [128, KT, 128], BF16)
    Vc = S.sbuf("Vc", [128, 2, 128], BF16)
    S.memset("pool", Vs, Vs[:], 1.0)
    S.memset("pool", Vw, Vw[:], 1.0)
    S.memset("pool", Vc, Vc[:], 1.0)
    S.memset("pool", Kc, Kc[:], 0.0)
    S.dma(Ks, Ks[0:64, :], k.c_selind[:, :])
    S.dma(Kw, Kw[0:64, :], k.c_kwrows[:, :])
    S.dma(Kc, Kc[0:64, :], k.c_kwrows[:, 0:256])
    QaT = [S.sbuf("Qa%d" % i, [128, 4, 512], BF16) for i in range(2)]
    Qav = [[S.view("Qa%d_%d" % (i, hh), QaT[i].ap[:, hh, :]) for hh in range(4)] for i in range(2)]
    for q in QaT:
        S.memset("pool", q, q[:], 0.0)
    for i in range(2):
        for hh in range(4):
            Qav[i][hh].last_w = QaT[i].last_w
    gbT = [S.sbuf("gb%d" % i, [128, 12, 512], F32) for i in range(2)]
    Pt = [S.sbuf("Pt%d" % i, [128, 512], BF16) for i in range(6)]
    cmk = [S.sbuf("cmk%d" % i, [128, 512], BF16) for i in range(2)]
    rd = [S.sbuf("rd%d" % i, [128, 512], F32) for i in range(2)]
    coef = [S.sbuf("coef%d" % i, [128, 512], F32) for i in range(2)]
    tmp = [S.sbuf("tmp%d" % i, [128, 512], F32) for i in range(2)]
    oacc = [S.sbuf("oacc%d" % i, [128, 512], F32) for i in range(4)]
    osbT = [S.sbuf("osb%d" % i, [128, 4, 512], BF16) for i in range(2)]
    impacc = S.sbuf("impacc", [128, 512], F32)
    vm = S.sbuf("vm", [128, 4, 64], F32)
    am = S.sbuf("am", [128, 4, 64], F32)
    sc = S.sbuf("sc", [128, 4, 64], F32)
    sc2 = S.sbuf("sc2", [128, 64], F32)
    mx = S.sbuf("mx", [128, 8], F32)
    thr = S.sbuf("thr", [128, 4], F32)
    selb = S.sbuf("selb", [128, 4, 64], BF16)
    pT = k.ps[0]
    pSs = [k.ps[1], k.ps[2], k.ps[6], k.ps[5]]
    pOs = [k.ps[3], k.ps[7]]
    pI, pTk = k.ps[4], k.ps[5]
    cnt = {"s": 0, "p": 0, "r": 0, "cm": 0, "o": 0, "po": 0}
    NP = len(Pt)
    LOOK = 3
    pend = []
    cur = {}

    def finish_branch(pO, hh, br, first, want_imp):
        i = cnt["r"] % 2
        cnt["r"] += 1
        r_, c_, t_ = rd[i], coef[i], tmp[i]
        gbt = cur["gb"]
        S.act(r_, r_[64:128, :], pO[64:128, :], AF.Ln, [pO], bias=(1e-30 if br == 0 else 0.0))
        S.act(r_, r_[64:128, :], r_[64:128, :], AF.Exp, [r_], scale=-1.0)
        S.tt("dve", c_, c_[64:128, :], r_[64:128, :], gbt[64:128, 3 * hh + br, :], ALU.mult, [r_, gbt])
        if first:
            S.tt("dve", oacc[hh], oacc[hh][0:64, :], pO[0:64, :], c_[64:128, :], ALU.mult, [pO, c_])
        else:
            S.tt("dve", t_, t_[0:64, :], pO[0:64, :], c_[64:128, :], ALU.mult, [pO, c_])
            S.tt("pool", oacc[hh], oacc[hh][0:64, :], oacc[hh][0:64, :], t_[0:64, :], ALU.add, [oacc[hh], t_])
        if want_imp:
            if hh == 0:
                S.tt("dve", impacc, impacc[0:64, :], pI[0:64, :], r_[64:128, :], ALU.mult, [pI, r_])
            else:
                S.tt("dve", t_, t_[0:64, :], pI[0:64, :], r_[64:128, :], ALU.mult, [pI, r_])
                S.tt("pool", impacc, impacc[0:64, :], impacc[0:64, :], t_[0:64, :], ALU.add, [impacc, t_])

    def emit_pv(item):
        (pO, lhsV, P, np_, clo, chi, first, last, ovl_ap, cb, vt_) = item
        S.mm(pO, pO[:, clo:chi], lhsV, P[0:np_, clo:chi], [vt_, P], start=first, stop=last)
        if ovl_ap is not None:
            S.mm(pI, pI[0:64, clo:chi], ovl_ap, P[0:np_, clo:chi], [ovl, P], start=first, stop=last)
        if cb is not None:
            cb()

    def push(item):
        pend.append(item)
        while len(pend) > LOOK:
            emit_pv(pend.pop(0))

    def flush():
        while pend:
            emit_pv(pend.pop(0))

    def attend(hh, h, Kt, Vt, tiles, br, first_branch):
        pO = pOs[cnt["po"] % 2]
        cnt["po"] += 1
        Qh = cur["Qa"][hh]
        for idx, (kt, clo, chi, masks) in enumerate(tiles):
            pS = pSs[cnt["s"] % len(pSs)]
            cnt["s"] += 1
            S.mm(pS, pS[:, clo:chi], Kt[:, kt * 128:(kt + 1) * 128], Qh[:, clo:chi], [Kt, Qh],
                 start=True, stop=(len(masks) == 0))
            for mi, (mk, c0) in enumerate(masks):
                S.mm(pS, pS[:, c0:c0 + 128], k.ident[:], mk[:], [k.ident, mk], start=False,
                     stop=(mi == len(masks) - 1))
            P = Pt[cnt["p"] % NP]
            cnt["p"] += 1
            S.act(P, P[:, clo:chi], pS[:, clo:chi], AF.Exp, [pS, sbias], bias=sbias[:, kt * 16 + h:kt * 16 + h + 1])
            last = (idx == len(tiles) - 1)
            cb = (lambda pO=pO, hh=hh, br=br, fb=first_branch: finish_branch(pO, hh, br, fb, False)) if last else None
            push((pO, Vt[:, kt, :], P, 128, clo, chi, idx == 0, last, None, cb, Vt))

    units = [(g, T) for g in range(4) for T in range(QT)]

    def load_unit(ui):
        g, T = units[ui]
        T0 = 512 * T
        par = ui % 2
        S.op("sp", lambda e: e.dma_start(out=gbT[par][64:128, :, :],
                                         in_=k.gT_d[12 * g:12 * g + 12, T0:T0 + 512].partition_broadcast(64)),
             [], [gbT[par]], dma_sem_tile=gbT[par])
        S.op("sp", lambda e: e.dma_start(out=QaT[par][64:128, :, :],
                                         in_=k.qT_d[256 * g:256 * g + 256, T0:T0 + 512].rearrange("(hh d) t -> d hh t", d=64)),
             [], Qav[par], dma_sem_tile=Qav[par][0])
        S.op("sp", lambda e: e.dma_start(out=QaT[par][0:1, :, :], in_=k.c_vrow[4 * g:4 * g + 4, T0:T0 + 512].rearrange("(o h) t -> o h t", o=1)),
             [], Qav[par], dma_sem_tile=Qav[par][0])

    def load_group(g):
        S.dma(Ks, Ks[64:128, :], k.kT_d[2, g * 64:(g + 1) * 64, :])
        S.dma(Kw, Kw[64:128, :], k.kT_d[3, g * 64:(g + 1) * 64, :])
        S.dma(Kc, Kc[64:128, 0:ncb], k.kcT_d[g, :, 0:ncb])
        for k0 in range(0, KT, 8):
            k1 = min(KT, k0 + 8)
            S.dma(Vs, Vs[:, k0:k1, 0:64],
                  k.vtok_d[0][k0 * 128:k1 * 128, g * 64:(g + 1) * 64].rearrange("(kt p) d -> p kt d", p=128))
            S.dma(Vw, Vw[:, k0:k1, 0:64],
                  k.vtok_d[1][k0 * 128:k1 * 128, g * 64:(g + 1) * 64].rearrange("(kt p) d -> p kt d", p=128))
        for bt in range(NBT):
            nb = min(128, ncb - bt * 128)
            S.dma(Vc, Vc[0:nb, bt, 0:64], k.vc_d[g, bt * 128:bt * 128 + nb, :])

    load_unit(0)
    for ui, (g, T) in enumerate(units):
        if T == 0:
            load_group(g)
        T0 = 512 * T
        par = ui % 2
        cur["Qa"] = Qav[par]
        cur["gb"] = gbT[par]
        Qa = Qav[par]
        bts = []
        for bt in range(NBT):
            nb = min(128, ncb - bt * 128)
            n_lo, n_hi = 128 * bt, 128 * bt + nb - 1
            if 16 * n_lo + 31 > T0 + 511:
                continue
            partial = 16 * n_hi + 31 > T0
            bts.append((bt, nb, partial))
        cms = {}
        for (bt, nb, partial) in bts:
            if partial:
                cm_ = cmk[cnt["cm"] % 2]
                cnt["cm"] += 1
                S.dma(cm_, cm_[:], k.c_cmask[T, bt])
                cms[bt] = cm_
        S.dma(vm, vm[:], k.c_vmask[T0:T0 + 512, :].rearrange("(n p) j -> p n j", p=128))
        S.dma(am, am[:], k.c_amask[T0:T0 + 512, :].rearrange("(n p) j -> p n j", p=128))
        for hh in range(4):
            h = 4 * g + hh
            pO = pOs[cnt["po"] % 2]
            cnt["po"] += 1
            for bi, (bt, nb, partial) in enumerate(bts):
                pS = pSs[cnt["s"] % len(pSs)]
                cnt["s"] += 1
                S.mm(pS, pS[0:nb, :], Kc[:, bt * 128:bt * 128 + nb], Qa[hh][:, :], [Kc, Qa[hh]],
                     start=True, stop=(not partial))
                if partial:
                    cm_ = cms[bt]
                    S.mm(pS, pS[0:nb, :], k.ident[0:nb, 0:nb], cm_[0:nb, :], [k.ident, cm_], start=False, stop=True)
                P = Pt[cnt["p"] % NP]
                cnt["p"] += 1
                S.act(P, P[0:nb, :], pS[0:nb, :], AF.Exp, [pS, cbias],
                      bias=cbias[0:nb, bt * 16 + h:bt * 16 + h + 1])
                last = (bi == len(bts) - 1)
                cb = (lambda pO=pO, hh=hh: finish_branch(pO, hh, 0, True, True)) if last else None
                push((pO, Vc[0:nb, bt, :], P, nb, 0, 512, bi == 0, last, ovl[0:nb, bt, :], cb, Vc))
        for hh in range(4):
            h = 4 * g + hh
            tiles = []
            for kt in range(max(0, 4 * T - 4), 4 * T + 4):
                m = kt - 4 * T
                n_lo, n_hi = max(m, 0), min(m + 4, 3)
                masks = []
                if m >= 0:
                    masks.append((maskd, 128 * m))
                if m <= -1:
                    masks.append((maskw, 128 * (m + 4)))
                tiles.append((kt, 128 * n_lo, 128 * (n_hi + 1), masks))
            tiles.sort(key=lambda tl: -(tl[2] - tl[1]))
            attend(hh, h, Kw, Vw, tiles, 2, False)
            if hh == 1:
                for n in range(4):
                    S.mm(pTk, pTk[:, n * 64:(n + 1) * 64], impacc[0:64, n * 128:(n + 1) * 128],
                         k.identf[0:64, 0:64], [impacc, k.identf])
                S.tt("dve", sc, sc[:].rearrange("p n j -> p (n j)"), pTk[:, 0:256],
                     vm[:].rearrange("p n j -> p (n j)"), ALU.mult, [pTk, vm])
                S.tt("pool", sc, sc[:], sc[:], am[:], ALU.add, [sc, am])
                for n in range(4):
                    S.op("dve", lambda e, n=n: e.max(mx[:], sc[:, n, :]), [sc], [mx])
                    S.op("dve", lambda e, n=n: e.match_replace(sc2[:], mx[:], sc[:, n, :], -1e9), [sc, mx], [sc2])
                    S.op("dve", lambda e: e.max(mx[:], sc2[:]), [sc2], [mx])
                    S.ts("dve", thr, thr[:, n:n + 1], mx[:, 7:8], -0.5, None, ALU.max, reads=[mx])
                    S.ts("dve", sc2, sc2[:], sc[:, n, :], thr[:, n:n + 1], None, ALU.is_ge, reads=[sc, thr])
                    S.ts("dve", selb, selb[:, n, :], sc2[:], -1.0, -NEG, ALU.add, ALU.mult, reads=[sc2])
        if ui + 1 < len(units):
            load_unit(ui + 1)
        for n in range(4):
            S.transpose(pT, pT[0:64, n * 128:(n + 1) * 128], selb[:, n, :], k.ident[:], [selb, k.ident])
        flush()
        for hh in range(4):
            S.copy("act", Qa[hh], Qa[hh][0:64, :], pT[0:64, 0:512], [pT])
        S.op("sp", lambda e, par=par, g=g, T0=T0: e.dma_start(
            out=QaT[par][0:1, :, :], in_=k.c_vrow[4 * g:4 * g + 4, T0:T0 + 512].rearrange("(o h) t -> o h t", o=1)),
            [], Qa, dma_sem_tile=Qa[0])
        for hh in range(4):
            h = 4 * g + hh
            tiles = []
            for kt in range(4 * T + 4):
                j = kt - 4 * T
                if j < 0:
                    tiles.append((kt, 0, 512, []))
                else:
                    tiles.append((kt, 128 * j, 512, [(maskd, 128 * j)]))
            attend(hh, h, Ks, Vs, tiles, 1, False)
        flush()
        ob = osbT[cnt["o"] % 2]
        cnt["o"] += 1
        for hh in range(4):
            S.copy("act", ob, ob[0:64, hh, :], oacc[hh][0:64, :], [oacc[hh]])
        S.dma(None, k.oT_d[256 * g:256 * g + 256, T0:T0 + 512].rearrange("(hh d) t -> d hh t", d=64),
              ob[0:64, :, :], in_t=ob)
    S.finish_wait("sp", osbT)


def phase_nsa_out(S, k, l, xin, xout, NT):
    TT = 512
    ntile = NT // TT
    load_consts(S, k)
    alloc_psum(S, k)
    g1row = S.sbuf("g1row", [128, 1024], F32)
    S.dma(g1row, g1row[:], k.modd[l, 2048:3072].partition_broadcast(128))
    Wo = S.sbuf("Wo", [128, 8, 1024], BF16)
    stg = [S.sbuf("stg%d" % i, [128, 1024], F32) for i in range(2)]
    load_weight_bf16(S, Wo, k.nsa_w_out[0], 8, 1024, stg, grow=g1row)
    oT = [S.sbuf("oT%d" % i, [128, 8, TT], BF16) for i in range(2)]
    xts = [S.sbuf("xt%d" % i, [128, 1024], F32) for i in range(2)]
    xo = [S.sbuf("xo%d" % i, [128, 1024], F32) for i in range(2)]
    n = 0
    for T0 in range(ntile):
        r0 = T0 * TT
        o_ = oT[T0 % 2]
        S.dma(o_, o_[:], k.oT_d[:, r0:r0 + TT].rearrange("(kc p) t -> p kc t", p=128))
        for sub in range(4):
            xt = xts[sub % 2]
            xo_ = xo[sub % 2]
            S.dma(xt, xt[:], xin[r0 + sub * 128:r0 + (sub + 1) * 128, :])
            for half in range(2):
                py = k.ps[1 + (n % 4)]
                n += 1
                for kc in range(8):
                    S.mm(py, py[:], o_[:, kc, sub * 128:(sub + 1) * 128], Wo[:, kc, half * 512:(half + 1) * 512],
                         [o_, Wo], start=(kc == 0), stop=(kc == 7))
                S.tt("dve", xo_, xo_[:, half * 512:(half + 1) * 512], py[:], xt[:, half * 512:(half + 1) * 512],
                     ALU.add, [py, xt])
            S.dma(None, xout[r0 + sub * 128:r0 + (sub + 1) * 128, :], xo_[:], in_t=xo_)
    S.finish_wait("sp", xo)


W_KEYS = ["ada_w", "ada_b", "norm_mix", "norm_ffn", "final_norm", "hg_w_in", "hg_w_out", "hg_gnorm", "hg_lb",
          "ffn_w_up", "ffn_conv_w", "ffn_conv_b", "ffn_w_down", "nsa_w_in", "nsa_w_out", "nsa_cmp_pe",
          "nsa_cmp_w1", "nsa_cmp_w2"]


def make_in_map(inp, x, c, NT, consts=None):
    im = {"x": np.ascontiguousarray(x, dtype=np.float32),
          "c_col": np.ascontiguousarray(np.asarray(c, dtype=np.float32).reshape(8, 128).T)}
    for k_ in W_KEYS:
        im[k_] = np.asarray(inp[k_], dtype=np.float32)
    if consts is None:
        consts = dict(host_consts())
        consts.update(nsa_host_consts(NT))
    im.update(consts)
    return im


_CACHE = {}


def kernel(**inputs):
    x = np.asarray(inputs["x"], dtype=np.float32)
    c = np.asarray(inputs["c"], dtype=np.float32)
    B, NT, _ = x.shape
    if "nc" not in _CACHE:
        _CACHE["nc"] = build_program(NT)
        consts = dict(host_consts())
        consts.update(nsa_host_consts(NT))
        _CACHE["consts"] = consts
    nc = _CACHE["nc"]
    in_maps = [make_in_map(inputs, x[b], c[b], NT, _CACHE["consts"]) for b in range(B)]
    res = run_bass_kernel_spmd(nc, in_maps, core_ids=list(range(B)))
    out = np.stack([np.asarray(r["out"], dtype=np.float32) for r in res.results], axis=0)
    return out
```

```python
import contextlib
import numpy as np
import concourse.bass as bass
import concourse.mybir as mybir

F32 = mybir.dt.float32
BF16 = mybir.dt.bfloat16
AF = mybir.ActivationFunctionType
ALU = mybir.AluOpType
AX = mybir.AxisListType

ENGS = ["pe", "act", "dve", "pool", "sp"]


class T:
    __slots__ = ("name", "ap", "last_w", "readers", "dsem", "dcnt", "uid", "excl")
    _n = [0]

    def __init__(self, name, ap=None):
        T._n[0] += 1
        self.uid = T._n[0]
        self.name = name
        self.ap = ap
        self.last_w = None
        self.readers = []
        self.dsem = None
        self.dcnt = 0
        self.excl = False

    def __getitem__(self, idx):
        return self.ap[idx]


class Sched:
    def __init__(self, nc, stack):
        self.nc = nc
        self.stack = stack
        self.ops = {e: [] for e in ENGS}
        self.cnt = {e: 0 for e in ENGS}
        self.clock = {e: {} for e in ENGS}
        self.sem = {}
        for e in ["pe", "act", "dve", "pool"]:
            self.sem[e] = stack.enter_context(nc.semaphore("s_" + e))
        self.sem["bar"] = stack.enter_context(nc.semaphore("s_bar"))
        self.bar_n = 0
        self.dma_live = {}
        self.gstack = stack
        self.nsem = 5
        self.final_waits = []
        self.n_wait = 0
        self.uid = 0
        self.dsem_pool = []
        self.dsem_owner = []

    def sbuf(self, name, shape, dtype):
        self.uid += 1
        name = "%s_u%d" % (name, self.uid)
        t = self.stack.enter_context(self.nc.sbuf_tensor(name, list(shape), dtype))
        return T(name, t)

    def psum(self, name, shape, dtype=F32):
        self.uid += 1
        name = "%s_u%d" % (name, self.uid)
        t = self.stack.enter_context(self.nc.psum_tensor(name, list(shape), dtype))
        tt_ = T(name, t)
        tt_.excl = True
        return tt_

    def view(self, name, ap):
        return T(name, ap)

    def _dsem(self, t):
        if t.dsem is None:
            if self.dsem_pool:
                t.dsem, t.dcnt = self.dsem_pool.pop()
            else:
                self.nsem += 1
                t.dsem = self.gstack.enter_context(self.nc.semaphore("dsem%d" % self.nsem))
                t.dcnt = 0
            self.dsem_owner.append(t)
        return t.dsem

    def phase_end(self):
        for t in self.dsem_owner:
            self.dsem_pool.append((t.dsem, t.dcnt))
            t.dsem = None
        self.dsem_owner = []
        self.dma_live = {}

    def _need(self, eng, ev, waits):
        key, val, snap = ev
        if eng == "pe" and key == "pe":
            return
        ck = self.clock[eng]
        if ck.get(key, 0) >= val:
            return
        waits[key] = max(waits.get(key, 0), val)
        ck[key] = val
        if snap:
            for k, v in snap.items():
                if ck.get(k, 0) < v:
                    ck[k] = v

    def op(self, eng, fn, reads=(), writes=(), dma_sem_tile=None):
        waits = {}
        ex = [t for t in reads if t.excl]
        if ex:
            reads = [t for t in reads if not t.excl]
            writes = list(writes) + [t for t in ex if t not in writes]
        for t in reads:
            if t.last_w is not None:
                self._need(eng, t.last_w, waits)
        for t in writes:
            if t.last_w is not None:
                self._need(eng, t.last_w, waits)
            for ev in t.readers:
                self._need(eng, ev, waits)
        if dma_sem_tile is not None:
            st = dma_sem_tile
            sem = self._dsem(st)
            st.dcnt += 16
            key = ("d", st.uid)
            self.sem[key] = sem
            ev = (key, st.dcnt, dict(self.clock[eng]))
            self.dma_live[key] = (st, st.dcnt)
            inc = (sem, 16)
        else:
            self.cnt[eng] += 1
            ev = (eng, self.cnt[eng], None)
            inc = (self.sem[eng], 1)
        self.ops[eng].append((list(waits.items()), fn, inc))
        self.n_wait += len(waits)
        if dma_sem_tile is None:
            snap = dict(self.clock[eng])
            ev = (eng, self.cnt[eng], snap)
        for t in writes:
            t.last_w = ev
            t.readers = []
        for t in reads:
            if t not in writes:
                t.readers.append(ev)
        return ev

    def finish_wait(self, eng, tiles):
        waits = {}
        for t in tiles:
            if t.last_w is not None:
                self._need(eng, t.last_w, waits)
            for ev in t.readers:
                self._need(eng, ev, waits)
        self.ops[eng].append((list(waits.items()), None, None))

    def barrier(self):
        evs = []
        for e in ["pe", "act", "dve", "pool"]:
            if self.cnt[e] > 0:
                evs.append((e, self.cnt[e], None))
        for key, (t, val) in self.dma_live.items():
            evs.append((key, val, None))
        for eng in ENGS:
            waits = {}
            for ev in evs:
                if ev[0] == eng and eng == "pe":
                    continue
                ck = self.clock[eng]
                if ck.get(ev[0], 0) < ev[1]:
                    waits[ev[0]] = ev[1]
                    ck[ev[0]] = ev[1]
            self.ops[eng].append((list(waits.items()), None, None))
        self.bar_n += 1
        for eng in ENGS:
            self.ops[eng].append(([], "barinc", None))
        for eng in ENGS:
            self.ops[eng].append(([("bar", 5 * self.bar_n)], None, None))

    def emit(self):
        nc = self.nc
        with nc.Block() as block:
            def run(eng_name):
                def body(e):
                    for waits, fn, inc in self.ops[eng_name]:
                        for key, val in waits:
                            e.wait_ge(self.sem[key], val)
                        if fn == "barinc":
                            e.sem_inc(self.sem["bar"], 1)
                        elif fn is not None:
                            ins = fn(e)
                            ins.then_inc(inc[0], inc[1])
                return body
            block.tensor(run("pe"))
            block.scalar(run("act"))
            block.vector(run("dve"))
            block.gpsimd(run("pool"))
            block.sync(run("sp"))
        self.ops = {e: [] for e in ENGS}

    def dma(self, out_t, out_ap, in_ap, in_t=None, eng="sp", **kw):
        reads = [in_t] if in_t is not None else []
        writes = [out_t] if out_t is not None else []
        st = out_t if out_t is not None else in_t
        return self.op(eng, lambda e: e.dma_start(out=out_ap, in_=in_ap, **kw), reads, writes,
                       dma_sem_tile=st)

    def mm(self, out_t, out_ap, lhsT, rhs, reads, start=True, stop=True, **kw):
        return self.op("pe", lambda e: e.matmul(out_ap, lhsT, rhs, start=start, stop=stop, **kw),
                       reads, [out_t])

    def transpose(self, out_t, out_ap, in_ap, ident_ap, reads):
        return self.op("pe", lambda e: e.transpose(out_ap, in_ap, ident_ap), reads, [out_t])

    def act(self, out_t, out_ap, in_ap, func, reads, bias=None, scale=None, accum_out=None,
            extra_writes=()):
        kw = {}
        if bias is not None:
            kw["bias"] = bias
        if scale is not None:
            kw["scale"] = scale
        if accum_out is not None:
            kw["accum_out"] = accum_out
        return self.op("act", lambda e: e.activation(out_ap, in_ap, func, **kw), reads,
                       [out_t] + list(extra_writes))

    def tt(self, eng, out_t, out_ap, in0, in1, op, reads):
        return self.op(eng, lambda e: e.tensor_tensor(out_ap, in0, in1, op), reads, [out_t])

    def ts(self, eng, out_t, out_ap, in0, s1, s2, op0, op1=None, reads=(), accum_out=None,
           extra_writes=()):
        def f(e):
            kw = {}
            if accum_out is not None:
                kw["accum_out"] = accum_out
            if op1 is None:
                return e.tensor_scalar(out_ap, in0, s1, None, op0, **kw)
            return e.tensor_scalar(out_ap, in0, s1, s2, op0, op1, **kw)
        return self.op(eng, f, reads, [out_t] + list(extra_writes))

    def stt(self, eng, out_t, out_ap, in0, scalar, in1, op0, op1, reads):
        eng = "dve"
        return self.op(eng, lambda e: e.scalar_tensor_tensor(out_ap, in0, scalar, in1, op0, op1),
                       reads, [out_t])

    def copy(self, eng, out_t, out_ap, in_ap, reads):
        if eng == "act":
            return self.op("act", lambda e: e.copy(out_ap, in_ap), reads, [out_t])
        return self.op(eng, lambda e: e.tensor_copy(out_ap, in_ap), reads, [out_t])

    def memset(self, eng, out_t, out_ap, val):
        return self.op(eng, lambda e: e.memset(out_ap, val), [], [out_t])

from concourse.bass_utils import run_bass_kernel_spmd

D = 1024
NH_HG = 8
DFF = 2816
NFC = DFF // 128
EPS = 1e-6
NEG = -30000.0


def bcast_rows(ap_row, n):
    return ap_row.partition_broadcast(n)


class K:
    pass


def load_weight_bf16(S, Wb, w_dram, KC, N, stg, grow=None, col_off=0, rowscale=None, kc_off=0):
    i = 0
    for kc in range(KC):
        for n0 in range(0, N, 1024):
            n1 = min(N, n0 + 1024)
            st = stg[i % len(stg)]
            S.dma(st, st[:, 0:n1 - n0], w_dram[kc * 128:(kc + 1) * 128, n0:n1])
            eng = "dve"
            o = Wb[:, kc_off + kc, col_off + n0:col_off + n1]
            if grow is None:
                S.copy("act" if i % 2 == 0 else "dve", Wb, o, st[:, 0:n1 - n0], [st])
            elif rowscale is None:
                S.tt(eng, Wb, o, st[:, 0:n1 - n0], grow[:, n0:n1], ALU.mult, [st, grow])
            else:
                S.stt(eng, Wb, o, st[:, 0:n1 - n0], rowscale[:, kc:kc + 1], grow[:, n0:n1],
                      ALU.mult, ALU.mult, [st, grow, rowscale])
            i += 1


def rstd_from_ss(S, rstd, ss, n, width):
    S.act(rstd, rstd[:, 0:width], ss[:, 0:width], AF.Ln, [ss], scale=1.0 / n, bias=EPS)
    S.act(rstd, rstd[:, 0:width], rstd[:, 0:width], AF.Exp, [rstd], scale=-0.5)


def norm_to_hT(S, k, xt_t, xt_ap, gcol, shcol, hT, col0, xn, sq, ss, rstd, pT):
    S.act(sq, sq[:], xt_ap, AF.Square, [xt_t], accum_out=ss[:, 0:1], extra_writes=[ss])
    rstd_from_ss(S, rstd, ss, D, 1)
    S.act(xn, xn[:], xt_ap, AF.Copy, [xt_t, rstd], scale=rstd[:, 0:1])
    for kc in range(8):
        S.transpose(pT, pT[:, kc * 128:(kc + 1) * 128], xn[:, kc * 128:(kc + 1) * 128],
                    k.ident[:], [xn, k.ident])
    for kc in range(8):
        eng = "dve" if kc % 2 == 0 else "pool"
        eng = "dve"
        S.ts(eng, hT, hT[:, kc, col0:col0 + 128], pT[:, kc * 128:(kc + 1) * 128],
             gcol[:, kc:kc + 1], shcol[:, kc:kc + 1], ALU.mult, ALU.add, reads=[pT, gcol, shcol])


def rows_to_cols(S, k, rows, name):
    n = sum(r.shape[0] // 128 for r in rows)
    assert n <= 128
    rt = S.sbuf(name + "_r", [n, 128], F32)
    ct = S.sbuf(name, [128, n], F32)
    j = 0
    for r in rows:
        m = r.shape[0] // 128
        S.dma(rt, rt[j:j + m, :], r.rearrange("(j p) -> j p", p=128))
        j += m
    ps = k.ps[1]
    S.mm(ps, ps[:, 0:n], rt[0:n, :], k.identf[0:n, 0:n], [rt, k.identf])
    S.copy("dve", ct, ct[:], ps[:, 0:n], [ps])
    return ct


def phase_prologue(S, k):
    cc = S.sbuf("cc", [128, 8], F32)
    ca = S.sbuf("ca", [128, 8], F32)
    S.dma(cc, cc[:], k.c_col[:, :])
    S.act(ca, ca[:], cc[:], AF.Silu, [cc])
    wst = [S.sbuf("adw%d" % i, [128, 8, 512], F32) for i in range(2)]
    brow = S.sbuf("brow", [1, 6144], F32)
    mrow = S.sbuf("mrow", [1, 6144], F32)
    i = 0
    for l in range(2):
        S.dma(brow, brow[:], k.ada_b[l:l + 1, :])
        for nt in range(12):
            wt = wst[i % 2]
            S.dma(wt, wt[:], k.ada_w[l].rearrange("(kc p) n -> p kc n", p=128)[:, :, nt * 512:(nt + 1) * 512])
            ps = k.ps[2 + (i % 2)]
            for kc in range(8):
                S.mm(ps, ps[0:1, :], ca[:, kc:kc + 1], wt[:, kc, :], [ca, wt],
                     start=(kc == 0), stop=(kc == 7))
            S.tt("dve", mrow, mrow[0:1, nt * 512:(nt + 1) * 512], ps[0:1, :],
                 brow[0:1, nt * 512:(nt + 1) * 512], ALU.add, [ps, brow])
            i += 1
        S.dma(None, k.modd[l:l + 1, :], mrow[:], in_t=mrow)
    S.finish_wait("sp", [mrow])


def load_consts(S, k):
    k.identf = S.sbuf("identf", [128, 128], F32)
    k.ident = S.sbuf("ident", [128, 128], BF16)
    S.dma(k.identf, k.identf[:], k.c_ident[:, :])
    S.copy("dve", k.ident, k.ident[:], k.identf[:], [k.identf])


def alloc_psum(S, k, bf7=False):
    k.ps = [S.psum("psb0", [128, 1024], BF16)] + [S.psum("ps%d" % i, [128, 512], F32) for i in range(1, 7)]
    if bf7:
        k.ps.append(S.psum("psb7", [128, 1024], BF16))
    else:
        k.ps.append(S.psum("ps7", [128, 512], F32))


def interleave(ga, gb, ra=1, rb=1):
    da = db = False
    while not (da and db):
        for _ in range(ra):
            if not da:
                try:
                    next(ga)
                except StopIteration:
                    da = True
        for _ in range(rb):
            if not db:
                try:
                    next(gb)
                except StopIteration:
                    db = True


def phase_hgrn(S, k, l, xin, xout, NT):
    TT = 256
    ntile = NT // TT
    load_consts(S, k)
    alloc_psum(S, k, bf7=True)
    cols = rows_to_cols(S, k, [k.modd[l, :], k.norm_mix[l, :], k.hg_lb[0, :], k.hg_lb[1, :],
                               k.hg_gnorm[0, :]], "hcols")
    gnc = S.sbuf("gnc", [128, 8], F32)
    S.copy("dve", gnc, gnc[:], cols[:, 72:73].to_broadcast([128, 8]), [cols])
    gcol = S.sbuf("gcol", [128, 8], F32)
    S.stt("dve", gcol, gcol[:], cols[:, 8:16], 1.0, cols[:, 48:56], ALU.add, ALU.mult, [cols])
    lbc = S.sbuf("lbc", [128, 8], F32)
    l1m = S.sbuf("l1m", [128, 8], F32)
    S.tt("dve", lbc, lbc[:], cols[:, 64:72], cols[:, 56:64], ALU.subtract, [cols])
    S.act(lbc, lbc[:], lbc[:], AF.Exp, [lbc])
    S.ts("dve", lbc, lbc[:], lbc[:], 1.0, None, ALU.add, reads=[lbc])
    S.op("dve", lambda e: e.reciprocal(lbc[:], lbc[:]), [lbc], [lbc])
    S.ts("dve", l1m, l1m[:], lbc[:], -1.0, 1.0, ALU.mult, ALU.add, reads=[lbc])
    S.act(l1m, l1m[:], l1m[:], AF.Ln, [l1m])
    sq = S.sbuf("sq", [128, 1024], F32)
    g1row = sq
    S.dma(g1row, g1row[:], k.modd[l, 2048:3072].partition_broadcast(128))
    Win = S.sbuf("Win", [128, 8, 4096], BF16)
    Wout = S.sbuf("Wout", [128, 8, 1024], BF16)
    otok = S.sbuf("otok", [128, 1024], F32)
    xo = [S.sbuf("xo%d" % i, [128, 1024], F32) for i in range(2)]
    stg = [otok, xo[0]]
    load_weight_bf16(S, Win, k.hg_w_in[0], 8, 4096, stg)
    load_weight_bf16(S, Wout, k.hg_w_out[0], 8, 1024, stg, grow=g1row, rowscale=gnc)
    rmask = S.sbuf("rmask", [128, TT], F32)
    S.memset("pool", rmask, rmask[:], 1.0)
    S.memset("pool", rmask, rmask[:].rearrange("p (c j) -> p c j", j=64)[:, :, 0:1], 0.0)
    cmask = S.sbuf("cmask", [128, 128], F32)
    S.dma(cmask, cmask[:], k.c_bdmask[:, :])
    st32s = [S.sbuf("st32_%d" % i, [128, 128], F32) for i in range(8)]
    stbs = [S.sbuf("stb_%d" % i, [128, 128], BF16) for i in range(8)]
    for i in range(8):
        S.memset("pool", st32s[i], st32s[i][:], 0.0)
        S.memset("pool", stbs[i], stbs[i][:], 0.0)
    sts = [S.sbuf("sts%d" % i, [128, 128], F32) for i in range(8)]
    sts2 = sts
    xts = [S.sbuf("xt%d" % i, [128, 1024], F32) for i in range(4)]
    xns = [S.sbuf("xn%d" % i, [128, 1024], BF16) for i in range(2)]
    sss = [S.sbuf("ss%d" % i, [128, 2], F32) for i in range(2)]
    rss = [S.sbuf("rs%d" % i, [128, 2], F32) for i in range(2)]
    hT = S.sbuf("hT", [128, 8, TT], BF16)
    NTMP = 2
    tu = [S.sbuf("tu%d" % i, [128, TT], F32) for i in range(NTMP)]
    tA = [S.sbuf("tA%d" % i, [128, TT], F32) for i in range(NTMP)]
    tB = [S.sbuf("tB%d" % i, [128, TT], F32) for i in range(NTMP)]
    tb = [S.sbuf("tb%d" % i, [128, TT], F32) for i in range(NTMP)]
    teb = [S.sbuf("teb%d" % i, [128, TT], F32) for i in range(NTMP)]
    t1 = [S.sbuf("t1%d" % i, [128, TT], F32) for i in range(NTMP)]
    qdT2 = [S.sbuf("qdT%d" % i, [128, 8, 2, 2, 128], BF16) for i in range(2)]
    kdT2 = [S.sbuf("kdT%d" % i, [128, 8, TT], BF16) for i in range(2)]
    kdtok2 = [S.sbuf("kdtok%d" % i, [128, 2, 8, 128], BF16) for i in range(2)]
    ebl2 = [S.sbuf("ebl%d" % i, [128, 8, 4], F32) for i in range(2)]
    vt2 = [S.sbuf("vt%d" % i, [128, 2, 1024], BF16) for i in range(2)]
    gs2 = [S.sbuf("gs%d" % i, [128, 2, 1024], BF16) for i in range(2)]
    oss = S.sbuf("oss", [128, 8], F32)
    orstd = S.sbuf("orstd", [128, 8], F32)
    on = S.sbuf("on", [128, 1024], BF16)
    oT = S.sbuf("oT", [128, 8, 128], BF16)
    sc4s = [S.sbuf("sc4_%d" % i, [128, 512], BF16) for i in range(2)]
    st1 = [S.sbuf("st1_%d" % i, [128, 128], F32) for i in range(8)]
    st1b = [S.sbuf("st1b_%d" % i, [128, 128], BF16) for i in range(8)]
    for q in qdT2:
        S.memset("pool", q, q[:], 0.0)
    pT, pT2 = k.ps[0], k.ps[7]
    pq = [k.ps[1], k.ps[2]]
    pvs = [k.ps[1], k.ps[2]]
    tq = [S.sbuf("tq%d" % i, [128, TT], F32) for i in range(NTMP)]

    def gen_A(T0):
        par = T0 % 2
        qdT, kdT, kdtok, ebl, vt, gs = qdT2[par], kdT2[par], kdtok2[par], ebl2[par], vt2[par], gs2[par]
        r0 = T0 * TT
        for sub in range(2):
            xt = xts[(T0 * 2 + sub) % 4]
            S.dma(xt, xt[:], xin[r0 + sub * 128:r0 + (sub + 1) * 128, :])
            S.act(sq, sq[:], xt[:], AF.Square, [xt], accum_out=sss[sub][:, 0:1], extra_writes=[sss[sub]])
            rstd_from_ss(S, rss[sub], sss[sub], D, 1)
            S.act(xns[sub], xns[sub][:], xt[:], AF.Copy, [xt, rss[sub]], scale=rss[sub][:, 0:1])
            yield
        for sub in range(2):
            for kc in range(8):
                S.transpose(pT, pT[:, kc * 128:(kc + 1) * 128], xns[sub][:, kc * 128:(kc + 1) * 128],
                            k.ident[:], [xns[sub], k.ident])
            for kc in range(8):
                S.act(hT, hT[:, kc, sub * 128:(sub + 1) * 128], pT[:, kc * 128:(kc + 1) * 128], AF.Identity,
                      [pT, gcol, cols], scale=gcol[:, kc:kc + 1], bias=cols[:, kc:kc + 1])
            yield
        def s1(h):
            i = h % NTMP
            pq_ = pq[h % 2]
            for kc in range(8):
                S.mm(pq_, pq_[:, 0:TT], Win[:, kc, h * 128:(h + 1) * 128], hT[:, kc, :], [Win, hT],
                     start=(kc == 0), stop=(kc == 7))
            for kc in range(8):
                S.mm(pq_, pq_[:, TT:2 * TT], Win[:, kc, 1024 + h * 128:1024 + (h + 1) * 128], hT[:, kc, :],
                     [Win, hT], start=(kc == 0), stop=(kc == 7))
            z = pq_[:, TT:2 * TT]
            S.act(tu[i], tu[i][:], z, AF.Exp, [pq_], scale=-1.0)
            S.act(tA[i], tA[i][:], tu[i][:], AF.Ln, [tu[i]], bias=1.0)
            S.act(tB[i], tB[i][:], tu[i][:], AF.Ln, [tu[i], lbc], bias=1.0, scale=lbc[:, h:h + 1])
            S.copy("dve", tq[i], tq[i][:], pq_[:, 0:TT], [pq_])
            S.tt("dve", t1[i], t1[i][:], z, tA[i][:], ALU.add, [pq_, tA[i]])

        def s2(h):
            i = h % NTMP
            pq_ = pq[h % 2]
            z = pq_[:, TT:2 * TT]
            S.tt("pool", tB[i], tB[i][:], tB[i][:], tA[i][:], ALU.subtract, [tB[i], tA[i]])
            S.op("dve", lambda e, o=tb[i], m=tB[i]: e.tensor_tensor_scan(o[:], rmask[:], m[:], 0.0, ALU.mult, ALU.add),
                 [rmask, tB[i]], [tb[i]])
            S.act(teb[i], teb[i][:], tb[i][:], AF.Exp, [tb[i]])
            S.tt("pool", t1[i], t1[i][:], t1[i][:], tb[i][:], ALU.add, [t1[i], tb[i]])

        def s3(h):
            i = h % NTMP
            pq_ = pq[h % 2]
            for sub in range(2):
                for c2 in range(2):
                    cs = sub * 128 + c2 * 64
                    S.tt("dve", qdT, qdT[:, h, sub, c2, c2 * 64:(c2 + 1) * 64], tq[i][:, cs:cs + 64],
                         teb[i][:, cs:cs + 64], ALU.mult, [tq[i], teb[i]])
            S.act(kdT, kdT[:, h, :], t1[i][:], AF.Exp, [t1[i], l1m], scale=-1.0, bias=l1m[:, h:h + 1])
            S.copy("pool", ebl, ebl[:, h, :], teb[i][:].rearrange("p (c j) -> p c j", j=64)[:, :, 63], [teb[i]])

        for it in range(10):
            if 0 <= it - 2 < 8:
                s3(it - 2)
            if 0 <= it - 1 < 8:
                s2(it - 1)
            if it < 8:
                s1(it)
            yield
        for sub in range(2):
            for half in range(4):
                c0 = 2048 + half * 512
                pv = pvs[half % 2]
                for kc in range(8):
                    S.mm(pv, pv[:], hT[:, kc, sub * 128:(sub + 1) * 128], Win[:, kc, c0:c0 + 512],
                         [hT, Win], start=(kc == 0), stop=(kc == 7))
                if half < 2:
                    S.copy("dve", vt, vt[:, sub, half * 512:(half + 1) * 512], pv[:], [pv])
                else:
                    S.act(gs, gs[:, sub, (half - 2) * 512:(half - 1) * 512], pv[:], AF.Silu, [pv])
                yield
        for sub in range(2):
            for h in range(8):
                S.transpose(pT, pT[:, h * 128:(h + 1) * 128], kdT[:, h, sub * 128:(sub + 1) * 128],
                            k.ident[:], [kdT, k.ident])
            S.copy("act", kdtok, kdtok[:, sub, :, :].rearrange("p h k -> p (h k)"), pT[:], [pT])
            yield

    def gen_B(T0):
        par = T0 % 2
        qdT, kdT, kdtok, ebl, vt, gs = qdT2[par], kdT2[par], kdtok2[par], ebl2[par], vt2[par], gs2[par]
        r0 = T0 * TT
        for sub in range(2):
            banks = [(k.ps[3], k.ps[4], k.ps[3]), (k.ps[5], k.ps[6], k.ps[5])]
            for hg in range(2):
                pA, pB, pC = banks[hg]
                hs = list(range(hg * 4, hg * 4 + 4))
                for i, h in enumerate(hs):
                    S.mm(pA, pA[:, i * 128:i * 128 + 64], kdT[:, h, sub * 128:(sub + 1) * 128],
                         qdT[:, h, sub, 0, 0:64], [kdT, qdT])
                    S.mm(pA, pA[:, i * 128 + 64:(i + 1) * 128], kdT[:, h, sub * 128:(sub + 1) * 128],
                         qdT[:, h, sub, 1, 64:128], [kdT, qdT])
                    S.mm(pB, pB[:, i * 128:(i + 1) * 128], kdtok[0:64, sub, h, :],
                         vt[0:64, sub, h * 128:(h + 1) * 128], [kdtok, vt])
                S.tt("dve", sc4s[hg], sc4s[hg][:].rearrange("p (i t) -> p i t", t=128),
                     pA[:].rearrange("p (i t) -> p i t", t=128),
                     cmask[:].unsqueeze(1).to_broadcast([128, 4, 128]), ALU.mult, [pA, cmask])
            yield
            for hg in range(2):
                pA, pB, pC = banks[hg]
                hs = list(range(hg * 4, hg * 4 + 4))
                c = sub * 2
                for i, h in enumerate(hs):
                    S.act(sts[h], sts[h][:], st32s[h][:], AF.Copy, [st32s[h], ebl], scale=ebl[:, h, c:c + 1])
                for i, h in enumerate(hs):
                    S.stt("dve", st1b[h], st1b[h][:], pB[:, i * 128:(i + 1) * 128], ebl[:, h, c:c + 1],
                          sts[h][:], ALU.mult, ALU.add, [pB, ebl, sts[h]])
                for i, h in enumerate(hs):
                    S.stt("dve", st1[h], st1[h][:], pB[:, i * 128:(i + 1) * 128], ebl[:, h, c:c + 1],
                          sts[h][:], ALU.mult, ALU.add, [pB, ebl, sts[h]])
            yield
            for hg in range(2):
                pA, pB, pC = banks[hg]
                hs = list(range(hg * 4, hg * 4 + 4))
                for i, h in enumerate(hs):
                    o_ap = pC[:, i * 128:(i + 1) * 128]
                    S.mm(pC, o_ap, sc4s[hg][:, i * 128:(i + 1) * 128], vt[:, sub, h * 128:(h + 1) * 128],
                         [sc4s[hg], vt], start=True, stop=False)
                    S.mm(pC, o_ap, qdT[:, h, sub, 0, :], stbs[h][:], [qdT, stbs[h]], start=False, stop=False)
                    S.mm(pC, o_ap, qdT[:, h, sub, 1, :], st1b[h][:], [qdT, st1b[h]], start=False, stop=True)
                for i, h in enumerate(hs):
                    S.mm(pB, pB[:, i * 128:(i + 1) * 128], kdtok[64:128, sub, h, :],
                         vt[64:128, sub, h * 128:(h + 1) * 128], [kdtok, vt])
            yield
            for hg in range(2):
                pA, pB, pC = banks[hg]
                hs = list(range(hg * 4, hg * 4 + 4))
                c = sub * 2 + 1
                for i, h in enumerate(hs):
                    S.act(sts2[h], sts2[h][:], st1[h][:], AF.Copy, [st1[h], ebl], scale=ebl[:, h, c:c + 1])
                for i, h in enumerate(hs):
                    S.stt("dve", stbs[h], stbs[h][:], pB[:, i * 128:(i + 1) * 128], ebl[:, h, c:c + 1],
                          sts2[h][:], ALU.mult, ALU.add, [pB, ebl, sts2[h]])
                for i, h in enumerate(hs):
                    S.stt("dve", st32s[h], st32s[h][:], pB[:, i * 128:(i + 1) * 128], ebl[:, h, c:c + 1],
                          sts2[h][:], ALU.mult, ALU.add, [pB, ebl, sts2[h]])
                S.copy("dve", otok, otok[:, hg * 512:(hg + 1) * 512], pC[:], [pC])
                for i, h in enumerate(hs):
                    S.act(sq, sq[:, 0:128], otok[:, h * 128:(h + 1) * 128], AF.Square, [otok],
                          accum_out=oss[:, h:h + 1], extra_writes=[oss])
            yield
            rstd_from_ss(S, orstd, oss, 128, 8)
            S.tt("dve", otok, otok[:].rearrange("p (h v) -> p h v", v=128),
                 otok[:].rearrange("p (h v) -> p h v", v=128),
                 orstd[:].unsqueeze(2).to_broadcast([128, 8, 128]), ALU.mult, [otok, orstd])
            S.tt("pool", on, on[:], otok[:], gs[:, sub, :], ALU.mult, [otok, gs])
            yield
            for kc in range(8):
                S.transpose(pT2, pT2[:, kc * 128:(kc + 1) * 128], on[:, kc * 128:(kc + 1) * 128],
                            k.ident[:], [on, k.ident])
            S.copy("act", oT, oT[:].rearrange("p c t -> p (c t)"), pT2[:], [pT2])
            yield
            xt = xts[(T0 * 2 + sub) % 4]
            xo_ = xo[sub]
            for half in range(2):
                py = pvs[half % 2]
                for kc in range(8):
                    S.mm(py, py[:], oT[:, kc, :], Wout[:, kc, half * 512:(half + 1) * 512],
                         [oT, Wout], start=(kc == 0), stop=(kc == 7))
                S.tt("dve", xo_, xo_[:, half * 512:(half + 1) * 512], py[:], xt[:, half * 512:(half + 1) * 512],
                     ALU.add, [py, xt])
            S.dma(None, xout[r0 + sub * 128:r0 + (sub + 1) * 128, :], xo_[:], in_t=xo_)
            yield

    for _ in gen_A(0):
        pass
    for T0 in range(ntile):
        if T0 + 1 < ntile:
            interleave(gen_B(T0), gen_A(T0 + 1), 2, 3)
        else:
            for _ in gen_B(T0):
                pass
    S.finish_wait("sp", xo)


def phase_ffn(S, k, l, xnorm, xres, xout, NT, fc0, fc1, final=False):
    TT = 512
    ntile = NT // TT
    nfc = fc1 - fc0
    W = nfc * 128
    same = xres is xnorm
    load_consts(S, k)
    alloc_psum(S, k, bf7=True)
    cols = rows_to_cols(S, k, [k.modd[l, :], k.norm_ffn[l, :]], "fcols")
    gcol = S.sbuf("gcol", [128, 8], F32)
    S.stt("dve", gcol, gcol[:], cols[:, 32:40], 1.0, cols[:, 48:56], ALU.add, ALU.mult, [cols])
    shc = S.sbuf("shc", [128, 8], F32)
    S.copy("dve", shc, shc[:], cols[:, 24:32], [cols])
    ccols = rows_to_cols(S, k, [k.ffn_conv_w[l, 0, :], k.ffn_conv_w[l, 1, :], k.ffn_conv_w[l, 2, :],
                                k.ffn_conv_b[l, :]], "ccols")
    sq = S.sbuf("sq", [128, 1024], F32)
    g2row = sq
    S.dma(g2row, g2row[:], k.modd[l, 5120:6144].partition_broadcast(128))
    if final:
        fnrow = S.sbuf("fnrow", [128, 1024], F32)
        S.dma(fnrow, fnrow[:], k.final_norm[:].partition_broadcast(128))
    Wup = S.sbuf("Wup", [128, 8, 2 * W], BF16)
    Wdn = S.sbuf("Wdn", [128, nfc, 1024], BF16)
    stg = [S.sbuf("stg%d" % i, [128, 1024], F32) for i in range(2)]
    load_weight_bf16(S, Wup, k.ffn_w_up[l][:, fc0 * 128:fc1 * 128], 8, W, stg)
    load_weight_bf16(S, Wup, k.ffn_w_up[l][:, DFF + fc0 * 128:DFF + fc1 * 128], 8, W, stg, col_off=W)
    load_weight_bf16(S, Wdn, k.ffn_w_down[l][fc0 * 128:fc1 * 128, :], nfc, 1024, stg, grow=g2row)
    xts = [S.sbuf("xt%d" % i, [128, 1024], F32) for i in range(2)]
    xrs = [S.sbuf("xr%d" % i, [128, 1024], F32) for i in range(2)]
    xn = S.sbuf("xn", [128, 1024], BF16)
    ss = S.sbuf("ss", [128, 8], F32)
    rstd = S.sbuf("rstd", [128, 8], F32)
    hTs = [S.sbuf("hT%d" % i, [128, 8, TT], BF16) for i in range(2)]
    halo = S.sbuf("halo", [128, nfc, 2], F32)
    S.memset("pool", halo, halo[:], 0.0)
    abuf = [S.sbuf("abuf%d" % i, [128, TT + 2], F32) for i in range(3)]
    c1 = [S.sbuf("c1_%d" % i, [128, TT], F32) for i in range(3)]
    c2 = [S.sbuf("c2_%d" % i, [128, TT], F32) for i in range(3)]
    mTs = [S.sbuf("mT%d" % i, [128, nfc, TT], BF16) for i in range(2)]
    xo = [S.sbuf("xo%d" % i, [128, 1024], F32) for i in range(2)]
    pT = k.ps[0]
    ncnt = [0]

    xns = [xn] + [S.sbuf("xn%d" % i, [128, 1024], BF16) for i in range(1, 4)]
    sss = [S.sbuf("ss%d" % i, [128, 2], F32) for i in range(4)]
    rss = [S.sbuf("rs%d" % i, [128, 2], F32) for i in range(4)]
    pTs = [k.ps[0], k.ps[7]]

    def norm_a(T0, sub):
        i = sub
        xt = xts[sub % 2]
        r = T0 * TT + sub * 128
        S.dma(xt, xt[:], xnorm[r:r + 128, :])
        S.act(sq, sq[:], xt[:], AF.Square, [xt], accum_out=sss[i][:, 0:1], extra_writes=[sss[i]])
        rstd_from_ss(S, rss[i], sss[i], D, 1)
        S.act(xns[i], xns[i][:], xt[:], AF.Copy, [xt, rss[i]], scale=rss[i][:, 0:1])

    def norm_b(T0, sub):
        i = sub
        hT_ = hTs[T0 % 2]
        pT_ = pTs[sub % 2]
        for kc in range(8):
            S.transpose(pT_, pT_[:, kc * 128:(kc + 1) * 128], xns[i][:, kc * 128:(kc + 1) * 128],
                        k.ident[:], [xns[i], k.ident])
        for kc in range(8):
            S.act(hT_, hT_[:, kc, sub * 128:(sub + 1) * 128], pT_[:, kc * 128:(kc + 1) * 128], AF.Identity,
                  [pT_, gcol, shc], scale=gcol[:, kc:kc + 1], bias=shc[:, kc:kc + 1])

    for sub in range(4):
        norm_a(0, sub)
        norm_b(0, sub)
    def down_proj(T0):
        r0 = T0 * TT
        mT = mTs[T0 % 2]
        for sub in range(4):
            xt = xrs[sub % 2]
            S.dma(xt, xt[:], xres[r0 + sub * 128:r0 + (sub + 1) * 128, :])
            xo_ = xo[sub % 2]
            for half in range(2):
                py = k.ps[1 + half]
                for j in range(nfc):
                    S.mm(py, py[:], mT[:, j, sub * 128:(sub + 1) * 128], Wdn[:, j, half * 512:(half + 1) * 512],
                         [mT, Wdn], start=(j == 0), stop=(j == nfc - 1))
                S.tt("dve", xo_, xo_[:, half * 512:(half + 1) * 512], py[:], xt[:, half * 512:(half + 1) * 512],
                     ALU.add, [py, xt])
            if final:
                S.act(sq, sq[:], xo_[:], AF.Square, [xo_], accum_out=ss[:, 1:2], extra_writes=[ss])
                rstd_from_ss(S, rstd, ss[:, 1:2] if False else ss, D, 2)
                S.stt("dve", xo_, xo_[:], xo_[:], rstd[:, 1:2], fnrow[:], ALU.mult, ALU.mult, [xo_, rstd, fnrow])
            S.dma(None, xout[r0 + sub * 128:r0 + (sub + 1) * 128, :], xo_[:], in_t=xo_)

    for T0 in range(ntile):
        r0 = T0 * TT
        hT = hTs[T0 % 2]
        mT = mTs[T0 % 2]
        def st_A(j):
            ab = abuf[j % 3]
            pa = k.ps[1 + (j % 2)]
            fc = fc0 + j
            S.copy("pool", ab, ab[:, 0:2], halo[:, j, :], [halo])
            S.copy("act", ab, ab[:, 2:TT + 2], pa[:], [pa])
            S.copy("pool", halo, halo[:, j, :], ab[:, TT:TT + 2], [ab])
            S.ts("pool", c1[j % 3], c1[j % 3][:], ab[:, 2:TT + 2], ccols[:, 44 + fc:45 + fc],
                 ccols[:, 66 + fc:67 + fc], ALU.mult, ALU.add, reads=[ab, ccols])

        def st_B(j):
            ab = abuf[j % 3]
            fc = fc0 + j
            c1_, c2_ = c1[j % 3], c2[j % 3]
            S.stt("dve", c2_, c2_[:], ab[:, 1:TT + 1], ccols[:, 22 + fc:23 + fc], c1_[:], ALU.mult, ALU.add,
                  [ab, ccols, c1_])
            S.stt("dve", c1_, c1_[:], ab[:, 0:TT], ccols[:, fc:fc + 1], c2_[:], ALU.mult, ALU.add,
                  [ab, ccols, c2_])

        def st_C(j):
            S.act(c2[j % 3], c2[j % 3][:], c1[j % 3][:], AF.Silu, [c1[j % 3]])

        def st_D(j):
            pv = k.ps[3 + (j % 4)]
            S.tt("dve", mT, mT[:, j, :], pv[:], c2[j % 3][:], ALU.mult, [pv, c2[j % 3]])

        for j in range(nfc + 2):
            if j < nfc:
                pa = k.ps[1 + (j % 2)]
                pv = k.ps[3 + (j % 4)]
                for kc in range(8):
                    S.mm(pa, pa[:], Wup[:, kc, j * 128:(j + 1) * 128], hT[:, kc, :], [Wup, hT],
                         start=(kc == 0), stop=(kc == 7))
                for kc in range(8):
                    S.mm(pv, pv[:], Wup[:, kc, W + j * 128:W + (j + 1) * 128], hT[:, kc, :], [Wup, hT],
                         start=(kc == 0), stop=(kc == 7))
                st_A(j)
            if 0 <= j - 1 < nfc:
                st_B(j - 1)
                st_C(j - 1)
            if 0 <= j - 2 < nfc:
                st_D(j - 2)
            if T0 + 1 < ntile:
                if j == 0:
                    for sub_ in range(4):
                        norm_a(T0 + 1, sub_)
                if j in (3, 5, 7, 9):
                    norm_b(T0 + 1, (j - 3) // 2)
            if j == 1 and T0 > 0:
                down_proj(T0 - 1)
    down_proj(ntile - 1)
    S.finish_wait("sp", xo)


def build_program(NT, phases=("pro", "hg", "ffn0", "nsa", "ffn1")):
    nc = bass.Bass("TRN2", target_bir_lowering=False)
    k = K()

    def inp(name, shape):
        return nc.dram_tensor(name, list(shape), F32, kind="ExternalInput").ap()

    k.x = inp("x", [NT, D])
    k.c_col = inp("c_col", [128, 8])
    k.ada_w = inp("ada_w", [2, D, 6 * D])
    k.ada_b = inp("ada_b", [2, 6 * D])
    k.norm_mix = inp("norm_mix", [2, D])
    k.norm_ffn = inp("norm_ffn", [2, D])
    k.final_norm = inp("final_norm", [D])
    k.hg_w_in = inp("hg_w_in", [1, D, 4096])
    k.hg_w_out = inp("hg_w_out", [1, D, D])
    k.hg_gnorm = inp("hg_gnorm", [1, 128])
    k.hg_lb = inp("hg_lb", [2, D])
    k.ffn_w_up = inp("ffn_w_up", [2, D, 2 * DFF])
    k.ffn_conv_w = inp("ffn_conv_w", [2, 3, DFF])
    k.ffn_conv_b = inp("ffn_conv_b", [2, DFF])
    k.ffn_w_down = inp("ffn_w_down", [2, DFF, D])
    k.c_ident = inp("c_ident", [128, 128])
    k.c_bdmask = inp("c_bdmask", [128, 128])
    nsa_declare(nc, k, NT)
    k.out = nc.dram_tensor("out", [NT, D], F32, kind="ExternalOutput").ap()
    k.modd = nc.dram_tensor("modd", [2, 6 * D], F32).ap()
    bufs = [nc.dram_tensor("xs%d" % i, [NT, D], F32).ap() for i in range(4)]
    xc = nc.dram_tensor("xc", [NT, D], F32).ap()
    main = [p for p in phases if p != "pro"]
    with contextlib.ExitStack() as gst:
        S = Sched(nc, gst)
        k.S = S
        plist = []
        if "pro" in phases:
            plist.append(lambda: (alloc_psum(S, k), phase_prologue(S, k)))
        src = k.x
        for i, p in enumerate(main):
            last = (i == len(main) - 1)
            dst = k.out if last else bufs[i]
            if p == "hg":
                plist.append(lambda src=src, dst=dst: phase_hgrn(S, k, 0, src, dst, NT))
            elif p in ("ffn0", "ffn1"):
                l = int(p[3])
                fin = (p == "ffn1")
                plist.append(lambda src=src, l=l: phase_ffn(S, k, l, src, src, xc, NT, 0, 11))
                plist.append(lambda src=src, dst=dst, l=l, fin=fin: phase_ffn(S, k, l, src, xc, dst, NT, 11, 22, final=fin))
            elif p == "nsa":
                import os
                stop = int(os.environ.get("NSA_STOP", "4"))
                plist.append(lambda src=src: phase_nsa_proj(S, k, 1, src, NT))
                if stop >= 2:
                    plist.append(lambda: phase_nsa_cmp(S, k, NT))
                if stop >= 3:
                    plist.append(lambda: phase_nsa_attn(S, k, NT))
                if stop >= 4:
                    plist.append(lambda src=src, dst=dst: phase_nsa_out(S, k, 1, src, dst, NT))
            src = dst
        for i, p in enumerate(plist):
            with contextlib.ExitStack() as st:
                S.stack = st
                p()
                S.barrier()
                S.emit()
                S.phase_end()
    return nc


def host_consts():
    ident = np.eye(128, dtype=np.float32)
    s = np.arange(128)[:, None]
    t = np.arange(128)[None, :]
    bd = ((s // 64 == t // 64) & (s <= t)).astype(np.float32)
    return {"c_ident": ident, "c_bdmask": bd}


NSA_W = 2608
SLOPES = [2.0 ** (-8.0 * (h + 1) / 16) for h in range(16)]


def nsa_dims(NT):
    ncb = (NT - 32) // 16 + 1
    nsb = NT // 64
    return ncb, nsb


def nsa_host_consts(NT):
    import ml_dtypes
    bf = ml_dtypes.bfloat16
    ncb, nsb = nsa_dims(NT)
    t = np.arange(NT)
    c = {}
    c["c_vrow"] = np.stack([-SLOPES[h] * t for h in range(16)]).astype(bf)
    KT = NT // 128
    i = np.arange(128)
    sb = np.zeros((128, KT * 16), np.float32)
    for kt in range(KT):
        for h in range(16):
            sb[:, kt * 16 + h] = SLOPES[h] * (128 * kt + i)
    c["c_sbias"] = sb
    cb = np.zeros((128, 2 * 16), np.float32)
    for bt in range(2):
        for h in range(16):
            cb[:, bt * 16 + h] = SLOPES[h] * (16 * (128 * bt + i) + 15.5)
    c["c_cbias"] = cb
    c["c_maskd"] = np.where(i[:, None] > i[None, :], NEG, 0.0).astype(bf)
    c["c_maskw"] = np.where(i[None, :] >= i[:, None], NEG, 0.0).astype(bf)
    QT = NT // 512
    cm = np.zeros((QT, 2, 128, 512), np.float32)
    for T in range(QT):
        for bt in range(2):
            n = 128 * bt + i
            tt = 512 * T + np.arange(512)
            cm[T, bt] = np.where(16 * n[:, None] + 31 <= tt[None, :], 0.0, NEG)
    c["c_cmask"] = cm.astype(bf)
    ov = np.zeros((256, 64), np.float32)
    ci = np.arange(256)[:, None] * 16
    sj = np.arange(64)[None, :] * 64
    ov[:] = ((ci <= sj + 63) & (ci + 31 >= sj))
    ov[ncb:] = 0
    ov[:, nsb:] = 0
    c["c_overlap"] = ov.astype(bf)
    blk = np.arange(64)[None, :]
    cur = (t // 64)[:, None]
    valid = blk * 64 <= t[:, None]
    forced = ((blk == 0) | (blk == cur) | (blk == cur - 1)) & valid
    vm = valid.astype(np.float32)
    am = np.where(forced, 1e4, np.where(valid, 0.0, -1.0)).astype(np.float32)
    vm[:, nsb:] = 0.0
    am[:, nsb:] = -1.0
    c["c_vmask"] = vm
    c["c_amask"] = am
    si = np.zeros((64, NT), np.float32)
    si[0] = 1.0
    for j in range(1, 64):
        si[j] = (t // 64 == j)
    c["c_selind"] = si.astype(bf)
    kw = np.zeros((64, NT), np.float32)
    kw[0] = 1.0
    c["c_kwrows"] = kw.astype(bf)
    return c


def nsa_declare(nc, k, NT):
    def inp(name, shape, dt=F32):
        return nc.dram_tensor(name, list(shape), dt, kind="ExternalInput").ap()
    QT = NT // 512
    KT = NT // 128
    k.nsa_w_in = inp("nsa_w_in", [1, D, NSA_W])
    k.nsa_w_out = inp("nsa_w_out", [1, D, D])
    k.nsa_cmp_pe = inp("nsa_cmp_pe", [1, 2, 32, 64])
    k.nsa_cmp_w1 = inp("nsa_cmp_w1", [1, 2, 2048, 64])
    k.nsa_cmp_w2 = inp("nsa_cmp_w2", [1, 2, 64, 64])
    k.c_vrow = inp("c_vrow", [16, NT], BF16)
    k.c_sbias = inp("c_sbias", [128, KT * 16])
    k.c_cbias = inp("c_cbias", [128, 32])
    k.c_maskd = inp("c_maskd", [128, 128], BF16)
    k.c_maskw = inp("c_maskw", [128, 128], BF16)
    k.c_cmask = inp("c_cmask", [QT, 2, 128, 512], BF16)
    k.c_overlap = inp("c_overlap", [256, 64], BF16)
    k.c_vmask = inp("c_vmask", [NT, 64])
    k.c_amask = inp("c_amask", [NT, 64])
    k.c_selind = inp("c_selind", [64, NT], BF16)
    k.c_kwrows = inp("c_kwrows", [64, NT], BF16)
    k.qT_d = nc.dram_tensor("qT_d", [1024, NT], BF16).ap()
    k.kT_d = nc.dram_tensor("kT_d", [4, 256, NT], BF16).ap()
    k.vtok_d = nc.dram_tensor("vtok_d", [2, NT, 256], BF16).ap()
    k.gT_d = nc.dram_tensor("gT_d", [48, NT], F32).ap()
    k.kcT_d = nc.dram_tensor("kcT_d", [4, 64, 256], BF16).ap()
    k.vc_d = nc.dram_tensor("vc_d", [4, 256, 64], BF16).ap()
    k.oT_d = nc.dram_tensor("oT_d", [1024, NT], BF16).ap()


def phase_nsa_proj(S, k, l, xin, NT):
    TT = 512
    ntile = NT // TT
    load_consts(S, k)
    alloc_psum(S, k)
    cols = rows_to_cols(S, k, [k.modd[l, :], k.norm_mix[l, :]], "ncols")
    gcol = S.sbuf("gcol", [128, 8], F32)
    S.stt("dve", gcol, gcol[:], cols[:, 8:16], 1.0, cols[:, 48:56], ALU.add, ALU.mult, [cols])
    W = S.sbuf("Wn", [128, 8, NSA_W], BF16)
    stg = [S.sbuf("stg%d" % i, [128, 1024], F32) for i in range(3)]
    load_weight_bf16(S, W, k.nsa_w_in[0], 8, NSA_W, stg)
    xts = [S.sbuf("xt%d" % i, [128, 1024], F32) for i in range(2)]
    xn = S.sbuf("xn", [128, 1024], BF16)
    sq = S.sbuf("sq", [128, 1024], F32)
    ss = S.sbuf("ss", [128, 8], F32)
    rstd = S.sbuf("rstd", [128, 8], F32)
    hT = S.sbuf("hT", [128, 8, TT], BF16)
    fsb = [S.sbuf("fsb%d" % i, [128, TT], BF16) for i in range(3)]
    vsb = [S.sbuf("vsb%d" % i, [128, 512], BF16) for i in range(2)]
    gsb = [S.sbuf("gsb%d" % i, [48, TT], F32) for i in range(2)]
    pT = k.ps[0]
    outs = fsb + vsb + gsb
    n = 0
    for T0 in range(ntile):
        r0 = T0 * TT
        for sub in range(4):
            xt = xts[sub % 2]
            S.dma(xt, xt[:], xin[r0 + sub * 128:r0 + (sub + 1) * 128, :])
            norm_to_hT(S, k, xt, xt[:], gcol, cols, hT, sub * 128, xn, sq, ss, rstd, pT)
        fm = [(hp * 128, ("q", hp)) for hp in range(8)]
        for kind, c0 in ((0, 1024), (1, 1280), (2, 1536), (3, 2048)):
            for cpart in range(2):
                fm.append((c0 + cpart * 128, ("k", kind, cpart)))
        for c0, tag in fm:
            ps = k.ps[1 + (n % 3)]
            f = fsb[n % 3]
            n += 1
            for kc in range(8):
                S.mm(ps, ps[:], W[:, kc, c0:c0 + 128], hT[:, kc, :], [W, hT], start=(kc == 0), stop=(kc == 7))
            if tag[0] == "q":
                S.act(f, f[:], ps[:], AF.Copy, [ps], scale=0.125)
                S.dma(None, k.qT_d[tag[1] * 128:(tag[1] + 1) * 128, r0:r0 + TT], f[:], in_t=f)
            else:
                S.copy("dve", f, f[:], ps[:], [ps])
                S.dma(None, k.kT_d[tag[1], tag[2] * 128:(tag[2] + 1) * 128, r0:r0 + TT], f[:], in_t=f)
        for sub in range(4):
            ps = k.ps[4 + (sub % 2)]
            v = vsb[sub % 2]
            for j, c0 in enumerate((1792, 2304)):
                for kc in range(8):
                    S.mm(ps, ps[:, j * 256:(j + 1) * 256], hT[:, kc, sub * 128:(sub + 1) * 128],
                         W[:, kc, c0:c0 + 256], [hT, W], start=(kc == 0), stop=(kc == 7))
            S.copy("dve", v, v[:], ps[:], [ps])
            for j in range(2):
                S.dma(None, k.vtok_d[j, r0 + sub * 128:r0 + (sub + 1) * 128, :], v[:, j * 256:(j + 1) * 256], in_t=v)
        ps = k.ps[6]
        g_ = gsb[T0 % 2]
        for kc in range(8):
            S.mm(ps, ps[0:48, :], W[:, kc, 2560:2608], hT[:, kc, :], [W, hT], start=(kc == 0), stop=(kc == 7))
        S.act(g_, g_[:], ps[0:48, :], AF.Sigmoid, [ps])
        S.dma(None, k.gT_d[:, r0:r0 + TT], g_[:], in_t=g_)
    S.finish_wait("sp", outs)


def phase_nsa_cmp(S, k, NT):
    ncb, nsb = nsa_dims(NT)
    load_consts(S, k)
    alloc_psum(S, k)
    xc = S.sbuf("xc", [64, 2, 4, NT], BF16)
    for kv in range(2):
        for g in range(4):
            S.dma(xc, xc[:, kv, g, :], k.kT_d[kv, g * 64:(g + 1) * 64, :])
    w1f = S.sbuf("w1f", [64, 2, 32, 64], F32)
    w1 = S.sbuf("w1", [64, 2, 32, 64], BF16)
    for kv in range(2):
        S.dma(w1f, w1f[:, kv, :, :], k.nsa_cmp_w1[0, kv].rearrange("(l d) e -> d l e", d=64))
    S.copy("pool", w1, w1[:], w1f[:], [w1f])
    w2f = S.sbuf("w2f", [64, 2, 64], F32)
    w2p = S.sbuf("w2p", [64, 2, 128], BF16)
    for kv in range(2):
        S.dma(w2f, w2f[:, kv, :], k.nsa_cmp_w2[0, kv])
    S.memset("pool", w2p, w2p[:], 0.0)
    S.copy("pool", w2p, w2p[:, :, 64:128], w2f[:], [w2f])
    pef = S.sbuf("pef", [32, 2, 64], F32)
    for kv in range(2):
        S.dma(pef, pef[:, kv, :], k.nsa_cmp_pe[0, kv])
    peT = S.sbuf("peT", [64, 2, 32], BF16)
    ps = k.ps[1]
    for kv in range(2):
        S.mm(ps, ps[0:64, kv * 32:(kv + 1) * 32], pef[:, kv, :], k.identf[0:32, 0:32], [pef, k.identf])
    S.copy("dve", peT, peT[:].rearrange("p a l -> p (a l)"), ps[0:64, 0:64], [ps])
    bias = S.sbuf("cbias", [64, 2], F32)
    ps = k.ps[2]
    for kv in range(2):
        for l in range(32):
            S.mm(ps, ps[0:64, kv:kv + 1], w1[:, kv, l, :], peT[:, kv, l:l + 1], [w1, peT],
                 start=(l == 0), stop=(l == 31))
    S.copy("dve", bias, bias[:], ps[0:64, 0:2], [ps])
    hid = [S.sbuf("hid%d" % i, [64, 256], BF16) for i in range(2)]
    osb = [S.sbuf("osb%d" % i, [128, 256], BF16) for i in range(2)]
    n = 0
    for kv in range(2):
        for g in range(4):
            ph = k.ps[3 + (n % 2)]
            hd = hid[n % 2]
            ob = osb[n % 2]
            for l in range(32):
                rhs = xc[:, kv, g, l:l + 16 * (ncb - 1) + 1:16]
                S.mm(ph, ph[0:64, 0:ncb], w1[:, kv, l, :], rhs, [w1, xc], start=(l == 0), stop=(l == 31))
            S.act(hd, hd[:, 0:ncb], ph[0:64, 0:ncb], AF.Silu, [ph, bias], bias=bias[:, kv:kv + 1])
            po = k.ps[5 + (n % 2)]
            if kv == 0:
                S.mm(po, po[:, 0:ncb], w2p[:, 0, :], hd[:, 0:ncb], [w2p, hd])
                S.copy("dve", ob, ob[64:128, 0:ncb], po[64:128, 0:ncb], [po])
                S.dma(None, k.kcT_d[g, :, 0:ncb], ob[64:128, 0:ncb], in_t=ob)
            else:
                for bt in range((ncb + 127) // 128):
                    nb = min(128, ncb - bt * 128)
                    S.mm(po, po[0:nb, bt * 64:(bt + 1) * 64], hd[:, bt * 128:bt * 128 + nb], w2p[:, 1, 64:128],
                         [hd, w2p])
                    S.copy("dve", ob, ob[0:nb, bt * 64:(bt + 1) * 64], po[0:nb, bt * 64:(bt + 1) * 64], [po])
                    S.dma(None, k.vc_d[g, bt * 128:bt * 128 + nb, :], ob[0:nb, bt * 64:(bt + 1) * 64], in_t=ob)
            n += 1
    S.finish_wait("sp", osb)


def phase_nsa_attn(S, k, NT):
    ncb, nsb = nsa_dims(NT)
    QT = NT // 512
    KT = NT // 128
    NBT = (ncb + 127) // 128
    load_consts(S, k)
    alloc_psum(S, k)
    sbias = S.sbuf("sbias", [128, KT * 16], F32)
    S.dma(sbias, sbias[:], k.c_sbias[:, :])
    cbias = S.sbuf("cbias", [128, 32], F32)
    S.dma(cbias, cbias[:], k.c_cbias[:, :])
    maskd = S.sbuf("maskd", [128, 128], BF16)
    S.dma(maskd, maskd[:], k.c_maskd[:, :])
    maskw = S.sbuf("maskw", [128, 128], BF16)
    S.dma(maskw, maskw[:], k.c_maskw[:, :])
    ovl = S.sbuf("ovl", [128, 2, 64], BF16)
    S.dma(ovl, ovl[:], k.c_overlap.rearrange("(bt p) j -> p bt j", p=128))
    Ks = S.sbuf("Ks", [128, NT], BF16)
    Kw = S.sbuf("Kw", [128, NT], BF16)
    Kc = S.sbuf("Kc", [128, 256], BF16)
    Vs = S.sbuf("Vs", [128, KT, 128], BF16)
    Vw = S.sbuf("Vw", [128, KT, 128], BF16)
    Vc = S.sbuf("Vc", [128, 2, 128], BF16)
    S.memset("pool", Vs, Vs[:], 1.0)
    S.memset("pool", Vw, Vw[:], 1.0)
    S.memset("pool", Vc, Vc[:], 1.0)
    S.memset("pool", Kc, Kc[:], 0.0)
    S.dma(Ks, Ks[0:64, :], k.c_selind[:, :])
    S.dma(Kw, Kw[0:64, :], k.c_kwrows[:, :])
    S.dma(Kc, Kc[0:64, :], k.c_kwrows[:, 0:256])
    QaT = [S.sbuf("Qa%d" % i, [128, 4, 512], BF16) for i in range(2)]
    Qav = [[S.view("Qa%d_%d" % (i, hh), QaT[i].ap[:, hh, :]) for hh in range(4)] for i in range(2)]
    for q in QaT:
        S.memset("pool", q, q[:], 0.0)
    for i in range(2):
        for hh in range(4):
            Qav[i][hh].last_w = QaT[i].last_w
    gbT = [S.sbuf("gb%d" % i, [128, 12, 512], F32) for i in range(2)]
    Pt = [S.sbuf("Pt%d" % i, [128, 512], BF16) for i in range(6)]
    cmk = [S.sbuf("cmk%d" % i, [128, 512], BF16) for i in range(2)]
    rd = [S.sbuf("rd%d" % i, [128, 512], F32) for i in range(2)]
    coef = [S.sbuf("coef%d" % i, [128, 512], F32) for i in range(2)]
    tmp = [S.sbuf("tmp%d" % i, [128, 512], F32) for i in range(2)]
    oacc = [S.sbuf("oacc%d" % i, [128, 512], F32) for i in range(4)]
    osbT = [S.sbuf("osb%d" % i, [128, 4, 512], BF16) for i in range(2)]
    impacc = S.sbuf("impacc", [128, 512], F32)
    vm = S.sbuf("vm", [128, 4, 64], F32)
    am = S.sbuf("am", [128, 4, 64], F32)
    sc = S.sbuf("sc", [128, 4, 64], F32)
    sc2 = S.sbuf("sc2", [128, 64], F32)
    mx = S.sbuf("mx", [128, 8], F32)
    thr = S.sbuf("thr", [128, 4], F32)
    selb = S.sbuf("selb", [128, 4, 64], BF16)
    pT = k.ps[0]
    pSs = [k.ps[1], k.ps[2], k.ps[6], k.ps[5]]
    pOs = [k.ps[3], k.ps[7]]
    pI, pTk = k.ps[4], k.ps[5]
    cnt = {"s": 0, "p": 0, "r": 0, "cm": 0, "o": 0, "po": 0}
    NP = len(Pt)
    LOOK = 3
    pend = []
    cur = {}

    def finish_branch(pO, hh, br, first, want_imp):
        i = cnt["r"] % 2
        cnt["r"] += 1
        r_, c_, t_ = rd[i], coef[i], tmp[i]
        gbt = cur["gb"]
        S.act(r_, r_[64:128, :], pO[64:128, :], AF.Ln, [pO], bias=(1e-30 if br == 0 else 0.0))
        S.act(r_, r_[64:128, :], r_[64:128, :], AF.Exp, [r_], scale=-1.0)
        S.tt("dve", c_, c_[64:128, :], r_[64:128, :], gbt[64:128, 3 * hh + br, :], ALU.mult, [r_, gbt])
        if first:
            S.tt("dve", oacc[hh], oacc[hh][0:64, :], pO[0:64, :], c_[64:128, :], ALU.mult, [pO, c_])
        else:
            S.tt("dve", t_, t_[0:64, :], pO[0:64, :], c_[64:128, :], ALU.mult, [pO, c_])
            S.tt("pool", oacc[hh], oacc[hh][0:64, :], oacc[hh][0:64, :], t_[0:64, :], ALU.add, [oacc[hh], t_])
        if want_imp:
            if hh == 0:
                S.tt("dve", impacc, impacc[0:64, :], pI[0:64, :], r_[64:128, :], ALU.mult, [pI, r_])
            else:
                S.tt("dve", t_, t_[0:64, :], pI[0:64, :], r_[64:128, :], ALU.mult, [pI, r_])
                S.tt("pool", impacc, impacc[0:64, :], impacc[0:64, :], t_[0:64, :], ALU.add, [impacc, t_])

    def emit_pv(item):
        (pO, lhsV, P, np_, clo, chi, first, last, ovl_ap, cb, vt_) = item
        S.mm(pO, pO[:, clo:chi], lhsV, P[0:np_, clo:chi], [vt_, P], start=first, stop=last)
        if ovl_ap is not None:
            S.mm(pI, pI[0:64, clo:chi], ovl_ap, P[0:np_, clo:chi], [ovl, P], start=first, stop=last)
        if cb is not None:
            cb()

    def push(item):
        pend.append(item)
        while len(pend) > LOOK:
            emit_pv(pend.pop(0))

    def flush():
        while pend:
            emit_pv(pend.pop(0))

    def attend(hh, h, Kt, Vt, tiles, br, first_branch):
        pO = pOs[cnt["po"] % 2]
        cnt["po"] += 1
        Qh = cur["Qa"][hh]
        for idx, (kt, clo, chi, masks) in enumerate(tiles):
            pS = pSs[cnt["s"] % len(pSs)]
            cnt["s"] += 1
            S.mm(pS, pS[:, clo:chi], Kt[:, kt * 128:(kt + 1) * 128], Qh[:, clo:chi], [Kt, Qh],
                 start=True, stop=(len(masks) == 0))
            for mi, (mk, c0) in enumerate(masks):
                S.mm(pS, pS[:, c0:c0 + 128], k.ident[:], mk[:], [k.ident, mk], start=False,
                     stop=(mi == len(masks) - 1))
            P = Pt[cnt["p"] % NP]
            cnt["p"] += 1
            S.act(P, P[:, clo:chi], pS[:, clo:chi], AF.Exp, [pS, sbias], bias=sbias[:, kt * 16 + h:kt * 16 + h + 1])
            last = (idx == len(tiles) - 1)
            cb = (lambda pO=pO, hh=hh, br=br, fb=first_branch: finish_branch(pO, hh, br, fb, False)) if last else None
            push((pO, Vt[:, kt, :], P, 128, clo, chi, idx == 0, last, None, cb, Vt))

    units = [(g, T) for g in range(4) for T in range(QT)]

    def load_unit(ui):
        g, T = units[ui]
        T0 = 512 * T
        par = ui % 2
        S.op("sp", lambda e: e.dma_start(out=gbT[par][64:128, :, :],
                                         in_=k.gT_d[12 * g:12 * g + 12, T0:T0 + 512].partition_broadcast(64)),
             [], [gbT[par]], dma_sem_tile=gbT[par])
        S.op("sp", lambda e: e.dma_start(out=QaT[par][64:128, :, :],
                                         in_=k.qT_d[256 * g:256 * g + 256, T0:T0 + 512].rearrange("(hh d) t -> d hh t", d=64)),
             [], Qav[par], dma_sem_tile=Qav[par][0])
        S.op("sp", lambda e: e.dma_start(out=QaT[par][0:1, :, :], in_=k.c_vrow[4 * g:4 * g + 4, T0:T0 + 512].rearrange("(o h) t -> o h t", o=1)),
             [], Qav[par], dma_sem_tile=Qav[par][0])

    def load_group(g):
        S.dma(Ks, Ks[64:128, :], k.kT_d[2, g * 64:(g + 1) * 64, :])
        S.dma(Kw, Kw[64:128, :], k.kT_d[3, g * 64:(g + 1) * 64, :])
        S.dma(Kc, Kc[64:128, 0:ncb], k.kcT_d[g, :, 0:ncb])
        for k0 in range(0, KT, 8):
            k1 = min(KT, k0 + 8)
            S.dma(Vs, Vs[:, k0:k1, 0:64],
                  k.vtok_d[0][k0 * 128:k1 * 128, g * 64:(g + 1) * 64].rearrange("(kt p) d -> p kt d", p=128))
            S.dma(Vw, Vw[:, k0:k1, 0:64],
                  k.vtok_d[1][k0 * 128:k1 * 128, g * 64:(g + 1) * 64].rearrange("(kt p) d -> p kt d", p=128))
        for bt in range(NBT):
            nb = min(128, ncb - bt * 128)
            S.dma(Vc, Vc[0:nb, bt, 0:64], k.vc_d[g, bt * 128:bt * 128 + nb, :])

    load_unit(0)
    for ui, (g, T) in enumerate(units):
        if T == 0:
            load_group(g)
        T0 = 512 * T
        par = ui % 2
        cur["Qa"] = Qav[par]
        cur["gb"] = gbT[par]
        Qa = Qav[par]
        bts = []
        for bt in range(NBT):
            nb = min(128, ncb - bt * 128)
            n_lo, n_hi = 128 * bt, 128 * bt + nb - 1
            if 16 * n_lo + 31 > T0 + 511:
                continue
            partial = 16 * n_hi + 31 > T0
            bts.append((bt, nb, partial))
        cms = {}
        for (bt, nb, partial) in bts:
            if partial:
                cm_ = cmk[cnt["cm"] % 2]
                cnt["cm"] += 1
                S.dma(cm_, cm_[:], k.c_cmask[T, bt])
                cms[bt] = cm_
        S.dma(vm, vm[:], k.c_vmask[T0:T0 + 512, :].rearrange("(n p) j -> p n j", p=128))
        S.dma(am, am[:], k.c_amask[T0:T0 + 512, :].rearrange("(n p) j -> p n j", p=128))
        for hh in range(4):
            h = 4 * g + hh
            pO = pOs[cnt["po"] % 2]
            cnt["po"] += 1
            for bi, (bt, nb, partial) in enumerate(bts):
                pS = pSs[cnt["s"] % len(pSs)]
                cnt["s"] += 1
                S.mm(pS, pS[0:nb, :], Kc[:, bt * 128:bt * 128 + nb], Qa[hh][:, :], [Kc, Qa[hh]],
                     start=True, stop=(not partial))
                if partial:
                    cm_ = cms[bt]
                    S.mm(pS, pS[0:nb, :], k.ident[0:nb, 0:nb], cm_[0:nb, :], [k.ident, cm_], start=False, stop=True)
                P = Pt[cnt["p"] % NP]
                cnt["p"] += 1
                S.act(P, P[0:nb, :], pS[0:nb, :], AF.Exp, [pS, cbias],
                      bias=cbias[0:nb, bt * 16 + h:bt * 16 + h + 1])
                last = (bi == len(bts) - 1)
                cb = (lambda pO=pO, hh=hh: finish_branch(pO, hh, 0, True, True)) if last else None
                push((pO, Vc[0:nb, bt, :], P, nb, 0, 512, bi == 0, last, ovl[0:nb, bt, :], cb, Vc))
        for hh in range(4):
            h = 4 * g + hh
            tiles = []
            for kt in range(max(0, 4 * T - 4), 4 * T + 4):
                m = kt - 4 * T
                n_lo, n_hi = max(m, 0), min(m + 4, 3)
                masks = []
                if m >= 0:
                    masks.append((maskd, 128 * m))
                if m <= -1:
                    masks.append((maskw, 128 * (m + 4)))
                tiles.append((kt, 128 * n_lo, 128 * (n_hi + 1), masks))
            tiles.sort(key=lambda tl: -(tl[2] - tl[1]))
            attend(hh, h, Kw, Vw, tiles, 2, False)
            if hh == 1:
                for n in range(4):
                    S.mm(pTk, pTk[:, n * 64:(n + 1) * 64], impacc[0:64, n * 128:(n + 1) * 128],
                         k.identf[0:64, 0:64], [impacc, k.identf])
                S.tt("dve", sc, sc[:].rearrange("p n j -> p (n j)"), pTk[:, 0:256],
                     vm[:].rearrange("p n j -> p (n j)"), ALU.mult, [pTk, vm])
                S.tt("pool", sc, sc[:], sc[:], am[:], ALU.add, [sc, am])
                for n in range(4):
                    S.op("dve", lambda e, n=n: e.max(mx[:], sc[:, n, :]), [sc], [mx])
                    S.op("dve", lambda e, n=n: e.match_replace(sc2[:], mx[:], sc[:, n, :], -1e9), [sc, mx], [sc2])
                    S.op("dve", lambda e: e.max(mx[:], sc2[:]), [sc2], [mx])
                    S.ts("dve", thr, thr[:, n:n + 1], mx[:, 7:8], -0.5, None, ALU.max, reads=[mx])
                    S.ts("dve", sc2, sc2[:], sc[:, n, :], thr[:, n:n + 1], None, ALU.is_ge, reads=[sc, thr])
                    S.ts("dve", selb, selb[:, n, :], sc2[:], -1.0, -NEG, ALU.add, ALU.mult, reads=[sc2])
        if ui + 1 < len(units):
            load_unit(ui + 1)
        for n in range(4):
            S.transpose(pT, pT[0:64, n * 128:(n + 1) * 128], selb[:, n, :], k.ident[:], [selb, k.ident])
        flush()
        for hh in range(4):
            S.copy("act", Qa[hh], Qa[hh][0:64, :], pT[0:64, 0:512], [pT])
        S.op("sp", lambda e, par=par, g=g, T0=T0: e.dma_start(
            out=QaT[par][0:1, :, :], in_=k.c_vrow[4 * g:4 * g + 4, T0:T0 + 512].rearrange("(o h) t -> o h t", o=1)),
            [], Qa, dma_sem_tile=Qa[0])
        for hh in range(4):
            h = 4 * g + hh
            tiles = []
            for kt in range(4 * T + 4):
                j = kt - 4 * T
                if j < 0:
                    tiles.append((kt, 0, 512, []))
                else:
                    tiles.append((kt, 128 * j, 512, [(maskd, 128 * j)]))
            attend(hh, h, Ks, Vs, tiles, 1, False)
        flush()
        ob = osbT[cnt["o"] % 2]
        cnt["o"] += 1
        for hh in range(4):
            S.copy("act", ob, ob[0:64, hh, :], oacc[hh][0:64, :], [oacc[hh]])
        S.dma(None, k.oT_d[256 * g:256 * g + 256, T0:T0 + 512].rearrange("(hh d) t -> d hh t", d=64),
              ob[0:64, :, :], in_t=ob)
    S.finish_wait("sp", osbT)


def phase_nsa_out(S, k, l, xin, xout, NT):
    TT = 512
    ntile = NT // TT
    load_consts(S, k)
    alloc_psum(S, k)
    g1row = S.sbuf("g1row", [128, 1024], F32)
    S.dma(g1row, g1row[:], k.modd[l, 2048:3072].partition_broadcast(128))
    Wo = S.sbuf("Wo", [128, 8, 1024], BF16)
    stg = [S.sbuf("stg%d" % i, [128, 1024], F32) for i in range(2)]
    load_weight_bf16(S, Wo, k.nsa_w_out[0], 8, 1024, stg, grow=g1row)
    oT = [S.sbuf("oT%d" % i, [128, 8, TT], BF16) for i in range(2)]
    xts = [S.sbuf("xt%d" % i, [128, 1024], F32) for i in range(2)]
    xo = [S.sbuf("xo%d" % i, [128, 1024], F32) for i in range(2)]
    n = 0
    for T0 in range(ntile):
        r0 = T0 * TT
        o_ = oT[T0 % 2]
        S.dma(o_, o_[:], k.oT_d[:, r0:r0 + TT].rearrange("(kc p) t -> p kc t", p=128))
        for sub in range(4):
            xt = xts[sub % 2]
            xo_ = xo[sub % 2]
            S.dma(xt, xt[:], xin[r0 + sub * 128:r0 + (sub + 1) * 128, :])
            for half in range(2):
                py = k.ps[1 + (n % 4)]
                n += 1
                for kc in range(8):
                    S.mm(py, py[:], o_[:, kc, sub * 128:(sub + 1) * 128], Wo[:, kc, half * 512:(half + 1) * 512],
                         [o_, Wo], start=(kc == 0), stop=(kc == 7))
                S.tt("dve", xo_, xo_[:, half * 512:(half + 1) * 512], py[:], xt[:, half * 512:(half + 1) * 512],
                     ALU.add, [py, xt])
            S.dma(None, xout[r0 + sub * 128:r0 + (sub + 1) * 128, :], xo_[:], in_t=xo_)
    S.finish_wait("sp", xo)


W_KEYS = ["ada_w", "ada_b", "norm_mix", "norm_ffn", "final_norm", "hg_w_in", "hg_w_out", "hg_gnorm", "hg_lb",
          "ffn_w_up", "ffn_conv_w", "ffn_conv_b", "ffn_w_down", "nsa_w_in", "nsa_w_out", "nsa_cmp_pe",
          "nsa_cmp_w1", "nsa_cmp_w2"]


def make_in_map(inp, x, c, NT, consts=None):
    im = {"x": np.ascontiguousarray(x, dtype=np.float32),
          "c_col": np.ascontiguousarray(np.asarray(c, dtype=np.float32).reshape(8, 128).T)}
    for k_ in W_KEYS:
        im[k_] = np.asarray(inp[k_], dtype=np.float32)
    if consts is None:
        consts = dict(host_consts())
        consts.update(nsa_host_consts(NT))
    im.update(consts)
    return im


_CACHE = {}


def kernel(**inputs):
    x = np.asarray(inputs["x"], dtype=np.float32)
    c = np.asarray(inputs["c"], dtype=np.float32)
    B, NT, _ = x.shape
    if "nc" not in _CACHE:
        _CACHE["nc"] = build_program(NT)
        consts = dict(host_consts())
        consts.update(nsa_host_consts(NT))
        _CACHE["consts"] = consts
    nc = _CACHE["nc"]
    in_maps = [make_in_map(inputs, x[b], c[b], NT, _CACHE["consts"]) for b in range(B)]
    res = run_bass_kernel_spmd(nc, in_maps, core_ids=list(range(B)))
    out = np.stack([np.asarray(r["out"], dtype=np.float32) for r in res.results], axis=0)
    return out
```

```python
import contextlib
import numpy as np
import concourse.bass as bass
import concourse.mybir as mybir

F32 = mybir.dt.float32
BF16 = mybir.dt.bfloat16
AF = mybir.ActivationFunctionType
ALU = mybir.AluOpType
AX = mybir.AxisListType

ENGS = ["pe", "act", "dve", "pool", "sp"]


class T:
    __slots__ = ("name", "ap", "last_w", "readers", "dsem", "dcnt", "uid", "excl")
    _n = [0]

    def __init__(self, name, ap=None):
        T._n[0] += 1
        self.uid = T._n[0]
        self.name = name
        self.ap = ap
        self.last_w = None
        self.readers = []
        self.dsem = None
        self.dcnt = 0
        self.excl = False

    def __getitem__(self, idx):
        return self.ap[idx]


class Sched:
    def __init__(self, nc, stack):
        self.nc = nc
        self.stack = stack
        self.ops = {e: [] for e in ENGS}
        self.cnt = {e: 0 for e in ENGS}
        self.clock = {e: {} for e in ENGS}
        self.sem = {}
        for e in ["pe", "act", "dve", "pool"]:
            self.sem[e] = stack.enter_context(nc.semaphore("s_" + e))
        self.sem["bar"] = stack.enter_context(nc.semaphore("s_bar"))
        self.bar_n = 0
        self.dma_live = {}
        self.gstack = stack
        self.nsem = 5
        self.final_waits = []
        self.n_wait = 0
        self.uid = 0
        self.dsem_pool = []
        self.dsem_owner = []

    def sbuf(self, name, shape, dtype):
        self.uid += 1
        name = "%s_u%d" % (name, self.uid)
        t = self.stack.enter_context(self.nc.sbuf_tensor(name, list(shape), dtype))
        return T(name, t)

    def psum(self, name, shape, dtype=F32):
        self.uid += 1
        name = "%s_u%d" % (name, self.uid)
        t = self.stack.enter_context(self.nc.psum_tensor(name, list(shape), dtype))
        tt_ = T(name, t)
        tt_.excl = True
        return tt_

    def view(self, name, ap):
        return T(name, ap)

    def _dsem(self, t):
        if t.dsem is None:
            if self.dsem_pool:
                t.dsem, t.dcnt = self.dsem_pool.pop()
            else:
                self.nsem += 1
                t.dsem = self.gstack.enter_context(self.nc.semaphore("dsem%d" % self.nsem))
                t.dcnt = 0
            self.dsem_owner.append(t)
        return t.dsem

    def phase_end(self):
        for t in self.dsem_owner:
            self.dsem_pool.append((t.dsem, t.dcnt))
            t.dsem = None
        self.dsem_owner = []
        self.dma_live = {}

    def _need(self, eng, ev, waits):
        key, val, snap = ev
        if eng == "pe" and key == "pe":
            return
        ck = self.clock[eng]
        if ck.get(key, 0) >= val:
            return
        waits[key] = max(waits.get(key, 0), val)
        ck[key] = val
        if snap:
            for k, v in snap.items():
                if ck.get(k, 0) < v:
                    ck[k] = v

    def op(self, eng, fn, reads=(), writes=(), dma_sem_tile=None):
        waits = {}
        ex = [t for t in reads if t.excl]
        if ex:
            reads = [t for t in reads if not t.excl]
            writes = list(writes) + [t for t in ex if t not in writes]
        for t in reads:
            if t.last_w is not None:
                self._need(eng, t.last_w, waits)
        for t in writes:
            if t.last_w is not None:
                self._need(eng, t.last_w, waits)
            for ev in t.readers:
                self._need(eng, ev, waits)
        if dma_sem_tile is not None:
            st = dma_sem_tile
            sem = self._dsem(st)
            st.dcnt += 16
            key = ("d", st.uid)
            self.sem[key] = sem
            ev = (key, st.dcnt, dict(self.clock[eng]))
            self.dma_live[key] = (st, st.dcnt)
            inc = (sem, 16)
        else:
            self.cnt[eng] += 1
            ev = (eng, self.cnt[eng], None)
            inc = (self.sem[eng], 1)
        self.ops[eng].append((list(waits.items()), fn, inc))
        self.n_wait += len(waits)
        if dma_sem_tile is None:
            snap = dict(self.clock[eng])
            ev = (eng, self.cnt[eng], snap)
        for t in writes:
            t.last_w = ev
            t.readers = []
        for t in reads:
            if t not in writes:
                t.readers.append(ev)
        return ev

    def finish_wait(self, eng, tiles):
        waits = {}
        for t in tiles:
            if t.last_w is not None:
                self._need(eng, t.last_w, waits)
            for ev in t.readers:
                self._need(eng, ev, waits)
        self.ops[eng].append((list(waits.items()), None, None))

    def barrier(self):
        evs = []
        for e in ["pe", "act", "dve", "pool"]:
            if self.cnt[e] > 0:
                evs.append((e, self.cnt[e], None))
        for key, (t, val) in self.dma_live.items():
            evs.append((key, val, None))
        for eng in ENGS:
            waits = {}
            for ev in evs:
                if ev[0] == eng and eng == "pe":
                    continue
                ck = self.clock[eng]
                if ck.get(ev[0], 0) < ev[1]:
                    waits[ev[0]] = ev[1]
                    ck[ev[0]] = ev[1]
            self.ops[eng].append((list(waits.items()), None, None))
        self.bar_n += 1
        for eng in ENGS:
            self.ops[eng].append(([], "barinc", None))
        for eng in ENGS:
            self.ops[eng].append(([("bar", 5 * self.bar_n)], None, None))

    def emit(self):
        nc = self.nc
        with nc.Block() as block:
            def run(eng_name):
                def body(e):
                    for waits, fn, inc in self.ops[eng_name]:
                        for key, val in waits:
                            e.wait_ge(self.sem[key], val)
                        if fn == "barinc":
                            e.sem_inc(self.sem["bar"], 1)
                        elif fn is not None:
                            ins = fn(e)
                            ins.then_inc(inc[0], inc[1])
                return body
            block.tensor(run("pe"))
            block.scalar(run("act"))
            block.vector(run("dve"))
            block.gpsimd(run("pool"))
            block.sync(run("sp"))
        self.ops = {e: [] for e in ENGS}

    def dma(self, out_t, out_ap, in_ap, in_t=None, eng="sp", **kw):
        reads = [in_t] if in_t is not None else []
        writes = [out_t] if out_t is not None else []
        st = out_t if out_t is not None else in_t
        return self.op(eng, lambda e: e.dma_start(out=out_ap, in_=in_ap, **kw), reads, writes,
                       dma_sem_tile=st)

    def mm(self, out_t, out_ap, lhsT, rhs, reads, start=True, stop=True, **kw):
        return self.op("pe", lambda e: e.matmul(out_ap, lhsT, rhs, start=start, stop=stop, **kw),
                       reads, [out_t])

    def transpose(self, out_t, out_ap, in_ap, ident_ap, reads):
        return self.op("pe", lambda e: e.transpose(out_ap, in_ap, ident_ap), reads, [out_t])

    def act(self, out_t, out_ap, in_ap, func, reads, bias=None, scale=None, accum_out=None,
            extra_writes=()):
        kw = {}
        if bias is not None:
            kw["bias"] = bias
        if scale is not None:
            kw["scale"] = scale
        if accum_out is not None:
            kw["accum_out"] = accum_out
        return self.op("act", lambda e: e.activation(out_ap, in_ap, func, **kw), reads,
                       [out_t] + list(extra_writes))

    def tt(self, eng, out_t, out_ap, in0, in1, op, reads):
        return self.op(eng, lambda e: e.tensor_tensor(out_ap, in0, in1, op), reads, [out_t])

    def ts(self, eng, out_t, out_ap, in0, s1, s2, op0, op1=None, reads=(), accum_out=None,
           extra_writes=()):
        def f(e):
            kw = {}
            if accum_out is not None:
                kw["accum_out"] = accum_out
            if op1 is None:
                return e.tensor_scalar(out_ap, in0, s1, None, op0, **kw)
            return e.tensor_scalar(out_ap, in0, s1, s2, op0, op1, **kw)
        return self.op(eng, f, reads, [out_t] + list(extra_writes))

    def stt(self, eng, out_t, out_ap, in0, scalar, in1, op0, op1, reads):
        eng = "dve"
        return self.op(eng, lambda e: e.scalar_tensor_tensor(out_ap, in0, scalar, in1, op0, op1),
                       reads, [out_t])

    def copy(self, eng, out_t, out_ap, in_ap, reads):
        if eng == "act":
            return self.op("act", lambda e: e.copy(out_ap, in_ap), reads, [out_t])
        return self.op(eng, lambda e: e.tensor_copy(out_ap, in_ap), reads, [out_t])

    def memset(self, eng, out_t, out_ap, val):
        return self.op(eng, lambda e: e.memset(out_ap, val), [], [out_t])

from concourse.bass_utils import run_bass_kernel_spmd

D = 1024
NH_HG = 8
DFF = 2816
NFC = DFF // 128
EPS = 1e-6
NEG = -30000.0


def bcast_rows(ap_row, n):
    return ap_row.partition_broadcast(n)


class K:
    pass


def load_weight_bf16(S, Wb, w_dram, KC, N, stg, grow=None, col_off=0, rowscale=None, kc_off=0):
    i = 0
    for kc in range(KC):
        for n0 in range(0, N, 1024):
            n1 = min(N, n0 + 1024)
            st = stg[i % len(stg)]
            S.dma(st, st[:, 0:n1 - n0], w_dram[kc * 128:(kc + 1) * 128, n0:n1])
            eng = "dve"
            o = Wb[:, kc_off + kc, col_off + n0:col_off + n1]
            if grow is None:
                S.copy("act" if i % 2 == 0 else "dve", Wb, o, st[:, 0:n1 - n0], [st])
            elif rowscale is None:
                S.tt(eng, Wb, o, st[:, 0:n1 - n0], grow[:, n0:n1], ALU.mult, [st, grow])
            else:
                S.stt(eng, Wb, o, st[:, 0:n1 - n0], rowscale[:, kc:kc + 1], grow[:, n0:n1],
                      ALU.mult, ALU.mult, [st, grow, rowscale])
            i += 1


def rstd_from_ss(S, rstd, ss, n, width):
    S.act(rstd, rstd[:, 0:width], ss[:, 0:width], AF.Ln, [ss], scale=1.0 / n, bias=EPS)
    S.act(rstd, rstd[:, 0:width], rstd[:, 0:width], AF.Exp, [rstd], scale=-0.5)


def norm_to_hT(S, k, xt_t, xt_ap, gcol, shcol, hT, col0, xn, sq, ss, rstd, pT):
    S.act(sq, sq[:], xt_ap, AF.Square, [xt_t], accum_out=ss[:, 0:1], extra_writes=[ss])
    rstd_from_ss(S, rstd, ss, D, 1)
    S.act(xn, xn[:], xt_ap, AF.Copy, [xt_t, rstd], scale=rstd[:, 0:1])
    for kc in range(8):
        S.transpose(pT, pT[:, kc * 128:(kc + 1) * 128], xn[:, kc * 128:(kc + 1) * 128],
                    k.ident[:], [xn, k.ident])
    for kc in range(8):
        eng = "dve" if kc % 2 == 0 else "pool"
        eng = "dve"
        S.ts(eng, hT, hT[:, kc, col0:col0 + 128], pT[:, kc * 128:(kc + 1) * 128],
             gcol[:, kc:kc + 1], shcol[:, kc:kc + 1], ALU.mult, ALU.add, reads=[pT, gcol, shcol])


def rows_to_cols(S, k, rows, name):
    n = sum(r.shape[0] // 128 for r in rows)
    assert n <= 128
    rt = S.sbuf(name + "_r", [n, 128], F32)
    ct = S.sbuf(name, [128, n], F32)
    j = 0
    for r in rows:
        m = r.shape[0] // 128
        S.dma(rt, rt[j:j + m, :], r.rearrange("(j p) -> j p", p=128))
        j += m
    ps = k.ps[1]
    S.mm(ps, ps[:, 0:n], rt[0:n, :], k.identf[0:n, 0:n], [rt, k.identf])
    S.copy("dve", ct, ct[:], ps[:, 0:n], [ps])
    return ct


def phase_prologue(S, k):
    cc = S.sbuf("cc", [128, 8], F32)
    ca = S.sbuf("ca", [128, 8], F32)
    S.dma(cc, cc[:], k.c_col[:, :])
    S.act(ca, ca[:], cc[:], AF.Silu, [cc])
    wst = [S.sbuf("adw%d" % i, [128, 8, 512], F32) for i in range(2)]
    brow = S.sbuf("brow", [1, 6144], F32)
    mrow = S.sbuf("mrow", [1, 6144], F32)
    i = 0
    for l in range(2):
        S.dma(brow, brow[:], k.ada_b[l:l + 1, :])
        for nt in range(12):
            wt = wst[i % 2]
            S.dma(wt, wt[:], k.ada_w[l].rearrange("(kc p) n -> p kc n", p=128)[:, :, nt * 512:(nt + 1) * 512])
            ps = k.ps[2 + (i % 2)]
            for kc in range(8):
                S.mm(ps, ps[0:1, :], ca[:, kc:kc + 1], wt[:, kc, :], [ca, wt],
                     start=(kc == 0), stop=(kc == 7))
            S.tt("dve", mrow, mrow[0:1, nt * 512:(nt + 1) * 512], ps[0:1, :],
                 brow[0:1, nt * 512:(nt + 1) * 512], ALU.add, [ps, brow])
            i += 1
        S.dma(None, k.modd[l:l + 1, :], mrow[:], in_t=mrow)
    S.finish_wait("sp", [mrow])


def load_consts(S, k):
    k.identf = S.sbuf("identf", [128, 128], F32)
    k.ident = S.sbuf("ident", [128, 128], BF16)
    S.dma(k.identf, k.identf[:], k.c_ident[:, :])
    S.copy("dve", k.ident, k.ident[:], k.identf[:], [k.identf])


def alloc_psum(S, k, bf7=False):
    k.ps = [S.psum("psb0", [128, 1024], BF16)] + [S.psum("ps%d" % i, [128, 512], F32) for i in range(1, 7)]
    if bf7:
        k.ps.append(S.psum("psb7", [128, 1024], BF16))
    else:
        k.ps.append(S.psum("ps7", [128, 512], F32))


def interleave(ga, gb, ra=1, rb=1):
    da = db = False
    while not (da and db):
        for _ in range(ra):
            if not da:
                try:
                    next(ga)
                except StopIteration:
                    da = True
        for _ in range(rb):
            if not db:
                try:
                    next(gb)
                except StopIteration:
                    db = True


def phase_hgrn(S, k, l, xin, xout, NT):
    TT = 256
    ntile = NT // TT
    load_consts(S, k)
    alloc_psum(S, k, bf7=True)
    cols = rows_to_cols(S, k, [k.modd[l, :], k.norm_mix[l, :], k.hg_lb[0, :], k.hg_lb[1, :],
                               k.hg_gnorm[0, :]], "hcols")
    gnc = S.sbuf("gnc", [128, 8], F32)
    S.copy("dve", gnc, gnc[:], cols[:, 72:73].to_broadcast([128, 8]), [cols])
    gcol = S.sbuf("gcol", [128, 8], F32)
    S.stt("dve", gcol, gcol[:], cols[:, 8:16], 1.0, cols[:, 48:56], ALU.add, ALU.mult, [cols])
    lbc = S.sbuf("lbc", [128, 8], F32)
    l1m = S.sbuf("l1m", [128, 8], F32)
    S.tt("dve", lbc, lbc[:], cols[:, 64:72], cols[:, 56:64], ALU.subtract, [cols])
    S.act(lbc, lbc[:], lbc[:], AF.Exp, [lbc])
    S.ts("dve", lbc, lbc[:], lbc[:], 1.0, None, ALU.add, reads=[lbc])
    S.op("dve", lambda e: e.reciprocal(lbc[:], lbc[:]), [lbc], [lbc])
    S.ts("dve", l1m, l1m[:], lbc[:], -1.0, 1.0, ALU.mult, ALU.add, reads=[lbc])
    S.act(l1m, l1m[:], l1m[:], AF.Ln, [l1m])
    sq = S.sbuf("sq", [128, 1024], F32)
    g1row = sq
    S.dma(g1row, g1row[:], k.modd[l, 2048:3072].partition_broadcast(128))
    Win = S.sbuf("Win", [128, 8, 4096], BF16)
    Wout = S.sbuf("Wout", [128, 8, 1024], BF16)
    otok = S.sbuf("otok", [128, 1024], F32)
    xo = [S.sbuf("xo%d" % i, [128, 1024], F32) for i in range(2)]
    stg = [otok, xo[0]]
    load_weight_bf16(S, Win, k.hg_w_in[0], 8, 4096, stg)
    load_weight_bf16(S, Wout, k.hg_w_out[0], 8, 1024, stg, grow=g1row, rowscale=gnc)
    rmask = S.sbuf("rmask", [128, TT], F32)
    S.memset("pool", rmask, rmask[:], 1.0)
    S.memset("pool", rmask, rmask[:].rearrange("p (c j) -> p c j", j=64)[:, :, 0:1], 0.0)
    cmask = S.sbuf("cmask", [128, 128], F32)
    S.dma(cmask, cmask[:], k.c_bdmask[:, :])
    st32s = [S.sbuf("st32_%d" % i, [128, 128], F32) for i in range(8)]
    stbs = [S.sbuf("stb_%d" % i, [128, 128], BF16) for i in range(8)]
    for i in range(8):
        S.memset("pool", st32s[i], st32s[i][:], 0.0)
        S.memset("pool", stbs[i], stbs[i][:], 0.0)
    sts = [S.sbuf("sts%d" % i, [128, 128], F32) for i in range(8)]
    sts2 = sts
    xts = [S.sbuf("xt%d" % i, [128, 1024], F32) for i in range(4)]
    xns = [S.sbuf("xn%d" % i, [128, 1024], BF16) for i in range(2)]
    sss = [S.sbuf("ss%d" % i, [128, 2], F32) for i in range(2)]
    rss = [S.sbuf("rs%d" % i, [128, 2], F32) for i in range(2)]
    hT = S.sbuf("hT", [128, 8, TT], BF16)
    NTMP = 2
    tu = [S.sbuf("tu%d" % i, [128, TT], F32) for i in range(NTMP)]
    tA = [S.sbuf("tA%d" % i, [128, TT], F32) for i in range(NTMP)]
    tB = [S.sbuf("tB%d" % i, [128, TT], F32) for i in range(NTMP)]
    tb = [S.sbuf("tb%d" % i, [128, TT], F32) for i in range(NTMP)]
    teb = [S.sbuf("teb%d" % i, [128, TT], F32) for i in range(NTMP)]
    t1 = [S.sbuf("t1%d" % i, [128, TT], F32) for i in range(NTMP)]
    qdT2 = [S.sbuf("qdT%d" % i, [128, 8, 2, 2, 128], BF16) for i in range(2)]
    kdT2 = [S.sbuf("kdT%d" % i, [128, 8, TT], BF16) for i in range(2)]
    kdtok2 = [S.sbuf("kdtok%d" % i, [128, 2, 8, 128], BF16) for i in range(2)]
    ebl2 = [S.sbuf("ebl%d" % i, [128, 8, 4], F32) for i in range(2)]
    vt2 = [S.sbuf("vt%d" % i, [128, 2, 1024], BF16) for i in range(2)]
    gs2 = [S.sbuf("gs%d" % i, [128, 2, 1024], BF16) for i in range(2)]
    oss = S.sbuf("oss", [128, 8], F32)
    orstd = S.sbuf("orstd", [128, 8], F32)
    on = S.sbuf("on", [128, 1024], BF16)
    oT = S.sbuf("oT", [128, 8, 128], BF16)
    sc4s = [S.sbuf("sc4_%d" % i, [128, 512], BF16) for i in range(2)]
    st1 = [S.sbuf("st1_%d" % i, [128, 128], F32) for i in range(8)]
    st1b = [S.sbuf("st1b_%d" % i, [128, 128], BF16) for i in range(8)]
    for q in qdT2:
        S.memset("pool", q, q[:], 0.0)
    pT, pT2 = k.ps[0], k.ps[7]
    pq = [k.ps[1], k.ps[2]]
    pvs = [k.ps[1], k.ps[2]]
    tq = [S.sbuf("tq%d" % i, [128, TT], F32) for i in range(NTMP)]

    def gen_A(T0):
        par = T0 % 2
        qdT, kdT, kdtok, ebl, vt, gs = qdT2[par], kdT2[par], kdtok2[par], ebl2[par], vt2[par], gs2[par]
        r0 = T0 * TT
        for sub in range(2):
            xt = xts[(T0 * 2 + sub) % 4]
            S.dma(xt, xt[:], xin[r0 + sub * 128:r0 + (sub + 1) * 128, :])
            S.act(sq, sq[:], xt[:], AF.Square, [xt], accum_out=sss[sub][:, 0:1], extra_writes=[sss[sub]])
            rstd_from_ss(S, rss[sub], sss[sub], D, 1)
            S.act(xns[sub], xns[sub][:], xt[:], AF.Copy, [xt, rss[sub]], scale=rss[sub][:, 0:1])
            yield
        for sub in range(2):
            for kc in range(8):
                S.transpose(pT, pT[:, kc * 128:(kc + 1) * 128], xns[sub][:, kc * 128:(kc + 1) * 128],
                            k.ident[:], [xns[sub], k.ident])
            for kc in range(8):
                S.act(hT, hT[:, kc, sub * 128:(sub + 1) * 128], pT[:, kc * 128:(kc + 1) * 128], AF.Identity,
                      [pT, gcol, cols], scale=gcol[:, kc:kc + 1], bias=cols[:, kc:kc + 1])
            yield
        def s1(h):
            i = h % NTMP
            pq_ = pq[h % 2]
            for kc in range(8):
                S.mm(pq_, pq_[:, 0:TT], Win[:, kc, h * 128:(h + 1) * 128], hT[:, kc, :], [Win, hT],
                     start=(kc == 0), stop=(kc == 7))
            for kc in range(8):
                S.mm(pq_, pq_[:, TT:2 * TT], Win[:, kc, 1024 + h * 128:1024 + (h + 1) * 128], hT[:, kc, :],
                     [Win, hT], start=(kc == 0), stop=(kc == 7))
            z = pq_[:, TT:2 * TT]
            S.act(tu[i], tu[i][:], z, AF.Exp, [pq_], scale=-1.0)
            S.act(tA[i], tA[i][:], tu[i][:], AF.Ln, [tu[i]], bias=1.0)
            S.act(tB[i], tB[i][:], tu[i][:], AF.Ln, [tu[i], lbc], bias=1.0, scale=lbc[:, h:h + 1])
            S.copy("dve", tq[i], tq[i][:], pq_[:, 0:TT], [pq_])
            S.tt("dve", t1[i], t1[i][:], z, tA[i][:], ALU.add, [pq_, tA[i]])

        def s2(h):
            i = h % NTMP
            pq_ = pq[h % 2]
            z = pq_[:, TT:2 * TT]
            S.tt("pool", tB[i], tB[i][:], tB[i][:], tA[i][:], ALU.subtract, [tB[i], tA[i]])
            S.op("dve", lambda e, o=tb[i], m=tB[i]: e.tensor_tensor_scan(o[:], rmask[:], m[:], 0.0, ALU.mult, ALU.add),
                 [rmask, tB[i]], [tb[i]])
            S.act(teb[i], teb[i][:], tb[i][:], AF.Exp, [tb[i]])
            S.tt("pool", t1[i], t1[i][:], t1[i][:], tb[i][:], ALU.add, [t1[i], tb[i]])

        def s3(h):
            i = h % NTMP
            pq_ = pq[h % 2]
            for sub in range(2):
                for c2 in range(2):
                    cs = sub * 128 + c2 * 64
                    S.tt("dve", qdT, qdT[:, h, sub, c2, c2 * 64:(c2 + 1) * 64], tq[i][:, cs:cs + 64],
                         teb[i][:, cs:cs + 64], ALU.mult, [tq[i], teb[i]])
            S.act(kdT, kdT[:, h, :], t1[i][:], AF.Exp, [t1[i], l1m], scale=-1.0, bias=l1m[:, h:h + 1])
            S.copy("pool", ebl, ebl[:, h, :], teb[i][:].rearrange("p (c j) -> p c j", j=64)[:, :, 63], [teb[i]])

        for it in range(10):
            if 0 <= it - 2 < 8:
                s3(it - 2)
            if 0 <= it - 1 < 8:
                s2(it - 1)
            if it < 8:
                s1(it)
            yield
        for sub in range(2):
            for half in range(4):
                c0 = 2048 + half * 512
                pv = pvs[half % 2]
                for kc in range(8):
                    S.mm(pv, pv[:], hT[:, kc, sub * 128:(sub + 1) * 128], Win[:, kc, c0:c0 + 512],
                         [hT, Win], start=(kc == 0), stop=(kc == 7))
                if half < 2:
                    S.copy("dve", vt, vt[:, sub, half * 512:(half + 1) * 512], pv[:], [pv])
                else:
                    S.act(gs, gs[:, sub, (half - 2) * 512:(half - 1) * 512], pv[:], AF.Silu, [pv])
                yield
        for sub in range(2):
            for h in range(8):
                S.transpose(pT, pT[:, h * 128:(h + 1) * 128], kdT[:, h, sub * 128:(sub + 1) * 128],
                            k.ident[:], [kdT, k.ident])
            S.copy("act", kdtok, kdtok[:, sub, :, :].rearrange("p h k -> p (h k)"), pT[:], [pT])
            yield

    def gen_B(T0):
        par = T0 % 2
        qdT, kdT, kdtok, ebl, vt, gs = qdT2[par], kdT2[par], kdtok2[par], ebl2[par], vt2[par], gs2[par]
        r0 = T0 * TT
        for sub in range(2):
            banks = [(k.ps[3], k.ps[4], k.ps[3]), (k.ps[5], k.ps[6], k.ps[5])]
            for hg in range(2):
                pA, pB, pC = banks[hg]
                hs = list(range(hg * 4, hg * 4 + 4))
                for i, h in enumerate(hs):
                    S.mm(pA, pA[:, i * 128:i * 128 + 64], kdT[:, h, sub * 128:(sub + 1) * 128],
                         qdT[:, h, sub, 0, 0:64], [kdT, qdT])
                    S.mm(pA, pA[:, i * 128 + 64:(i + 1) * 128], kdT[:, h, sub * 128:(sub + 1) * 128],
                         qdT[:, h, sub, 1, 64:128], [kdT, qdT])
                    S.mm(pB, pB[:, i * 128:(i + 1) * 128], kdtok[0:64, sub, h, :],
                         vt[0:64, sub, h * 128:(h + 1) * 128], [kdtok, vt])
                S.tt("dve", sc4s[hg], sc4s[hg][:].rearrange("p (i t) -> p i t", t=128),
                     pA[:].rearrange("p (i t) -> p i t", t=128),
                     cmask[:].unsqueeze(1).to_broadcast([128, 4, 128]), ALU.mult, [pA, cmask])
            yield
            for hg in range(2):
                pA, pB, pC = banks[hg]
                hs = list(range(hg * 4, hg * 4 + 4))
                c = sub * 2
                for i, h in enumerate(hs):
                    S.act(sts[h], sts[h][:], st32s[h][:], AF.Copy, [st32s[h], ebl], scale=ebl[:, h, c:c + 1])
                for i, h in enumerate(hs):
                    S.stt("dve", st1b[h], st1b[h][:], pB[:, i * 128:(i + 1) * 128], ebl[:, h, c:c + 1],
                          sts[h][:], ALU.mult, ALU.add, [pB, ebl, sts[h]])
                for i, h in enumerate(hs):
                    S.stt("dve", st1[h], st1[h][:], pB[:, i * 128:(i + 1) * 128], ebl[:, h, c:c + 1],
                          sts[h][:], ALU.mult, ALU.add, [pB, ebl, sts[h]])
            yield
            for hg in range(2):
                pA, pB, pC = banks[hg]
                hs = list(range(hg * 4, hg * 4 + 4))
                for i, h in enumerate(hs):
                    o_ap = pC[:, i * 128:(i + 1) * 128]
                    S.mm(pC, o_ap, sc4s[hg][:, i * 128:(i + 1) * 128], vt[:, sub, h * 128:(h + 1) * 128],
                         [sc4s[hg], vt], start=True, stop=False)
                    S.mm(pC, o_ap, qdT[:, h, sub, 0, :], stbs[h][:], [qdT, stbs[h]], start=False, stop=False)
                    S.mm(pC, o_ap, qdT[:, h, sub, 1, :], st1b[h][:], [qdT, st1b[h]], start=False, stop=True)
                for i, h in enumerate(hs):
                    S.mm(pB, pB[:, i * 128:(i + 1) * 128], kdtok[64:128, sub, h, :],
                         vt[64:128, sub, h * 128:(h + 1) * 128], [kdtok, vt])
            yield
            for hg in range(2):
                pA, pB, pC = banks[hg]
                hs = list(range(hg * 4, hg * 4 + 4))
                c = sub * 2 + 1
                for i, h in enumerate(hs):
                    S.act(sts2[h], sts2[h][:], st1[h][:], AF.Copy, [st1[h], ebl], scale=ebl[:, h, c:c + 1])
                for i, h in enumerate(hs):
                    S.stt("dve", stbs[h], stbs[h][:], pB[:, i * 128:(i + 1) * 128], ebl[:, h, c:c + 1],
                          sts2[h][:], ALU.mult, ALU.add, [pB, ebl, sts2[h]])
                for i, h in enumerate(hs):
                    S.stt("dve", st32s[h], st32s[h][:], pB[:, i * 128:(i + 1) * 128], ebl[:, h, c:c + 1],
                          sts2[h][:], ALU.mult, ALU.add, [pB, ebl, sts2[h]])
                S.copy("dve", otok, otok[:, hg * 512:(hg + 1) * 512], pC[:], [pC])
                for i, h in enumerate(hs):
                    S.act(sq, sq[:, 0:128], otok[:, h * 128:(h + 1) * 128], AF.Square, [otok],
                          accum_out=oss[:, h:h + 1], extra_writes=[oss])
            yield
            rstd_from_ss(S, orstd, oss, 128, 8)
            S.tt("dve", otok, otok[:].rearrange("p (h v) -> p h v", v=128),
                 otok[:].rearrange("p (h v) -> p h v", v=128),
                 orstd[:].unsqueeze(2).to_broadcast([128, 8, 128]), ALU.mult, [otok, orstd])
            S.tt("pool", on, on[:], otok[:], gs[:, sub, :], ALU.mult, [otok, gs])
            yield
            for kc in range(8):
                S.transpose(pT2, pT2[:, kc * 128:(kc + 1) * 128], on[:, kc * 128:(kc + 1) * 128],
                            k.ident[:], [on, k.ident])
            S.copy("act", oT, oT[:].rearrange("p c t -> p (c t)"), pT2[:], [pT2])
            yield
            xt = xts[(T0 * 2 + sub) % 4]
            xo_ = xo[sub]
            for half in range(2):
                py = pvs[half % 2]
                for kc in range(8):
                    S.mm(py, py[:], oT[:, kc, :], Wout[:, kc, half * 512:(half + 1) * 512],
                         [oT, Wout], start=(kc == 0), stop=(kc == 7))
                S.tt("dve", xo_, xo_[:, half * 512:(half + 1) * 512], py[:], xt[:, half * 512:(half + 1) * 512],
                     ALU.add, [py, xt])
            S.dma(None, xout[r0 + sub * 128:r0 + (sub + 1) * 128, :], xo_[:], in_t=xo_)
            yield

    for _ in gen_A(0):
        pass
    for T0 in range(ntile):
        if T0 + 1 < ntile:
            interleave(gen_B(T0), gen_A(T0 + 1), 1, 1)
        else:
            for _ in gen_B(T0):
                pass
    S.finish_wait("sp", xo)


def phase_ffn(S, k, l, xnorm, xres, xout, NT, fc0, fc1, final=False):
    TT = 512
    ntile = NT // TT
    nfc = fc1 - fc0
    W = nfc * 128
    same = xres is xnorm
    load_consts(S, k)
    alloc_psum(S, k, bf7=True)
    cols = rows_to_cols(S, k, [k.modd[l, :], k.norm_ffn[l, :]], "fcols")
    gcol = S.sbuf("gcol", [128, 8], F32)
    S.stt("dve", gcol, gcol[:], cols[:, 32:40], 1.0, cols[:, 48:56], ALU.add, ALU.mult, [cols])
    shc = S.sbuf("shc", [128, 8], F32)
    S.copy("dve", shc, shc[:], cols[:, 24:32], [cols])
    ccols = rows_to_cols(S, k, [k.ffn_conv_w[l, 0, :], k.ffn_conv_w[l, 1, :], k.ffn_conv_w[l, 2, :],
                                k.ffn_conv_b[l, :]], "ccols")
    sq = S.sbuf("sq", [128, 1024], F32)
    g2row = sq
    S.dma(g2row, g2row[:], k.modd[l, 5120:6144].partition_broadcast(128))
    if final:
        fnrow = S.sbuf("fnrow", [128, 1024], F32)
        S.dma(fnrow, fnrow[:], k.final_norm[:].partition_broadcast(128))
    Wup = S.sbuf("Wup", [128, 8, 2 * W], BF16)
    Wdn = S.sbuf("Wdn", [128, nfc, 1024], BF16)
    stg = [S.sbuf("stg%d" % i, [128, 1024], F32) for i in range(2)]
    load_weight_bf16(S, Wup, k.ffn_w_up[l][:, fc0 * 128:fc1 * 128], 8, W, stg)
    load_weight_bf16(S, Wup, k.ffn_w_up[l][:, DFF + fc0 * 128:DFF + fc1 * 128], 8, W, stg, col_off=W)
    load_weight_bf16(S, Wdn, k.ffn_w_down[l][fc0 * 128:fc1 * 128, :], nfc, 1024, stg, grow=g2row)
    xts = [S.sbuf("xt%d" % i, [128, 1024], F32) for i in range(2)]
    xrs = [S.sbuf("xr%d" % i, [128, 1024], F32) for i in range(2)]
    xn = S.sbuf("xn", [128, 1024], BF16)
    ss = S.sbuf("ss", [128, 8], F32)
    rstd = S.sbuf("rstd", [128, 8], F32)
    hTs = [S.sbuf("hT%d" % i, [128, 8, TT], BF16) for i in range(2)]
    halo = S.sbuf("halo", [128, nfc, 2], F32)
    S.memset("pool", halo, halo[:], 0.0)
    abuf = [S.sbuf("abuf%d" % i, [128, TT + 2], F32) for i in range(3)]
    c1 = [S.sbuf("c1_%d" % i, [128, TT], F32) for i in range(3)]
    c2 = [S.sbuf("c2_%d" % i, [128, TT], F32) for i in range(3)]
    mTs = [S.sbuf("mT%d" % i, [128, nfc, TT], BF16) for i in range(2)]
    xo = [S.sbuf("xo%d" % i, [128, 1024], F32) for i in range(2)]
    pT = k.ps[0]
    ncnt = [0]

    xns = [xn] + [S.sbuf("xn%d" % i, [128, 1024], BF16) for i in range(1, 4)]
    sss = [S.sbuf("ss%d" % i, [128, 2], F32) for i in range(4)]
    rss = [S.sbuf("rs%d" % i, [128, 2], F32) for i in range(4)]
    pTs = [k.ps[0], k.ps[7]]

    def norm_a(T0, sub):
        i = sub
        xt = xts[sub % 2]
        r = T0 * TT + sub * 128
        S.dma(xt, xt[:], xnorm[r:r + 128, :])
        S.act(sq, sq[:], xt[:], AF.Square, [xt], accum_out=sss[i][:, 0:1], extra_writes=[sss[i]])
        rstd_from_ss(S, rss[i], sss[i], D, 1)
        S.act(xns[i], xns[i][:], xt[:], AF.Copy, [xt, rss[i]], scale=rss[i][:, 0:1])

    def norm_b(T0, sub):
        i = sub
        hT_ = hTs[T0 % 2]
        pT_ = pTs[sub % 2]
        for kc in range(8):
            S.transpose(pT_, pT_[:, kc * 128:(kc + 1) * 128], xns[i][:, kc * 128:(kc + 1) * 128],
                        k.ident[:], [xns[i], k.ident])
        for kc in range(8):
            S.act(hT_, hT_[:, kc, sub * 128:(sub + 1) * 128], pT_[:, kc * 128:(kc + 1) * 128], AF.Identity,
                  [pT_, gcol, shc], scale=gcol[:, kc:kc + 1], bias=shc[:, kc:kc + 1])

    for sub in range(4):
        norm_a(0, sub)
        norm_b(0, sub)
    def down_proj(T0):
        r0 = T0 * TT
        mT = mTs[T0 % 2]
        for sub in range(4):
            xt = xrs[sub % 2]
            S.dma(xt, xt[:], xres[r0 + sub * 128:r0 + (sub + 1) * 128, :])
            xo_ = xo[sub % 2]
            for half in range(2):
                py = k.ps[1 + half]
                for j in range(nfc):
                    S.mm(py, py[:], mT[:, j, sub * 128:(sub + 1) * 128], Wdn[:, j, half * 512:(half + 1) * 512],
                         [mT, Wdn], start=(j == 0), stop=(j == nfc - 1))
                S.tt("dve", xo_, xo_[:, half * 512:(half + 1) * 512], py[:], xt[:, half * 512:(half + 1) * 512],
                     ALU.add, [py, xt])
            if final:
                S.act(sq, sq[:], xo_[:], AF.Square, [xo_], accum_out=ss[:, 1:2], extra_writes=[ss])
                rstd_from_ss(S, rstd, ss[:, 1:2] if False else ss, D, 2)
                S.stt("dve", xo_, xo_[:], xo_[:], rstd[:, 1:2], fnrow[:], ALU.mult, ALU.mult, [xo_, rstd, fnrow])
            S.dma(None, xout[r0 + sub * 128:r0 + (sub + 1) * 128, :], xo_[:], in_t=xo_)

    for T0 in range(ntile):
        r0 = T0 * TT
        hT = hTs[T0 % 2]
        mT = mTs[T0 % 2]
        def st_A(j):
            ab = abuf[j % 3]
            pa = k.ps[1 + (j % 2)]
            fc = fc0 + j
            S.copy("pool", ab, ab[:, 0:2], halo[:, j, :], [halo])
            S.copy("act", ab, ab[:, 2:TT + 2], pa[:], [pa])
            S.copy("pool", halo, halo[:, j, :], ab[:, TT:TT + 2], [ab])
            S.ts("pool", c1[j % 3], c1[j % 3][:], ab[:, 2:TT + 2], ccols[:, 44 + fc:45 + fc],
                 ccols[:, 66 + fc:67 + fc], ALU.mult, ALU.add, reads=[ab, ccols])

        def st_B(j):
            ab = abuf[j % 3]
            fc = fc0 + j
            c1_, c2_ = c1[j % 3], c2[j % 3]
            S.stt("dve", c2_, c2_[:], ab[:, 1:TT + 1], ccols[:, 22 + fc:23 + fc], c1_[:], ALU.mult, ALU.add,
                  [ab, ccols, c1_])
            S.stt("dve", c1_, c1_[:], ab[:, 0:TT], ccols[:, fc:fc + 1], c2_[:], ALU.mult, ALU.add,
                  [ab, ccols, c2_])

        def st_C(j):
            S.act(c2[j % 3], c2[j % 3][:], c1[j % 3][:], AF.Silu, [c1[j % 3]])

        def st_D(j):
            pv = k.ps[3 + (j % 4)]
            S.tt("dve", mT, mT[:, j, :], pv[:], c2[j % 3][:], ALU.mult, [pv, c2[j % 3]])

        for j in range(nfc + 2):
            if j < nfc:
                pa = k.ps[1 + (j % 2)]
                pv = k.ps[3 + (j % 4)]
                for kc in range(8):
                    S.mm(pa, pa[:], Wup[:, kc, j * 128:(j + 1) * 128], hT[:, kc, :], [Wup, hT],
                         start=(kc == 0), stop=(kc == 7))
                for kc in range(8):
                    S.mm(pv, pv[:], Wup[:, kc, W + j * 128:W + (j + 1) * 128], hT[:, kc, :], [Wup, hT],
                         start=(kc == 0), stop=(kc == 7))
                st_A(j)
            if 0 <= j - 1 < nfc:
                st_B(j - 1)
                st_C(j - 1)
            if 0 <= j - 2 < nfc:
                st_D(j - 2)
            if T0 + 1 < ntile:
                if j == 0:
                    for sub_ in range(4):
                        norm_a(T0 + 1, sub_)
                if j in (3, 5, 7, 9):
                    norm_b(T0 + 1, (j - 3) // 2)
            if j == 1 and T0 > 0:
                down_proj(T0 - 1)
    down_proj(ntile - 1)
    S.finish_wait("sp", xo)


def build_program(NT, phases=("pro", "hg", "ffn0", "nsa", "ffn1")):
    nc = bass.Bass("TRN2", target_bir_lowering=False)
    k = K()

    def inp(name, shape):
        return nc.dram_tensor(name, list(shape), F32, kind="ExternalInput").ap()

    k.x = inp("x", [NT, D])
    k.c_col = inp("c_col", [128, 8])
    k.ada_w = inp("ada_w", [2, D, 6 * D])
    k.ada_b = inp("ada_b", [2, 6 * D])
    k.norm_mix = inp("norm_mix", [2, D])
    k.norm_ffn = inp("norm_ffn", [2, D])
    k.final_norm = inp("final_norm", [D])
    k.hg_w_in = inp("hg_w_in", [1, D, 4096])
    k.hg_w_out = inp("hg_w_out", [1, D, D])
    k.hg_gnorm = inp("hg_gnorm", [1, 128])
    k.hg_lb = inp("hg_lb", [2, D])
    k.ffn_w_up = inp("ffn_w_up", [2, D, 2 * DFF])
    k.ffn_conv_w = inp("ffn_conv_w", [2, 3, DFF])
    k.ffn_conv_b = inp("ffn_conv_b", [2, DFF])
    k.ffn_w_down = inp("ffn_w_down", [2, DFF, D])
    k.c_ident = inp("c_ident", [128, 128])
    k.c_bdmask = inp("c_bdmask", [128, 128])
    nsa_declare(nc, k, NT)
    k.out = nc.dram_tensor("out", [NT, D], F32, kind="ExternalOutput").ap()
    k.modd = nc.dram_tensor("modd", [2, 6 * D], F32).ap()
    bufs = [nc.dram_tensor("xs%d" % i, [NT, D], F32).ap() for i in range(4)]
    xc = nc.dram_tensor("xc", [NT, D], F32).ap()
    main = [p for p in phases if p != "pro"]
    with contextlib.ExitStack() as gst:
        S = Sched(nc, gst)
        k.S = S
        plist = []
        if "pro" in phases:
            plist.append(lambda: (alloc_psum(S, k), phase_prologue(S, k)))
        src = k.x
        for i, p in enumerate(main):
            last = (i == len(main) - 1)
            dst = k.out if last else bufs[i]
            if p == "hg":
                plist.append(lambda src=src, dst=dst: phase_hgrn(S, k, 0, src, dst, NT))
            elif p in ("ffn0", "ffn1"):
                l = int(p[3])
                fin = (p == "ffn1")
                plist.append(lambda src=src, l=l: phase_ffn(S, k, l, src, src, xc, NT, 0, 11))
                plist.append(lambda src=src, dst=dst, l=l, fin=fin: phase_ffn(S, k, l, src, xc, dst, NT, 11, 22, final=fin))
            elif p == "nsa":
                import os
                stop = int(os.environ.get("NSA_STOP", "4"))
                plist.append(lambda src=src: phase_nsa_proj(S, k, 1, src, NT))
                if stop >= 2:
                    plist.append(lambda: phase_nsa_cmp(S, k, NT))
                if stop >= 3:
                    plist.append(lambda: phase_nsa_attn(S, k, NT))
                if stop >= 4:
                    plist.append(lambda src=src, dst=dst: phase_nsa_out(S, k, 1, src, dst, NT))
            src = dst
        for i, p in enumerate(plist):
            with contextlib.ExitStack() as st:
                S.stack = st
                p()
                S.barrier()
                S.emit()
                S.phase_end()
    return nc


def host_consts():
    ident = np.eye(128, dtype=np.float32)
    s = np.arange(128)[:, None]
    t = np.arange(128)[None, :]
    bd = ((s // 64 == t // 64) & (s <= t)).astype(np.float32)
    return {"c_ident": ident, "c_bdmask": bd}


NSA_W = 2608
POW_RECIP = False
SLOPES = [2.0 ** (-8.0 * (h + 1) / 16) for h in range(16)]


def nsa_dims(NT):
    ncb = (NT - 32) // 16 + 1
    nsb = NT // 64
    return ncb, nsb


def nsa_host_consts(NT):
    import ml_dtypes
    bf = ml_dtypes.bfloat16
    ncb, nsb = nsa_dims(NT)
    t = np.arange(NT)
    c = {}
    c["c_vrow"] = np.stack([-SLOPES[h] * t for h in range(16)]).astype(bf)
    KT = NT // 128
    i = np.arange(128)
    sb = np.zeros((128, KT * 16), np.float32)
    for kt in range(KT):
        for h in range(16):
            sb[:, kt * 16 + h] = SLOPES[h] * (128 * kt + i)
    c["c_sbias"] = sb
    cb = np.zeros((128, 2 * 16), np.float32)
    for bt in range(2):
        for h in range(16):
            cb[:, bt * 16 + h] = SLOPES[h] * (16 * (128 * bt + i) + 15.5)
    c["c_cbias"] = cb
    c["c_maskd"] = np.where(i[:, None] > i[None, :], NEG, 0.0).astype(bf)
    c["c_maskw"] = np.where(i[None, :] >= i[:, None], NEG, 0.0).astype(bf)
    QT = NT // 512
    cm = np.zeros((QT, 2, 128, 512), np.float32)
    for T in range(QT):
        for bt in range(2):
            n = 128 * bt + i
            tt = 512 * T + np.arange(512)
            cm[T, bt] = np.where(16 * n[:, None] + 31 <= tt[None, :], 0.0, NEG)
    c["c_cmask"] = cm.astype(bf)
    ov = np.zeros((256, 64), np.float32)
    ci = np.arange(256)[:, None] * 16
    sj = np.arange(64)[None, :] * 64
    ov[:] = ((ci <= sj + 63) & (ci + 31 >= sj))
    ov[ncb:] = 0
    ov[:, nsb:] = 0
    c["c_overlap"] = ov.astype(bf)
    blk = np.arange(64)[None, :]
    cur = (t // 64)[:, None]
    valid = blk * 64 <= t[:, None]
    forced = ((blk == 0) | (blk == cur) | (blk == cur - 1)) & valid
    vm = valid.astype(np.float32)
    am = np.where(forced, 1e4, np.where(valid, 0.0, -1.0)).astype(np.float32)
    vm[:, nsb:] = 0.0
    am[:, nsb:] = -1.0
    c["c_vmask"] = vm
    c["c_amask"] = am
    si = np.zeros((64, NT), np.float32)
    si[0] = 1.0
    for j in range(1, 64):
        si[j] = (t // 64 == j)
    c["c_selind"] = si.astype(bf)
    kw = np.zeros((64, NT), np.float32)
    kw[0] = 1.0
    c["c_kwrows"] = kw.astype(bf)
    return c


def nsa_declare(nc, k, NT):
    def inp(name, shape, dt=F32):
        return nc.dram_tensor(name, list(shape), dt, kind="ExternalInput").ap()
    QT = NT // 512
    KT = NT // 128
    k.nsa_w_in = inp("nsa_w_in", [1, D, NSA_W])
    k.nsa_w_out = inp("nsa_w_out", [1, D, D])
    k.nsa_cmp_pe = inp("nsa_cmp_pe", [1, 2, 32, 64])
    k.nsa_cmp_w1 = inp("nsa_cmp_w1", [1, 2, 2048, 64])
    k.nsa_cmp_w2 = inp("nsa_cmp_w2", [1, 2, 64, 64])
    k.c_vrow = inp("c_vrow", [16, NT], BF16)
    k.c_sbias = inp("c_sbias", [128, KT * 16])
    k.c_cbias = inp("c_cbias", [128, 32])
    k.c_maskd = inp("c_maskd", [128, 128], BF16)
    k.c_maskw = inp("c_maskw", [128, 128], BF16)
    k.c_cmask = inp("c_cmask", [QT, 2, 128, 512], BF16)
    k.c_overlap = inp("c_overlap", [256, 64], BF16)
    k.c_vmask = inp("c_vmask", [NT, 64])
    k.c_amask = inp("c_amask", [NT, 64])
    k.c_selind = inp("c_selind", [64, NT], BF16)
    k.c_kwrows = inp("c_kwrows", [64, NT], BF16)
    k.qT_d = nc.dram_tensor("qT_d", [1024, NT], BF16).ap()
    k.kT_d = nc.dram_tensor("kT_d", [4, 256, NT], BF16).ap()
    k.vtok_d = nc.dram_tensor("vtok_d", [2, NT, 256], BF16).ap()
    k.gT_d = nc.dram_tensor("gT_d", [48, NT], F32).ap()
    k.kcT_d = nc.dram_tensor("kcT_d", [4, 64, 256], BF16).ap()
    k.vc_d = nc.dram_tensor("vc_d", [4, 256, 64], BF16).ap()
    k.oT_d = nc.dram_tensor("oT_d", [1024, NT], BF16).ap()


def phase_nsa_proj(S, k, l, xin, NT):
    TT = 512
    ntile = NT // TT
    load_consts(S, k)
    alloc_psum(S, k)
    cols = rows_to_cols(S, k, [k.modd[l, :], k.norm_mix[l, :]], "ncols")
    gcol = S.sbuf("gcol", [128, 8], F32)
    S.stt("dve", gcol, gcol[:], cols[:, 8:16], 1.0, cols[:, 48:56], ALU.add, ALU.mult, [cols])
    W = S.sbuf("Wn", [128, 8, NSA_W], BF16)
    stg = [S.sbuf("stg%d" % i, [128, 1024], F32) for i in range(3)]
    load_weight_bf16(S, W, k.nsa_w_in[0], 8, NSA_W, stg)
    xts = [S.sbuf("xt%d" % i, [128, 1024], F32) for i in range(2)]
    xn = S.sbuf("xn", [128, 1024], BF16)
    sq = S.sbuf("sq", [128, 1024], F32)
    ss = S.sbuf("ss", [128, 8], F32)
    rstd = S.sbuf("rstd", [128, 8], F32)
    hT = S.sbuf("hT", [128, 8, TT], BF16)
    fsb = [S.sbuf("fsb%d" % i, [128, TT], BF16) for i in range(3)]
    vsb = [S.sbuf("vsb%d" % i, [128, 512], BF16) for i in range(2)]
    gsb = [S.sbuf("gsb%d" % i, [48, TT], F32) for i in range(2)]
    pT = k.ps[0]
    outs = fsb + vsb + gsb
    n = 0
    for T0 in range(ntile):
        r0 = T0 * TT
        for sub in range(4):
            xt = xts[sub % 2]
            S.dma(xt, xt[:], xin[r0 + sub * 128:r0 + (sub + 1) * 128, :])
            norm_to_hT(S, k, xt, xt[:], gcol, cols, hT, sub * 128, xn, sq, ss, rstd, pT)
        fm = [(hp * 128, ("q", hp)) for hp in range(8)]
        for kind, c0 in ((0, 1024), (1, 1280), (2, 1536), (3, 2048)):
            for cpart in range(2):
                fm.append((c0 + cpart * 128, ("k", kind, cpart)))
        for c0, tag in fm:
            ps = k.ps[1 + (n % 3)]
            f = fsb[n % 3]
            n += 1
            for kc in range(8):
                S.mm(ps, ps[:], W[:, kc, c0:c0 + 128], hT[:, kc, :], [W, hT], start=(kc == 0), stop=(kc == 7))
            if tag[0] == "q":
                S.act(f, f[:], ps[:], AF.Copy, [ps], scale=0.125)
                S.dma(None, k.qT_d[tag[1] * 128:(tag[1] + 1) * 128, r0:r0 + TT], f[:], in_t=f)
            else:
                S.copy("dve", f, f[:], ps[:], [ps])
                S.dma(None, k.kT_d[tag[1], tag[2] * 128:(tag[2] + 1) * 128, r0:r0 + TT], f[:], in_t=f)
        for sub in range(4):
            ps = k.ps[4 + (sub % 2)]
            v = vsb[sub % 2]
            for j, c0 in enumerate((1792, 2304)):
                for kc in range(8):
                    S.mm(ps, ps[:, j * 256:(j + 1) * 256], hT[:, kc, sub * 128:(sub + 1) * 128],
                         W[:, kc, c0:c0 + 256], [hT, W], start=(kc == 0), stop=(kc == 7))
            S.copy("dve", v, v[:], ps[:], [ps])
            for j in range(2):
                S.dma(None, k.vtok_d[j, r0 + sub * 128:r0 + (sub + 1) * 128, :], v[:, j * 256:(j + 1) * 256], in_t=v)
        ps = k.ps[6]
        g_ = gsb[T0 % 2]
        for kc in range(8):
            S.mm(ps, ps[0:48, :], W[:, kc, 2560:2608], hT[:, kc, :], [W, hT], start=(kc == 0), stop=(kc == 7))
        S.act(g_, g_[:], ps[0:48, :], AF.Sigmoid, [ps])
        S.dma(None, k.gT_d[:, r0:r0 + TT], g_[:], in_t=g_)
    S.finish_wait("sp", outs)


def phase_nsa_cmp(S, k, NT):
    ncb, nsb = nsa_dims(NT)
    load_consts(S, k)
    alloc_psum(S, k)
    xc = S.sbuf("xc", [64, 2, 4, NT], BF16)
    for kv in range(2):
        for g in range(4):
            S.dma(xc, xc[:, kv, g, :], k.kT_d[kv, g * 64:(g + 1) * 64, :])
    w1f = S.sbuf("w1f", [64, 2, 32, 64], F32)
    w1 = S.sbuf("w1", [64, 2, 32, 64], BF16)
    for kv in range(2):
        S.dma(w1f, w1f[:, kv, :, :], k.nsa_cmp_w1[0, kv].rearrange("(l d) e -> d l e", d=64))
    S.copy("pool", w1, w1[:], w1f[:], [w1f])
    w2f = S.sbuf("w2f", [64, 2, 64], F32)
    w2p = S.sbuf("w2p", [64, 2, 128], BF16)
    for kv in range(2):
        S.dma(w2f, w2f[:, kv, :], k.nsa_cmp_w2[0, kv])
    S.memset("pool", w2p, w2p[:], 0.0)
    S.copy("pool", w2p, w2p[:, :, 64:128], w2f[:], [w2f])
    pef = S.sbuf("pef", [32, 2, 64], F32)
    for kv in range(2):
        S.dma(pef, pef[:, kv, :], k.nsa_cmp_pe[0, kv])
    peT = S.sbuf("peT", [64, 2, 32], BF16)
    ps = k.ps[1]
    for kv in range(2):
        S.mm(ps, ps[0:64, kv * 32:(kv + 1) * 32], pef[:, kv, :], k.identf[0:32, 0:32], [pef, k.identf])
    S.copy("dve", peT, peT[:].rearrange("p a l -> p (a l)"), ps[0:64, 0:64], [ps])
    bias = S.sbuf("cbias", [64, 2], F32)
    ps = k.ps[2]
    for kv in range(2):
        for l in range(32):
            S.mm(ps, ps[0:64, kv:kv + 1], w1[:, kv, l, :], peT[:, kv, l:l + 1], [w1, peT],
                 start=(l == 0), stop=(l == 31))
    S.copy("dve", bias, bias[:], ps[0:64, 0:2], [ps])
    hid = [S.sbuf("hid%d" % i, [64, 256], BF16) for i in range(2)]
    osb = [S.sbuf("osb%d" % i, [128, 256], BF16) for i in range(2)]
    n = 0
    for kv in range(2):
        for g in range(4):
            ph = k.ps[3 + (n % 2)]
            hd = hid[n % 2]
            ob = osb[n % 2]
            for l in range(32):
                rhs = xc[:, kv, g, l:l + 16 * (ncb - 1) + 1:16]
                S.mm(ph, ph[0:64, 0:ncb], w1[:, kv, l, :], rhs, [w1, xc], start=(l == 0), stop=(l == 31))
            S.act(hd, hd[:, 0:ncb], ph[0:64, 0:ncb], AF.Silu, [ph, bias], bias=bias[:, kv:kv + 1])
            po = k.ps[5 + (n % 2)]
            if kv == 0:
                S.mm(po, po[:, 0:ncb], w2p[:, 0, :], hd[:, 0:ncb], [w2p, hd])
                S.copy("dve", ob, ob[64:128, 0:ncb], po[64:128, 0:ncb], [po])
                S.dma(None, k.kcT_d[g, :, 0:ncb], ob[64:128, 0:ncb], in_t=ob)
            else:
                for bt in range((ncb + 127) // 128):
                    nb = min(128, ncb - bt * 128)
                    S.mm(po, po[0:nb, bt * 64:(bt + 1) * 64], hd[:, bt * 128:bt * 128 + nb], w2p[:, 1, 64:128],
                         [hd, w2p])
                    S.copy("dve", ob, ob[0:nb, bt * 64:(bt + 1) * 64], po[0:nb, bt * 64:(bt + 1) * 64], [po])
                    S.dma(None, k.vc_d[g, bt * 128:bt * 128 + nb, :], ob[0:nb, bt * 64:(bt + 1) * 64], in_t=ob)
            n += 1
    S.finish_wait("sp", osb)


def phase_nsa_attn(S, k, NT):
    ncb, nsb = nsa_dims(NT)
    QT = NT // 512
    KT = NT // 128
    NBT = (ncb + 127) // 128
    load_consts(S, k)
    alloc_psum(S, k)
    sbias = S.sbuf("sbias", [128, KT * 16], F32)
    S.dma(sbias, sbias[:], k.c_sbias[:, :])
    cbias = S.sbuf("cbias", [128, 32], F32)
    S.dma(cbias, cbias[:], k.c_cbias[:, :])
    maskd = S.sbuf("maskd", [128, 128], BF16)
    S.dma(maskd, maskd[:], k.c_maskd[:, :])
    maskw = S.sbuf("maskw", [128, 128], BF16)
    S.dma(maskw, maskw[:], k.c_maskw[:, :])
    ovl = S.sbuf("ovl", [128, 2, 64], BF16)
    S.dma(ovl, ovl[:], k.c_overlap.rearrange("(bt p) j -> p bt j", p=128))
    Ks = S.sbuf("Ks", [128, NT], BF16)
    Kw = S.sbuf("Kw", [128, NT], BF16)
    Kc = S.sbuf("Kc", [128, 256], BF16)
    Vs = S.sbuf("Vs", [128, KT, 128], BF16)
    Vw = S.sbuf("Vw", [128, KT, 128], BF16)
    Vc = S.sbuf("Vc", [128, 2, 128], BF16)
    S.memset("pool", Vs, Vs[:], 1.0)
    S.memset("pool", Vw, Vw[:], 1.0)
    S.memset("pool", Vc, Vc[:], 1.0)
    S.memset("pool", Kc, Kc[:], 0.0)
    S.dma(Ks, Ks[0:64, :], k.c_selind[:, :])
    S.dma(Kw, Kw[0:64, :], k.c_kwrows[:, :])
    S.dma(Kc, Kc[0:64, :], k.c_kwrows[:, 0:256])
    QaT = [S.sbuf("Qa%d" % i, [128, 4, 512], BF16) for i in range(2)]
    Qav = [[S.view("Qa%d_%d" % (i, hh), QaT[i].ap[:, hh, :]) for hh in range(4)] for i in range(2)]
    for q in QaT:
        S.memset("pool", q, q[:], 0.0)
    for i in range(2):
        for hh in range(4):
            Qav[i][hh].last_w = QaT[i].last_w
    gbT = [S.sbuf("gb%d" % i, [128, 12, 512], F32) for i in range(2)]
    Pt = [S.sbuf("Pt%d" % i, [128, 512], BF16) for i in range(6)]
    cmk = [S.sbuf("cmk%d" % i, [128, 512], BF16) for i in range(2)]
    rd = [S.sbuf("rd%d" % i, [128, 512], F32) for i in range(2)]
    coef = [S.sbuf("coef%d" % i, [128, 512], F32) for i in range(2)]
    tmp = [S.sbuf("tmp%d" % i, [128, 512], F32) for i in range(2)]
    oacc = [S.sbuf("oacc%d" % i, [128, 512], F32) for i in range(4)]
    osbT = [S.sbuf("osb%d" % i, [128, 4, 512], BF16) for i in range(2)]
    impacc = S.sbuf("impacc", [128, 512], F32)
    negone = S.sbuf("negone", [128, 512], F32)
    S.memset("pool", negone, negone[:], -1.0)
    vm = S.sbuf("vm", [128, 4, 64], F32)
    am = S.sbuf("am", [128, 4, 64], F32)
    sc = S.sbuf("sc", [128, 4, 64], F32)
    sc2 = S.sbuf("sc2", [128, 64], F32)
    mx = S.sbuf("mx", [128, 8], F32)
    thr = S.sbuf("thr", [128, 4], F32)
    selb = S.sbuf("selb", [128, 4, 64], BF16)
    pT = k.ps[0]
    pSs = [k.ps[1], k.ps[2], k.ps[6], k.ps[5]]
    pOs = [k.ps[3], k.ps[7]]
    pI, pTk = k.ps[4], k.ps[5]
    cnt = {"s": 0, "p": 0, "r": 0, "cm": 0, "o": 0, "po": 0}
    NP = len(Pt)
    LOOK = 3
    pend = []
    cur = {}

    def finish_branch(pO, hh, br, first, want_imp):
        i = cnt["r"] % 2
        cnt["r"] += 1
        r_, c_, t_ = rd[i], coef[i], tmp[i]
        gbt = cur["gb"]
        if POW_RECIP:
            S.ts("dve", t_, t_[64:128, :], pO[64:128, :], 1e-30, None, ALU.add, reads=[pO])
            S.tt("pool", r_, r_[64:128, :], t_[64:128, :], negone[64:128, :], ALU.pow, [t_, negone])
        else:
            S.act(r_, r_[64:128, :], pO[64:128, :], AF.Ln, [pO], bias=(1e-30 if br == 0 else 0.0))
            S.act(r_, r_[64:128, :], r_[64:128, :], AF.Exp, [r_], scale=-1.0)
        S.tt("dve", c_, c_[64:128, :], r_[64:128, :], gbt[64:128, 3 * hh + br, :], ALU.mult, [r_, gbt])
        if first:
            S.tt("dve", oacc[hh], oacc[hh][0:64, :], pO[0:64, :], c_[64:128, :], ALU.mult, [pO, c_])
        else:
            S.tt("dve", t_, t_[0:64, :], pO[0:64, :], c_[64:128, :], ALU.mult, [pO, c_])
            S.tt("pool", oacc[hh], oacc[hh][0:64, :], oacc[hh][0:64, :], t_[0:64, :], ALU.add, [oacc[hh], t_])
        if want_imp:
            if hh == 0:
                S.tt("dve", impacc, impacc[0:64, :], pI[0:64, :], r_[64:128, :], ALU.mult, [pI, r_])
            else:
                S.tt("dve", t_, t_[0:64, :], pI[0:64, :], r_[64:128, :], ALU.mult, [pI, r_])
                S.tt("pool", impacc, impacc[0:64, :], impacc[0:64, :], t_[0:64, :], ALU.add, [impacc, t_])

    def emit_pv(item):
        (pO, lhsV, P, np_, clo, chi, first, last, ovl_ap, cb, vt_) = item
        S.mm(pO, pO[:, clo:chi], lhsV, P[0:np_, clo:chi], [vt_, P], start=first, stop=last)
        if ovl_ap is not None:
            S.mm(pI, pI[0:64, clo:chi], ovl_ap, P[0:np_, clo:chi], [ovl, P], start=first, stop=last)
        if cb is not None:
            cb()

    def push(item):
        pend.append(item)
        while len(pend) > LOOK:
            emit_pv(pend.pop(0))

    def flush():
        while pend:
            emit_pv(pend.pop(0))

    def attend(hh, h, Kt, Vt, tiles, br, first_branch):
        pO = pOs[cnt["po"] % 2]
        cnt["po"] += 1
        Qh = cur["Qa"][hh]
        for idx, (kt, clo, chi, masks) in enumerate(tiles):
            pS = pSs[cnt["s"] % len(pSs)]
            cnt["s"] += 1
            S.mm(pS, pS[:, clo:chi], Kt[:, kt * 128:(kt + 1) * 128], Qh[:, clo:chi], [Kt, Qh],
                 start=True, stop=(len(masks) == 0))
            for mi, (mk, c0) in enumerate(masks):
                S.mm(pS, pS[:, c0:c0 + 128], k.ident[:], mk[:], [k.ident, mk], start=False,
                     stop=(mi == len(masks) - 1))
            P = Pt[cnt["p"] % NP]
            cnt["p"] += 1
            S.act(P, P[:, clo:chi], pS[:, clo:chi], AF.Exp, [pS, sbias], bias=sbias[:, kt * 16 + h:kt * 16 + h + 1])
            last = (idx == len(tiles) - 1)
            cb = (lambda pO=pO, hh=hh, br=br, fb=first_branch: finish_branch(pO, hh, br, fb, False)) if last else None
            push((pO, Vt[:, kt, :], P, 128, clo, chi, idx == 0, last, None, cb, Vt))

    units = [(g, T) for g in range(4) for T in range(QT)]

    def load_unit(ui):
        g, T = units[ui]
        T0 = 512 * T
        par = ui % 2
        S.op("sp", lambda e: e.dma_start(out=gbT[par][64:128, :, :],
                                         in_=k.gT_d[12 * g:12 * g + 12, T0:T0 + 512].partition_broadcast(64)),
             [], [gbT[par]], dma_sem_tile=gbT[par])
        S.op("sp", lambda e: e.dma_start(out=QaT[par][64:128, :, :],
                                         in_=k.qT_d[256 * g:256 * g + 256, T0:T0 + 512].rearrange("(hh d) t -> d hh t", d=64)),
             [], Qav[par], dma_sem_tile=Qav[par][0])
        S.op("sp", lambda e: e.dma_start(out=QaT[par][0:1, :, :], in_=k.c_vrow[4 * g:4 * g + 4, T0:T0 + 512].rearrange("(o h) t -> o h t", o=1)),
             [], Qav[par], dma_sem_tile=Qav[par][0])

    def load_group(g):
        S.dma(Ks, Ks[64:128, :], k.kT_d[2, g * 64:(g + 1) * 64, :])
        S.dma(Kw, Kw[64:128, :], k.kT_d[3, g * 64:(g + 1) * 64, :])
        S.dma(Kc, Kc[64:128, 0:ncb], k.kcT_d[g, :, 0:ncb])
        for k0 in range(0, KT, 8):
            k1 = min(KT, k0 + 8)
            S.dma(Vs, Vs[:, k0:k1, 0:64],
                  k.vtok_d[0][k0 * 128:k1 * 128, g * 64:(g + 1) * 64].rearrange("(kt p) d -> p kt d", p=128))
            S.dma(Vw, Vw[:, k0:k1, 0:64],
                  k.vtok_d[1][k0 * 128:k1 * 128, g * 64:(g + 1) * 64].rearrange("(kt p) d -> p kt d", p=128))
        for bt in range(NBT):
            nb = min(128, ncb - bt * 128)
            S.dma(Vc, Vc[0:nb, bt, 0:64], k.vc_d[g, bt * 128:bt * 128 + nb, :])

    load_unit(0)
    for ui, (g, T) in enumerate(units):
        if T == 0:
            load_group(g)
        T0 = 512 * T
        par = ui % 2
        cur["Qa"] = Qav[par]
        cur["gb"] = gbT[par]
        Qa = Qav[par]
        bts = []
        for bt in range(NBT):
            nb = min(128, ncb - bt * 128)
            n_lo, n_hi = 128 * bt, 128 * bt + nb - 1
            if 16 * n_lo + 31 > T0 + 511:
                continue
            partial = 16 * n_hi + 31 > T0
            bts.append((bt, nb, partial))
        cms = {}
        for (bt, nb, partial) in bts:
            if partial:
                cm_ = cmk[cnt["cm"] % 2]
                cnt["cm"] += 1
                S.dma(cm_, cm_[:], k.c_cmask[T, bt])
                cms[bt] = cm_
        S.dma(vm, vm[:], k.c_vmask[T0:T0 + 512, :].rearrange("(n p) j -> p n j", p=128))
        S.dma(am, am[:], k.c_amask[T0:T0 + 512, :].rearrange("(n p) j -> p n j", p=128))
        for hh in range(4):
            h = 4 * g + hh
            pO = pOs[cnt["po"] % 2]
            cnt["po"] += 1
            for bi, (bt, nb, partial) in enumerate(bts):
                pS = pSs[cnt["s"] % len(pSs)]
                cnt["s"] += 1
                S.mm(pS, pS[0:nb, :], Kc[:, bt * 128:bt * 128 + nb], Qa[hh][:, :], [Kc, Qa[hh]],
                     start=True, stop=(not partial))
                if partial:
                    cm_ = cms[bt]
                    S.mm(pS, pS[0:nb, :], k.ident[0:nb, 0:nb], cm_[0:nb, :], [k.ident, cm_], start=False, stop=True)
                P = Pt[cnt["p"] % NP]
                cnt["p"] += 1
                S.act(P, P[0:nb, :], pS[0:nb, :], AF.Exp, [pS, cbias],
                      bias=cbias[0:nb, bt * 16 + h:bt * 16 + h + 1])
                last = (bi == len(bts) - 1)
                cb = (lambda pO=pO, hh=hh: finish_branch(pO, hh, 0, True, True)) if last else None
                push((pO, Vc[0:nb, bt, :], P, nb, 0, 512, bi == 0, last, ovl[0:nb, bt, :], cb, Vc))
        for hh in range(4):
            h = 4 * g + hh
            tiles = []
            for kt in range(max(0, 4 * T - 4), 4 * T + 4):
                m = kt - 4 * T
                n_lo, n_hi = max(m, 0), min(m + 4, 3)
                masks = []
                if m >= 0:
                    masks.append((maskd, 128 * m))
                if m <= -1:
                    masks.append((maskw, 128 * (m + 4)))
                tiles.append((kt, 128 * n_lo, 128 * (n_hi + 1), masks))
            tiles.sort(key=lambda tl: -(tl[2] - tl[1]))
            attend(hh, h, Kw, Vw, tiles, 2, False)
            if hh == 1:
                for n in range(4):
                    S.mm(pTk, pTk[:, n * 64:(n + 1) * 64], impacc[0:64, n * 128:(n + 1) * 128],
                         k.identf[0:64, 0:64], [impacc, k.identf])
                S.tt("dve", sc, sc[:].rearrange("p n j -> p (n j)"), pTk[:, 0:256],
                     vm[:].rearrange("p n j -> p (n j)"), ALU.mult, [pTk, vm])
                S.tt("pool", sc, sc[:], sc[:], am[:], ALU.add, [sc, am])
                for n in range(4):
                    S.op("dve", lambda e, n=n: e.max(mx[:], sc[:, n, :]), [sc], [mx])
                    S.op("dve", lambda e, n=n: e.match_replace(sc2[:], mx[:], sc[:, n, :], -1e9), [sc, mx], [sc2])
                    S.op("dve", lambda e: e.max(mx[:], sc2[:]), [sc2], [mx])
                    S.ts("dve", thr, thr[:, n:n + 1], mx[:, 7:8], -0.5, None, ALU.max, reads=[mx])
                    S.ts("dve", sc2, sc2[:], sc[:, n, :], thr[:, n:n + 1], None, ALU.is_ge, reads=[sc, thr])
                    S.ts("dve", selb, selb[:, n, :], sc2[:], -1.0, -NEG, ALU.add, ALU.mult, reads=[sc2])
        if ui + 1 < len(units):
            load_unit(ui + 1)
        for n in range(4):
            S.transpose(pT, pT[0:64, n * 128:(n + 1) * 128], selb[:, n, :], k.ident[:], [selb, k.ident])
        flush()
        for hh in range(4):
            S.copy("act", Qa[hh], Qa[hh][0:64, :], pT[0:64, 0:512], [pT])
        S.op("sp", lambda e, par=par, g=g, T0=T0: e.dma_start(
            out=QaT[par][0:1, :, :], in_=k.c_vrow[4 * g:4 * g + 4, T0:T0 + 512].rearrange("(o h) t -> o h t", o=1)),
            [], Qa, dma_sem_tile=Qa[0])
        for hh in range(4):
            h = 4 * g + hh
            tiles = []
            for kt in range(4 * T + 4):
                j = kt - 4 * T
                if j < 0:
                    tiles.append((kt, 0, 512, []))
                else:
                    tiles.append((kt, 128 * j, 512, [(maskd, 128 * j)]))
            attend(hh, h, Ks, Vs, tiles, 1, False)
        flush()
        ob = osbT[cnt["o"] % 2]
        cnt["o"] += 1
        for hh in range(4):
            S.copy("act", ob, ob[0:64, hh, :], oacc[hh][0:64, :], [oacc[hh]])
        S.dma(None, k.oT_d[256 * g:256 * g + 256, T0:T0 + 512].rearrange("(hh d) t -> d hh t", d=64),
              ob[0:64, :, :], in_t=ob)
    S.finish_wait("sp", osbT)


def phase_nsa_out(S, k, l, xin, xout, NT):
    TT = 512
    ntile = NT // TT
    load_consts(S, k)
    alloc_psum(S, k)
    g1row = S.sbuf("g1row", [128, 1024], F32)
    S.dma(g1row, g1row[:], k.modd[l, 2048:3072].partition_broadcast(128))
    Wo = S.sbuf("Wo", [128, 8, 1024], BF16)
    stg = [S.sbuf("stg%d" % i, [128, 1024], F32) for i in range(2)]
    load_weight_bf16(S, Wo, k.nsa_w_out[0], 8, 1024, stg, grow=g1row)
    oT = [S.sbuf("oT%d" % i, [128, 8, TT], BF16) for i in range(2)]
    xts = [S.sbuf("xt%d" % i, [128, 1024], F32) for i in range(2)]
    xo = [S.sbuf("xo%d" % i, [128, 1024], F32) for i in range(2)]
    n = 0
    for T0 in range(ntile):
        r0 = T0 * TT
        o_ = oT[T0 % 2]
        S.dma(o_, o_[:], k.oT_d[:, r0:r0 + TT].rearrange("(kc p) t -> p kc t", p=128))
        for sub in range(4):
            xt = xts[sub % 2]
            xo_ = xo[sub % 2]
            S.dma(xt, xt[:], xin[r0 + sub * 128:r0 + (sub + 1) * 128, :])
            for half in range(2):
                py = k.ps[1 + (n % 4)]
                n += 1
                for kc in range(8):
                    S.mm(py, py[:], o_[:, kc, sub * 128:(sub + 1) * 128], Wo[:, kc, half * 512:(half + 1) * 512],
                         [o_, Wo], start=(kc == 0), stop=(kc == 7))
                S.tt("dve", xo_, xo_[:, half * 512:(half + 1) * 512], py[:], xt[:, half * 512:(half + 1) * 512],
                     ALU.add, [py, xt])
            S.dma(None, xout[r0 + sub * 128:r0 + (sub + 1) * 128, :], xo_[:], in_t=xo_)
    S.finish_wait("sp", xo)


W_KEYS = ["ada_w", "ada_b", "norm_mix", "norm_ffn", "final_norm", "hg_w_in", "hg_w_out", "hg_gnorm", "hg_lb",
          "ffn_w_up", "ffn_conv_w", "ffn_conv_b", "ffn_w_down", "nsa_w_in", "nsa_w_out", "nsa_cmp_pe",
          "nsa_cmp_w1", "nsa_cmp_w2"]


def make_in_map(inp, x, c, NT, consts=None):
    im = {"x": np.ascontiguousarray(x, dtype=np.float32),
          "c_col": np.ascontiguousarray(np.asarray(c, dtype=np.float32).reshape(8, 128).T)}
    for k_ in W_KEYS:
        im[k_] = np.asarray(inp[k_], dtype=np.float32)
    if consts is None:
        consts = dict(host_consts())
        consts.update(nsa_host_consts(NT))
    im.update(consts)
    return im


_CACHE = {}


def kernel(**inputs):
    x = np.asarray(inputs["x"], dtype=np.float32)
    c = np.asarray(inputs["c"], dtype=np.float32)
    B, NT, _ = x.shape
    if "nc" not in _CACHE:
        _CACHE["nc"] = build_program(NT)
        consts = dict(host_consts())
        consts.update(nsa_host_consts(NT))
        _CACHE["consts"] = consts
    nc = _CACHE["nc"]
    in_maps = [make_in_map(inputs, x[b], c[b], NT, _CACHE["consts"]) for b in range(B)]
    res = run_bass_kernel_spmd(nc, in_maps, core_ids=list(range(B)))
    out = np.stack([np.asarray(r["out"], dtype=np.float32) for r in res.results], axis=0)
    return out
```

```python
import contextlib
import numpy as np
import concourse.bass as bass
import concourse.mybir as mybir

F32 = mybir.dt.float32
BF16 = mybir.dt.bfloat16
AF = mybir.ActivationFunctionType
ALU = mybir.AluOpType
AX = mybir.AxisListType

ENGS = ["pe", "act", "dve", "pool", "sp"]


class T:
    __slots__ = ("name", "ap", "last_w", "readers", "dsem", "dcnt", "uid", "excl")
    _n = [0]

    def __init__(self, name, ap=None):
        T._n[0] += 1
        self.uid = T._n[0]
        self.name = name
        self.ap = ap
        self.last_w = None
        self.readers = []
        self.dsem = None
        self.dcnt = 0
        self.excl = False

    def __getitem__(self, idx):
        return self.ap[idx]


class Sched:
    def __init__(self, nc, stack):
        self.nc = nc
        self.stack = stack
        self.ops = {e: [] for e in ENGS}
        self.cnt = {e: 0 for e in ENGS}
        self.clock = {e: {} for e in ENGS}
        self.sem = {}
        for e in ["pe", "act", "dve", "pool"]:
            self.sem[e] = stack.enter_context(nc.semaphore("s_" + e))
        self.sem["bar"] = stack.enter_context(nc.semaphore("s_bar"))
        self.bar_n = 0
        self.dma_live = {}
        self.gstack = stack
        self.nsem = 5
        self.final_waits = []
        self.n_wait = 0
        self.uid = 0
        self.dsem_pool = []
        self.dsem_owner = []

    def sbuf(self, name, shape, dtype):
        self.uid += 1
        name = "%s_u%d" % (name, self.uid)
        t = self.stack.enter_context(self.nc.sbuf_tensor(name, list(shape), dtype))
        return T(name, t)

    def psum(self, name, shape, dtype=F32):
        self.uid += 1
        name = "%s_u%d" % (name, self.uid)
        t = self.stack.enter_context(self.nc.psum_tensor(name, list(shape), dtype))
        tt_ = T(name, t)
        tt_.excl = True
        return tt_

    def view(self, name, ap):
        return T(name, ap)

    def _dsem(self, t):
        if t.dsem is None:
            if self.dsem_pool:
                t.dsem, t.dcnt = self.dsem_pool.pop()
            else:
                self.nsem += 1
                t.dsem = self.gstack.enter_context(self.nc.semaphore("dsem%d" % self.nsem))
                t.dcnt = 0
            self.dsem_owner.append(t)
        return t.dsem

    def phase_end(self):
        for t in self.dsem_owner:
            self.dsem_pool.append((t.dsem, t.dcnt))
            t.dsem = None
        self.dsem_owner = []
        self.dma_live = {}

    def _need(self, eng, ev, waits):
        key, val, snap = ev
        if eng == "pe" and key == "pe":
            return
        ck = self.clock[eng]
        if ck.get(key, 0) >= val:
            return
        waits[key] = max(waits.get(key, 0), val)
        ck[key] = val
        if snap:
            for k, v in snap.items():
                if ck.get(k, 0) < v:
                    ck[k] = v

    def op(self, eng, fn, reads=(), writes=(), dma_sem_tile=None):
        waits = {}
        ex = [t for t in reads if t.excl]
        if ex:
            reads = [t for t in reads if not t.excl]
            writes = list(writes) + [t for t in ex if t not in writes]
        for t in reads:
            if t.last_w is not None:
                self._need(eng, t.last_w, waits)
        for t in writes:
            if t.last_w is not None:
                self._need(eng, t.last_w, waits)
            for ev in t.readers:
                self._need(eng, ev, waits)
        if dma_sem_tile is not None:
            st = dma_sem_tile
            sem = self._dsem(st)
            st.dcnt += 16
            key = ("d", st.uid)
            self.sem[key] = sem
            ev = (key, st.dcnt, dict(self.clock[eng]))
            self.dma_live[key] = (st, st.dcnt)
            inc = (sem, 16)
        else:
            self.cnt[eng] += 1
            ev = (eng, self.cnt[eng], None)
            inc = (self.sem[eng], 1)
        self.ops[eng].append((list(waits.items()), fn, inc))
        self.n_wait += len(waits)
        if dma_sem_tile is None:
            snap = dict(self.clock[eng])
            ev = (eng, self.cnt[eng], snap)
        for t in writes:
            t.last_w = ev
            t.readers = []
        for t in reads:
            if t not in writes:
                t.readers.append(ev)
        return ev

    def finish_wait(self, eng, tiles):
        waits = {}
        for t in tiles:
            if t.last_w is not None:
                self._need(eng, t.last_w, waits)
            for ev in t.readers:
                self._need(eng, ev, waits)
        self.ops[eng].append((list(waits.items()), None, None))

    def barrier(self):
        evs = []
        for e in ["pe", "act", "dve", "pool"]:
            if self.cnt[e] > 0:
                evs.append((e, self.cnt[e], None))
        for key, (t, val) in self.dma_live.items():
            evs.append((key, val, None))
        for eng in ENGS:
            waits = {}
            for ev in evs:
                if ev[0] == eng and eng == "pe":
                    continue
                ck = self.clock[eng]
                if ck.get(ev[0], 0) < ev[1]:
                    waits[ev[0]] = ev[1]
                    ck[ev[0]] = ev[1]
            self.ops[eng].append((list(waits.items()), None, None))
        self.bar_n += 1
        for eng in ENGS:
            self.ops[eng].append(([], "barinc", None))
        for eng in ENGS:
            self.ops[eng].append(([("bar", 5 * self.bar_n)], None, None))

    def emit(self):
        nc = self.nc
        with nc.Block() as block:
            def run(eng_name):
                def body(e):
                    for waits, fn, inc in self.ops[eng_name]:
                        for key, val in waits:
                            e.wait_ge(self.sem[key], val)
                        if fn == "barinc":
                            e.sem_inc(self.sem["bar"], 1)
                        elif fn is not None:
                            ins = fn(e)
                            ins.then_inc(inc[0], inc[1])
                return body
            block.tensor(run("pe"))
            block.scalar(run("act"))
            block.vector(run("dve"))
            block.gpsimd(run("pool"))
            block.sync(run("sp"))
        self.ops = {e: [] for e in ENGS}

    def dma(self, out_t, out_ap, in_ap, in_t=None, eng="sp", **kw):
        reads = [in_t] if in_t is not None else []
        writes = [out_t] if out_t is not None else []
        st = out_t if out_t is not None else in_t
        return self.op(eng, lambda e: e.dma_start(out=out_ap, in_=in_ap, **kw), reads, writes,
                       dma_sem_tile=st)

    def mm(self, out_t, out_ap, lhsT, rhs, reads, start=True, stop=True, **kw):
        return self.op("pe", lambda e: e.matmul(out_ap, lhsT, rhs, start=start, stop=stop, **kw),
                       reads, [out_t])

    def transpose(self, out_t, out_ap, in_ap, ident_ap, reads):
        return self.op("pe", lambda e: e.transpose(out_ap, in_ap, ident_ap), reads, [out_t])

    def act(self, out_t, out_ap, in_ap, func, reads, bias=None, scale=None, accum_out=None,
            extra_writes=()):
        kw = {}
        if bias is not None:
            kw["bias"] = bias
        if scale is not None:
            kw["scale"] = scale
        if accum_out is not None:
            kw["accum_out"] = accum_out
        return self.op("act", lambda e: e.activation(out_ap, in_ap, func, **kw), reads,
                       [out_t] + list(extra_writes))

    def tt(self, eng, out_t, out_ap, in0, in1, op, reads):
        return self.op(eng, lambda e: e.tensor_tensor(out_ap, in0, in1, op), reads, [out_t])

    def ts(self, eng, out_t, out_ap, in0, s1, s2, op0, op1=None, reads=(), accum_out=None,
           extra_writes=()):
        def f(e):
            kw = {}
            if accum_out is not None:
                kw["accum_out"] = accum_out
            if op1 is None:
                return e.tensor_scalar(out_ap, in0, s1, None, op0, **kw)
            return e.tensor_scalar(out_ap, in0, s1, s2, op0, op1, **kw)
        return self.op(eng, f, reads, [out_t] + list(extra_writes))

    def stt(self, eng, out_t, out_ap, in0, scalar, in1, op0, op1, reads):
        eng = "dve"
        return self.op(eng, lambda e: e.scalar_tensor_tensor(out_ap, in0, scalar, in1, op0, op1),
                       reads, [out_t])

    def copy(self, eng, out_t, out_ap, in_ap, reads):
        if eng == "act":
            return self.op("act", lambda e: e.copy(out_ap, in_ap), reads, [out_t])
        return self.op(eng, lambda e: e.tensor_copy(out_ap, in_ap), reads, [out_t])

    def memset(self, eng, out_t, out_ap, val):
        return self.op(eng, lambda e: e.memset(out_ap, val), [], [out_t])

from concourse.bass_utils import run_bass_kernel_spmd

D = 1024
NH_HG = 8
DFF = 2816
NFC = DFF // 128
EPS = 1e-6
NEG = -30000.0


def bcast_rows(ap_row, n):
    return ap_row.partition_broadcast(n)


class K:
    pass


def load_weight_bf16(S, Wb, w_dram, KC, N, stg, grow=None, col_off=0, rowscale=None, kc_off=0):
    i = 0
    for kc in range(KC):
        for n0 in range(0, N, 1024):
            n1 = min(N, n0 + 1024)
            st = stg[i % len(stg)]
            S.dma(st, st[:, 0:n1 - n0], w_dram[kc * 128:(kc + 1) * 128, n0:n1])
            eng = "dve"
            o = Wb[:, kc_off + kc, col_off + n0:col_off + n1]
            if grow is None:
                S.copy("act" if i % 2 == 0 else "dve", Wb, o, st[:, 0:n1 - n0], [st])
            elif rowscale is None:
                S.tt(eng, Wb, o, st[:, 0:n1 - n0], grow[:, n0:n1], ALU.mult, [st, grow])
            else:
                S.stt(eng, Wb, o, st[:, 0:n1 - n0], rowscale[:, kc:kc + 1], grow[:, n0:n1],
                      ALU.mult, ALU.mult, [st, grow, rowscale])
            i += 1


def rstd_from_ss(S, rstd, ss, n, width):
    S.act(rstd, rstd[:, 0:width], ss[:, 0:width], AF.Ln, [ss], scale=1.0 / n, bias=EPS)
    S.act(rstd, rstd[:, 0:width], rstd[:, 0:width], AF.Exp, [rstd], scale=-0.5)


def norm_to_hT(S, k, xt_t, xt_ap, gcol, shcol, hT, col0, xn, sq, ss, rstd, pT):
    S.act(sq, sq[:], xt_ap, AF.Square, [xt_t], accum_out=ss[:, 0:1], extra_writes=[ss])
    rstd_from_ss(S, rstd, ss, D, 1)
    S.act(xn, xn[:], xt_ap, AF.Copy, [xt_t, rstd], scale=rstd[:, 0:1])
    for kc in range(8):
        S.transpose(pT, pT[:, kc * 128:(kc + 1) * 128], xn[:, kc * 128:(kc + 1) * 128],
                    k.ident[:], [xn, k.ident])
    for kc in range(8):
        eng = "dve" if kc % 2 == 0 else "pool"
        eng = "dve"
        S.ts(eng, hT, hT[:, kc, col0:col0 + 128], pT[:, kc * 128:(kc + 1) * 128],
             gcol[:, kc:kc + 1], shcol[:, kc:kc + 1], ALU.mult, ALU.add, reads=[pT, gcol, shcol])


def rows_to_cols(S, k, rows, name):
    n = sum(r.shape[0] // 128 for r in rows)
    assert n <= 128
    rt = S.sbuf(name + "_r", [n, 128], F32)
    ct = S.sbuf(name, [128, n], F32)
    j = 0
    for r in rows:
        m = r.shape[0] // 128
        S.dma(rt, rt[j:j + m, :], r.rearrange("(j p) -> j p", p=128))
        j += m
    ps = k.ps[1]
    S.mm(ps, ps[:, 0:n], rt[0:n, :], k.identf[0:n, 0:n], [rt, k.identf])
    S.copy("dve", ct, ct[:], ps[:, 0:n], [ps])
    return ct


def phase_prologue(S, k):
    cc = S.sbuf("cc", [128, 8], F32)
    ca = S.sbuf("ca", [128, 8], F32)
    S.dma(cc, cc[:], k.c_col[:, :])
    S.act(ca, ca[:], cc[:], AF.Silu, [cc])
    wst = [S.sbuf("adw%d" % i, [128, 8, 512], F32) for i in range(2)]
    brow = S.sbuf("brow", [1, 6144], F32)
    mrow = S.sbuf("mrow", [1, 6144], F32)
    i = 0
    for l in range(2):
        S.dma(brow, brow[:], k.ada_b[l:l + 1, :])
        for nt in range(12):
            wt = wst[i % 2]
            S.dma(wt, wt[:], k.ada_w[l].rearrange("(kc p) n -> p kc n", p=128)[:, :, nt * 512:(nt + 1) * 512])
            ps = k.ps[2 + (i % 2)]
            for kc in range(8):
                S.mm(ps, ps[0:1, :], ca[:, kc:kc + 1], wt[:, kc, :], [ca, wt],
                     start=(kc == 0), stop=(kc == 7))
            S.tt("dve", mrow, mrow[0:1, nt * 512:(nt + 1) * 512], ps[0:1, :],
                 brow[0:1, nt * 512:(nt + 1) * 512], ALU.add, [ps, brow])
            i += 1
        S.dma(None, k.modd[l:l + 1, :], mrow[:], in_t=mrow)
    S.finish_wait("sp", [mrow])


def load_consts(S, k):
    k.identf = S.sbuf("identf", [128, 128], F32)
    k.ident = S.sbuf("ident", [128, 128], BF16)
    S.dma(k.identf, k.identf[:], k.c_ident[:, :])
    S.copy("dve", k.ident, k.ident[:], k.identf[:], [k.identf])


def alloc_psum(S, k, bf7=False):
    k.ps = [S.psum("psb0", [128, 1024], BF16)] + [S.psum("ps%d" % i, [128, 512], F32) for i in range(1, 7)]
    if bf7:
        k.ps.append(S.psum("psb7", [128, 1024], BF16))
    else:
        k.ps.append(S.psum("ps7", [128, 512], F32))


def interleave(ga, gb, ra=1, rb=1):
    da = db = False
    while not (da and db):
        for _ in range(ra):
            if not da:
                try:
                    next(ga)
                except StopIteration:
                    da = True
        for _ in range(rb):
            if not db:
                try:
                    next(gb)
                except StopIteration:
                    db = True


def phase_hgrn(S, k, l, xin, xout, NT):
    TT = 256
    ntile = NT // TT
    load_consts(S, k)
    alloc_psum(S, k, bf7=True)
    cols = rows_to_cols(S, k, [k.modd[l, :], k.norm_mix[l, :], k.hg_lb[0, :], k.hg_lb[1, :],
                               k.hg_gnorm[0, :]], "hcols")
    gnc = S.sbuf("gnc", [128, 8], F32)
    S.copy("dve", gnc, gnc[:], cols[:, 72:73].to_broadcast([128, 8]), [cols])
    gcol = S.sbuf("gcol", [128, 8], F32)
    S.stt("dve", gcol, gcol[:], cols[:, 8:16], 1.0, cols[:, 48:56], ALU.add, ALU.mult, [cols])
    lbc = S.sbuf("lbc", [128, 8], F32)
    l1m = S.sbuf("l1m", [128, 8], F32)
    S.tt("dve", lbc, lbc[:], cols[:, 64:72], cols[:, 56:64], ALU.subtract, [cols])
    S.act(lbc, lbc[:], lbc[:], AF.Exp, [lbc])
    S.ts("dve", lbc, lbc[:], lbc[:], 1.0, None, ALU.add, reads=[lbc])
    S.op("dve", lambda e: e.reciprocal(lbc[:], lbc[:]), [lbc], [lbc])
    S.ts("dve", l1m, l1m[:], lbc[:], -1.0, 1.0, ALU.mult, ALU.add, reads=[lbc])
    S.act(l1m, l1m[:], l1m[:], AF.Ln, [l1m])
    sq = S.sbuf("sq", [128, 1024], F32)
    g1row = sq
    S.dma(g1row, g1row[:], k.modd[l, 2048:3072].partition_broadcast(128))
    Win = S.sbuf("Win", [128, 8, 4096], BF16)
    Wout = S.sbuf("Wout", [128, 8, 1024], BF16)
    otok = S.sbuf("otok", [128, 1024], F32)
    xo = [S.sbuf("xo%d" % i, [128, 1024], F32) for i in range(2)]
    stg = [otok, xo[0]]
    load_weight_bf16(S, Win, k.hg_w_in[0], 8, 4096, stg)
    load_weight_bf16(S, Wout, k.hg_w_out[0], 8, 1024, stg, grow=g1row, rowscale=gnc)
    rmask = S.sbuf("rmask", [128, TT], F32)
    S.memset("pool", rmask, rmask[:], 1.0)
    S.memset("pool", rmask, rmask[:].rearrange("p (c j) -> p c j", j=64)[:, :, 0:1], 0.0)
    cmask = S.sbuf("cmask", [128, 128], F32)
    S.dma(cmask, cmask[:], k.c_bdmask[:, :])
    st32s = [S.sbuf("st32_%d" % i, [128, 128], F32) for i in range(8)]
    stbs = [S.sbuf("stb_%d" % i, [128, 128], BF16) for i in range(8)]
    for i in range(8):
        S.memset("pool", st32s[i], st32s[i][:], 0.0)
        S.memset("pool", stbs[i], stbs[i][:], 0.0)
    sts = [S.sbuf("sts%d" % i, [128, 128], F32) for i in range(8)]
    sts2 = sts
    xts = [S.sbuf("xt%d" % i, [128, 1024], F32) for i in range(4)]
    xns = [S.sbuf("xn%d" % i, [128, 1024], BF16) for i in range(2)]
    sss = [S.sbuf("ss%d" % i, [128, 2], F32) for i in range(2)]
    rss = [S.sbuf("rs%d" % i, [128, 2], F32) for i in range(2)]
    hT = S.sbuf("hT", [128, 8, TT], BF16)
    NTMP = 2
    tu = [S.sbuf("tu%d" % i, [128, TT], F32) for i in range(NTMP)]
    tA = [S.sbuf("tA%d" % i, [128, TT], F32) for i in range(NTMP)]
    tB = [S.sbuf("tB%d" % i, [128, TT], F32) for i in range(NTMP)]
    tb = [S.sbuf("tb%d" % i, [128, TT], F32) for i in range(NTMP)]
    teb = [S.sbuf("teb%d" % i, [128, TT], F32) for i in range(NTMP)]
    t1 = [S.sbuf("t1%d" % i, [128, TT], F32) for i in range(NTMP)]
    qdT2 = [S.sbuf("qdT%d" % i, [128, 8, 2, 2, 128], BF16) for i in range(2)]
    kdT2 = [S.sbuf("kdT%d" % i, [128, 8, TT], BF16) for i in range(2)]
    kdtok2 = [S.sbuf("kdtok%d" % i, [128, 2, 8, 128], BF16) for i in range(2)]
    ebl2 = [S.sbuf("ebl%d" % i, [128, 8, 4], F32) for i in range(2)]
    vt2 = [S.sbuf("vt%d" % i, [128, 2, 1024], BF16) for i in range(2)]
    gs2 = [S.sbuf("gs%d" % i, [128, 2, 1024], BF16) for i in range(2)]
    oss = S.sbuf("oss", [128, 8], F32)
    orstd = S.sbuf("orstd", [128, 8], F32)
    on = S.sbuf("on", [128, 1024], BF16)
    oT = S.sbuf("oT", [128, 8, 128], BF16)
    sc4s = [S.sbuf("sc4_%d" % i, [128, 512], BF16) for i in range(2)]
    st1 = [S.sbuf("st1_%d" % i, [128, 128], F32) for i in range(8)]
    st1b = [S.sbuf("st1b_%d" % i, [128, 128], BF16) for i in range(8)]
    for q in qdT2:
        S.memset("pool", q, q[:], 0.0)
    pT, pT2 = k.ps[0], k.ps[7]
    pq = [k.ps[1], k.ps[2]]
    pvs = [k.ps[1], k.ps[2]]
    tq = [S.sbuf("tq%d" % i, [128, TT], F32) for i in range(NTMP)]

    def gen_A(T0):
        par = T0 % 2
        qdT, kdT, kdtok, ebl, vt, gs = qdT2[par], kdT2[par], kdtok2[par], ebl2[par], vt2[par], gs2[par]
        r0 = T0 * TT
        for sub in range(2):
            xt = xts[(T0 * 2 + sub) % 4]
            S.dma(xt, xt[:], xin[r0 + sub * 128:r0 + (sub + 1) * 128, :])
            S.act(sq, sq[:], xt[:], AF.Square, [xt], accum_out=sss[sub][:, 0:1], extra_writes=[sss[sub]])
            rstd_from_ss(S, rss[sub], sss[sub], D, 1)
            S.act(xns[sub], xns[sub][:], xt[:], AF.Copy, [xt, rss[sub]], scale=rss[sub][:, 0:1])
            yield
        for sub in range(2):
            for kc in range(8):
                S.transpose(pT, pT[:, kc * 128:(kc + 1) * 128], xns[sub][:, kc * 128:(kc + 1) * 128],
                            k.ident[:], [xns[sub], k.ident])
            for kc in range(8):
                S.act(hT, hT[:, kc, sub * 128:(sub + 1) * 128], pT[:, kc * 128:(kc + 1) * 128], AF.Identity,
                      [pT, gcol, cols], scale=gcol[:, kc:kc + 1], bias=cols[:, kc:kc + 1])
            yield
        def s1(h):
            i = h % NTMP
            pq_ = pq[h % 2]
            for kc in range(8):
                S.mm(pq_, pq_[:, 0:TT], Win[:, kc, h * 128:(h + 1) * 128], hT[:, kc, :], [Win, hT],
                     start=(kc == 0), stop=(kc == 7))
            for kc in range(8):
                S.mm(pq_, pq_[:, TT:2 * TT], Win[:, kc, 1024 + h * 128:1024 + (h + 1) * 128], hT[:, kc, :],
                     [Win, hT], start=(kc == 0), stop=(kc == 7))
            z = pq_[:, TT:2 * TT]
            S.act(tu[i], tu[i][:], z, AF.Exp, [pq_], scale=-1.0)
            S.act(tA[i], tA[i][:], tu[i][:], AF.Ln, [tu[i]], bias=1.0)
            S.act(tB[i], tB[i][:], tu[i][:], AF.Ln, [tu[i], lbc], bias=1.0, scale=lbc[:, h:h + 1])
            S.copy("dve", tq[i], tq[i][:], pq_[:, 0:TT], [pq_])
            S.tt("dve", t1[i], t1[i][:], z, tA[i][:], ALU.add, [pq_, tA[i]])

        def s2(h):
            i = h % NTMP
            pq_ = pq[h % 2]
            z = pq_[:, TT:2 * TT]
            S.tt("pool", tB[i], tB[i][:], tB[i][:], tA[i][:], ALU.subtract, [tB[i], tA[i]])
            S.op("dve", lambda e, o=tb[i], m=tB[i]: e.tensor_tensor_scan(o[:], rmask[:], m[:], 0.0, ALU.mult, ALU.add),
                 [rmask, tB[i]], [tb[i]])
            S.act(teb[i], teb[i][:], tb[i][:], AF.Exp, [tb[i]])
            S.tt("pool", t1[i], t1[i][:], t1[i][:], tb[i][:], ALU.add, [t1[i], tb[i]])

        def s3(h):
            i = h % NTMP
            pq_ = pq[h % 2]
            for sub in range(2):
                for c2 in range(2):
                    cs = sub * 128 + c2 * 64
                    S.tt("dve", qdT, qdT[:, h, sub, c2, c2 * 64:(c2 + 1) * 64], tq[i][:, cs:cs + 64],
                         teb[i][:, cs:cs + 64], ALU.mult, [tq[i], teb[i]])
            S.act(kdT, kdT[:, h, :], t1[i][:], AF.Exp, [t1[i], l1m], scale=-1.0, bias=l1m[:, h:h + 1])
            S.copy("pool", ebl, ebl[:, h, :], teb[i][:].rearrange("p (c j) -> p c j", j=64)[:, :, 63], [teb[i]])

        for it in range(10):
            if 0 <= it - 2 < 8:
                s3(it - 2)
            if 0 <= it - 1 < 8:
                s2(it - 1)
            if it < 8:
                s1(it)
            yield
        for sub in range(2):
            for half in range(4):
                c0 = 2048 + half * 512
                pv = pvs[half % 2]
                for kc in range(8):
                    S.mm(pv, pv[:], hT[:, kc, sub * 128:(sub + 1) * 128], Win[:, kc, c0:c0 + 512],
                         [hT, Win], start=(kc == 0), stop=(kc == 7))
                if half < 2:
                    S.copy("dve", vt, vt[:, sub, half * 512:(half + 1) * 512], pv[:], [pv])
                else:
                    S.act(gs, gs[:, sub, (half - 2) * 512:(half - 1) * 512], pv[:], AF.Silu, [pv])
                yield
        for sub in range(2):
            for h in range(8):
                S.transpose(pT, pT[:, h * 128:(h + 1) * 128], kdT[:, h, sub * 128:(sub + 1) * 128],
                            k.ident[:], [kdT, k.ident])
            S.copy("act", kdtok, kdtok[:, sub, :, :].rearrange("p h k -> p (h k)"), pT[:], [pT])
            yield

    def gen_B(T0):
        par = T0 % 2
        qdT, kdT, kdtok, ebl, vt, gs = qdT2[par], kdT2[par], kdtok2[par], ebl2[par], vt2[par], gs2[par]
        r0 = T0 * TT
        for sub in range(2):
            banks = [(k.ps[3], k.ps[4], k.ps[3]), (k.ps[5], k.ps[6], k.ps[5])]
            for hg in range(2):
                pA, pB, pC = banks[hg]
                hs = list(range(hg * 4, hg * 4 + 4))
                for i, h in enumerate(hs):
                    S.mm(pA, pA[:, i * 128:i * 128 + 64], kdT[:, h, sub * 128:(sub + 1) * 128],
                         qdT[:, h, sub, 0, 0:64], [kdT, qdT])
                    S.mm(pA, pA[:, i * 128 + 64:(i + 1) * 128], kdT[:, h, sub * 128:(sub + 1) * 128],
                         qdT[:, h, sub, 1, 64:128], [kdT, qdT])
                    S.mm(pB, pB[:, i * 128:(i + 1) * 128], kdtok[0:64, sub, h, :],
                         vt[0:64, sub, h * 128:(h + 1) * 128], [kdtok, vt])
                S.tt("dve", sc4s[hg], sc4s[hg][:].rearrange("p (i t) -> p i t", t=128),
                     pA[:].rearrange("p (i t) -> p i t", t=128),
                     cmask[:].unsqueeze(1).to_broadcast([128, 4, 128]), ALU.mult, [pA, cmask])
            yield
            for hg in range(2):
                pA, pB, pC = banks[hg]
                hs = list(range(hg * 4, hg * 4 + 4))
                c = sub * 2
                for i, h in enumerate(hs):
                    S.act(sts[h], sts[h][:], st32s[h][:], AF.Copy, [st32s[h], ebl], scale=ebl[:, h, c:c + 1])
                for i, h in enumerate(hs):
                    S.stt("dve", st1b[h], st1b[h][:], pB[:, i * 128:(i + 1) * 128], ebl[:, h, c:c + 1],
                          sts[h][:], ALU.mult, ALU.add, [pB, ebl, sts[h]])
                for i, h in enumerate(hs):
                    S.stt("dve", st1[h], st1[h][:], pB[:, i * 128:(i + 1) * 128], ebl[:, h, c:c + 1],
                          sts[h][:], ALU.mult, ALU.add, [pB, ebl, sts[h]])
            yield
            for hg in range(2):
                pA, pB, pC = banks[hg]
                hs = list(range(hg * 4, hg * 4 + 4))
                for i, h in enumerate(hs):
                    o_ap = pC[:, i * 128:(i + 1) * 128]
                    S.mm(pC, o_ap, sc4s[hg][:, i * 128:(i + 1) * 128], vt[:, sub, h * 128:(h + 1) * 128],
                         [sc4s[hg], vt], start=True, stop=False)
                    S.mm(pC, o_ap, qdT[:, h, sub, 0, :], stbs[h][:], [qdT, stbs[h]], start=False, stop=False)
                    S.mm(pC, o_ap, qdT[:, h, sub, 1, :], st1b[h][:], [qdT, st1b[h]], start=False, stop=True)
                for i, h in enumerate(hs):
                    S.mm(pB, pB[:, i * 128:(i + 1) * 128], kdtok[64:128, sub, h, :],
                         vt[64:128, sub, h * 128:(h + 1) * 128], [kdtok, vt])
            yield
            for hg in range(2):
                pA, pB, pC = banks[hg]
                hs = list(range(hg * 4, hg * 4 + 4))
                c = sub * 2 + 1
                for i, h in enumerate(hs):
                    S.act(sts2[h], sts2[h][:], st1[h][:], AF.Copy, [st1[h], ebl], scale=ebl[:, h, c:c + 1])
                for i, h in enumerate(hs):
                    S.stt("dve", stbs[h], stbs[h][:], pB[:, i * 128:(i + 1) * 128], ebl[:, h, c:c + 1],
                          sts2[h][:], ALU.mult, ALU.add, [pB, ebl, sts2[h]])
                for i, h in enumerate(hs):
                    S.stt("dve", st32s[h], st32s[h][:], pB[:, i * 128:(i + 1) * 128], ebl[:, h, c:c + 1],
                          sts2[h][:], ALU.mult, ALU.add, [pB, ebl, sts2[h]])
                S.copy("dve", otok, otok[:, hg * 512:(hg + 1) * 512], pC[:], [pC])
                for i, h in enumerate(hs):
                    S.act(sq, sq[:, 0:128], otok[:, h * 128:(h + 1) * 128], AF.Square, [otok],
                          accum_out=oss[:, h:h + 1], extra_writes=[oss])
            yield
            rstd_from_ss(S, orstd, oss, 128, 8)
            S.tt("dve", otok, otok[:].rearrange("p (h v) -> p h v", v=128),
                 otok[:].rearrange("p (h v) -> p h v", v=128),
                 orstd[:].unsqueeze(2).to_broadcast([128, 8, 128]), ALU.mult, [otok, orstd])
            S.tt("pool", on, on[:], otok[:], gs[:, sub, :], ALU.mult, [otok, gs])
            yield
            for kc in range(8):
                S.transpose(pT2, pT2[:, kc * 128:(kc + 1) * 128], on[:, kc * 128:(kc + 1) * 128],
                            k.ident[:], [on, k.ident])
            S.copy("act", oT, oT[:].rearrange("p c t -> p (c t)"), pT2[:], [pT2])
            yield
            xt = xts[(T0 * 2 + sub) % 4]
            xo_ = xo[sub]
            for half in range(2):
                py = pvs[half % 2]
                for kc in range(8):
                    S.mm(py, py[:], oT[:, kc, :], Wout[:, kc, half * 512:(half + 1) * 512],
                         [oT, Wout], start=(kc == 0), stop=(kc == 7))
                S.tt("dve", xo_, xo_[:, half * 512:(half + 1) * 512], py[:], xt[:, half * 512:(half + 1) * 512],
                     ALU.add, [py, xt])
            S.dma(None, xout[r0 + sub * 128:r0 + (sub + 1) * 128, :], xo_[:], in_t=xo_)
            yield

    for _ in gen_A(0):
        pass
    for T0 in range(ntile):
        if T0 + 1 < ntile:
            interleave(gen_B(T0), gen_A(T0 + 1), 1, 1)
        else:
            for _ in gen_B(T0):
                pass
    S.finish_wait("sp", xo)


def phase_ffn(S, k, l, xnorm, xres, xout, NT, fc0, fc1, final=False):
    TT = 512
    ntile = NT // TT
    nfc = fc1 - fc0
    W = nfc * 128
    same = xres is xnorm
    load_consts(S, k)
    alloc_psum(S, k, bf7=True)
    cols = rows_to_cols(S, k, [k.modd[l, :], k.norm_ffn[l, :]], "fcols")
    gcol = S.sbuf("gcol", [128, 8], F32)
    S.stt("dve", gcol, gcol[:], cols[:, 32:40], 1.0, cols[:, 48:56], ALU.add, ALU.mult, [cols])
    shc = S.sbuf("shc", [128, 8], F32)
    S.copy("dve", shc, shc[:], cols[:, 24:32], [cols])
    ccols = rows_to_cols(S, k, [k.ffn_conv_w[l, 0, :], k.ffn_conv_w[l, 1, :], k.ffn_conv_w[l, 2, :],
                                k.ffn_conv_b[l, :]], "ccols")
    sq = S.sbuf("sq", [128, 1024], F32)
    g2row = sq
    S.dma(g2row, g2row[:], k.modd[l, 5120:6144].partition_broadcast(128))
    if final:
        fnrow = S.sbuf("fnrow", [128, 1024], F32)
        S.dma(fnrow, fnrow[:], k.final_norm[:].partition_broadcast(128))
    Wup = S.sbuf("Wup", [128, 8, 2 * W], BF16)
    Wdn = S.sbuf("Wdn", [128, nfc, 1024], BF16)
    stg = [S.sbuf("stg%d" % i, [128, 1024], F32) for i in range(2)]
    load_weight_bf16(S, Wup, k.ffn_w_up[l][:, fc0 * 128:fc1 * 128], 8, W, stg)
    load_weight_bf16(S, Wup, k.ffn_w_up[l][:, DFF + fc0 * 128:DFF + fc1 * 128], 8, W, stg, col_off=W)
    load_weight_bf16(S, Wdn, k.ffn_w_down[l][fc0 * 128:fc1 * 128, :], nfc, 1024, stg, grow=g2row)
    xts = [S.sbuf("xt%d" % i, [128, 1024], F32) for i in range(2)]
    xrs = [S.sbuf("xr%d" % i, [128, 1024], F32) for i in range(2)]
    xn = S.sbuf("xn", [128, 1024], BF16)
    ss = S.sbuf("ss", [128, 8], F32)
    rstd = S.sbuf("rstd", [128, 8], F32)
    hTs = [S.sbuf("hT%d" % i, [128, 8, TT], BF16) for i in range(2)]
    halo = S.sbuf("halo", [128, nfc, 2], F32)
    S.memset("pool", halo, halo[:], 0.0)
    abuf = [S.sbuf("abuf%d" % i, [128, TT + 2], F32) for i in range(3)]
    c1 = [S.sbuf("c1_%d" % i, [128, TT], F32) for i in range(3)]
    c2 = [S.sbuf("c2_%d" % i, [128, TT], F32) for i in range(3)]
    mTs = [S.sbuf("mT%d" % i, [128, nfc, TT], BF16) for i in range(2)]
    xo = [S.sbuf("xo%d" % i, [128, 1024], F32) for i in range(2)]
    pT = k.ps[0]
    ncnt = [0]

    xns = [xn] + [S.sbuf("xn%d" % i, [128, 1024], BF16) for i in range(1, 4)]
    sss = [S.sbuf("ss%d" % i, [128, 2], F32) for i in range(4)]
    rss = [S.sbuf("rs%d" % i, [128, 2], F32) for i in range(4)]
    pTs = [k.ps[0], k.ps[7]]

    def norm_a(T0, sub):
        i = sub
        xt = xts[sub % 2]
        r = T0 * TT + sub * 128
        S.dma(xt, xt[:], xnorm[r:r + 128, :])
        S.act(sq, sq[:], xt[:], AF.Square, [xt], accum_out=sss[i][:, 0:1], extra_writes=[sss[i]])
        rstd_from_ss(S, rss[i], sss[i], D, 1)
        S.act(xns[i], xns[i][:], xt[:], AF.Copy, [xt, rss[i]], scale=rss[i][:, 0:1])

    def norm_b(T0, sub):
        i = sub
        hT_ = hTs[T0 % 2]
        pT_ = pTs[sub % 2]
        for kc in range(8):
            S.transpose(pT_, pT_[:, kc * 128:(kc + 1) * 128], xns[i][:, kc * 128:(kc + 1) * 128],
                        k.ident[:], [xns[i], k.ident])
        for kc in range(8):
            S.act(hT_, hT_[:, kc, sub * 128:(sub + 1) * 128], pT_[:, kc * 128:(kc + 1) * 128], AF.Identity,
                  [pT_, gcol, shc], scale=gcol[:, kc:kc + 1], bias=shc[:, kc:kc + 1])

    for sub in range(4):
        norm_a(0, sub)
        norm_b(0, sub)
    def down_proj(T0):
        r0 = T0 * TT
        mT = mTs[T0 % 2]
        for sub in range(4):
            xt = xrs[sub % 2]
            S.dma(xt, xt[:], xres[r0 + sub * 128:r0 + (sub + 1) * 128, :])
            xo_ = xo[sub % 2]
            for half in range(2):
                py = k.ps[1 + half]
                for j in range(nfc):
                    S.mm(py, py[:], mT[:, j, sub * 128:(sub + 1) * 128], Wdn[:, j, half * 512:(half + 1) * 512],
                         [mT, Wdn], start=(j == 0), stop=(j == nfc - 1))
                S.tt("dve", xo_, xo_[:, half * 512:(half + 1) * 512], py[:], xt[:, half * 512:(half + 1) * 512],
                     ALU.add, [py, xt])
            if final:
                S.act(sq, sq[:], xo_[:], AF.Square, [xo_], accum_out=ss[:, 1:2], extra_writes=[ss])
                rstd_from_ss(S, rstd, ss[:, 1:2] if False else ss, D, 2)
                S.stt("dve", xo_, xo_[:], xo_[:], rstd[:, 1:2], fnrow[:], ALU.mult, ALU.mult, [xo_, rstd, fnrow])
            S.dma(None, xout[r0 + sub * 128:r0 + (sub + 1) * 128, :], xo_[:], in_t=xo_)

    for T0 in range(ntile):
        r0 = T0 * TT
        hT = hTs[T0 % 2]
        mT = mTs[T0 % 2]
        def st_A(j):
            ab = abuf[j % 3]
            pa = k.ps[1 + (j % 2)]
            fc = fc0 + j
            S.copy("pool", ab, ab[:, 0:2], halo[:, j, :], [halo])
            S.copy("act", ab, ab[:, 2:TT + 2], pa[:], [pa])
            S.copy("pool", halo, halo[:, j, :], ab[:, TT:TT + 2], [ab])
            S.ts("pool", c1[j % 3], c1[j % 3][:], ab[:, 2:TT + 2], ccols[:, 44 + fc:45 + fc],
                 ccols[:, 66 + fc:67 + fc], ALU.mult, ALU.add, reads=[ab, ccols])

        def st_B(j):
            ab = abuf[j % 3]
            fc = fc0 + j
            c1_, c2_ = c1[j % 3], c2[j % 3]
            S.stt("dve", c2_, c2_[:], ab[:, 1:TT + 1], ccols[:, 22 + fc:23 + fc], c1_[:], ALU.mult, ALU.add,
                  [ab, ccols, c1_])
            S.stt("dve", c1_, c1_[:], ab[:, 0:TT], ccols[:, fc:fc + 1], c2_[:], ALU.mult, ALU.add,
                  [ab, ccols, c2_])

        def st_C(j):
            S.act(c2[j % 3], c2[j % 3][:], c1[j % 3][:], AF.Silu, [c1[j % 3]])

        def st_D(j):
            pv = k.ps[3 + (j % 4)]
            S.tt("dve", mT, mT[:, j, :], pv[:], c2[j % 3][:], ALU.mult, [pv, c2[j % 3]])

        for j in range(nfc + 2):
            if j < nfc:
                pa = k.ps[1 + (j % 2)]
                pv = k.ps[3 + (j % 4)]
                for kc in range(8):
                    S.mm(pa, pa[:], Wup[:, kc, j * 128:(j + 1) * 128], hT[:, kc, :], [Wup, hT],
                         start=(kc == 0), stop=(kc == 7))
                for kc in range(8):
                    S.mm(pv, pv[:], Wup[:, kc, W + j * 128:W + (j + 1) * 128], hT[:, kc, :], [Wup, hT],
                         start=(kc == 0), stop=(kc == 7))
                st_A(j)
            if 0 <= j - 1 < nfc:
                st_B(j - 1)
                st_C(j - 1)
            if 0 <= j - 2 < nfc:
                st_D(j - 2)
            if T0 + 1 < ntile:
                if j == 0:
                    for sub_ in range(4):
                        norm_a(T0 + 1, sub_)
                if j in (3, 5, 7, 9):
                    norm_b(T0 + 1, (j - 3) // 2)
            if j == 1 and T0 > 0:
                down_proj(T0 - 1)
    down_proj(ntile - 1)
    S.finish_wait("sp", xo)


def build_program(NT, phases=("pro", "hg", "ffn0", "nsa", "ffn1")):
    nc = bass.Bass("TRN2", target_bir_lowering=False)
    k = K()

    def inp(name, shape):
        return nc.dram_tensor(name, list(shape), F32, kind="ExternalInput").ap()

    k.x = inp("x", [NT, D])
    k.c_col = inp("c_col", [128, 8])
    k.ada_w = inp("ada_w", [2, D, 6 * D])
    k.ada_b = inp("ada_b", [2, 6 * D])
    k.norm_mix = inp("norm_mix", [2, D])
    k.norm_ffn = inp("norm_ffn", [2, D])
    k.final_norm = inp("final_norm", [D])
    k.hg_w_in = inp("hg_w_in", [1, D, 4096])
    k.hg_w_out = inp("hg_w_out", [1, D, D])
    k.hg_gnorm = inp("hg_gnorm", [1, 128])
    k.hg_lb = inp("hg_lb", [2, D])
    k.ffn_w_up = inp("ffn_w_up", [2, D, 2 * DFF])
    k.ffn_conv_w = inp("ffn_conv_w", [2, 3, DFF])
    k.ffn_conv_b = inp("ffn_conv_b", [2, DFF])
    k.ffn_w_down = inp("ffn_w_down", [2, DFF, D])
    k.c_ident = inp("c_ident", [128, 128])
    k.c_bdmask = inp("c_bdmask", [128, 128])
    nsa_declare(nc, k, NT)
    k.out = nc.dram_tensor("out", [NT, D], F32, kind="ExternalOutput").ap()
    k.modd = nc.dram_tensor("modd", [2, 6 * D], F32).ap()
    bufs = [nc.dram_tensor("xs%d" % i, [NT, D], F32).ap() for i in range(4)]
    xc = nc.dram_tensor("xc", [NT, D], F32).ap()
    main = [p for p in phases if p != "pro"]
    with contextlib.ExitStack() as gst:
        S = Sched(nc, gst)
        k.S = S
        plist = []
        if "pro" in phases:
            plist.append(lambda: (alloc_psum(S, k), phase_prologue(S, k)))
        src = k.x
        for i, p in enumerate(main):
            last = (i == len(main) - 1)
            dst = k.out if last else bufs[i]
            if p == "hg":
                plist.append(lambda src=src, dst=dst: phase_hgrn(S, k, 0, src, dst, NT))
            elif p in ("ffn0", "ffn1"):
                l = int(p[3])
                fin = (p == "ffn1")
                plist.append(lambda src=src, l=l: phase_ffn(S, k, l, src, src, xc, NT, 0, 11))
                plist.append(lambda src=src, dst=dst, l=l, fin=fin: phase_ffn(S, k, l, src, xc, dst, NT, 11, 22, final=fin))
            elif p == "nsa":
                import os
                stop = int(os.environ.get("NSA_STOP", "4"))
                plist.append(lambda src=src: phase_nsa_proj(S, k, 1, src, NT))
                if stop >= 2:
                    plist.append(lambda: phase_nsa_cmp(S, k, NT))
                if stop >= 3:
                    plist.append(lambda: phase_nsa_attn(S, k, NT))
                if stop >= 4:
                    plist.append(lambda src=src, dst=dst: phase_nsa_out(S, k, 1, src, dst, NT))
            src = dst
        for i, p in enumerate(plist):
            with contextlib.ExitStack() as st:
                S.stack = st
                p()
                S.barrier()
                S.emit()
                S.phase_end()
    return nc


def host_consts():
    ident = np.eye(128, dtype=np.float32)
    s = np.arange(128)[:, None]
    t = np.arange(128)[None, :]
    bd = ((s // 64 == t // 64) & (s <= t)).astype(np.float32)
    return {"c_ident": ident, "c_bdmask": bd}


NSA_W = 2608
POW_RECIP = False
SLOPES = [2.0 ** (-8.0 * (h + 1) / 16) for h in range(16)]


def nsa_dims(NT):
    ncb = (NT - 32) // 16 + 1
    nsb = NT // 64
    return ncb, nsb


def nsa_host_consts(NT):
    import ml_dtypes
    bf = ml_dtypes.bfloat16
    ncb, nsb = nsa_dims(NT)
    t = np.arange(NT)
    c = {}
    c["c_vrow"] = np.stack([-SLOPES[h] * t for h in range(16)]).astype(bf)
    KT = NT // 128
    i = np.arange(128)
    sb = np.zeros((128, KT * 16), np.float32)
    for kt in range(KT):
        for h in range(16):
            sb[:, kt * 16 + h] = SLOPES[h] * (128 * kt + i)
    c["c_sbias"] = sb
    cb = np.zeros((128, 2 * 16), np.float32)
    for bt in range(2):
        for h in range(16):
            cb[:, bt * 16 + h] = SLOPES[h] * (16 * (128 * bt + i) + 15.5)
    c["c_cbias"] = cb
    c["c_maskd"] = np.where(i[:, None] > i[None, :], NEG, 0.0).astype(bf)
    c["c_maskw"] = np.where(i[None, :] >= i[:, None], NEG, 0.0).astype(bf)
    QT = NT // 512
    cm = np.zeros((QT, 2, 128, 512), np.float32)
    for T in range(QT):
        for bt in range(2):
            n = 128 * bt + i
            tt = 512 * T + np.arange(512)
            cm[T, bt] = np.where(16 * n[:, None] + 31 <= tt[None, :], 0.0, NEG)
    c["c_cmask"] = cm.astype(bf)
    ov = np.zeros((256, 64), np.float32)
    ci = np.arange(256)[:, None] * 16
    sj = np.arange(64)[None, :] * 64
    ov[:] = ((ci <= sj + 63) & (ci + 31 >= sj))
    ov[ncb:] = 0
    ov[:, nsb:] = 0
    c["c_overlap"] = ov.astype(bf)
    blk = np.arange(64)[None, :]
    cur = (t // 64)[:, None]
    valid = blk * 64 <= t[:, None]
    forced = ((blk == 0) | (blk == cur) | (blk == cur - 1)) & valid
    vm = valid.astype(np.float32)
    am = np.where(forced, 1e4, np.where(valid, 0.0, -1.0)).astype(np.float32)
    vm[:, nsb:] = 0.0
    am[:, nsb:] = -1.0
    c["c_vmask"] = vm
    c["c_amask"] = am
    si = np.zeros((64, NT), np.float32)
    si[0] = 1.0
    for j in range(1, 64):
        si[j] = (t // 64 == j)
    c["c_selind"] = si.astype(bf)
    kw = np.zeros((64, NT), np.float32)
    kw[0] = 1.0
    c["c_kwrows"] = kw.astype(bf)
    return c


def nsa_declare(nc, k, NT):
    def inp(name, shape, dt=F32):
        return nc.dram_tensor(name, list(shape), dt, kind="ExternalInput").ap()
    QT = NT // 512
    KT = NT // 128
    k.nsa_w_in = inp("nsa_w_in", [1, D, NSA_W])
    k.nsa_w_out = inp("nsa_w_out", [1, D, D])
    k.nsa_cmp_pe = inp("nsa_cmp_pe", [1, 2, 32, 64])
    k.nsa_cmp_w1 = inp("nsa_cmp_w1", [1, 2, 2048, 64])
    k.nsa_cmp_w2 = inp("nsa_cmp_w2", [1, 2, 64, 64])
    k.c_vrow = inp("c_vrow", [16, NT], BF16)
    k.c_sbias = inp("c_sbias", [128, KT * 16])
    k.c_cbias = inp("c_cbias", [128, 32])
    k.c_maskd = inp("c_maskd", [128, 128], BF16)
    k.c_maskw = inp("c_maskw", [128, 128], BF16)
    k.c_cmask = inp("c_cmask", [QT, 2, 128, 512], BF16)
    k.c_overlap = inp("c_overlap", [256, 64], BF16)
    k.c_vmask = inp("c_vmask", [NT, 64])
    k.c_amask = inp("c_amask", [NT, 64])
    k.c_selind = inp("c_selind", [64, NT], BF16)
    k.c_kwrows = inp("c_kwrows", [64, NT], BF16)
    k.qT_d = nc.dram_tensor("qT_d", [1024, NT], BF16).ap()
    k.kT_d = nc.dram_tensor("kT_d", [4, 256, NT], BF16).ap()
    k.vtok_d = nc.dram_tensor("vtok_d", [2, NT, 256], BF16).ap()
    k.gT_d = nc.dram_tensor("gT_d", [48, NT], F32).ap()
    k.kcT_d = nc.dram_tensor("kcT_d", [4, 64, 256], BF16).ap()
    k.vc_d = nc.dram_tensor("vc_d", [4, 256, 64], BF16).ap()
    k.oT_d = nc.dram_tensor("oT_d", [1024, NT], BF16).ap()


def phase_nsa_proj(S, k, l, xin, NT):
    TT = 512
    ntile = NT // TT
    load_consts(S, k)
    alloc_psum(S, k)
    cols = rows_to_cols(S, k, [k.modd[l, :], k.norm_mix[l, :]], "ncols")
    gcol = S.sbuf("gcol", [128, 8], F32)
    S.stt("dve", gcol, gcol[:], cols[:, 8:16], 1.0, cols[:, 48:56], ALU.add, ALU.mult, [cols])
    W = S.sbuf("Wn", [128, 8, NSA_W], BF16)
    stg = [S.sbuf("stg%d" % i, [128, 1024], F32) for i in range(3)]
    load_weight_bf16(S, W, k.nsa_w_in[0], 8, NSA_W, stg)
    xts = [S.sbuf("xt%d" % i, [128, 1024], F32) for i in range(2)]
    xns = [S.sbuf("xn%d" % i, [128, 1024], BF16) for i in range(4)]
    sq = S.sbuf("sq", [128, 1024], F32)
    sss = [S.sbuf("ss%d" % i, [128, 2], F32) for i in range(4)]
    rss = [S.sbuf("rs%d" % i, [128, 2], F32) for i in range(4)]
    hTs = [S.sbuf("hT%d" % i, [128, 8, TT], BF16) for i in range(2)]
    fsb = [S.sbuf("fsb%d" % i, [128, TT], BF16) for i in range(3)]
    vsb = [S.sbuf("vsb%d" % i, [128, 512], BF16) for i in range(2)]
    gsb = [S.sbuf("gsb%d" % i, [48, TT], F32) for i in range(2)]
    pT = k.ps[0]
    outs = fsb + vsb + gsb
    n = 0

    def norm_a(T0, sub):
        xt = xts[sub % 2]
        r = T0 * TT + sub * 128
        S.dma(xt, xt[:], xin[r:r + 128, :])
        S.act(sq, sq[:], xt[:], AF.Square, [xt], accum_out=sss[sub][:, 0:1], extra_writes=[sss[sub]])
        rstd_from_ss(S, rss[sub], sss[sub], D, 1)
        S.act(xns[sub], xns[sub][:], xt[:], AF.Copy, [xt, rss[sub]], scale=rss[sub][:, 0:1])

    def norm_b(T0, sub):
        hT_ = hTs[T0 % 2]
        for kc in range(8):
            S.transpose(pT, pT[:, kc * 128:(kc + 1) * 128], xns[sub][:, kc * 128:(kc + 1) * 128],
                        k.ident[:], [xns[sub], k.ident])
        for kc in range(8):
            S.act(hT_, hT_[:, kc, sub * 128:(sub + 1) * 128], pT[:, kc * 128:(kc + 1) * 128], AF.Identity,
                  [pT, gcol, cols], scale=gcol[:, kc:kc + 1], bias=cols[:, kc:kc + 1])

    for sub in range(4):
        norm_a(0, sub)
        norm_b(0, sub)
    for T0 in range(ntile):
        r0 = T0 * TT
        hT = hTs[T0 % 2]
        fm = [(hp * 128, ("q", hp)) for hp in range(8)]
        for kind, c0 in ((0, 1024), (1, 1280), (2, 1536), (3, 2048)):
            for cpart in range(2):
                fm.append((c0 + cpart * 128, ("k", kind, cpart)))
        for gi, (c0, tag) in enumerate(fm):
            if T0 + 1 < ntile:
                if gi == 0:
                    for sub_ in range(4):
                        norm_a(T0 + 1, sub_)
                if gi in (3, 6, 9, 12):
                    norm_b(T0 + 1, (gi - 3) // 3)
            ps = k.ps[1 + (n % 3)]
            f = fsb[n % 3]
            n += 1
            for kc in range(8):
                S.mm(ps, ps[:], W[:, kc, c0:c0 + 128], hT[:, kc, :], [W, hT], start=(kc == 0), stop=(kc == 7))
            if tag[0] == "q":
                S.act(f, f[:], ps[:], AF.Copy, [ps], scale=0.125)
                S.dma(None, k.qT_d[tag[1] * 128:(tag[1] + 1) * 128, r0:r0 + TT], f[:], in_t=f)
            else:
                S.copy("dve", f, f[:], ps[:], [ps])
                S.dma(None, k.kT_d[tag[1], tag[2] * 128:(tag[2] + 1) * 128, r0:r0 + TT], f[:], in_t=f)
        for sub in range(4):
            ps = k.ps[4 + (sub % 2)]
            v = vsb[sub % 2]
            for j, c0 in enumerate((1792, 2304)):
                for kc in range(8):
                    S.mm(ps, ps[:, j * 256:(j + 1) * 256], hT[:, kc, sub * 128:(sub + 1) * 128],
                         W[:, kc, c0:c0 + 256], [hT, W], start=(kc == 0), stop=(kc == 7))
            S.copy("dve", v, v[:], ps[:], [ps])
            for j in range(2):
                S.dma(None, k.vtok_d[j, r0 + sub * 128:r0 + (sub + 1) * 128, :], v[:, j * 256:(j + 1) * 256], in_t=v)
        ps = k.ps[6]
        g_ = gsb[T0 % 2]
        for kc in range(8):
            S.mm(ps, ps[0:48, :], W[:, kc, 2560:2608], hT[:, kc, :], [W, hT], start=(kc == 0), stop=(kc == 7))
        S.act(g_, g_[:], ps[0:48, :], AF.Sigmoid, [ps])
        S.dma(None, k.gT_d[:, r0:r0 + TT], g_[:], in_t=g_)
    S.finish_wait("sp", outs)


def phase_nsa_cmp(S, k, NT):
    ncb, nsb = nsa_dims(NT)
    load_consts(S, k)
    alloc_psum(S, k)
    xc = S.sbuf("xc", [64, 2, 4, NT], BF16)
    for kv in range(2):
        for g in range(4):
            S.dma(xc, xc[:, kv, g, :], k.kT_d[kv, g * 64:(g + 1) * 64, :])
    w1f = S.sbuf("w1f", [64, 2, 32, 64], F32)
    w1 = S.sbuf("w1", [64, 2, 32, 64], BF16)
    for kv in range(2):
        S.dma(w1f, w1f[:, kv, :, :], k.nsa_cmp_w1[0, kv].rearrange("(l d) e -> d l e", d=64))
    S.copy("pool", w1, w1[:], w1f[:], [w1f])
    w2f = S.sbuf("w2f", [64, 2, 64], F32)
    w2p = S.sbuf("w2p", [64, 2, 128], BF16)
    for kv in range(2):
        S.dma(w2f, w2f[:, kv, :], k.nsa_cmp_w2[0, kv])
    S.memset("pool", w2p, w2p[:], 0.0)
    S.copy("pool", w2p, w2p[:, :, 64:128], w2f[:], [w2f])
    pef = S.sbuf("pef", [32, 2, 64], F32)
    for kv in range(2):
        S.dma(pef, pef[:, kv, :], k.nsa_cmp_pe[0, kv])
    peT = S.sbuf("peT", [64, 2, 32], BF16)
    ps = k.ps[1]
    for kv in range(2):
        S.mm(ps, ps[0:64, kv * 32:(kv + 1) * 32], pef[:, kv, :], k.identf[0:32, 0:32], [pef, k.identf])
    S.copy("dve", peT, peT[:].rearrange("p a l -> p (a l)"), ps[0:64, 0:64], [ps])
    bias = S.sbuf("cbias", [64, 2], F32)
    ps = k.ps[2]
    for kv in range(2):
        for l in range(32):
            S.mm(ps, ps[0:64, kv:kv + 1], w1[:, kv, l, :], peT[:, kv, l:l + 1], [w1, peT],
                 start=(l == 0), stop=(l == 31))
    S.copy("dve", bias, bias[:], ps[0:64, 0:2], [ps])
    hid = [S.sbuf("hid%d" % i, [64, 256], BF16) for i in range(2)]
    osb = [S.sbuf("osb%d" % i, [128, 256], BF16) for i in range(2)]
    n = 0
    for kv in range(2):
        for g in range(4):
            ph = k.ps[3 + (n % 2)]
            hd = hid[n % 2]
            ob = osb[n % 2]
            for l in range(32):
                rhs = xc[:, kv, g, l:l + 16 * (ncb - 1) + 1:16]
                S.mm(ph, ph[0:64, 0:ncb], w1[:, kv, l, :], rhs, [w1, xc], start=(l == 0), stop=(l == 31))
            S.act(hd, hd[:, 0:ncb], ph[0:64, 0:ncb], AF.Silu, [ph, bias], bias=bias[:, kv:kv + 1])
            po = k.ps[5 + (n % 2)]
            if kv == 0:
                S.mm(po, po[:, 0:ncb], w2p[:, 0, :], hd[:, 0:ncb], [w2p, hd])
                S.copy("dve", ob, ob[64:128, 0:ncb], po[64:128, 0:ncb], [po])
                S.dma(None, k.kcT_d[g, :, 0:ncb], ob[64:128, 0:ncb], in_t=ob)
            else:
                for bt in range((ncb + 127) // 128):
                    nb = min(128, ncb - bt * 128)
                    S.mm(po, po[0:nb, bt * 64:(bt + 1) * 64], hd[:, bt * 128:bt * 128 + nb], w2p[:, 1, 64:128],
                         [hd, w2p])
                    S.copy("dve", ob, ob[0:nb, bt * 64:(bt + 1) * 64], po[0:nb, bt * 64:(bt + 1) * 64], [po])
                    S.dma(None, k.vc_d[g, bt * 128:bt * 128 + nb, :], ob[0:nb, bt * 64:(bt + 1) * 64], in_t=ob)
            n += 1
    S.finish_wait("sp", osb)


def phase_nsa_attn(S, k, NT):
    ncb, nsb = nsa_dims(NT)
    QT = NT // 512
    KT = NT // 128
    NBT = (ncb + 127) // 128
    load_consts(S, k)
    alloc_psum(S, k)
    sbias = S.sbuf("sbias", [128, KT * 16], F32)
    S.dma(sbias, sbias[:], k.c_sbias[:, :])
    cbias = S.sbuf("cbias", [128, 32], F32)
    S.dma(cbias, cbias[:], k.c_cbias[:, :])
    maskd = S.sbuf("maskd", [128, 128], BF16)
    S.dma(maskd, maskd[:], k.c_maskd[:, :])
    maskw = S.sbuf("maskw", [128, 128], BF16)
    S.dma(maskw, maskw[:], k.c_maskw[:, :])
    ovl = S.sbuf("ovl", [128, 2, 64], BF16)
    S.dma(ovl, ovl[:], k.c_overlap.rearrange("(bt p) j -> p bt j", p=128))
    Ks = S.sbuf("Ks", [128, NT], BF16)
    Kw = S.sbuf("Kw", [128, NT], BF16)
    Kc = S.sbuf("Kc", [128, 256], BF16)
    Vs = S.sbuf("Vs", [128, KT, 128], BF16)
    Vw = S.sbuf("Vw", [128, KT, 128], BF16)
    Vc = S.sbuf("Vc", [128, 2, 128], BF16)
    S.memset("pool", Vs, Vs[:], 1.0)
    S.memset("pool", Vw, Vw[:], 1.0)
    S.memset("pool", Vc, Vc[:], 1.0)
    S.memset("pool", Kc, Kc[:], 0.0)
    S.dma(Ks, Ks[0:64, :], k.c_selind[:, :])
    S.dma(Kw, Kw[0:64, :], k.c_kwrows[:, :])
    S.dma(Kc, Kc[0:64, :], k.c_kwrows[:, 0:256])
    QaT = [S.sbuf("Qa%d" % i, [128, 4, 512], BF16) for i in range(2)]
    Qav = [[S.view("Qa%d_%d" % (i, hh), QaT[i].ap[:, hh, :]) for hh in range(4)] for i in range(2)]
    for q in QaT:
        S.memset("pool", q, q[:], 0.0)
    for i in range(2):
        for hh in range(4):
            Qav[i][hh].last_w = QaT[i].last_w
    gbT = [S.sbuf("gb%d" % i, [128, 12, 512], F32) for i in range(2)]
    Pt = [S.sbuf("Pt%d" % i, [128, 512], BF16) for i in range(6)]
    cmk = [S.sbuf("cmk%d" % i, [128, 512], BF16) for i in range(2)]
    rd = [S.sbuf("rd%d" % i, [128, 512], F32) for i in range(2)]
    coef = [S.sbuf("coef%d" % i, [128, 512], F32) for i in range(2)]
    tmp = [S.sbuf("tmp%d" % i, [128, 512], F32) for i in range(2)]
    oacc = [S.sbuf("oacc%d" % i, [128, 512], F32) for i in range(4)]
    osbT = [S.sbuf("osb%d" % i, [128, 4, 512], BF16) for i in range(2)]
    impacc = S.sbuf("impacc", [128, 512], F32)
    negone = S.sbuf("negone", [128, 512], F32)
    S.memset("pool", negone, negone[:], -1.0)
    vm = S.sbuf("vm", [128, 4, 64], F32)
    am = S.sbuf("am", [128, 4, 64], F32)
    sc = S.sbuf("sc", [128, 4, 64], F32)
    sc2 = S.sbuf("sc2", [128, 64], F32)
    mx = S.sbuf("mx", [128, 8], F32)
    thr = S.sbuf("thr", [128, 4], F32)
    selb = S.sbuf("selb", [128, 4, 64], BF16)
    pT = k.ps[0]
    pSs = [k.ps[1], k.ps[2], k.ps[6], k.ps[5]]
    pOs = [k.ps[3], k.ps[7]]
    pI, pTk = k.ps[4], k.ps[5]
    cnt = {"s": 0, "p": 0, "r": 0, "cm": 0, "o": 0, "po": 0}
    NP = len(Pt)
    LOOK = 3
    pend = []
    cur = {}

    def finish_branch(pO, hh, br, first, want_imp):
        i = cnt["r"] % 2
        cnt["r"] += 1
        r_, c_, t_ = rd[i], coef[i], tmp[i]
        gbt = cur["gb"]
        if POW_RECIP:
            S.ts("dve", t_, t_[64:128, :], pO[64:128, :], 1e-30, None, ALU.add, reads=[pO])
            S.tt("pool", r_, r_[64:128, :], t_[64:128, :], negone[64:128, :], ALU.pow, [t_, negone])
        else:
            S.act(r_, r_[64:128, :], pO[64:128, :], AF.Ln, [pO], bias=(1e-30 if br == 0 else 0.0))
            S.act(r_, r_[64:128, :], r_[64:128, :], AF.Exp, [r_], scale=-1.0)
        S.tt("dve", c_, c_[64:128, :], r_[64:128, :], gbt[64:128, 3 * hh + br, :], ALU.mult, [r_, gbt])
        if first:
            S.tt("dve", oacc[hh], oacc[hh][0:64, :], pO[0:64, :], c_[64:128, :], ALU.mult, [pO, c_])
        else:
            S.tt("dve", t_, t_[0:64, :], pO[0:64, :], c_[64:128, :], ALU.mult, [pO, c_])
            S.tt("pool", oacc[hh], oacc[hh][0:64, :], oacc[hh][0:64, :], t_[0:64, :], ALU.add, [oacc[hh], t_])
        if want_imp:
            if hh == 0:
                S.tt("dve", impacc, impacc[0:64, :], pI[0:64, :], r_[64:128, :], ALU.mult, [pI, r_])
            else:
                S.tt("dve", t_, t_[0:64, :], pI[0:64, :], r_[64:128, :], ALU.mult, [pI, r_])
                S.tt("pool", impacc, impacc[0:64, :], impacc[0:64, :], t_[0:64, :], ALU.add, [impacc, t_])

    def emit_pv(item):
        (pO, lhsV, P, np_, clo, chi, first, last, ovl_ap, cb, vt_) = item
        S.mm(pO, pO[:, clo:chi], lhsV, P[0:np_, clo:chi], [vt_, P], start=first, stop=last)
        if ovl_ap is not None:
            S.mm(pI, pI[0:64, clo:chi], ovl_ap, P[0:np_, clo:chi], [ovl, P], start=first, stop=last)
        if cb is not None:
            cb()

    def push(item):
        pend.append(item)
        while len(pend) > LOOK:
            emit_pv(pend.pop(0))

    def flush():
        while pend:
            emit_pv(pend.pop(0))

    def attend(hh, h, Kt, Vt, tiles, br, first_branch):
        pO = pOs[cnt["po"] % 2]
        cnt["po"] += 1
        Qh = cur["Qa"][hh]
        for idx, (kt, clo, chi, masks) in enumerate(tiles):
            pS = pSs[cnt["s"] % len(pSs)]
            cnt["s"] += 1
            S.mm(pS, pS[:, clo:chi], Kt[:, kt * 128:(kt + 1) * 128], Qh[:, clo:chi], [Kt, Qh],
                 start=True, stop=(len(masks) == 0))
            for mi, (mk, c0) in enumerate(masks):
                S.mm(pS, pS[:, c0:c0 + 128], k.ident[:], mk[:], [k.ident, mk], start=False,
                     stop=(mi == len(masks) - 1))
            P = Pt[cnt["p"] % NP]
            cnt["p"] += 1
            S.act(P, P[:, clo:chi], pS[:, clo:chi], AF.Exp, [pS, sbias], bias=sbias[:, kt * 16 + h:kt * 16 + h + 1])
            last = (idx == len(tiles) - 1)
            cb = (lambda pO=pO, hh=hh, br=br, fb=first_branch: finish_branch(pO, hh, br, fb, False)) if last else None
            push((pO, Vt[:, kt, :], P, 128, clo, chi, idx == 0, last, None, cb, Vt))

    units = [(g, T) for g in range(4) for T in range(QT)]

    def load_unit(ui):
        g, T = units[ui]
        T0 = 512 * T
        par = ui % 2
        S.op("sp", lambda e: e.dma_start(out=gbT[par][64:128, :, :],
                                         in_=k.gT_d[12 * g:12 * g + 12, T0:T0 + 512].partition_broadcast(64)),
             [], [gbT[par]], dma_sem_tile=gbT[par])
        S.op("sp", lambda e: e.dma_start(out=QaT[par][64:128, :, :],
                                         in_=k.qT_d[256 * g:256 * g + 256, T0:T0 + 512].rearrange("(hh d) t -> d hh t", d=64)),
             [], Qav[par], dma_sem_tile=Qav[par][0])
        S.op("sp", lambda e: e.dma_start(out=QaT[par][0:1, :, :], in_=k.c_vrow[4 * g:4 * g + 4, T0:T0 + 512].rearrange("(o h) t -> o h t", o=1)),
             [], Qav[par], dma_sem_tile=Qav[par][0])

    def load_group(g):
        S.dma(Ks, Ks[64:128, :], k.kT_d[2, g * 64:(g + 1) * 64, :])
        S.dma(Kw, Kw[64:128, :], k.kT_d[3, g * 64:(g + 1) * 64, :])
        S.dma(Kc, Kc[64:128, 0:ncb], k.kcT_d[g, :, 0:ncb])
        for k0 in range(0, KT, 8):
            k1 = min(KT, k0 + 8)
            S.dma(Vs, Vs[:, k0:k1, 0:64],
                  k.vtok_d[0][k0 * 128:k1 * 128, g * 64:(g + 1) * 64].rearrange("(kt p) d -> p kt d", p=128))
            S.dma(Vw, Vw[:, k0:k1, 0:64],
                  k.vtok_d[1][k0 * 128:k1 * 128, g * 64:(g + 1) * 64].rearrange("(kt p) d -> p kt d", p=128))
        for bt in range(NBT):
            nb = min(128, ncb - bt * 128)
            S.dma(Vc, Vc[0:nb, bt, 0:64], k.vc_d[g, bt * 128:bt * 128 + nb, :])

    load_unit(0)
    for ui, (g, T) in enumerate(units):
        if T == 0:
            load_group(g)
        T0 = 512 * T
        par = ui % 2
        cur["Qa"] = Qav[par]
        cur["gb"] = gbT[par]
        Qa = Qav[par]
        bts = []
        for bt in range(NBT):
            nb = min(128, ncb - bt * 128)
            n_lo, n_hi = 128 * bt, 128 * bt + nb - 1
            if 16 * n_lo + 31 > T0 + 511:
                continue
            partial = 16 * n_hi + 31 > T0
            bts.append((bt, nb, partial))
        cms = {}
        for (bt, nb, partial) in bts:
            if partial:
                cm_ = cmk[cnt["cm"] % 2]
                cnt["cm"] += 1
                S.dma(cm_, cm_[:], k.c_cmask[T, bt])
                cms[bt] = cm_
        S.dma(vm, vm[:], k.c_vmask[T0:T0 + 512, :].rearrange("(n p) j -> p n j", p=128))
        S.dma(am, am[:], k.c_amask[T0:T0 + 512, :].rearrange("(n p) j -> p n j", p=128))
        for hh in range(4):
            h = 4 * g + hh
            pO = pOs[cnt["po"] % 2]
            cnt["po"] += 1
            for bi, (bt, nb, partial) in enumerate(bts):
                pS = pSs[cnt["s"] % len(pSs)]
                cnt["s"] += 1
                S.mm(pS, pS[0:nb, :], Kc[:, bt * 128:bt * 128 + nb], Qa[hh][:, :], [Kc, Qa[hh]],
                     start=True, stop=(not partial))
                if partial:
                    cm_ = cms[bt]
                    S.mm(pS, pS[0:nb, :], k.ident[0:nb, 0:nb], cm_[0:nb, :], [k.ident, cm_], start=False, stop=True)
                P = Pt[cnt["p"] % NP]
                cnt["p"] += 1
                S.act(P, P[0:nb, :], pS[0:nb, :], AF.Exp, [pS, cbias],
                      bias=cbias[0:nb, bt * 16 + h:bt * 16 + h + 1])
                last = (bi == len(bts) - 1)
                cb = (lambda pO=pO, hh=hh: finish_branch(pO, hh, 0, True, True)) if last else None
                push((pO, Vc[0:nb, bt, :], P, nb, 0, 512, bi == 0, last, ovl[0:nb, bt, :], cb, Vc))
        for hh in range(4):
            h = 4 * g + hh
            tiles = []
            for kt in range(max(0, 4 * T - 4), 4 * T + 4):
                m = kt - 4 * T
                n_lo, n_hi = max(m, 0), min(m + 4, 3)
                masks = []
                if m >= 0:
                    masks.append((maskd, 128 * m))
                if m <= -1:
                    masks.append((maskw, 128 * (m + 4)))
                tiles.append((kt, 128 * n_lo, 128 * (n_hi + 1), masks))
            tiles.sort(key=lambda tl: -(tl[2] - tl[1]))
            attend(hh, h, Kw, Vw, tiles, 2, False)
            if hh == 1:
                for n in range(4):
                    S.mm(pTk, pTk[:, n * 64:(n + 1) * 64], impacc[0:64, n * 128:(n + 1) * 128],
                         k.identf[0:64, 0:64], [impacc, k.identf])
                S.tt("dve", sc, sc[:].rearrange("p n j -> p (n j)"), pTk[:, 0:256],
                     vm[:].rearrange("p n j -> p (n j)"), ALU.mult, [pTk, vm])
                S.tt("pool", sc, sc[:], sc[:], am[:], ALU.add, [sc, am])
                for n in range(4):
                    S.op("dve", lambda e, n=n: e.max(mx[:], sc[:, n, :]), [sc], [mx])
                    S.op("dve", lambda e, n=n: e.match_replace(sc2[:], mx[:], sc[:, n, :], -1e9), [sc, mx], [sc2])
                    S.op("dve", lambda e: e.max(mx[:], sc2[:]), [sc2], [mx])
                    S.ts("dve", thr, thr[:, n:n + 1], mx[:, 7:8], -0.5, None, ALU.max, reads=[mx])
                    S.ts("dve", sc2, sc2[:], sc[:, n, :], thr[:, n:n + 1], None, ALU.is_ge, reads=[sc, thr])
                    S.ts("dve", selb, selb[:, n, :], sc2[:], -1.0, -NEG, ALU.add, ALU.mult, reads=[sc2])
        if ui + 1 < len(units):
            load_unit(ui + 1)
        for n in range(4):
            S.transpose(pT, pT[0:64, n * 128:(n + 1) * 128], selb[:, n, :], k.ident[:], [selb, k.ident])
        flush()
        for hh in range(4):
            S.copy("act", Qa[hh], Qa[hh][0:64, :], pT[0:64, 0:512], [pT])
        S.op("sp", lambda e, par=par, g=g, T0=T0: e.dma_start(
            out=QaT[par][0:1, :, :], in_=k.c_vrow[4 * g:4 * g + 4, T0:T0 + 512].rearrange("(o h) t -> o h t", o=1)),
            [], Qa, dma_sem_tile=Qa[0])
        for hh in range(4):
            h = 4 * g + hh
            tiles = []
            for kt in range(4 * T + 4):
                j = kt - 4 * T
                if j < 0:
                    tiles.append((kt, 0, 512, []))
                else:
                    tiles.append((kt, 128 * j, 512, [(maskd, 128 * j)]))
            attend(hh, h, Ks, Vs, tiles, 1, False)
        flush()
        ob = osbT[cnt["o"] % 2]
        cnt["o"] += 1
        for hh in range(4):
            S.copy("act", ob, ob[0:64, hh, :], oacc[hh][0:64, :], [oacc[hh]])
        S.dma(None, k.oT_d[256 * g:256 * g + 256, T0:T0 + 512].rearrange("(hh d) t -> d hh t", d=64),
              ob[0:64, :, :], in_t=ob)
    S.finish_wait("sp", osbT)


def phase_nsa_out(S, k, l, xin, xout, NT):
    TT = 512
    ntile = NT // TT
    load_consts(S, k)
    alloc_psum(S, k)
    g1row = S.sbuf("g1row", [128, 1024], F32)
    S.dma(g1row, g1row[:], k.modd[l, 2048:3072].partition_broadcast(128))
    Wo = S.sbuf("Wo", [128, 8, 1024], BF16)
    stg = [S.sbuf("stg%d" % i, [128, 1024], F32) for i in range(2)]
    load_weight_bf16(S, Wo, k.nsa_w_out[0], 8, 1024, stg, grow=g1row)
    oT = [S.sbuf("oT%d" % i, [128, 8, TT], BF16) for i in range(2)]
    xts = [S.sbuf("xt%d" % i, [128, 1024], F32) for i in range(2)]
    xo = [S.sbuf("xo%d" % i, [128, 1024], F32) for i in range(2)]
    n = 0
    for T0 in range(ntile):
        r0 = T0 * TT
        o_ = oT[T0 % 2]
        S.dma(o_, o_[:], k.oT_d[:, r0:r0 + TT].rearrange("(kc p) t -> p kc t", p=128))
        for sub in range(4):
            xt = xts[sub % 2]
            xo_ = xo[sub % 2]
            S.dma(xt, xt[:], xin[r0 + sub * 128:r0 + (sub + 1) * 128, :])
            for half in range(2):
                py = k.ps[1 + (n % 4)]
                n += 1
                for kc in range(8):
                    S.mm(py, py[:], o_[:, kc, sub * 128:(sub + 1) * 128], Wo[:, kc, half * 512:(half + 1) * 512],
                         [o_, Wo], start=(kc == 0), stop=(kc == 7))
                S.tt("dve", xo_, xo_[:, half * 512:(half + 1) * 512], py[:], xt[:, half * 512:(half + 1) * 512],
                     ALU.add, [py, xt])
            S.dma(None, xout[r0 + sub * 128:r0 + (sub + 1) * 128, :], xo_[:], in_t=xo_)
    S.finish_wait("sp", xo)


W_KEYS = ["ada_w", "ada_b", "norm_mix", "norm_ffn", "final_norm", "hg_w_in", "hg_w_out", "hg_gnorm", "hg_lb",
          "ffn_w_up", "ffn_conv_w", "ffn_conv_b", "ffn_w_down", "nsa_w_in", "nsa_w_out", "nsa_cmp_pe",
          "nsa_cmp_w1", "nsa_cmp_w2"]


def make_in_map(inp, x, c, NT, consts=None):
    im = {"x": np.ascontiguousarray(x, dtype=np.float32),
          "c_col": np.ascontiguousarray(np.asarray(c, dtype=np.float32).reshape(8, 128).T)}
    for k_ in W_KEYS:
        im[k_] = np.asarray(inp[k_], dtype=np.float32)
    if consts is None:
        consts = dict(host_consts())
        consts.update(nsa_host_consts(NT))
    im.update(consts)
    return im


_CACHE = {}


def kernel(**inputs):
    x = np.asarray(inputs["x"], dtype=np.float32)
    c = np.asarray(inputs["c"], dtype=np.float32)
    B, NT, _ = x.shape
    if "nc" not in _CACHE:
        _CACHE["nc"] = build_program(NT)
        consts = dict(host_consts())
        consts.update(nsa_host_consts(NT))
        _CACHE["consts"] = consts
    nc = _CACHE["nc"]
    in_maps = [make_in_map(inputs, x[b], c[b], NT, _CACHE["consts"]) for b in range(B)]
    res = run_bass_kernel_spmd(nc, in_maps, core_ids=list(range(B)))
    out = np.stack([np.asarray(r["out"], dtype=np.float32) for r in res.results], axis=0)
    return out
```

```python
import contextlib
import numpy as np
import concourse.bass as bass
import concourse.mybir as mybir

F32 = mybir.dt.float32
BF16 = mybir.dt.bfloat16
AF = mybir.ActivationFunctionType
ALU = mybir.AluOpType
AX = mybir.AxisListType

ENGS = ["pe", "act", "dve", "pool", "sp"]


class T:
    __slots__ = ("name", "ap", "last_w", "readers", "dsem", "dcnt", "uid", "excl")
    _n = [0]

    def __init__(self, name, ap=None):
        T._n[0] += 1
        self.uid = T._n[0]
        self.name = name
        self.ap = ap
        self.last_w = None
        self.readers = []
        self.dsem = None
        self.dcnt = 0
        self.excl = False

    def __getitem__(self, idx):
        return self.ap[idx]


class Sched:
    def __init__(self, nc, stack):
        self.nc = nc
        self.stack = stack
        self.ops = {e: [] for e in ENGS}
        self.cnt = {e: 0 for e in ENGS}
        self.clock = {e: {} for e in ENGS}
        self.sem = {}
        for e in ["pe", "act", "dve", "pool"]:
            self.sem[e] = stack.enter_context(nc.semaphore("s_" + e))
        self.sem["bar"] = stack.enter_context(nc.semaphore("s_bar"))
        self.bar_n = 0
        self.dma_live = {}
        self.gstack = stack
        self.nsem = 5
        self.final_waits = []
        self.n_wait = 0
        self.uid = 0
        self.dsem_pool = []
        self.dsem_owner = []

    def sbuf(self, name, shape, dtype):
        self.uid += 1
        name = "%s_u%d" % (name, self.uid)
        t = self.stack.enter_context(self.nc.sbuf_tensor(name, list(shape), dtype))
        return T(name, t)

    def psum(self, name, shape, dtype=F32):
        self.uid += 1
        name = "%s_u%d" % (name, self.uid)
        t = self.stack.enter_context(self.nc.psum_tensor(name, list(shape), dtype))
        tt_ = T(name, t)
        tt_.excl = True
        return tt_

    def view(self, name, ap):
        return T(name, ap)

    def _dsem(self, t):
        if t.dsem is None:
            if self.dsem_pool:
                t.dsem, t.dcnt = self.dsem_pool.pop()
            else:
                self.nsem += 1
                t.dsem = self.gstack.enter_context(self.nc.semaphore("dsem%d" % self.nsem))
                t.dcnt = 0
            self.dsem_owner.append(t)
        return t.dsem

    def phase_end(self):
        for t in self.dsem_owner:
            self.dsem_pool.append((t.dsem, t.dcnt))
            t.dsem = None
        self.dsem_owner = []
        self.dma_live = {}

    def _need(self, eng, ev, waits):
        key, val, snap = ev
        if eng == "pe" and key == "pe":
            return
        ck = self.clock[eng]
        if ck.get(key, 0) >= val:
            return
        waits[key] = max(waits.get(key, 0), val)
        ck[key] = val
        if snap:
            for k, v in snap.items():
                if ck.get(k, 0) < v:
                    ck[k] = v

    def op(self, eng, fn, reads=(), writes=(), dma_sem_tile=None):
        waits = {}
        ex = [t for t in reads if t.excl]
        if ex:
            reads = [t for t in reads if not t.excl]
            writes = list(writes) + [t for t in ex if t not in writes]
        for t in reads:
            if t.last_w is not None:
                self._need(eng, t.last_w, waits)
        for t in writes:
            if t.last_w is not None:
                self._need(eng, t.last_w, waits)
            for ev in t.readers:
                self._need(eng, ev, waits)
        if dma_sem_tile is not None:
            st = dma_sem_tile
            sem = self._dsem(st)
            st.dcnt += 16
            key = ("d", st.uid)
            self.sem[key] = sem
            ev = (key, st.dcnt, dict(self.clock[eng]))
            self.dma_live[key] = (st, st.dcnt)
            inc = (sem, 16)
        else:
            self.cnt[eng] += 1
            ev = (eng, self.cnt[eng], None)
            inc = (self.sem[eng], 1)
        self.ops[eng].append((list(waits.items()), fn, inc))
        self.n_wait += len(waits)
        if dma_sem_tile is None:
            snap = dict(self.clock[eng])
            ev = (eng, self.cnt[eng], snap)
        for t in writes:
            t.last_w = ev
            t.readers = []
        for t in reads:
            if t not in writes:
                t.readers.append(ev)
        return ev

    def finish_wait(self, eng, tiles):
        waits = {}
        for t in tiles:
            if t.last_w is not None:
                self._need(eng, t.last_w, waits)
            for ev in t.readers:
                self._need(eng, ev, waits)
        self.ops[eng].append((list(waits.items()), None, None))

    def barrier(self):
        evs = []
        for e in ["pe", "act", "dve", "pool"]:
            if self.cnt[e] > 0:
                evs.append((e, self.cnt[e], None))
        for key, (t, val) in self.dma_live.items():
            evs.append((key, val, None))
        for eng in ENGS:
            waits = {}
            for ev in evs:
                if ev[0] == eng and eng == "pe":
                    continue
                ck = self.clock[eng]
                if ck.get(ev[0], 0) < ev[1]:
                    waits[ev[0]] = ev[1]
                    ck[ev[0]] = ev[1]
            self.ops[eng].append((list(waits.items()), None, None))
        self.bar_n += 1
        for eng in ENGS:
            self.ops[eng].append(([], "barinc", None))
        for eng in ENGS:
            self.ops[eng].append(([("bar", 5 * self.bar_n)], None, None))

    def emit(self):
        nc = self.nc
        with nc.Block() as block:
            def run(eng_name):
                def body(e):
                    for waits, fn, inc in self.ops[eng_name]:
                        for key, val in waits:
                            e.wait_ge(self.sem[key], val)
                        if fn == "barinc":
                            e.sem_inc(self.sem["bar"], 1)
                        elif fn is not None:
                            ins = fn(e)
                            ins.then_inc(inc[0], inc[1])
                return body
            block.tensor(run("pe"))
            block.scalar(run("act"))
            block.vector(run("dve"))
            block.gpsimd(run("pool"))
            block.sync(run("sp"))
        self.ops = {e: [] for e in ENGS}

    def dma(self, out_t, out_ap, in_ap, in_t=None, eng="sp", **kw):
        reads = [in_t] if in_t is not None else []
        writes = [out_t] if out_t is not None else []
        st = out_t if out_t is not None else in_t
        return self.op(eng, lambda e: e.dma_start(out=out_ap, in_=in_ap, **kw), reads, writes,
                       dma_sem_tile=st)

    def mm(self, out_t, out_ap, lhsT, rhs, reads, start=True, stop=True, **kw):
        return self.op("pe", lambda e: e.matmul(out_ap, lhsT, rhs, start=start, stop=stop, **kw),
                       reads, [out_t])

    def transpose(self, out_t, out_ap, in_ap, ident_ap, reads):
        return self.op("pe", lambda e: e.transpose(out_ap, in_ap, ident_ap), reads, [out_t])

    def act(self, out_t, out_ap, in_ap, func, reads, bias=None, scale=None, accum_out=None,
            extra_writes=()):
        kw = {}
        if bias is not None:
            kw["bias"] = bias
        if scale is not None:
            kw["scale"] = scale
        if accum_out is not None:
            kw["accum_out"] = accum_out
        return self.op("act", lambda e: e.activation(out_ap, in_ap, func, **kw), reads,
                       [out_t] + list(extra_writes))

    def tt(self, eng, out_t, out_ap, in0, in1, op, reads):
        return self.op(eng, lambda e: e.tensor_tensor(out_ap, in0, in1, op), reads, [out_t])

    def ts(self, eng, out_t, out_ap, in0, s1, s2, op0, op1=None, reads=(), accum_out=None,
           extra_writes=()):
        def f(e):
            kw = {}
            if accum_out is not None:
                kw["accum_out"] = accum_out
            if op1 is None:
                return e.tensor_scalar(out_ap, in0, s1, None, op0, **kw)
            return e.tensor_scalar(out_ap, in0, s1, s2, op0, op1, **kw)
        return self.op(eng, f, reads, [out_t] + list(extra_writes))

    def stt(self, eng, out_t, out_ap, in0, scalar, in1, op0, op1, reads):
        eng = "dve"
        return self.op(eng, lambda e: e.scalar_tensor_tensor(out_ap, in0, scalar, in1, op0, op1),
                       reads, [out_t])

    def copy(self, eng, out_t, out_ap, in_ap, reads):
        if eng == "act":
            return self.op("act", lambda e: e.copy(out_ap, in_ap), reads, [out_t])
        return self.op(eng, lambda e: e.tensor_copy(out_ap, in_ap), reads, [out_t])

    def memset(self, eng, out_t, out_ap, val):
        return self.op(eng, lambda e: e.memset(out_ap, val), [], [out_t])

from concourse.bass_utils import run_bass_kernel_spmd

D = 1024
NH_HG = 8
DFF = 2816
NFC = DFF // 128
EPS = 1e-6
NEG = -30000.0


def bcast_rows(ap_row, n):
    return ap_row.partition_broadcast(n)


class K:
    pass


def load_weight_bf16(S, Wb, w_dram, KC, N, stg, grow=None, col_off=0, rowscale=None, kc_off=0):
    i = 0
    for kc in range(KC):
        for n0 in range(0, N, 1024):
            n1 = min(N, n0 + 1024)
            st = stg[i % len(stg)]
            S.dma(st, st[:, 0:n1 - n0], w_dram[kc * 128:(kc + 1) * 128, n0:n1])
            eng = "dve"
            o = Wb[:, kc_off + kc, col_off + n0:col_off + n1]
            if grow is None:
                S.copy("act" if i % 2 == 0 else "dve", Wb, o, st[:, 0:n1 - n0], [st])
            elif rowscale is None:
                S.tt(eng, Wb, o, st[:, 0:n1 - n0], grow[:, n0:n1], ALU.mult, [st, grow])
            else:
                S.stt(eng, Wb, o, st[:, 0:n1 - n0], rowscale[:, kc:kc + 1], grow[:, n0:n1],
                      ALU.mult, ALU.mult, [st, grow, rowscale])
            i += 1


def rstd_from_ss(S, rstd, ss, n, width):
    S.act(rstd, rstd[:, 0:width], ss[:, 0:width], AF.Ln, [ss], scale=1.0 / n, bias=EPS)
    S.act(rstd, rstd[:, 0:width], rstd[:, 0:width], AF.Exp, [rstd], scale=-0.5)


def norm_to_hT(S, k, xt_t, xt_ap, gcol, shcol, hT, col0, xn, sq, ss, rstd, pT):
    S.act(sq, sq[:], xt_ap, AF.Square, [xt_t], accum_out=ss[:, 0:1], extra_writes=[ss])
    rstd_from_ss(S, rstd, ss, D, 1)
    S.act(xn, xn[:], xt_ap, AF.Copy, [xt_t, rstd], scale=rstd[:, 0:1])
    for kc in range(8):
        S.transpose(pT, pT[:, kc * 128:(kc + 1) * 128], xn[:, kc * 128:(kc + 1) * 128],
                    k.ident[:], [xn, k.ident])
    for kc in range(8):
        eng = "dve" if kc % 2 == 0 else "pool"
        eng = "dve"
        S.ts(eng, hT, hT[:, kc, col0:col0 + 128], pT[:, kc * 128:(kc + 1) * 128],
             gcol[:, kc:kc + 1], shcol[:, kc:kc + 1], ALU.mult, ALU.add, reads=[pT, gcol, shcol])


def rows_to_cols(S, k, rows, name):
    n = sum(r.shape[0] // 128 for r in rows)
    assert n <= 128
    rt = S.sbuf(name + "_r", [n, 128], F32)
    ct = S.sbuf(name, [128, n], F32)
    j = 0
    for r in rows:
        m = r.shape[0] // 128
        S.dma(rt, rt[j:j + m, :], r.rearrange("(j p) -> j p", p=128))
        j += m
    ps = k.ps[1]
    S.mm(ps, ps[:, 0:n], rt[0:n, :], k.identf[0:n, 0:n], [rt, k.identf])
    S.copy("dve", ct, ct[:], ps[:, 0:n], [ps])
    return ct


def phase_prologue(S, k):
    cc = S.sbuf("cc", [128, 8], F32)
    ca = S.sbuf("ca", [128, 8], F32)
    S.dma(cc, cc[:], k.c_col[:, :])
    S.act(ca, ca[:], cc[:], AF.Silu, [cc])
    wst = [S.sbuf("adw%d" % i, [128, 8, 512], F32) for i in range(2)]
    brow = S.sbuf("brow", [1, 6144], F32)
    mrow = S.sbuf("mrow", [1, 6144], F32)
    i = 0
    for l in range(2):
        S.dma(brow, brow[:], k.ada_b[l:l + 1, :])
        for nt in range(12):
            wt = wst[i % 2]
            S.dma(wt, wt[:], k.ada_w[l].rearrange("(kc p) n -> p kc n", p=128)[:, :, nt * 512:(nt + 1) * 512])
            ps = k.ps[2 + (i % 2)]
            for kc in range(8):
                S.mm(ps, ps[0:1, :], ca[:, kc:kc + 1], wt[:, kc, :], [ca, wt],
                     start=(kc == 0), stop=(kc == 7))
            S.tt("dve", mrow, mrow[0:1, nt * 512:(nt + 1) * 512], ps[0:1, :],
                 brow[0:1, nt * 512:(nt + 1) * 512], ALU.add, [ps, brow])
            i += 1
        S.dma(None, k.modd[l:l + 1, :], mrow[:], in_t=mrow)
    S.finish_wait("sp", [mrow])


def load_consts(S, k):
    k.identf = S.sbuf("identf", [128, 128], F32)
    k.ident = S.sbuf("ident", [128, 128], BF16)
    S.dma(k.identf, k.identf[:], k.c_ident[:, :])
    S.copy("dve", k.ident, k.ident[:], k.identf[:], [k.identf])


def alloc_psum(S, k, bf7=False):
    k.ps = [S.psum("psb0", [128, 1024], BF16)] + [S.psum("ps%d" % i, [128, 512], F32) for i in range(1, 7)]
    if bf7:
        k.ps.append(S.psum("psb7", [128, 1024], BF16))
    else:
        k.ps.append(S.psum("ps7", [128, 512], F32))


def interleave(ga, gb, ra=1, rb=1):
    da = db = False
    while not (da and db):
        for _ in range(ra):
            if not da:
                try:
                    next(ga)
                except StopIteration:
                    da = True
        for _ in range(rb):
            if not db:
                try:
                    next(gb)
                except StopIteration:
                    db = True


def phase_hgrn(S, k, l, xin, xout, NT):
    TT = 256
    ntile = NT // TT
    load_consts(S, k)
    alloc_psum(S, k, bf7=True)
    cols = rows_to_cols(S, k, [k.modd[l, :], k.norm_mix[l, :], k.hg_lb[0, :], k.hg_lb[1, :],
                               k.hg_gnorm[0, :]], "hcols")
    gnc = S.sbuf("gnc", [128, 8], F32)
    S.copy("dve", gnc, gnc[:], cols[:, 72:73].to_broadcast([128, 8]), [cols])
    gcol = S.sbuf("gcol", [128, 8], F32)
    S.stt("dve", gcol, gcol[:], cols[:, 8:16], 1.0, cols[:, 48:56], ALU.add, ALU.mult, [cols])
    lbc = S.sbuf("lbc", [128, 8], F32)
    l1m = S.sbuf("l1m", [128, 8], F32)
    S.tt("dve", lbc, lbc[:], cols[:, 64:72], cols[:, 56:64], ALU.subtract, [cols])
    S.act(lbc, lbc[:], lbc[:], AF.Exp, [lbc])
    S.ts("dve", lbc, lbc[:], lbc[:], 1.0, None, ALU.add, reads=[lbc])
    S.op("dve", lambda e: e.reciprocal(lbc[:], lbc[:]), [lbc], [lbc])
    S.ts("dve", l1m, l1m[:], lbc[:], -1.0, 1.0, ALU.mult, ALU.add, reads=[lbc])
    S.act(l1m, l1m[:], l1m[:], AF.Ln, [l1m])
    sq = S.sbuf("sq", [128, 1024], F32)
    g1row = sq
    S.dma(g1row, g1row[:], k.modd[l, 2048:3072].partition_broadcast(128))
    Win = S.sbuf("Win", [128, 8, 4096], BF16)
    Wout = S.sbuf("Wout", [128, 8, 1024], BF16)
    otok = S.sbuf("otok", [128, 1024], F32)
    xo = [S.sbuf("xo%d" % i, [128, 1024], F32) for i in range(2)]
    stg = [otok, xo[0]]
    load_weight_bf16(S, Win, k.hg_w_in[0], 8, 4096, stg)
    load_weight_bf16(S, Wout, k.hg_w_out[0], 8, 1024, stg, grow=g1row, rowscale=gnc)
    rmask = S.sbuf("rmask", [128, TT], F32)
    S.memset("pool", rmask, rmask[:], 1.0)
    S.memset("pool", rmask, rmask[:].rearrange("p (c j) -> p c j", j=64)[:, :, 0:1], 0.0)
    cmask = S.sbuf("cmask", [128, 128], F32)
    S.dma(cmask, cmask[:], k.c_bdmask[:, :])
    st32s = [S.sbuf("st32_%d" % i, [128, 128], F32) for i in range(8)]
    stbs = [S.sbuf("stb_%d" % i, [128, 128], BF16) for i in range(8)]
    for i in range(8):
        S.memset("pool", st32s[i], st32s[i][:], 0.0)
        S.memset("pool", stbs[i], stbs[i][:], 0.0)
    sts = [S.sbuf("sts%d" % i, [128, 128], F32) for i in range(8)]
    sts2 = sts
    xts = [S.sbuf("xt%d" % i, [128, 1024], F32) for i in range(4)]
    xns = [S.sbuf("xn%d" % i, [128, 1024], BF16) for i in range(2)]
    sss = [S.sbuf("ss%d" % i, [128, 2], F32) for i in range(2)]
    rss = [S.sbuf("rs%d" % i, [128, 2], F32) for i in range(2)]
    hT = S.sbuf("hT", [128, 8, TT], BF16)
    NTMP = 2
    tu = [S.sbuf("tu%d" % i, [128, TT], F32) for i in range(NTMP)]
    tA = [S.sbuf("tA%d" % i, [128, TT], F32) for i in range(NTMP)]
    tB = [S.sbuf("tB%d" % i, [128, TT], F32) for i in range(NTMP)]
    tb = [S.sbuf("tb%d" % i, [128, TT], F32) for i in range(NTMP)]
    teb = [S.sbuf("teb%d" % i, [128, TT], F32) for i in range(NTMP)]
    t1 = [S.sbuf("t1%d" % i, [128, TT], F32) for i in range(NTMP)]
    qdT2 = [S.sbuf("qdT%d" % i, [128, 8, 2, 2, 128], BF16) for i in range(2)]
    kdT2 = [S.sbuf("kdT%d" % i, [128, 8, TT], BF16) for i in range(2)]
    kdtok2 = [S.sbuf("kdtok%d" % i, [128, 2, 8, 128], BF16) for i in range(2)]
    ebl2 = [S.sbuf("ebl%d" % i, [128, 8, 4], F32) for i in range(2)]
    vt2 = [S.sbuf("vt%d" % i, [128, 2, 1024], BF16) for i in range(2)]
    gs2 = [S.sbuf("gs%d" % i, [128, 2, 1024], BF16) for i in range(2)]
    oss = S.sbuf("oss", [128, 8], F32)
    orstd = S.sbuf("orstd", [128, 8], F32)
    on = S.sbuf("on", [128, 1024], BF16)
    oT = S.sbuf("oT", [128, 8, 128], BF16)
    sc4s = [S.sbuf("sc4_%d" % i, [128, 512], BF16) for i in range(2)]
    st1 = [S.sbuf("st1_%d" % i, [128, 128], F32) for i in range(8)]
    st1b = [S.sbuf("st1b_%d" % i, [128, 128], BF16) for i in range(8)]
    for q in qdT2:
        S.memset("pool", q, q[:], 0.0)
    pT, pT2 = k.ps[0], k.ps[7]
    pq = [k.ps[1], k.ps[2]]
    pvs = [k.ps[1], k.ps[2]]
    tq = [S.sbuf("tq%d" % i, [128, TT], F32) for i in range(NTMP)]

    def gen_A(T0):
        par = T0 % 2
        qdT, kdT, kdtok, ebl, vt, gs = qdT2[par], kdT2[par], kdtok2[par], ebl2[par], vt2[par], gs2[par]
        r0 = T0 * TT
        for sub in range(2):
            xt = xts[(T0 * 2 + sub) % 4]
            S.dma(xt, xt[:], xin[r0 + sub * 128:r0 + (sub + 1) * 128, :])
            S.act(sq, sq[:], xt[:], AF.Square, [xt], accum_out=sss[sub][:, 0:1], extra_writes=[sss[sub]])
            rstd_from_ss(S, rss[sub], sss[sub], D, 1)
            S.act(xns[sub], xns[sub][:], xt[:], AF.Copy, [xt, rss[sub]], scale=rss[sub][:, 0:1])
            yield
        for sub in range(2):
            for kc in range(8):
                S.transpose(pT, pT[:, kc * 128:(kc + 1) * 128], xns[sub][:, kc * 128:(kc + 1) * 128],
                            k.ident[:], [xns[sub], k.ident])
            for kc in range(8):
                S.act(hT, hT[:, kc, sub * 128:(sub + 1) * 128], pT[:, kc * 128:(kc + 1) * 128], AF.Identity,
                      [pT, gcol, cols], scale=gcol[:, kc:kc + 1], bias=cols[:, kc:kc + 1])
            yield
        def s1(h):
            i = h % NTMP
            pq_ = pq[h % 2]
            for kc in range(8):
                S.mm(pq_, pq_[:, 0:TT], Win[:, kc, h * 128:(h + 1) * 128], hT[:, kc, :], [Win, hT],
                     start=(kc == 0), stop=(kc == 7))
            for kc in range(8):
                S.mm(pq_, pq_[:, TT:2 * TT], Win[:, kc, 1024 + h * 128:1024 + (h + 1) * 128], hT[:, kc, :],
                     [Win, hT], start=(kc == 0), stop=(kc == 7))
            z = pq_[:, TT:2 * TT]
            S.act(tu[i], tu[i][:], z, AF.Exp, [pq_], scale=-1.0)
            S.act(tA[i], tA[i][:], tu[i][:], AF.Ln, [tu[i]], bias=1.0)
            S.act(tB[i], tB[i][:], tu[i][:], AF.Ln, [tu[i], lbc], bias=1.0, scale=lbc[:, h:h + 1])
            S.copy("dve", tq[i], tq[i][:], pq_[:, 0:TT], [pq_])
            S.tt("dve", t1[i], t1[i][:], z, tA[i][:], ALU.add, [pq_, tA[i]])

        def s2(h):
            i = h % NTMP
            pq_ = pq[h % 2]
            z = pq_[:, TT:2 * TT]
            S.tt("pool", tB[i], tB[i][:], tB[i][:], tA[i][:], ALU.subtract, [tB[i], tA[i]])
            S.op("dve", lambda e, o=tb[i], m=tB[i]: e.tensor_tensor_scan(o[:], rmask[:], m[:], 0.0, ALU.mult, ALU.add),
                 [rmask, tB[i]], [tb[i]])
            S.act(teb[i], teb[i][:], tb[i][:], AF.Exp, [tb[i]])
            S.tt("pool", t1[i], t1[i][:], t1[i][:], tb[i][:], ALU.add, [t1[i], tb[i]])

        def s3(h):
            i = h % NTMP
            pq_ = pq[h % 2]
            for sub in range(2):
                for c2 in range(2):
                    cs = sub * 128 + c2 * 64
                    S.tt("dve", qdT, qdT[:, h, sub, c2, c2 * 64:(c2 + 1) * 64], tq[i][:, cs:cs + 64],
                         teb[i][:, cs:cs + 64], ALU.mult, [tq[i], teb[i]])
            S.act(kdT, kdT[:, h, :], t1[i][:], AF.Exp, [t1[i], l1m], scale=-1.0, bias=l1m[:, h:h + 1])
            S.copy("pool", ebl, ebl[:, h, :], teb[i][:].rearrange("p (c j) -> p c j", j=64)[:, :, 63], [teb[i]])

        for it in range(10):
            if 0 <= it - 2 < 8:
                s3(it - 2)
            if 0 <= it - 1 < 8:
                s2(it - 1)
            if it < 8:
                s1(it)
            yield
        for sub in range(2):
            for half in range(4):
                c0 = 2048 + half * 512
                pv = pvs[half % 2]
                for kc in range(8):
                    S.mm(pv, pv[:], hT[:, kc, sub * 128:(sub + 1) * 128], Win[:, kc, c0:c0 + 512],
                         [hT, Win], start=(kc == 0), stop=(kc == 7))
                if half < 2:
                    S.copy("dve", vt, vt[:, sub, half * 512:(half + 1) * 512], pv[:], [pv])
                else:
                    S.act(gs, gs[:, sub, (half - 2) * 512:(half - 1) * 512], pv[:], AF.Silu, [pv])
                yield
        for sub in range(2):
            for h in range(8):
                S.transpose(pT, pT[:, h * 128:(h + 1) * 128], kdT[:, h, sub * 128:(sub + 1) * 128],
                            k.ident[:], [kdT, k.ident])
            S.copy("act", kdtok, kdtok[:, sub, :, :].rearrange("p h k -> p (h k)"), pT[:], [pT])
            yield

    def gen_B(T0):
        par = T0 % 2
        qdT, kdT, kdtok, ebl, vt, gs = qdT2[par], kdT2[par], kdtok2[par], ebl2[par], vt2[par], gs2[par]
        r0 = T0 * TT
        for sub in range(2):
            banks = [(k.ps[3], k.ps[4], k.ps[3]), (k.ps[5], k.ps[6], k.ps[5])]
            for hg in range(2):
                pA, pB, pC = banks[hg]
                hs = list(range(hg * 4, hg * 4 + 4))
                for i, h in enumerate(hs):
                    S.mm(pA, pA[:, i * 128:i * 128 + 64], kdT[:, h, sub * 128:(sub + 1) * 128],
                         qdT[:, h, sub, 0, 0:64], [kdT, qdT])
                    S.mm(pA, pA[:, i * 128 + 64:(i + 1) * 128], kdT[:, h, sub * 128:(sub + 1) * 128],
                         qdT[:, h, sub, 1, 64:128], [kdT, qdT])
                    S.mm(pB, pB[:, i * 128:(i + 1) * 128], kdtok[0:64, sub, h, :],
                         vt[0:64, sub, h * 128:(h + 1) * 128], [kdtok, vt])
                S.tt("dve", sc4s[hg], sc4s[hg][:].rearrange("p (i t) -> p i t", t=128),
                     pA[:].rearrange("p (i t) -> p i t", t=128),
                     cmask[:].unsqueeze(1).to_broadcast([128, 4, 128]), ALU.mult, [pA, cmask])
            yield
            for hg in range(2):
                pA, pB, pC = banks[hg]
                hs = list(range(hg * 4, hg * 4 + 4))
                c = sub * 2
                for i, h in enumerate(hs):
                    S.act(sts[h], sts[h][:], st32s[h][:], AF.Copy, [st32s[h], ebl], scale=ebl[:, h, c:c + 1])
                for i, h in enumerate(hs):
                    S.stt("dve", st1b[h], st1b[h][:], pB[:, i * 128:(i + 1) * 128], ebl[:, h, c:c + 1],
                          sts[h][:], ALU.mult, ALU.add, [pB, ebl, sts[h]])
                for i, h in enumerate(hs):
                    S.stt("dve", st1[h], st1[h][:], pB[:, i * 128:(i + 1) * 128], ebl[:, h, c:c + 1],
                          sts[h][:], ALU.mult, ALU.add, [pB, ebl, sts[h]])
            yield
            for hg in range(2):
                pA, pB, pC = banks[hg]
                hs = list(range(hg * 4, hg * 4 + 4))
                for i, h in enumerate(hs):
                    o_ap = pC[:, i * 128:(i + 1) * 128]
                    S.mm(pC, o_ap, sc4s[hg][:, i * 128:(i + 1) * 128], vt[:, sub, h * 128:(h + 1) * 128],
                         [sc4s[hg], vt], start=True, stop=False)
                    S.mm(pC, o_ap, qdT[:, h, sub, 0, :], stbs[h][:], [qdT, stbs[h]], start=False, stop=False)
                    S.mm(pC, o_ap, qdT[:, h, sub, 1, :], st1b[h][:], [qdT, st1b[h]], start=False, stop=True)
                for i, h in enumerate(hs):
                    S.mm(pB, pB[:, i * 128:(i + 1) * 128], kdtok[64:128, sub, h, :],
                         vt[64:128, sub, h * 128:(h + 1) * 128], [kdtok, vt])
            yield
            for hg in range(2):
                pA, pB, pC = banks[hg]
                hs = list(range(hg * 4, hg * 4 + 4))
                c = sub * 2 + 1
                for i, h in enumerate(hs):
                    S.act(sts2[h], sts2[h][:], st1[h][:], AF.Copy, [st1[h], ebl], scale=ebl[:, h, c:c + 1])
                for i, h in enumerate(hs):
                    S.stt("dve", stbs[h], stbs[h][:], pB[:, i * 128:(i + 1) * 128], ebl[:, h, c:c + 1],
                          sts2[h][:], ALU.mult, ALU.add, [pB, ebl, sts2[h]])
                for i, h in enumerate(hs):
                    S.stt("dve", st32s[h], st32s[h][:], pB[:, i * 128:(i + 1) * 128], ebl[:, h, c:c + 1],
                          sts2[h][:], ALU.mult, ALU.add, [pB, ebl, sts2[h]])
                S.copy("dve", otok, otok[:, hg * 512:(hg + 1) * 512], pC[:], [pC])
                for i, h in enumerate(hs):
                    S.act(sq, sq[:, 0:128], otok[:, h * 128:(h + 1) * 128], AF.Square, [otok],
                          accum_out=oss[:, h:h + 1], extra_writes=[oss])
            yield
            rstd_from_ss(S, orstd, oss, 128, 8)
            S.tt("dve", otok, otok[:].rearrange("p (h v) -> p h v", v=128),
                 otok[:].rearrange("p (h v) -> p h v", v=128),
                 orstd[:].unsqueeze(2).to_broadcast([128, 8, 128]), ALU.mult, [otok, orstd])
            S.tt("pool", on, on[:], otok[:], gs[:, sub, :], ALU.mult, [otok, gs])
            yield
            for kc in range(8):
                S.transpose(pT2, pT2[:, kc * 128:(kc + 1) * 128], on[:, kc * 128:(kc + 1) * 128],
                            k.ident[:], [on, k.ident])
            S.copy("act", oT, oT[:].rearrange("p c t -> p (c t)"), pT2[:], [pT2])
            yield
            xt = xts[(T0 * 2 + sub) % 4]
            xo_ = xo[sub]
            for half in range(2):
                py = pvs[half % 2]
                for kc in range(8):
                    S.mm(py, py[:], oT[:, kc, :], Wout[:, kc, half * 512:(half + 1) * 512],
                         [oT, Wout], start=(kc == 0), stop=(kc == 7))
                S.tt("dve", xo_, xo_[:, half * 512:(half + 1) * 512], py[:], xt[:, half * 512:(half + 1) * 512],
                     ALU.add, [py, xt])
            S.dma(None, xout[r0 + sub * 128:r0 + (sub + 1) * 128, :], xo_[:], in_t=xo_)
            yield

    for _ in gen_A(0):
        pass
    for T0 in range(ntile):
        if T0 + 1 < ntile:
            interleave(gen_B(T0), gen_A(T0 + 1), 1, 1)
        else:
            for _ in gen_B(T0):
                pass
    S.finish_wait("sp", xo)


def phase_ffn(S, k, l, xnorm, xres, xout, NT, fc0, fc1, final=False):
    TT = 512
    ntile = NT // TT
    nfc = fc1 - fc0
    W = nfc * 128
    same = xres is xnorm
    load_consts(S, k)
    alloc_psum(S, k, bf7=True)
    cols = rows_to_cols(S, k, [k.modd[l, :], k.norm_ffn[l, :]], "fcols")
    gcol = S.sbuf("gcol", [128, 8], F32)
    S.stt("dve", gcol, gcol[:], cols[:, 32:40], 1.0, cols[:, 48:56], ALU.add, ALU.mult, [cols])
    shc = S.sbuf("shc", [128, 8], F32)
    S.copy("dve", shc, shc[:], cols[:, 24:32], [cols])
    ccols = rows_to_cols(S, k, [k.ffn_conv_w[l, 0, :], k.ffn_conv_w[l, 1, :], k.ffn_conv_w[l, 2, :],
                                k.ffn_conv_b[l, :]], "ccols")
    sq = S.sbuf("sq", [128, 1024], F32)
    g2row = sq
    S.dma(g2row, g2row[:], k.modd[l, 5120:6144].partition_broadcast(128))
    if final:
        fnrow = S.sbuf("fnrow", [128, 1024], F32)
        S.dma(fnrow, fnrow[:], k.final_norm[:].partition_broadcast(128))
    Wup = S.sbuf("Wup", [128, 8, 2 * W], BF16)
    Wdn = S.sbuf("Wdn", [128, nfc, 1024], BF16)
    stg = [S.sbuf("stg%d" % i, [128, 1024], F32) for i in range(2)]
    load_weight_bf16(S, Wup, k.ffn_w_up[l][:, fc0 * 128:fc1 * 128], 8, W, stg)
    load_weight_bf16(S, Wup, k.ffn_w_up[l][:, DFF + fc0 * 128:DFF + fc1 * 128], 8, W, stg, col_off=W)
    load_weight_bf16(S, Wdn, k.ffn_w_down[l][fc0 * 128:fc1 * 128, :], nfc, 1024, stg, grow=g2row)
    xts = [S.sbuf("xt%d" % i, [128, 1024], F32) for i in range(2)]
    xrs = [S.sbuf("xr%d" % i, [128, 1024], F32) for i in range(2)]
    xn = S.sbuf("xn", [128, 1024], BF16)
    ss = S.sbuf("ss", [128, 8], F32)
    rstd = S.sbuf("rstd", [128, 8], F32)
    hTs = [S.sbuf("hT%d" % i, [128, 8, TT], BF16) for i in range(2)]
    halo = S.sbuf("halo", [128, nfc, 2], F32)
    S.memset("pool", halo, halo[:], 0.0)
    abuf = [S.sbuf("abuf%d" % i, [128, TT + 2], F32) for i in range(3)]
    c1 = [S.sbuf("c1_%d" % i, [128, TT], F32) for i in range(3)]
    c2 = [S.sbuf("c2_%d" % i, [128, TT], F32) for i in range(3)]
    mTs = [S.sbuf("mT%d" % i, [128, nfc, TT], BF16) for i in range(2)]
    xo = [S.sbuf("xo%d" % i, [128, 1024], F32) for i in range(2)]
    pT = k.ps[0]
    ncnt = [0]

    xns = [xn] + [S.sbuf("xn%d" % i, [128, 1024], BF16) for i in range(1, 4)]
    sss = [S.sbuf("ss%d" % i, [128, 2], F32) for i in range(4)]
    rss = [S.sbuf("rs%d" % i, [128, 2], F32) for i in range(4)]
    pTs = [k.ps[0], k.ps[7]]

    def norm_a(T0, sub):
        i = sub
        xt = xts[sub % 2]
        r = T0 * TT + sub * 128
        S.dma(xt, xt[:], xnorm[r:r + 128, :])
        S.act(sq, sq[:], xt[:], AF.Square, [xt], accum_out=sss[i][:, 0:1], extra_writes=[sss[i]])
        rstd_from_ss(S, rss[i], sss[i], D, 1)
        S.act(xns[i], xns[i][:], xt[:], AF.Copy, [xt, rss[i]], scale=rss[i][:, 0:1])

    def norm_b(T0, sub):
        i = sub
        hT_ = hTs[T0 % 2]
        pT_ = pTs[sub % 2]
        for kc in range(8):
            S.transpose(pT_, pT_[:, kc * 128:(kc + 1) * 128], xns[i][:, kc * 128:(kc + 1) * 128],
                        k.ident[:], [xns[i], k.ident])
        for kc in range(8):
            S.act(hT_, hT_[:, kc, sub * 128:(sub + 1) * 128], pT_[:, kc * 128:(kc + 1) * 128], AF.Identity,
                  [pT_, gcol, shc], scale=gcol[:, kc:kc + 1], bias=shc[:, kc:kc + 1])

    for sub in range(4):
        norm_a(0, sub)
        norm_b(0, sub)
    def down_proj(T0):
        r0 = T0 * TT
        mT = mTs[T0 % 2]
        for sub in range(4):
            xt = xrs[sub % 2]
            S.dma(xt, xt[:], xres[r0 + sub * 128:r0 + (sub + 1) * 128, :])
            xo_ = xo[sub % 2]
            for half in range(2):
                py = k.ps[1 + half]
                for j in range(nfc):
                    S.mm(py, py[:], mT[:, j, sub * 128:(sub + 1) * 128], Wdn[:, j, half * 512:(half + 1) * 512],
                         [mT, Wdn], start=(j == 0), stop=(j == nfc - 1))
                S.tt("dve", xo_, xo_[:, half * 512:(half + 1) * 512], py[:], xt[:, half * 512:(half + 1) * 512],
                     ALU.add, [py, xt])
            if final:
                S.act(sq, sq[:], xo_[:], AF.Square, [xo_], accum_out=ss[:, 1:2], extra_writes=[ss])
                rstd_from_ss(S, rstd, ss[:, 1:2] if False else ss, D, 2)
                S.stt("dve", xo_, xo_[:], xo_[:], rstd[:, 1:2], fnrow[:], ALU.mult, ALU.mult, [xo_, rstd, fnrow])
            S.dma(None, xout[r0 + sub * 128:r0 + (sub + 1) * 128, :], xo_[:], in_t=xo_)

    for T0 in range(ntile):
        r0 = T0 * TT
        hT = hTs[T0 % 2]
        mT = mTs[T0 % 2]
        def st_A(j):
            ab = abuf[j % 3]
            pa = k.ps[1 + (j % 2)]
            fc = fc0 + j
            S.copy("pool", ab, ab[:, 0:2], halo[:, j, :], [halo])
            S.copy("act", ab, ab[:, 2:TT + 2], pa[:], [pa])
            S.copy("pool", halo, halo[:, j, :], ab[:, TT:TT + 2], [ab])
            S.ts("pool", c1[j % 3], c1[j % 3][:], ab[:, 2:TT + 2], ccols[:, 44 + fc:45 + fc],
                 ccols[:, 66 + fc:67 + fc], ALU.mult, ALU.add, reads=[ab, ccols])

        def st_B(j):
            ab = abuf[j % 3]
            fc = fc0 + j
            c1_, c2_ = c1[j % 3], c2[j % 3]
            S.stt("dve", c2_, c2_[:], ab[:, 1:TT + 1], ccols[:, 22 + fc:23 + fc], c1_[:], ALU.mult, ALU.add,
                  [ab, ccols, c1_])
            S.stt("dve", c1_, c1_[:], ab[:, 0:TT], ccols[:, fc:fc + 1], c2_[:], ALU.mult, ALU.add,
                  [ab, ccols, c2_])

        def st_C(j):
            S.act(c2[j % 3], c2[j % 3][:], c1[j % 3][:], AF.Silu, [c1[j % 3]])

        def st_D(j):
            pv = k.ps[3 + (j % 4)]
            S.tt("dve", mT, mT[:, j, :], pv[:], c2[j % 3][:], ALU.mult, [pv, c2[j % 3]])

        for j in range(nfc + 2):
            if j < nfc:
                pa = k.ps[1 + (j % 2)]
                pv = k.ps[3 + (j % 4)]
                for kc in range(8):
                    S.mm(pa, pa[:], Wup[:, kc, j * 128:(j + 1) * 128], hT[:, kc, :], [Wup, hT],
                         start=(kc == 0), stop=(kc == 7))
                for kc in range(8):
                    S.mm(pv, pv[:], Wup[:, kc, W + j * 128:W + (j + 1) * 128], hT[:, kc, :], [Wup, hT],
                         start=(kc == 0), stop=(kc == 7))
                st_A(j)
            if 0 <= j - 1 < nfc:
                st_B(j - 1)
                st_C(j - 1)
            if 0 <= j - 2 < nfc:
                st_D(j - 2)
            if T0 + 1 < ntile:
                if j == 0:
                    for sub_ in range(4):
                        norm_a(T0 + 1, sub_)
                if j in (3, 5, 7, 9):
                    norm_b(T0 + 1, (j - 3) // 2)
            if j == 1 and T0 > 0:
                down_proj(T0 - 1)
    down_proj(ntile - 1)
    S.finish_wait("sp", xo)


def build_program(NT, phases=("pro", "hg", "ffn0", "nsa", "ffn1")):
    nc = bass.Bass("TRN2", target_bir_lowering=False)
    k = K()

    def inp(name, shape):
        return nc.dram_tensor(name, list(shape), F32, kind="ExternalInput").ap()

    k.x = inp("x", [NT, D])
    k.c_col = inp("c_col", [128, 8])
    k.ada_w = inp("ada_w", [2, D, 6 * D])
    k.ada_b = inp("ada_b", [2, 6 * D])
    k.norm_mix = inp("norm_mix", [2, D])
    k.norm_ffn = inp("norm_ffn", [2, D])
    k.final_norm = inp("final_norm", [D])
    k.hg_w_in = inp("hg_w_in", [1, D, 4096])
    k.hg_w_out = inp("hg_w_out", [1, D, D])
    k.hg_gnorm = inp("hg_gnorm", [1, 128])
    k.hg_lb = inp("hg_lb", [2, D])
    k.ffn_w_up = inp("ffn_w_up", [2, D, 2 * DFF])
    k.ffn_conv_w = inp("ffn_conv_w", [2, 3, DFF])
    k.ffn_conv_b = inp("ffn_conv_b", [2, DFF])
    k.ffn_w_down = inp("ffn_w_down", [2, DFF, D])
    k.c_ident = inp("c_ident", [128, 128])
    k.c_bdmask = inp("c_bdmask", [128, 128])
    nsa_declare(nc, k, NT)
    k.out = nc.dram_tensor("out", [NT, D], F32, kind="ExternalOutput").ap()
    k.modd = nc.dram_tensor("modd", [2, 6 * D], F32).ap()
    bufs = [nc.dram_tensor("xs%d" % i, [NT, D], F32).ap() for i in range(4)]
    xc = nc.dram_tensor("xc", [NT, D], F32).ap()
    main = [p for p in phases if p != "pro"]
    with contextlib.ExitStack() as gst:
        S = Sched(nc, gst)
        k.S = S
        plist = []
        if "pro" in phases:
            plist.append(lambda: (alloc_psum(S, k), phase_prologue(S, k)))
        src = k.x
        for i, p in enumerate(main):
            last = (i == len(main) - 1)
            dst = k.out if last else bufs[i]
            if p == "hg":
                plist.append(lambda src=src, dst=dst: phase_hgrn(S, k, 0, src, dst, NT))
            elif p in ("ffn0", "ffn1"):
                l = int(p[3])
                fin = (p == "ffn1")
                plist.append(lambda src=src, l=l: phase_ffn(S, k, l, src, src, xc, NT, 0, 11))
                plist.append(lambda src=src, dst=dst, l=l, fin=fin: phase_ffn(S, k, l, src, xc, dst, NT, 11, 22, final=fin))
            elif p == "nsa":
                import os
                stop = int(os.environ.get("NSA_STOP", "4"))
                plist.append(lambda src=src: phase_nsa_proj(S, k, 1, src, NT))
                if stop >= 2:
                    plist.append(lambda: phase_nsa_cmp(S, k, NT))
                if stop >= 3:
                    plist.append(lambda: phase_nsa_attn(S, k, NT))
                if stop >= 4:
                    plist.append(lambda src=src, dst=dst: phase_nsa_out(S, k, 1, src, dst, NT))
            src = dst
        for i, p in enumerate(plist):
            with contextlib.ExitStack() as st:
                S.stack = st
                p()
                S.barrier()
                S.emit()
                S.phase_end()
    return nc


def host_consts():
    ident = np.eye(128, dtype=np.float32)
    s = np.arange(128)[:, None]
    t = np.arange(128)[None, :]
    bd = ((s // 64 == t // 64) & (s <= t)).astype(np.float32)
    return {"c_ident": ident, "c_bdmask": bd}


NSA_W = 2608
POW_RECIP = False
SLOPES = [2.0 ** (-8.0 * (h + 1) / 16) for h in range(16)]


def nsa_dims(NT):
    ncb = (NT - 32) // 16 + 1
    nsb = NT // 64
    return ncb, nsb


def nsa_host_consts(NT):
    import ml_dtypes
    bf = ml_dtypes.bfloat16
    ncb, nsb = nsa_dims(NT)
    t = np.arange(NT)
    c = {}
    c["c_vrow"] = np.stack([-SLOPES[h] * t for h in range(16)]).astype(bf)
    KT = NT // 128
    i = np.arange(128)
    sb = np.zeros((128, KT * 16), np.float32)
    for kt in range(KT):
        for h in range(16):
            sb[:, kt * 16 + h] = SLOPES[h] * (128 * kt + i)
    c["c_sbias"] = sb
    cb = np.zeros((128, 2 * 16), np.float32)
    for bt in range(2):
        for h in range(16):
            cb[:, bt * 16 + h] = SLOPES[h] * (16 * (128 * bt + i) + 15.5)
    c["c_cbias"] = cb
    c["c_maskd"] = np.where(i[:, None] > i[None, :], NEG, 0.0).astype(bf)
    c["c_maskw"] = np.where(i[None, :] >= i[:, None], NEG, 0.0).astype(bf)
    QT = NT // 512
    cm = np.zeros((QT, 2, 128, 512), np.float32)
    for T in range(QT):
        for bt in range(2):
            n = 128 * bt + i
            tt = 512 * T + np.arange(512)
            cm[T, bt] = np.where(16 * n[:, None] + 31 <= tt[None, :], 0.0, NEG)
    c["c_cmask"] = cm.astype(bf)
    ov = np.zeros((256, 64), np.float32)
    ci = np.arange(256)[:, None] * 16
    sj = np.arange(64)[None, :] * 64
    ov[:] = ((ci <= sj + 63) & (ci + 31 >= sj))
    ov[ncb:] = 0
    ov[:, nsb:] = 0
    c["c_overlap"] = ov.astype(bf)
    blk = np.arange(64)[None, :]
    cur = (t // 64)[:, None]
    valid = blk * 64 <= t[:, None]
    forced = ((blk == 0) | (blk == cur) | (blk == cur - 1)) & valid
    vm = valid.astype(np.float32)
    am = np.where(forced, 1e4, np.where(valid, 0.0, -1.0)).astype(np.float32)
    vm[:, nsb:] = 0.0
    am[:, nsb:] = -1.0
    c["c_vmask"] = vm
    c["c_amask"] = am
    si = np.zeros((64, NT), np.float32)
    si[0] = 1.0
    for j in range(1, 64):
        si[j] = (t // 64 == j)
    c["c_selind"] = si.astype(bf)
    kw = np.zeros((64, NT), np.float32)
    kw[0] = 1.0
    c["c_kwrows"] = kw.astype(bf)
    return c


def nsa_declare(nc, k, NT):
    def inp(name, shape, dt=F32):
        return nc.dram_tensor(name, list(shape), dt, kind="ExternalInput").ap()
    QT = NT // 512
    KT = NT // 128
    k.nsa_w_in = inp("nsa_w_in", [1, D, NSA_W])
    k.nsa_w_out = inp("nsa_w_out", [1, D, D])
    k.nsa_cmp_pe = inp("nsa_cmp_pe", [1, 2, 32, 64])
    k.nsa_cmp_w1 = inp("nsa_cmp_w1", [1, 2, 2048, 64])
    k.nsa_cmp_w2 = inp("nsa_cmp_w2", [1, 2, 64, 64])
    k.c_vrow = inp("c_vrow", [16, NT], BF16)
    k.c_sbias = inp("c_sbias", [128, KT * 16])
    k.c_cbias = inp("c_cbias", [128, 32])
    k.c_maskd = inp("c_maskd", [128, 128], BF16)
    k.c_maskw = inp("c_maskw", [128, 128], BF16)
    k.c_cmask = inp("c_cmask", [QT, 2, 128, 512], BF16)
    k.c_overlap = inp("c_overlap", [256, 64], BF16)
    k.c_vmask = inp("c_vmask", [NT, 64])
    k.c_amask = inp("c_amask", [NT, 64])
    k.c_selind = inp("c_selind", [64, NT], BF16)
    k.c_kwrows = inp("c_kwrows", [64, NT], BF16)
    k.qT_d = nc.dram_tensor("qT_d", [1024, NT], BF16).ap()
    k.kT_d = nc.dram_tensor("kT_d", [4, 256, NT], BF16).ap()
    k.vtok_d = nc.dram_tensor("vtok_d", [2, NT, 256], BF16).ap()
    k.gT_d = nc.dram_tensor("gT_d", [48, NT], F32).ap()
    k.kcT_d = nc.dram_tensor("kcT_d", [4, 64, 256], BF16).ap()
    k.vc_d = nc.dram_tensor("vc_d", [4, 256, 64], BF16).ap()
    k.oT_d = nc.dram_tensor("oT_d", [1024, NT], BF16).ap()


def phase_nsa_proj(S, k, l, xin, NT):
    TT = 512
    ntile = NT // TT
    load_consts(S, k)
    alloc_psum(S, k)
    cols = rows_to_cols(S, k, [k.modd[l, :], k.norm_mix[l, :]], "ncols")
    gcol = S.sbuf("gcol", [128, 8], F32)
    S.stt("dve", gcol, gcol[:], cols[:, 8:16], 1.0, cols[:, 48:56], ALU.add, ALU.mult, [cols])
    W = S.sbuf("Wn", [128, 8, NSA_W], BF16)
    stg = [S.sbuf("stg%d" % i, [128, 1024], F32) for i in range(3)]
    load_weight_bf16(S, W, k.nsa_w_in[0], 8, NSA_W, stg)
    xts = [S.sbuf("xt%d" % i, [128, 1024], F32) for i in range(2)]
    xns = [S.sbuf("xn%d" % i, [128, 1024], BF16) for i in range(4)]
    sq = S.sbuf("sq", [128, 1024], F32)
    sss = [S.sbuf("ss%d" % i, [128, 2], F32) for i in range(4)]
    rss = [S.sbuf("rs%d" % i, [128, 2], F32) for i in range(4)]
    hTs = [S.sbuf("hT%d" % i, [128, 8, TT], BF16) for i in range(2)]
    fsb = [S.sbuf("fsb%d" % i, [128, TT], BF16) for i in range(3)]
    vsb = [S.sbuf("vsb%d" % i, [128, 512], BF16) for i in range(2)]
    gsb = [S.sbuf("gsb%d" % i, [48, TT], F32) for i in range(2)]
    pT = k.ps[0]
    outs = fsb + vsb + gsb
    n = 0

    def norm_a(T0, sub):
        xt = xts[sub % 2]
        r = T0 * TT + sub * 128
        S.dma(xt, xt[:], xin[r:r + 128, :])
        S.act(sq, sq[:], xt[:], AF.Square, [xt], accum_out=sss[sub][:, 0:1], extra_writes=[sss[sub]])
        rstd_from_ss(S, rss[sub], sss[sub], D, 1)
        S.act(xns[sub], xns[sub][:], xt[:], AF.Copy, [xt, rss[sub]], scale=rss[sub][:, 0:1])

    def norm_b(T0, sub):
        hT_ = hTs[T0 % 2]
        for kc in range(8):
            S.transpose(pT, pT[:, kc * 128:(kc + 1) * 128], xns[sub][:, kc * 128:(kc + 1) * 128],
                        k.ident[:], [xns[sub], k.ident])
        for kc in range(8):
            S.act(hT_, hT_[:, kc, sub * 128:(sub + 1) * 128], pT[:, kc * 128:(kc + 1) * 128], AF.Identity,
                  [pT, gcol, cols], scale=gcol[:, kc:kc + 1], bias=cols[:, kc:kc + 1])

    for sub in range(4):
        norm_a(0, sub)
        norm_b(0, sub)
    for T0 in range(ntile):
        r0 = T0 * TT
        hT = hTs[T0 % 2]
        fm = [(hp * 128, ("q", hp)) for hp in range(8)]
        for kind, c0 in ((0, 1024), (1, 1280), (2, 1536), (3, 2048)):
            for cpart in range(2):
                fm.append((c0 + cpart * 128, ("k", kind, cpart)))
        for gi, (c0, tag) in enumerate(fm):
            if T0 + 1 < ntile:
                if gi == 0:
                    for sub_ in range(4):
                        norm_a(T0 + 1, sub_)
                if gi in (3, 6, 9, 12):
                    norm_b(T0 + 1, (gi - 3) // 3)
            ps = k.ps[1 + (n % 3)]
            f = fsb[n % 3]
            n += 1
            for kc in range(8):
                S.mm(ps, ps[:], W[:, kc, c0:c0 + 128], hT[:, kc, :], [W, hT], start=(kc == 0), stop=(kc == 7))
            if tag[0] == "q":
                S.act(f, f[:], ps[:], AF.Copy, [ps], scale=0.125)
                S.dma(None, k.qT_d[tag[1] * 128:(tag[1] + 1) * 128, r0:r0 + TT], f[:], in_t=f)
            else:
                S.copy("dve", f, f[:], ps[:], [ps])
                S.dma(None, k.kT_d[tag[1], tag[2] * 128:(tag[2] + 1) * 128, r0:r0 + TT], f[:], in_t=f)
        for sub in range(4):
            ps = k.ps[4 + (sub % 2)]
            v = vsb[sub % 2]
            for j, c0 in enumerate((1792, 2304)):
                for kc in range(8):
                    S.mm(ps, ps[:, j * 256:(j + 1) * 256], hT[:, kc, sub * 128:(sub + 1) * 128],
                         W[:, kc, c0:c0 + 256], [hT, W], start=(kc == 0), stop=(kc == 7))
            S.copy("dve", v, v[:], ps[:], [ps])
            for j in range(2):
                S.dma(None, k.vtok_d[j, r0 + sub * 128:r0 + (sub + 1) * 128, :], v[:, j * 256:(j + 1) * 256], in_t=v)
        ps = k.ps[6]
        g_ = gsb[T0 % 2]
        for kc in range(8):
            S.mm(ps, ps[0:48, :], W[:, kc, 2560:2608], hT[:, kc, :], [W, hT], start=(kc == 0), stop=(kc == 7))
        S.act(g_, g_[:], ps[0:48, :], AF.Sigmoid, [ps])
        S.dma(None, k.gT_d[:, r0:r0 + TT], g_[:], in_t=g_)
    S.finish_wait("sp", outs)


def phase_nsa_cmp(S, k, NT):
    ncb, nsb = nsa_dims(NT)
    load_consts(S, k)
    alloc_psum(S, k)
    xc = S.sbuf("xc", [64, 2, 4, NT], BF16)
    for kv in range(2):
        for g in range(4):
            S.dma(xc, xc[:, kv, g, :], k.kT_d[kv, g * 64:(g + 1) * 64, :])
    w1f = S.sbuf("w1f", [64, 2, 32, 64], F32)
    w1 = S.sbuf("w1", [64, 2, 32, 64], BF16)
    for kv in range(2):
        S.dma(w1f, w1f[:, kv, :, :], k.nsa_cmp_w1[0, kv].rearrange("(l d) e -> d l e", d=64))
    S.copy("pool", w1, w1[:], w1f[:], [w1f])
    w2f = S.sbuf("w2f", [64, 2, 64], F32)
    w2p = S.sbuf("w2p", [64, 2, 128], BF16)
    for kv in range(2):
        S.dma(w2f, w2f[:, kv, :], k.nsa_cmp_w2[0, kv])
    S.memset("pool", w2p, w2p[:], 0.0)
    S.copy("pool", w2p, w2p[:, :, 64:128], w2f[:], [w2f])
    pef = S.sbuf("pef", [32, 2, 64], F32)
    for kv in range(2):
        S.dma(pef, pef[:, kv, :], k.nsa_cmp_pe[0, kv])
    peT = S.sbuf("peT", [64, 2, 32], BF16)
    ps = k.ps[1]
    for kv in range(2):
        S.mm(ps, ps[0:64, kv * 32:(kv + 1) * 32], pef[:, kv, :], k.identf[0:32, 0:32], [pef, k.identf])
    S.copy("dve", peT, peT[:].rearrange("p a l -> p (a l)"), ps[0:64, 0:64], [ps])
    bias = S.sbuf("cbias", [64, 2], F32)
    ps = k.ps[2]
    for kv in range(2):
        for l in range(32):
            S.mm(ps, ps[0:64, kv:kv + 1], w1[:, kv, l, :], peT[:, kv, l:l + 1], [w1, peT],
                 start=(l == 0), stop=(l == 31))
    S.copy("dve", bias, bias[:], ps[0:64, 0:2], [ps])
    hid = [S.sbuf("hid%d" % i, [64, 256], BF16) for i in range(2)]
    osb = [S.sbuf("osb%d" % i, [128, 256], BF16) for i in range(2)]
    n = 0
    for kv in range(2):
        for g in range(4):
            ph = k.ps[3 + (n % 2)]
            hd = hid[n % 2]
            ob = osb[n % 2]
            for l in range(32):
                rhs = xc[:, kv, g, l:l + 16 * (ncb - 1) + 1:16]
                S.mm(ph, ph[0:64, 0:ncb], w1[:, kv, l, :], rhs, [w1, xc], start=(l == 0), stop=(l == 31))
            S.act(hd, hd[:, 0:ncb], ph[0:64, 0:ncb], AF.Silu, [ph, bias], bias=bias[:, kv:kv + 1])
            po = k.ps[5 + (n % 2)]
            if kv == 0:
                S.mm(po, po[:, 0:ncb], w2p[:, 0, :], hd[:, 0:ncb], [w2p, hd])
                S.copy("dve", ob, ob[64:128, 0:ncb], po[64:128, 0:ncb], [po])
                S.dma(None, k.kcT_d[g, :, 0:ncb], ob[64:128, 0:ncb], in_t=ob)
            else:
                for bt in range((ncb + 127) // 128):
                    nb = min(128, ncb - bt * 128)
                    S.mm(po, po[0:nb, bt * 64:(bt + 1) * 64], hd[:, bt * 128:bt * 128 + nb], w2p[:, 1, 64:128],
                         [hd, w2p])
                    S.copy("dve", ob, ob[0:nb, bt * 64:(bt + 1) * 64], po[0:nb, bt * 64:(bt + 1) * 64], [po])
                    S.dma(None, k.vc_d[g, bt * 128:bt * 128 + nb, :], ob[0:nb, bt * 64:(bt + 1) * 64], in_t=ob)
            n += 1
    S.finish_wait("sp", osb)


def phase_nsa_attn(S, k, NT):
    ncb, nsb = nsa_dims(NT)
    QT = NT // 512
    KT = NT // 128
    NBT = (ncb + 127) // 128
    load_consts(S, k)
    alloc_psum(S, k)
    sbias = S.sbuf("sbias", [128, KT * 16], F32)
    S.dma(sbias, sbias[:], k.c_sbias[:, :])
    cbias = S.sbuf("cbias", [128, 32], F32)
    S.dma(cbias, cbias[:], k.c_cbias[:, :])
    maskd = S.sbuf("maskd", [128, 128], BF16)
    S.dma(maskd, maskd[:], k.c_maskd[:, :])
    maskw = S.sbuf("maskw", [128, 128], BF16)
    S.dma(maskw, maskw[:], k.c_maskw[:, :])
    ovl = S.sbuf("ovl", [128, 2, 64], BF16)
    S.dma(ovl, ovl[:], k.c_overlap.rearrange("(bt p) j -> p bt j", p=128))
    Ks = S.sbuf("Ks", [128, NT], BF16)
    Kw = S.sbuf("Kw", [128, NT], BF16)
    Kc = S.sbuf("Kc", [128, 256], BF16)
    Vs = S.sbuf("Vs", [128, KT, 128], BF16)
    Vw = S.sbuf("Vw", [128, KT, 128], BF16)
    Vc = S.sbuf("Vc", [128, 2, 128], BF16)
    S.memset("pool", Vs, Vs[:], 1.0)
    S.memset("pool", Vw, Vw[:], 1.0)
    S.memset("pool", Vc, Vc[:], 1.0)
    S.memset("pool", Kc, Kc[:], 0.0)
    S.dma(Ks, Ks[0:64, :], k.c_selind[:, :])
    S.dma(Kw, Kw[0:64, :], k.c_kwrows[:, :])
    S.dma(Kc, Kc[0:64, :], k.c_kwrows[:, 0:256])
    QaT = [S.sbuf("Qa%d" % i, [128, 4, 512], BF16) for i in range(2)]
    Qav = [[S.view("Qa%d_%d" % (i, hh), QaT[i].ap[:, hh, :]) for hh in range(4)] for i in range(2)]
    for q in QaT:
        S.memset("pool", q, q[:], 0.0)
    for i in range(2):
        for hh in range(4):
            Qav[i][hh].last_w = QaT[i].last_w
    gbT = [S.sbuf("gb%d" % i, [128, 12, 512], F32) for i in range(2)]
    Pt = [S.sbuf("Pt%d" % i, [128, 512], BF16) for i in range(6)]
    cmk = [S.sbuf("cmk%d" % i, [128, 512], BF16) for i in range(2)]
    rd = [S.sbuf("rd%d" % i, [128, 512], F32) for i in range(2)]
    coef = [S.sbuf("coef%d" % i, [128, 512], F32) for i in range(2)]
    tmp = [S.sbuf("tmp%d" % i, [128, 512], F32) for i in range(2)]
    oacc = [S.sbuf("oacc%d" % i, [128, 512], F32) for i in range(4)]
    osbT = [S.sbuf("osb%d" % i, [128, 4, 512], BF16) for i in range(2)]
    impacc = S.sbuf("impacc", [128, 512], F32)
    negone = S.sbuf("negone", [128, 512], F32)
    S.memset("pool", negone, negone[:], -1.0)
    vm = S.sbuf("vm", [128, 4, 64], F32)
    am = S.sbuf("am", [128, 4, 64], F32)
    sc = S.sbuf("sc", [128, 4, 64], F32)
    sc2 = S.sbuf("sc2", [128, 64], F32)
    mx = S.sbuf("mx", [128, 8], F32)
    thr = S.sbuf("thr", [128, 4], F32)
    selb = S.sbuf("selb", [128, 4, 64], BF16)
    pT = k.ps[0]
    pSs = [k.ps[1], k.ps[2], k.ps[6], k.ps[5]]
    pOs = [k.ps[3], k.ps[7]]
    pI, pTk = k.ps[4], k.ps[5]
    cnt = {"s": 0, "p": 0, "r": 0, "cm": 0, "o": 0, "po": 0}
    NP = len(Pt)
    LOOK = 3
    pend = []
    cur = {}

    def finish_branch(pO, hh, br, first, want_imp):
        i = cnt["r"] % 2
        cnt["r"] += 1
        r_, c_, t_ = rd[i], coef[i], tmp[i]
        gbt = cur["gb"]
        if POW_RECIP:
            S.ts("dve", t_, t_[64:128, :], pO[64:128, :], 1e-30, None, ALU.add, reads=[pO])
            S.tt("pool", r_, r_[64:128, :], t_[64:128, :], negone[64:128, :], ALU.pow, [t_, negone])
        else:
            S.act(r_, r_[64:128, :], pO[64:128, :], AF.Ln, [pO], bias=(1e-30 if br == 0 else 0.0))
            S.act(r_, r_[64:128, :], r_[64:128, :], AF.Exp, [r_], scale=-1.0)
        S.tt("dve", c_, c_[64:128, :], r_[64:128, :], gbt[64:128, 3 * hh + br, :], ALU.mult, [r_, gbt])
        if first:
            S.tt("dve", oacc[hh], oacc[hh][0:64, :], pO[0:64, :], c_[64:128, :], ALU.mult, [pO, c_])
        else:
            S.tt("dve", t_, t_[0:64, :], pO[0:64, :], c_[64:128, :], ALU.mult, [pO, c_])
            if br == 1:
                ob_ = cur["ob"]
                S.tt("pool", ob_, ob_[0:64, hh, :], oacc[hh][0:64, :], t_[0:64, :], ALU.add, [oacc[hh], t_])
            else:
                S.tt("pool", oacc[hh], oacc[hh][0:64, :], oacc[hh][0:64, :], t_[0:64, :], ALU.add, [oacc[hh], t_])
        if want_imp:
            if hh == 0:
                S.tt("dve", impacc, impacc[0:64, :], pI[0:64, :], r_[64:128, :], ALU.mult, [pI, r_])
            else:
                S.tt("dve", t_, t_[0:64, :], pI[0:64, :], r_[64:128, :], ALU.mult, [pI, r_])
                S.tt("pool", impacc, impacc[0:64, :], impacc[0:64, :], t_[0:64, :], ALU.add, [impacc, t_])

    def emit_pv(item):
        (pO, lhsV, P, np_, clo, chi, first, last, ovl_ap, cb, vt_) = item
        S.mm(pO, pO[:, clo:chi], lhsV, P[0:np_, clo:chi], [vt_, P], start=first, stop=last)
        if ovl_ap is not None:
            S.mm(pI, pI[0:64, clo:chi], ovl_ap, P[0:np_, clo:chi], [ovl, P], start=first, stop=last)
        if cb is not None:
            cb()

    def push(item):
        pend.append(item)
        while len(pend) > LOOK:
            emit_pv(pend.pop(0))

    def flush():
        while pend:
            emit_pv(pend.pop(0))

    def attend(hh, h, Kt, Vt, tiles, br, first_branch):
        pO = pOs[cnt["po"] % 2]
        cnt["po"] += 1
        Qh = cur["Qa"][hh]
        for idx, (kt, clo, chi, masks) in enumerate(tiles):
            pS = pSs[cnt["s"] % len(pSs)]
            cnt["s"] += 1
            S.mm(pS, pS[:, clo:chi], Kt[:, kt * 128:(kt + 1) * 128], Qh[:, clo:chi], [Kt, Qh],
                 start=True, stop=(len(masks) == 0))
            for mi, (mk, c0) in enumerate(masks):
                S.mm(pS, pS[:, c0:c0 + 128], k.ident[:], mk[:], [k.ident, mk], start=False,
                     stop=(mi == len(masks) - 1))
            P = Pt[cnt["p"] % NP]
            cnt["p"] += 1
            S.act(P, P[:, clo:chi], pS[:, clo:chi], AF.Exp, [pS, sbias], bias=sbias[:, kt * 16 + h:kt * 16 + h + 1])
            last = (idx == len(tiles) - 1)
            cb = (lambda pO=pO, hh=hh, br=br, fb=first_branch: finish_branch(pO, hh, br, fb, False)) if last else None
            push((pO, Vt[:, kt, :], P, 128, clo, chi, idx == 0, last, None, cb, Vt))

    units = [(g, T) for g in range(4) for T in range(QT)]

    def load_unit(ui):
        g, T = units[ui]
        T0 = 512 * T
        par = ui % 2
        S.op("sp", lambda e: e.dma_start(out=gbT[par][64:128, :, :],
                                         in_=k.gT_d[12 * g:12 * g + 12, T0:T0 + 512].partition_broadcast(64)),
             [], [gbT[par]], dma_sem_tile=gbT[par])
        S.op("sp", lambda e: e.dma_start(out=QaT[par][64:128, :, :],
                                         in_=k.qT_d[256 * g:256 * g + 256, T0:T0 + 512].rearrange("(hh d) t -> d hh t", d=64)),
             [], Qav[par], dma_sem_tile=Qav[par][0])
        S.op("sp", lambda e: e.dma_start(out=QaT[par][0:1, :, :], in_=k.c_vrow[4 * g:4 * g + 4, T0:T0 + 512].rearrange("(o h) t -> o h t", o=1)),
             [], Qav[par], dma_sem_tile=Qav[par][0])

    def load_group(g):
        S.dma(Ks, Ks[64:128, :], k.kT_d[2, g * 64:(g + 1) * 64, :])
        S.dma(Kw, Kw[64:128, :], k.kT_d[3, g * 64:(g + 1) * 64, :])
        S.dma(Kc, Kc[64:128, 0:ncb], k.kcT_d[g, :, 0:ncb])
        for k0 in range(0, KT, 8):
            k1 = min(KT, k0 + 8)
            S.dma(Vs, Vs[:, k0:k1, 0:64],
                  k.vtok_d[0][k0 * 128:k1 * 128, g * 64:(g + 1) * 64].rearrange("(kt p) d -> p kt d", p=128))
            S.dma(Vw, Vw[:, k0:k1, 0:64],
                  k.vtok_d[1][k0 * 128:k1 * 128, g * 64:(g + 1) * 64].rearrange("(kt p) d -> p kt d", p=128))
        for bt in range(NBT):
            nb = min(128, ncb - bt * 128)
            S.dma(Vc, Vc[0:nb, bt, 0:64], k.vc_d[g, bt * 128:bt * 128 + nb, :])

    load_unit(0)
    for ui, (g, T) in enumerate(units):
        if T == 0:
            load_group(g)
        T0 = 512 * T
        par = ui % 2
        cur["Qa"] = Qav[par]
        cur["gb"] = gbT[par]
        cur["ob"] = osbT[cnt["o"] % 2]
        cnt["o"] += 1
        Qa = Qav[par]
        bts = []
        for bt in range(NBT):
            nb = min(128, ncb - bt * 128)
            n_lo, n_hi = 128 * bt, 128 * bt + nb - 1
            if 16 * n_lo + 31 > T0 + 511:
                continue
            partial = 16 * n_hi + 31 > T0
            bts.append((bt, nb, partial))
        cms = {}
        for (bt, nb, partial) in bts:
            if partial:
                cm_ = cmk[cnt["cm"] % 2]
                cnt["cm"] += 1
                S.dma(cm_, cm_[:], k.c_cmask[T, bt])
                cms[bt] = cm_
        S.dma(vm, vm[:], k.c_vmask[T0:T0 + 512, :].rearrange("(n p) j -> p n j", p=128))
        S.dma(am, am[:], k.c_amask[T0:T0 + 512, :].rearrange("(n p) j -> p n j", p=128))
        for hh in range(4):
            h = 4 * g + hh
            pO = pOs[cnt["po"] % 2]
            cnt["po"] += 1
            for bi, (bt, nb, partial) in enumerate(bts):
                pS = pSs[cnt["s"] % len(pSs)]
                cnt["s"] += 1
                S.mm(pS, pS[0:nb, :], Kc[:, bt * 128:bt * 128 + nb], Qa[hh][:, :], [Kc, Qa[hh]],
                     start=True, stop=(not partial))
                if partial:
                    cm_ = cms[bt]
                    S.mm(pS, pS[0:nb, :], k.ident[0:nb, 0:nb], cm_[0:nb, :], [k.ident, cm_], start=False, stop=True)
                P = Pt[cnt["p"] % NP]
                cnt["p"] += 1
                S.act(P, P[0:nb, :], pS[0:nb, :], AF.Exp, [pS, cbias],
                      bias=cbias[0:nb, bt * 16 + h:bt * 16 + h + 1])
                last = (bi == len(bts) - 1)
                cb = (lambda pO=pO, hh=hh: finish_branch(pO, hh, 0, True, True)) if last else None
                push((pO, Vc[0:nb, bt, :], P, nb, 0, 512, bi == 0, last, ovl[0:nb, bt, :], cb, Vc))
        for hh in range(4):
            h = 4 * g + hh
            tiles = []
            for kt in range(max(0, 4 * T - 4), 4 * T + 4):
                m = kt - 4 * T
                n_lo, n_hi = max(m, 0), min(m + 4, 3)
                masks = []
                if m >= 0:
                    masks.append((maskd, 128 * m))
                if m <= -1:
                    masks.append((maskw, 128 * (m + 4)))
                tiles.append((kt, 128 * n_lo, 128 * (n_hi + 1), masks))
            tiles.sort(key=lambda tl: -(tl[2] - tl[1]))
            attend(hh, h, Kw, Vw, tiles, 2, False)
            if hh == 1:
                for n in range(4):
                    S.mm(pTk, pTk[:, n * 64:(n + 1) * 64], impacc[0:64, n * 128:(n + 1) * 128],
                         k.identf[0:64, 0:64], [impacc, k.identf])
                S.tt("dve", sc, sc[:].rearrange("p n j -> p (n j)"), pTk[:, 0:256],
                     vm[:].rearrange("p n j -> p (n j)"), ALU.mult, [pTk, vm])
                S.tt("pool", sc, sc[:], sc[:], am[:], ALU.add, [sc, am])
                for n in range(4):
                    S.op("dve", lambda e, n=n: e.max(mx[:], sc[:, n, :]), [sc], [mx])
                    S.op("dve", lambda e, n=n: e.match_replace(sc2[:], mx[:], sc[:, n, :], -1e9), [sc, mx], [sc2])
                    S.op("dve", lambda e: e.max(mx[:], sc2[:]), [sc2], [mx])
                    S.ts("dve", thr, thr[:, n:n + 1], mx[:, 7:8], -0.5, None, ALU.max, reads=[mx])
                    S.ts("dve", sc2, sc2[:], sc[:, n, :], thr[:, n:n + 1], None, ALU.is_ge, reads=[sc, thr])
                    S.ts("dve", selb, selb[:, n, :], sc2[:], -1.0, -NEG, ALU.add, ALU.mult, reads=[sc2])
        if ui + 1 < len(units):
            load_unit(ui + 1)
        for n in range(4):
            S.transpose(pT, pT[0:64, n * 128:(n + 1) * 128], selb[:, n, :], k.ident[:], [selb, k.ident])
        flush()
        for hh in range(4):
            S.copy("act", Qa[hh], Qa[hh][0:64, :], pT[0:64, 0:512], [pT])
        S.op("sp", lambda e, par=par, g=g, T0=T0: e.dma_start(
            out=QaT[par][0:1, :, :], in_=k.c_vrow[4 * g:4 * g + 4, T0:T0 + 512].rearrange("(o h) t -> o h t", o=1)),
            [], Qa, dma_sem_tile=Qa[0])
        for hh in range(4):
            h = 4 * g + hh
            tiles = []
            for kt in range(4 * T + 4):
                j = kt - 4 * T
                if j < 0:
                    tiles.append((kt, 0, 512, []))
                else:
                    tiles.append((kt, 128 * j, 512, [(maskd, 128 * j)]))
            attend(hh, h, Ks, Vs, tiles, 1, False)
        flush()
        ob = cur["ob"]
        S.dma(None, k.oT_d[256 * g:256 * g + 256, T0:T0 + 512].rearrange("(hh d) t -> d hh t", d=64),
              ob[0:64, :, :], in_t=ob)
    S.finish_wait("sp", osbT)


def phase_nsa_out(S, k, l, xin, xout, NT):
    TT = 512
    ntile = NT // TT
    load_consts(S, k)
    alloc_psum(S, k)
    g1row = S.sbuf("g1row", [128, 1024], F32)
    S.dma(g1row, g1row[:], k.modd[l, 2048:3072].partition_broadcast(128))
    Wo = S.sbuf("Wo", [128, 8, 1024], BF16)
    stg = [S.sbuf("stg%d" % i, [128, 1024], F32) for i in range(2)]
    load_weight_bf16(S, Wo, k.nsa_w_out[0], 8, 1024, stg, grow=g1row)
    oT = [S.sbuf("oT%d" % i, [128, 8, TT], BF16) for i in range(2)]
    xts = [S.sbuf("xt%d" % i, [128, 1024], F32) for i in range(2)]
    xo = [S.sbuf("xo%d" % i, [128, 1024], F32) for i in range(2)]
    n = 0
    for T0 in range(ntile):
        r0 = T0 * TT
        o_ = oT[T0 % 2]
        S.dma(o_, o_[:], k.oT_d[:, r0:r0 + TT].rearrange("(kc p) t -> p kc t", p=128))
        for sub in range(4):
            xt = xts[sub % 2]
            xo_ = xo[sub % 2]
            S.dma(xt, xt[:], xin[r0 + sub * 128:r0 + (sub + 1) * 128, :])
            for half in range(2):
                py = k.ps[1 + (n % 4)]
                n += 1
                for kc in range(8):
                    S.mm(py, py[:], o_[:, kc, sub * 128:(sub + 1) * 128], Wo[:, kc, half * 512:(half + 1) * 512],
                         [o_, Wo], start=(kc == 0), stop=(kc == 7))
                S.tt("dve", xo_, xo_[:, half * 512:(half + 1) * 512], py[:], xt[:, half * 512:(half + 1) * 512],
                     ALU.add, [py, xt])
            S.dma(None, xout[r0 + sub * 128:r0 + (sub + 1) * 128, :], xo_[:], in_t=xo_)
    S.finish_wait("sp", xo)


W_KEYS = ["ada_w", "ada_b", "norm_mix", "norm_ffn", "final_norm", "hg_w_in", "hg_w_out", "hg_gnorm", "hg_lb",
          "ffn_w_up", "ffn_conv_w", "ffn_conv_b", "ffn_w_down", "nsa_w_in", "nsa_w_out", "nsa_cmp_pe",
          "nsa_cmp_w1", "nsa_cmp_w2"]


def make_in_map(inp, x, c, NT, consts=None):
    im = {"x": np.ascontiguousarray(x, dtype=np.float32),
          "c_col": np.ascontiguousarray(np.asarray(c, dtype=np.float32).reshape(8, 128).T)}
    for k_ in W_KEYS:
        im[k_] = np.asarray(inp[k_], dtype=np.float32)
    if consts is None:
        consts = dict(host_consts())
        consts.update(nsa_host_consts(NT))
    im.update(consts)
    return im


_CACHE = {}


def kernel(**inputs):
    x = np.asarray(inputs["x"], dtype=np.float32)
    c = np.asarray(inputs["c"], dtype=np.float32)
    B, NT, _ = x.shape
    if "nc" not in _CACHE:
        _CACHE["nc"] = build_program(NT)
        consts = dict(host_consts())
        consts.update(nsa_host_consts(NT))
        _CACHE["consts"] = consts
    nc = _CACHE["nc"]
    in_maps = [make_in_map(inputs, x[b], c[b], NT, _CACHE["consts"]) for b in range(B)]
    res = run_bass_kernel_spmd(nc, in_maps, core_ids=list(range(B)))
    out = np.stack([np.asarray(r["out"], dtype=np.float32) for r in res.results], axis=0)
    return out
```
